# Optimizing a Trainium2 kernel written in Bass

```python
import jax, jax.numpy as jnp
from jax import lax
import numpy as np

D_MODEL = 1024
BATCH = 8
SEQ = 2048
DEPTH = 2
DEC_BATCH = 16
DEC_SEQ = 32
PAST_LEN = 2048

CHUNK = 64
HEAD_DIM = 64
N_EVEN = (DEPTH + 1) // 2
N_ODD = DEPTH // 2
NORM_EPS = 1e-6
ROPE_THETA = 500000.0
ROPE_DIM = HEAD_DIM // 4
FOX_HEADS = 8
FOX_DIM = FOX_HEADS * HEAD_DIM
FOX_COLS = 3 * FOX_DIM + FOX_HEADS
FOX_BLOCK = 128
RWKV_HEADS = 8
RWKV_DIM = RWKV_HEADS * HEAD_DIM
DECAY_LORA = 64
ICLR_LORA = 64
GATE_LORA = 128
RWKV_COLS = 3 * RWKV_DIM + DECAY_LORA + ICLR_LORA + GATE_LORA
RWKV_GN_EPS = 64e-5
SWA_HEADS = 8
SWA_KV_HEADS = 2
SWA_GROUP = SWA_HEADS // SWA_KV_HEADS
SWA_Q = SWA_HEADS * HEAD_DIM
SWA_KV = SWA_KV_HEADS * HEAD_DIM
SWA_COLS = SWA_Q + 2 * SWA_KV
WINDOW = 128
WINDOW_CHUNKS = WINDOW // CHUNK
SGU_GROUPS = 8
SGU_DIM = SGU_GROUPS * HEAD_DIM
SGU_CHUNK = 128
SGU_COLS = 2 * SGU_DIM
EVEN_COLS = FOX_COLS + RWKV_COLS
ODD_COLS = SWA_COLS + SGU_COLS
MIX_OUT = FOX_DIM + RWKV_DIM
MEM_TOKENS = 256
MEM_HEADS = 4
MEM_DIM = MEM_HEADS * HEAD_DIM
D_FF = 2816

kernel_name = "hybrid_streaming_encoder_step"


def rmsnorm(x, g):
    xf = x.astype(jnp.float32)
    y = xf * lax.rsqrt(jnp.mean(xf * xf, axis=-1, keepdims=True) + NORM_EPS)
    return (y * g.astype(jnp.float32)).astype(x.dtype)


def partial_rope(x, pos):
    half = ROPE_DIM // 2
    inv_freq = jnp.power(ROPE_THETA, -jnp.arange(half, dtype=jnp.float32) / half)
    ang = pos.astype(jnp.float32)[:, None] * inv_freq[None, :]
    cos, sin = jnp.cos(ang)[:, None, :], jnp.sin(ang)[:, None, :]
    xr = x[..., :ROPE_DIM].astype(jnp.float32)
    x1, x2 = xr[..., :half], xr[..., half:]
    rot = jnp.concatenate([x1 * cos - x2 * sin, x2 * cos + x1 * sin], axis=-1)
    return jnp.concatenate([rot.astype(x.dtype), x[..., ROPE_DIM:]], axis=-1)


def swiglu(x, w_gate, w_up, w_down):
    return (jax.nn.silu(x @ w_gate) * (x @ w_up)) @ w_down


def fox_project(h, b_f, q_norm, k_norm):
    B, T, _ = h.shape
    shp = (B, T, FOX_HEADS, HEAD_DIM)
    q = rmsnorm(h[..., :FOX_DIM].reshape(shp), q_norm)
    k = rmsnorm(h[..., FOX_DIM:2 * FOX_DIM].reshape(shp), k_norm)
    v = h[..., 2 * FOX_DIM:3 * FOX_DIM].reshape(shp)
    log_f = jax.nn.log_sigmoid((h[..., 3 * FOX_DIM:] + b_f).astype(jnp.float32))
    return q, k, v, log_f


def fox_attend(q, k, v, cum_q, cum_k, q_pos, k_pos):
    logits = jnp.einsum("bqhd,bkhd->bhqk", q, k).astype(jnp.float32) * HEAD_DIM ** -0.5
    bias = jnp.transpose(cum_q, (0, 2, 1))[..., :, None] - jnp.transpose(cum_k, (0, 2, 1))[..., None, :]
    causal = k_pos[None, :] <= q_pos[:, None]
    p = jax.nn.softmax(jnp.where(causal, logits + bias, -jnp.inf), axis=-1)
    return jnp.einsum("bhqk,bkhd->bqhd", p.astype(v.dtype), v)


def fox_prompt_attention(q, k, v, log_f):
    B, T, H, d = q.shape
    nb = T // FOX_BLOCK
    cum = jnp.cumsum(log_f, axis=1)
    pos = jnp.arange(T)
    q_blocks = jnp.moveaxis(q.reshape(B, nb, FOX_BLOCK, H, d), 1, 0)
    c_blocks = jnp.moveaxis(cum.reshape(B, nb, FOX_BLOCK, H), 1, 0)
    p_blocks = pos.reshape(nb, FOX_BLOCK)
    out = lax.map(lambda blk: fox_attend(blk[0], k, v, blk[1], cum, blk[2], pos),
                  (q_blocks, c_blocks, p_blocks))
    return jnp.moveaxis(out, 0, 1).reshape(B, T, H * d)


def fox_sample_attention(q, k, v, log_f, cache_k, cache_v, cache_log_f):
    B, T = q.shape[:2]
    past = cache_k.shape[1]
    k_all = jnp.concatenate([cache_k, k], axis=1)
    v_all = jnp.concatenate([cache_v, v], axis=1)
    cum = jnp.cumsum(jnp.concatenate([cache_log_f.astype(jnp.float32), log_f], axis=1), axis=1)
    o = fox_attend(q, k_all, v_all, cum[:, past:], cum, past + jnp.arange(T), jnp.arange(past + T))
    return o.reshape(B, T, FOX_DIM)


def rwkv_scan(r, decay, k, v, kk, a, state0):
    def step(S, inp):
        r_t, w_t, k_t, v_t, kk_t, a_t = inp
        removal = jnp.einsum("bhvk,bhk->bhv", S, kk_t)
        S = (S * w_t[:, :, None, :]
             - removal[..., None] * (kk_t * a_t)[:, :, None, :]
             + v_t[..., None] * k_t[:, :, None, :])
        return S, jnp.einsum("bhvk,bhk->bhv", S, r_t)
    xs = (jnp.moveaxis(r, 1, 0), jnp.moveaxis(decay, 1, 0), jnp.moveaxis(k, 1, 0),
          jnp.moveaxis(v, 1, 0), jnp.moveaxis(kk, 1, 0), jnp.moveaxis(a, 1, 0))
    state, out = lax.scan(step, state0, xs)
    return jnp.moveaxis(out, 0, 1), state


def rwkv_mixer(h, state0, shift0, mu, w0, w2, a0, a2, g2, k_k, k_a, r_k, ln_g, ln_b):
    B, T, _ = h.shape
    prev = jnp.concatenate([shift0.astype(h.dtype), h[:, :-1]], axis=1)
    hx = h + (prev - h) * mu
    cut = [RWKV_DIM, 2 * RWKV_DIM, 3 * RWKV_DIM, 3 * RWKV_DIM + DECAY_LORA,
           3 * RWKV_DIM + DECAY_LORA + ICLR_LORA]
    r, k, v, xw, xa, xg = jnp.split(hx, cut, axis=-1)
    w_logit = (w0 + jnp.tanh(xw) @ w2).astype(jnp.float32)
    decay = jnp.exp(-jnp.exp(-jax.nn.softplus(-w_logit) - 0.5))
    a = jax.nn.sigmoid(a0 + xa @ a2)
    g = jax.nn.sigmoid(xg) @ g2
    heads = lambda t: t.reshape(B, T, RWKV_HEADS, HEAD_DIM).astype(jnp.float32)
    kk = heads(k * k_k)
    kk = kk / jnp.maximum(jnp.linalg.norm(kk, axis=-1, keepdims=True), 1e-12)
    k = k * (1.0 + (a - 1.0) * k_a)
    r, k, v, a, decay = heads(r), heads(k), heads(v), heads(a), heads(decay)
    out, state = rwkv_scan(r, decay, k, v, kk, a, state0.astype(jnp.float32))
    mean = jnp.mean(out, axis=-1, keepdims=True)
    var = jnp.mean(jnp.square(out - mean), axis=-1, keepdims=True)
    out = ((out - mean) * lax.rsqrt(var + RWKV_GN_EPS)).reshape(B, T, RWKV_DIM) * ln_g + ln_b
    bonus = (jnp.sum(r * k * r_k, axis=-1, keepdims=True) * v).reshape(B, T, RWKV_DIM)
    return ((out + bonus) * g).astype(h.dtype), state, h[:, -1:]


def swa_project(h, q_norm, k_norm, pos):
    B, T, _ = h.shape
    q = h[..., :SWA_Q].reshape(B, T, SWA_HEADS, HEAD_DIM)
    k = h[..., SWA_Q:SWA_Q + SWA_KV].reshape(B, T, SWA_KV_HEADS, HEAD_DIM)
    v = h[..., SWA_Q + SWA_KV:].reshape(B, T, SWA_KV_HEADS, HEAD_DIM)
    q = partial_rope(rmsnorm(q, q_norm), pos).reshape(B, T, SWA_KV_HEADS, SWA_GROUP, HEAD_DIM)
    k = partial_rope(rmsnorm(k, k_norm), pos)
    return q, k, v


def sink_attend(q, k, v, mask, sinks):
    logits = jnp.einsum("...qngd,...knd->...ngqk", q, k).astype(jnp.float32) * HEAD_DIM ** -0.5
    logits = jnp.where(mask[..., None, None, :, :], logits, -jnp.inf)
    sink = jnp.broadcast_to(sinks.reshape(SWA_KV_HEADS, SWA_GROUP, 1, 1).astype(jnp.float32),
                            logits.shape[:-1] + (1,))
    p = jax.nn.softmax(jnp.concatenate([logits, sink], axis=-1), axis=-1)[..., :-1]
    return jnp.einsum("...ngqk,...knd->...qngd", p.astype(v.dtype), v)


def swa_prompt_attention(q, k, v, sinks):
    B, T = q.shape[:2]
    nc = T // CHUNK
    span = (WINDOW_CHUNKS + 1) * CHUNK

    def band(x):
        xc = jnp.pad(x.reshape(B, nc, CHUNK, SWA_KV_HEADS, HEAD_DIM),
                     ((0, 0), (WINDOW_CHUNKS, 0), (0, 0), (0, 0), (0, 0)))
        return jnp.concatenate([xc[:, j:j + nc] for j in range(WINDOW_CHUNKS + 1)], axis=2)

    key_chunk = jnp.arange(nc)[:, None] + jnp.arange(WINDOW_CHUNKS + 1)[None, :] - WINDOW_CHUNKS
    valid = jnp.repeat(key_chunk >= 0, CHUNK, axis=1)
    mask = jnp.broadcast_to(valid[:, None, :], (nc, CHUNK, span))
    o = sink_attend(q.reshape(B, nc, CHUNK, SWA_KV_HEADS, SWA_GROUP, HEAD_DIM),
                    band(k), band(v), mask, sinks)
    return o.reshape(B, T, SWA_Q)


def swa_sample_attention(q, k, v, cache_k, cache_v, sinks, past):
    B, T = q.shape[:2]
    rows = cache_k.shape[1]
    k_all = jnp.concatenate([cache_k, k], axis=1)
    v_all = jnp.concatenate([cache_v, v], axis=1)
    k_chunk = (past - rows + jnp.arange(rows + T)) // CHUNK
    q_chunk = (past + jnp.arange(T)) // CHUNK
    mask = (k_chunk[None, :] <= q_chunk[:, None]) & (k_chunk[None, :] >= q_chunk[:, None] - WINDOW_CHUNKS)
    o = sink_attend(q, k_all, v_all, mask, sinks)
    return o.reshape(B, T, SWA_Q), k_all[:, -rows:], v_all[:, -rows:]


def sgu_project(h, v_norm):
    u = jax.nn.gelu(h[..., :SGU_DIM])
    v = rmsnorm(jax.nn.gelu(h[..., SGU_DIM:]), v_norm)
    return u, v


def sgu_mix(u, v, w_s, b):
    L = v.shape[2]
    w = jnp.tril(w_s[:, :L, :L])
    vg = v.reshape(v.shape[:3] + (SGU_GROUPS, HEAD_DIM))
    mixed = jnp.einsum("gts,bcsgd->bctgd", w, vg) + jnp.transpose(b[:, :L])[:, :, None]
    return u * mixed.reshape(u.shape)


def mem_project_kv(mem, g, w_kv, k_norm):
    B, M, _ = mem.shape
    kv = rmsnorm(mem, g) @ w_kv
    k = rmsnorm(kv[..., :MEM_DIM].reshape(B, M, MEM_HEADS, HEAD_DIM), k_norm)
    v = kv[..., MEM_DIM:].reshape(B, M, MEM_HEADS, HEAD_DIM)
    return k, v


def mem_attend(xn, mem_k, mem_v, w_q, q_norm, w_o):
    B, T, _ = xn.shape
    q = rmsnorm((xn @ w_q).reshape(B, T, MEM_HEADS, HEAD_DIM), q_norm)
    logits = jnp.einsum("bqhd,bkhd->bhqk", q, mem_k).astype(jnp.float32) * HEAD_DIM ** -0.5
    p = jax.nn.softmax(logits, axis=-1).astype(mem_v.dtype)
    o = jnp.einsum("bhqk,bkhd->bqhd", p, mem_v).reshape(B, T, MEM_DIM)
    return o @ w_o


def setup_inputs(seed: int = 0) -> dict:
    key = jax.random.key(seed)
    keys = iter(jax.random.split(key, 64))

    def normal(shape, scale=1.0):
        return scale * jax.random.normal(next(keys), shape, jnp.float32)

    def dense(shape):
        return normal(shape, shape[-2] ** -0.5)

    def gain(shape):
        return 1.0 + normal(shape, 0.1)

    def uniform(shape, lo, hi):
        return jax.random.uniform(next(keys), shape, jnp.float32, lo, hi)

    swa_rows = min(WINDOW, PAST_LEN)
    return {
        "x_prompt": normal((BATCH, SEQ, D_MODEL)),
        "x_sample": normal((DEC_BATCH, DEC_SEQ, D_MODEL)),
        "cache_fox_k": normal((N_EVEN, DEC_BATCH, PAST_LEN, FOX_HEADS, HEAD_DIM)),
        "cache_fox_v": normal((N_EVEN, DEC_BATCH, PAST_LEN, FOX_HEADS, HEAD_DIM)),
        "cache_fox_logf": jax.nn.log_sigmoid(2.0 + normal((N_EVEN, DEC_BATCH, PAST_LEN, FOX_HEADS))),
        "state_rwkv": normal((N_EVEN, DEC_BATCH, RWKV_HEADS, HEAD_DIM, HEAD_DIM)),
        "state_rwkv_shift": normal((N_EVEN, DEC_BATCH, 1, RWKV_COLS)),
        "cache_swa_k": normal((N_ODD, DEC_BATCH, swa_rows, SWA_KV_HEADS, HEAD_DIM)),
        "cache_swa_v": normal((N_ODD, DEC_BATCH, swa_rows, SWA_KV_HEADS, HEAD_DIM)),
        "cache_mem_k": normal((DEPTH, DEC_BATCH, MEM_TOKENS, MEM_HEADS, HEAD_DIM)),
        "cache_mem_v": normal((DEPTH, DEC_BATCH, MEM_TOKENS, MEM_HEADS, HEAD_DIM)),
        "mem_prompt": normal((BATCH, MEM_TOKENS, D_MODEL)),
        "ffn1_norm": gain((DEPTH, D_MODEL)),
        "ffn1_w_gate": dense((DEPTH, D_MODEL, D_FF)),
        "ffn1_w_up": dense((DEPTH, D_MODEL, D_FF)),
        "ffn1_w_down": dense((DEPTH, D_FF, D_MODEL)),
        "mix_norm": gain((DEPTH, D_MODEL)),
        "ev_w_in": dense((N_EVEN, D_MODEL, EVEN_COLS)),
        "fox_b_f": 2.0 + normal((N_EVEN, FOX_HEADS), 0.1),
        "fox_q_norm": gain((N_EVEN, HEAD_DIM)),
        "fox_k_norm": gain((N_EVEN, HEAD_DIM)),
        "rwkv_mu": uniform((N_EVEN, RWKV_COLS), 0.0, 1.0),
        "rwkv_w0": uniform((N_EVEN, RWKV_DIM), -4.0, 0.0),
        "rwkv_w2": dense((N_EVEN, DECAY_LORA, RWKV_DIM)),
        "rwkv_a0": normal((N_EVEN, RWKV_DIM), 0.1),
        "rwkv_a2": dense((N_EVEN, ICLR_LORA, RWKV_DIM)),
        "rwkv_g2": dense((N_EVEN, GATE_LORA, RWKV_DIM)),
        "rwkv_k_k": uniform((N_EVEN, RWKV_DIM), 0.5, 1.0),
        "rwkv_k_a": gain((N_EVEN, RWKV_DIM)),
        "rwkv_r_k": normal((N_EVEN, RWKV_HEADS, HEAD_DIM), 0.1),
        "rwkv_ln_g": gain((N_EVEN, RWKV_DIM)),
        "rwkv_ln_b": normal((N_EVEN, RWKV_DIM), 0.02),
        "ev_w_out": dense((N_EVEN, MIX_OUT, D_MODEL)),
        "od_w_in": dense((N_ODD, D_MODEL, ODD_COLS)),
        "swa_q_norm": gain((N_ODD, HEAD_DIM)),
        "swa_k_norm": gain((N_ODD, HEAD_DIM)),
        "swa_sinks": normal((N_ODD, SWA_HEADS)),
        "sgu_v_norm": gain((N_ODD, SGU_DIM)),
        "sgu_w_s": dense((N_ODD, SGU_GROUPS, SGU_CHUNK, SGU_CHUNK)),
        "sgu_b": gain((N_ODD, SGU_GROUPS, SGU_CHUNK)),
        "od_w_out": dense((N_ODD, MIX_OUT, D_MODEL)),
        "xattn_norm": gain((DEPTH, D_MODEL)),
        "mem_norm": gain((DEPTH, D_MODEL)),
        "xattn_wq": dense((DEPTH, D_MODEL, MEM_DIM)),
        "xattn_wkv": dense((DEPTH, D_MODEL, 2 * MEM_DIM)),
        "xattn_q_norm": gain((DEPTH, HEAD_DIM)),
        "xattn_k_norm": gain((DEPTH, HEAD_DIM)),
        "xattn_wo": dense((DEPTH, MEM_DIM, D_MODEL)),
        "ffn2_norm": gain((DEPTH, D_MODEL)),
        "ffn2_w_gate": dense((DEPTH, D_MODEL, D_FF)),
        "ffn2_w_up": dense((DEPTH, D_MODEL, D_FF)),
        "ffn2_w_down": dense((DEPTH, D_FF, D_MODEL)),
    }


def reference(x_prompt, x_sample,
              cache_fox_k, cache_fox_v, cache_fox_logf, state_rwkv, state_rwkv_shift,
              cache_swa_k, cache_swa_v, cache_mem_k, cache_mem_v,
              mem_prompt,
              ffn1_norm, ffn1_w_gate, ffn1_w_up, ffn1_w_down,
              mix_norm,
              ev_w_in, fox_b_f, fox_q_norm, fox_k_norm,
              rwkv_mu, rwkv_w0, rwkv_w2, rwkv_a0, rwkv_a2, rwkv_g2, rwkv_k_k, rwkv_k_a,
              rwkv_r_k, rwkv_ln_g, rwkv_ln_b,
              ev_w_out,
              od_w_in, swa_q_norm, swa_k_norm, swa_sinks, sgu_v_norm, sgu_w_s, sgu_b, od_w_out,
              xattn_norm, mem_norm, xattn_wq, xattn_wkv, xattn_q_norm, xattn_k_norm, xattn_wo,
              ffn2_norm, ffn2_w_gate, ffn2_w_up, ffn2_w_down):
    Bp, Tp, _ = x_prompt.shape
    Bs, Ts, _ = x_sample.shape
    past = cache_fox_k.shape[2]
    pos_p = jnp.arange(Tp)
    pos_s = past + jnp.arange(Ts)
    xp, xs = x_prompt, x_sample
    p_fox_k, p_fox_v, p_fox_logf, p_rwkv_state, p_rwkv_shift = [], [], [], [], []
    p_swa_k, p_swa_v, p_mem_k, p_mem_v = [], [], [], []
    s_fox_k, s_fox_v, s_fox_logf, s_rwkv_state, s_rwkv_shift = [], [], [], [], []
    s_swa_k, s_swa_v, s_sgu_v = [], [], []

    for l in range(DEPTH):
        xp = xp + 0.5 * swiglu(rmsnorm(xp, ffn1_norm[l]), ffn1_w_gate[l], ffn1_w_up[l], ffn1_w_down[l])
        xs = xs + 0.5 * swiglu(rmsnorm(xs, ffn1_norm[l]), ffn1_w_gate[l], ffn1_w_up[l], ffn1_w_down[l])
        hp = rmsnorm(xp, mix_norm[l])
        hs = rmsnorm(xs, mix_norm[l])
        if l % 2 == 0:
            e = l // 2
            hp, hs = hp @ ev_w_in[e], hs @ ev_w_in[e]
            qp, kp, vp, fp = fox_project(hp[..., :FOX_COLS], fox_b_f[e], fox_q_norm[e], fox_k_norm[e])
            qs, ks, vs, fs = fox_project(hs[..., :FOX_COLS], fox_b_f[e], fox_q_norm[e], fox_k_norm[e])
            a_p = fox_prompt_attention(qp, kp, vp, fp)
            a_s = fox_sample_attention(qs, ks, vs, fs, cache_fox_k[e], cache_fox_v[e], cache_fox_logf[e])
            rw = (rwkv_mu[e], rwkv_w0[e], rwkv_w2[e], rwkv_a0[e], rwkv_a2[e], rwkv_g2[e],
                  rwkv_k_k[e], rwkv_k_a[e], rwkv_r_k[e], rwkv_ln_g[e], rwkv_ln_b[e])
            state0 = jnp.zeros((Bp, RWKV_HEADS, HEAD_DIM, HEAD_DIM), jnp.float32)
            shift0 = jnp.zeros((Bp, 1, RWKV_COLS), hp.dtype)
            b_p, st_p, sh_p = rwkv_mixer(hp[..., FOX_COLS:], state0, shift0, *rw)
            b_s, st_s, sh_s = rwkv_mixer(hs[..., FOX_COLS:], state_rwkv[e], state_rwkv_shift[e], *rw)
            xp = xp + jnp.concatenate([a_p, b_p], axis=-1) @ ev_w_out[e]
            xs = xs + jnp.concatenate([a_s, b_s], axis=-1) @ ev_w_out[e]
            p_fox_k.append(kp); p_fox_v.append(vp); p_fox_logf.append(fp)
            p_rwkv_state.append(st_p); p_rwkv_shift.append(sh_p)
            s_fox_k.append(ks); s_fox_v.append(vs); s_fox_logf.append(fs)
            s_rwkv_state.append(st_s); s_rwkv_shift.append(sh_s)
        else:
            j = l // 2
            hp, hs = hp @ od_w_in[j], hs @ od_w_in[j]
            qp, kp, vp = swa_project(hp[..., :SWA_COLS], swa_q_norm[j], swa_k_norm[j], pos_p)
            qs, ks, vs = swa_project(hs[..., :SWA_COLS], swa_q_norm[j], swa_k_norm[j], pos_s)
            c_p = swa_prompt_attention(qp, kp, vp, swa_sinks[j])
            c_s, nk_s, nv_s = swa_sample_attention(qs, ks, vs, cache_swa_k[j], cache_swa_v[j], swa_sinks[j], past)
            u_p, g_p = sgu_project(hp[..., SWA_COLS:], sgu_v_norm[j])
            u_s, g_s = sgu_project(hs[..., SWA_COLS:], sgu_v_norm[j])
            nc = Tp // SGU_CHUNK
            d_p = sgu_mix(u_p.reshape(Bp, nc, SGU_CHUNK, SGU_DIM), g_p.reshape(Bp, nc, SGU_CHUNK, SGU_DIM),
                          sgu_w_s[j], sgu_b[j]).reshape(Bp, Tp, SGU_DIM)
            d_s = sgu_mix(u_s[:, None], g_s[:, None], sgu_w_s[j], sgu_b[j])[:, 0]
            xp = xp + jnp.concatenate([c_p, d_p], axis=-1) @ od_w_out[j]
            xs = xs + jnp.concatenate([c_s, d_s], axis=-1) @ od_w_out[j]
            p_swa_k.append(kp[:, -WINDOW:]); p_swa_v.append(vp[:, -WINDOW:])
            s_swa_k.append(nk_s); s_swa_v.append(nv_s); s_sgu_v.append(g_s)
        mk_p, mv_p = mem_project_kv(mem_prompt, mem_norm[l], xattn_wkv[l], xattn_k_norm[l])
        xp = xp + mem_attend(rmsnorm(xp, xattn_norm[l]), mk_p, mv_p, xattn_wq[l], xattn_q_norm[l], xattn_wo[l])
        xs = xs + mem_attend(rmsnorm(xs, xattn_norm[l]), cache_mem_k[l], cache_mem_v[l],
                             xattn_wq[l], xattn_q_norm[l], xattn_wo[l])
        p_mem_k.append(mk_p); p_mem_v.append(mv_p)
        xp = xp + 0.5 * swiglu(rmsnorm(xp, ffn2_norm[l]), ffn2_w_gate[l], ffn2_w_up[l], ffn2_w_down[l])
        xs = xs + 0.5 * swiglu(rmsnorm(xs, ffn2_norm[l]), ffn2_w_gate[l], ffn2_w_up[l], ffn2_w_down[l])

    return (xp, xs,
            jnp.stack(p_fox_k), jnp.stack(p_fox_v), jnp.stack(p_fox_logf),
            jnp.stack(p_rwkv_state), jnp.stack(p_rwkv_shift),
            jnp.stack(p_swa_k), jnp.stack(p_swa_v),
            jnp.stack(p_mem_k), jnp.stack(p_mem_v),
            jnp.stack(s_fox_k), jnp.stack(s_fox_v), jnp.stack(s_fox_logf),
            jnp.stack(s_rwkv_state), jnp.stack(s_rwkv_shift),
            jnp.stack(s_swa_k), jnp.stack(s_swa_v),
            jnp.stack(s_sgu_v))
```

```python
import numpy as np
from contextlib import ExitStack
import concourse.bass as bass
import concourse.mybir as mybir
from concourse.bass_utils import run_bass_kernel_spmd

F32 = mybir.dt.float32
BF16 = mybir.dt.bfloat16
AF = mybir.ActivationFunctionType
ALU = mybir.AluOpType
AX = mybir.AxisListType

COMPUTE = ("pe", "act", "dve", "pool")
NSLOT = 12
NT = 17
NTOK = 2112
D = 1024
DFF = 2816
EPS = 1e-6


class Buf:
    __slots__ = ("name", "w", "r")

    def __init__(self, name=""):
        self.name = name
        self.w = {}
        self.r = {}


class _Op:
    __slots__ = ("fn", "waits", "signal", "dma", "val")

    def __init__(self, fn, waits, dma):
        self.fn = fn
        self.waits = waits
        self.signal = False
        self.dma = dma
        self.val = 0


class FW:
    def __init__(self, nc, stack):
        self.nc = nc
        self.streams = {k: [] for k in ("pe", "act", "dve", "pool", "sp")}
        self.known = {k: {} for k in self.streams}
        self.snapc = {k: None for k in self.streams}
        self.tl = {}
        self.slot_rr = {"sp": 0, "pool": 0}
        self.sems = {}
        for k in COMPUTE:
            self.sems[k] = stack.enter_context(nc.semaphore("s_" + k))
        for q in ("sp", "pool"):
            for s in range(NSLOT):
                sk = "d_%s_%d" % (q, s)
                self.sems[sk] = stack.enter_context(nc.semaphore(sk))
        self.sigcount = {k: 0 for k in COMPUTE}
        self.emitted = {k: 0 for k in self.streams}
        self.ninstr = 0
        self.strict = False

    def _snap(self, s):
        if self.snapc[s] is None:
            self.snapc[s] = dict(self.known[s])
        return self.snapc[s]

    def _wait(self, s, tlk, idx, waits):
        kn = self.known[s]
        if kn.get(tlk, -1) >= idx:
            return
        st, oi, snap = self.tl[tlk][idx]
        assert oi >= self.emitted[st], "dependency on already-emitted op"
        self.streams[st][oi].signal = True
        waits.append((tlk, idx))
        kn[tlk] = idx
        for k, v in snap.items():
            if kn.get(k, -1) < v:
                kn[k] = v
        self.snapc[s] = None

    def op(self, eng, fn, reads=(), writes=()):
        deps = {}
        for b in reads:
            for k, i in b.w.items():
                if deps.get(k, -1) < i:
                    deps[k] = i
        strict = self.strict and eng != "pe"
        for b in writes:
            for k, i in b.w.items():
                if (k != eng or strict) and deps.get(k, -1) < i:
                    deps[k] = i
            for k, i in b.r.items():
                if (k != eng or strict) and deps.get(k, -1) < i:
                    deps[k] = i
        waits = []
        for k, i in deps.items():
            self._wait(eng, k, i, waits)
        o = _Op(fn, waits, None)
        st = self.streams[eng]
        st.append(o)
        tl = self.tl.setdefault(eng, [])
        idx = len(tl)
        tl.append((eng, len(st) - 1, self._snap(eng)))
        for b in reads:
            b.r[eng] = idx
        for b in writes:
            b.w = {eng: idx}
            b.r = {}
        return idx

    def dma(self, q, fn, reads=(), writes=()):
        deps = {}
        for b in reads:
            for k, i in b.w.items():
                if deps.get(k, -1) < i:
                    deps[k] = i
        for b in writes:
            for k, i in b.w.items():
                if deps.get(k, -1) < i:
                    deps[k] = i
            for k, i in b.r.items():
                if deps.get(k, -1) < i:
                    deps[k] = i
        slot = self.slot_rr[q]
        self.slot_rr[q] = (slot + 1) % NSLOT
        sk = "d_%s_%d" % (q, slot)
        tl = self.tl.setdefault(sk, [])
        idx = len(tl)
        if idx > 0 and deps.get(sk, -1) < idx - 1:
            deps[sk] = idx - 1
        waits = []
        for k, i in deps.items():
            self._wait(q, k, i, waits)
        o = _Op(fn, waits, (sk, idx))
        st = self.streams[q]
        st.append(o)
        tl.append((q, len(st) - 1, self._snap(q)))
        for b in reads:
            b.r[sk] = idx
        for b in writes:
            b.w = {sk: idx}
            b.r = {}

    def _val_of(self, tlk, idx):
        if tlk.startswith("d_"):
            return 16 * (idx + 1)
        st, oi, _ = self.tl[tlk][idx]
        o = self.streams[st][oi]
        assert o.signal and o.val > 0
        return o.val

    def flush(self):
        for s in self.streams:
            waits = []
            for tlk, lst in self.tl.items():
                if lst:
                    self._wait(s, tlk, len(lst) - 1, waits)
            if waits:
                self.streams[s].append(_Op(None, waits, None))
        for k in COMPUTE:
            c = self.sigcount[k]
            for o in self.streams[k][self.emitted[k]:]:
                if o.signal:
                    c += 1
                o.val = c
            self.sigcount[k] = c
        sems = self.sems

        def run(key, e):
            ops = self.streams[key]
            for o in ops[self.emitted[key]:]:
                for (tlk, idx) in o.waits:
                    e.wait_ge(sems[tlk], self._val_of(tlk, idx))
                    self.ninstr += 1
                if o.fn is None:
                    continue
                ins = o.fn(e)
                self.ninstr += 1
                if o.dma is not None:
                    ins.then_inc(sems[o.dma[0]], 16)
                elif o.signal:
                    ins.then_inc(sems[key], 1)
            self.emitted[key] = len(ops)

        with self.nc.Block() as block:
            @block.tensor
            def _(e):
                run("pe", e)

            @block.scalar
            def _(e):
                run("act", e)

            @block.vector
            def _(e):
                run("dve", e)

            @block.gpsimd
            def _(e):
                run("pool", e)

            @block.sync
            def _(e):
                run("sp", e)


def tile_rows(i):
    return 128 if i < 16 else 64


TOK_GROUPS = [(0, 512, [0, 1, 2, 3]), (512, 512, [4, 5, 6, 7]), (1024, 512, [8, 9, 10, 11]),
              (1536, 512, [12, 13, 14, 15]), (2048, 64, [16])]
FF_GROUPS = [(0, 3), (3, 6), (6, 9), (9, 12), (12, 15), (15, 18), (18, 20), (20, 22)]


class Ctx:
    pass


def build_program(cfg):
    nc = bass.Bass("TRN2", target_bir_lowering=False)
    C = Ctx()
    C.nc = nc
    dr = {}

    def din(name, shape):
        dr[name] = nc.dram_tensor(name, list(shape), F32, kind="ExternalInput").ap()
        return dr[name]

    def dout(name, shape):
        dr[name] = nc.dram_tensor(name, list(shape), F32, kind="ExternalOutput").ap()
        return dr[name]

    din("xp", (2048, D))
    din("xs", (64, D))
    din("c_ident", (128, 128))
    for nm in ("ffn1", "ffn2"):
        din(nm + "_norm", (2, D))
        din(nm + "_w_gate", (2, D, DFF))
        din(nm + "_w_up", (2, D, DFF))
        din(nm + "_w_down", (2, DFF, D))
    din("memp", (256, D))
    din("cmk", (2, 2, 256, 256))
    din("cmv", (2, 2, 256, 256))
    din("xattn_norm", (2, D))
    din("mem_norm", (2, D))
    din("xattn_wq", (2, D, 256))
    din("xattn_wkv", (2, D, 512))
    din("xattn_q_norm", (2, 64))
    din("xattn_k_norm", (2, 64))
    din("xattn_wo", (2, 256, D))
    din("mix_norm", (2, D))
    din("ev_w_in", (1, D, 3336))
    din("fox_b_f", (1, 8))
    din("fox_k_norm", (1, 64))
    din("fox_q_norm", (1, 64))
    din("ev_w_out", (1, D, D))
    din("cfk", (2, 2048, 512))
    din("cfv", (2, 2048, 512))
    din("cfl", (2, 2048, 8))
    for nm, n_ in (("rwkv_mu", 1792), ("rwkv_w0", 512), ("rwkv_a0", 512), ("rwkv_k_k", 512), ("rwkv_k_a", 512), ("rwkv_r_k", 512),
                   ("rwkv_ln_g", 512), ("rwkv_ln_b", 512)):
        din(nm, (1, n_))
    din("rwkv_w2", (1, 64, 512))
    din("rwkv_a2", (1, 64, 512))
    din("rwkv_g2", (1, 128, 512))
    din("srw", (2, 8, 64, 64))
    din("srs", (2, 1792))
    din("c_bd", (128, 128))
    din("c_msk", (64, 3, 64))
    din("c_rst", (128, 256))
    dout("rwkv_state", (3, 8, 64, 64))
    din("c_tri", (128, 128))
    din("c_trib", (64, 64))
    din("od_w_in", (1, D, 1792))
    din("od_w_out", (1, D, D))
    din("swa_q_norm", (1, 64))
    din("swa_k_norm", (1, 64))
    din("swa_sinks", (1, 8))
    din("sgu_v_norm", (1, 512))
    din("sgu_w_s", (1, 8, 128, 128))
    din("sgu_b", (1, 8, 128))
    din("csk", (2, 128, 128))
    din("csv", (2, 128, 128))
    din("c_rope", (128, 2, NT, 8))
    dout("swa_ko", (3, 128, 128))
    dout("swa_vo", (3, 128, 128))
    dout("sgu_vo", (64, 512))
    dout("y_prompt", (2048, D))
    dout("y_sample", (64, D))
    dout("fox_k", (NTOK, 512))
    dout("fox_v", (NTOK, 512))
    dout("fox_logf", (NTOK, 8))
    dout("rwkv_shift", (3, 1792))
    dout("p_mem_k", (2, 256, 256))
    dout("p_mem_v", (2, 256, 256))

    with ExitStack() as top:
        fw = FW(nc, top)
        fw.strict = bool(cfg.get("strict", False))
        C.fw = fw

        uid = [0]

        C.sb_cur = 0
        C.sb_max = 0

        def sb(st, name, shape, dt=F32):
            uid[0] += 1
            nb = int(np.prod(shape[1:])) * (4 if dt == F32 else 2)
            nb = (nb + 31) // 32 * 32

            def _rel(nb=nb):
                C.sb_cur -= nb
            st.callback(_rel)
            C.sb_cur += nb
            if C.sb_cur > C.sb_max:
                C.sb_max = C.sb_cur
                C.sb_max_at = name
            return st.enter_context(nc.sbuf_tensor("%s_%d" % (name, uid[0]), list(shape), dt))

        def ps(st, name, shape, dt=F32):
            nbytes = int(np.prod(shape[1:])) * (4 if dt == F32 else 2)
            assert nbytes % 2048 == 0, ("psum tile must be whole banks", name, shape)
            uid[0] += 1
            return st.enter_context(nc.psum_tensor("%s_%d" % (name, uid[0]), list(shape), dt))

        X = sb(top, "X", (128, NT, D))
        bX = [Buf("X%d" % i) for i in range(NT)]
        ident_f = sb(top, "ident_f", (128, 128))
        ident = sb(top, "ident", (128, 128), BF16)
        b_ident = Buf("ident")
        fw.dma("sp", lambda e: e.dma_start(out=ident_f[:], in_=dr["c_ident"]), writes=[b_ident])
        fw.op("dve", lambda e: e.tensor_copy(out=ident[:], in_=ident_f[:]), reads=[b_ident], writes=[b_ident])
        xp_v = dr["xp"].rearrange("(t p) d -> p t d", p=128)
        for g in range(4):
            fw.dma("sp", lambda e, g=g: e.dma_start(out=X[:, 4 * g:4 * g + 4, :], in_=xp_v[:, 4 * g:4 * g + 4, :]),
                   writes=bX[4 * g:4 * g + 4])
        fw.dma("sp", lambda e: e.dma_start(out=X[0:64, 16, :], in_=dr["xs"]), writes=[bX[16]])

        def norm_T(st, gain_row_ap, xT, b_xT, pfx):
            G = sb(st, pfx + "G", (128, D))
            bG = Buf()
            fw.dma("sp", lambda e: e.dma_start(out=G[:], in_=gain_row_ap.broadcast_to([128, D])), writes=[bG])
            ss = sb(st, pfx + "ss", (128, NT))
            sd = sb(st, pfx + "sd", (128, NT))
            rs = sb(st, pfx + "rs", (128, NT))
            bss = Buf()
            junk = sb(st, pfx + "junk", (128, D))
            bj = Buf()
            fw.op("pool", lambda e: e.memset(ss[:], 1.0), writes=[bss])
            for i in range(NT):
                r = tile_rows(i)
                fw.op("act", lambda e, i=i, r=r: e.activation(out=junk[0:r, :], in_=X[0:r, i, :], func=AF.Square,
                                                              accum_out=ss[0:r, i:i + 1]),
                      reads=[bX[i]], writes=[bj, bss])
            fw.op("act", lambda e: e.activation(out=sd[:], in_=ss[:], func=AF.Sqrt, scale=1.0 / D, bias=EPS),
                  reads=[bss], writes=[bss])
            fw.op("dve", lambda e: e.reciprocal(out=rs[:], in_=sd[:]), reads=[bss], writes=[bss])
            xn = [sb(st, pfx + "xn%d" % j, (128, D), BF16) for j in range(2)]
            bxn = [Buf(), Buf()]
            ptr = [ps(st, pfx + "ptr%d" % j, (128, 8, 128), BF16) for j in range(2)]
            bptr = [Buf(), Buf()]
            for i in range(NT):
                r = tile_rows(i)
                j = i % 2
                fw.op("dve", lambda e, i=i, r=r, j=j: e.scalar_tensor_tensor(
                    out=xn[j][0:r, :], in0=X[0:r, i, :], scalar=rs[0:r, i:i + 1], in1=G[0:r, :],
                    op0=ALU.mult, op1=ALU.mult), reads=[bX[i], bss, bG], writes=[bxn[j]])
                for k in range(8):
                    fw.op("pe", lambda e, r=r, j=j, k=k: e.transpose(
                        out=ptr[j][:, k, 0:r], in_=xn[j][0:r, k * 128:(k + 1) * 128], identity=ident[0:r, 0:r]),
                        reads=[bxn[j], b_ident], writes=[bptr[j]])
                eng = "act" if i % 2 == 0 else "dve"
                if eng == "act":
                    fw.op("act", lambda e, i=i, r=r, j=j: e.copy(out=xT[:, :, i * 128:i * 128 + r], in_=ptr[j][:, :, 0:r]),
                          reads=[bptr[j]], writes=[b_xT[i // 4]])
                else:
                    fw.op("dve", lambda e, i=i, r=r, j=j: e.tensor_copy(out=xT[:, :, i * 128:i * 128 + r], in_=ptr[j][:, :, 0:r]),
                          reads=[bptr[j]], writes=[b_xT[i // 4]])

        def ffn(l, nm):
            with ExitStack() as st:
                xT = sb(st, "xT", (128, 8, NTOK), BF16)
                b_xT = [Buf() for _ in range(5)]
                with ExitStack() as st2:
                    norm_T(st2, dr[nm + "_norm"][l:l + 1, :], xT, b_xT, "n_")
                    fw.flush()
                wg_d = dr[nm + "_w_gate"][l].rearrange("(k p) n -> p k n", p=128)
                wu_d = dr[nm + "_w_up"][l].rearrange("(k p) n -> p k n", p=128)
                wd_d = dr[nm + "_w_down"][l].rearrange("(j p) n -> p j n", p=128)
                NS = 4
                stg = [sb(st, "stg%d" % i, (128, 1536)) for i in range(NS)]
                bstg = [Buf() for _ in range(NS)]
                WG = [sb(st, "WG%d" % i, (128, 8, 384), BF16) for i in range(2)]
                WU = [sb(st, "WU%d" % i, (128, 8, 384), BF16) for i in range(2)]
                WD = [sb(st, "WD%d" % i, (128, 3, D), BF16) for i in range(2)]
                bWG = [Buf(), Buf()]
                bWU = [Buf(), Buf()]
                bWD = [Buf(), Buf()]
                SG = [sb(st, "SG%d" % i, (128, 512)) for i in range(2)]
                bSG = [Buf(), Buf()]
                AT = [sb(st, "AT%d" % i, (128, 3, 512), BF16) for i in range(2)]
                bAT = [Buf(), Buf()]
                pg = [ps(st, "pg%d" % i, (128, 512)) for i in range(2)]
                pu = [ps(st, "pu%d" % i, (128, 512)) for i in range(2)]
                pd = [ps(st, "pd%d" % i, (128, 512)) for i in range(2)]
                bpg = [Buf(), Buf()]
                bpu = [Buf(), Buf()]
                bpd = [Buf(), Buf()]
                sc = [0]
                cnt = {"g": 0, "d": 0, "a": 0}

                def load_cast(src_ap, dst_ap, bdst, shape3):
                    s = sc[0] % NS
                    sc[0] += 1
                    a, b_ = shape3
                    sv = stg[s][:, 0:a * b_].rearrange("p (a b) -> p a b", a=a)
                    fw.dma("sp", lambda e: e.dma_start(out=sv, in_=src_ap), writes=[bstg[s]])
                    fw.op("pool", lambda e: e.tensor_copy(out=dst_ap, in_=sv), reads=[bstg[s]], writes=[bdst])

                def load_group(gi):
                    c0, c1 = FF_GROUPS[gi]
                    nch = c1 - c0
                    ncol = nch * 128
                    s = gi % 2
                    for h in range(2):
                        load_cast(wg_d[:, 4 * h:4 * h + 4, c0 * 128:c0 * 128 + ncol], WG[s][:, 4 * h:4 * h + 4, 0:ncol],
                                  bWG[s], (4, ncol))
                        load_cast(wu_d[:, 4 * h:4 * h + 4, c0 * 128:c0 * 128 + ncol], WU[s][:, 4 * h:4 * h + 4, 0:ncol],
                                  bWU[s], (4, ncol))
                    for j in range(nch):
                        load_cast(wd_d[:, c0 + j:c0 + j + 1, :], WD[s][:, j:j + 1, :], bWD[s], (1, D))

                load_group(0)
                for gi in range(len(FF_GROUPS)):
                    if gi + 1 < len(FF_GROUPS):
                        load_group(gi + 1)
                    c0, c1 = FF_GROUPS[gi]
                    nch = c1 - c0
                    s = gi % 2
                    for tgi, (t0, n, tiles) in enumerate(TOK_GROUPS):
                        a = cnt["a"] % 2
                        cnt["a"] += 1
                        for j in range(nch):
                            q = cnt["g"] % 2
                            cnt["g"] += 1
                            for k in range(8):
                                fw.op("pe", lambda e, q=q, s=s, k=k, j=j, t0=t0, n=n: e.matmul(
                                    pg[q][:, 0:n], lhsT=WG[s][:, k, j * 128:(j + 1) * 128], rhs=xT[:, k, t0:t0 + n],
                                    start=(k == 0), stop=(k == 7)), reads=[bWG[s], b_xT[tgi]], writes=[bpg[q]])
                            for k in range(8):
                                fw.op("pe", lambda e, q=q, s=s, k=k, j=j, t0=t0, n=n: e.matmul(
                                    pu[q][:, 0:n], lhsT=WU[s][:, k, j * 128:(j + 1) * 128], rhs=xT[:, k, t0:t0 + n],
                                    start=(k == 0), stop=(k == 7)), reads=[bWU[s], b_xT[tgi]], writes=[bpu[q]])
                            fw.op("act", lambda e, q=q, n=n: e.activation(out=SG[q][:, 0:n], in_=pg[q][:, 0:n], func=AF.Silu),
                                  reads=[bpg[q]], writes=[bSG[q]])
                            fw.op("dve", lambda e, q=q, n=n, a=a, j=j: e.tensor_tensor(
                                out=AT[a][:, j, 0:n], in0=SG[q][:, 0:n], in1=pu[q][:, 0:n], op=ALU.mult),
                                reads=[bSG[q], bpu[q]], writes=[bAT[a]])
                        for tl_i, ti in enumerate(tiles):
                            r = tile_rows(ti)
                            for half in range(2):
                                q = cnt["d"] % 2
                                cnt["d"] += 1
                                for j in range(nch):
                                    fw.op("pe", lambda e, q=q, a=a, j=j, tl_i=tl_i, r=r, half=half, s=s: e.matmul(
                                        pd[q][0:r, :], lhsT=AT[a][:, j, tl_i * 128:tl_i * 128 + r],
                                        rhs=WD[s][:, j, half * 512:(half + 1) * 512],
                                        start=(j == 0), stop=(j == nch - 1)), reads=[bAT[a], bWD[s]], writes=[bpd[q]])
                                fw.op("dve", lambda e, q=q, r=r, ti=ti, half=half: e.scalar_tensor_tensor(
                                    out=X[0:r, ti, half * 512:(half + 1) * 512], in0=pd[q][0:r, :], scalar=0.5,
                                    in1=X[0:r, ti, half * 512:(half + 1) * 512], op0=ALU.mult, op1=ALU.add),
                                    reads=[bpd[q], bX[ti]], writes=[bX[ti]])
                fw.flush()


        ones_bf = sb(top, "ones_bf", (128, 64), BF16)
        b_ones = Buf("ones")
        fw.op("pool", lambda e: e.memset(ones_bf[:], 1.0), writes=[b_ones])

        def bc3(ap2, n):
            p, h = ap2.shape
            return ap2.unsqueeze(2).broadcast_to([p, h, n])

        def bcm(ap2, h):
            p, n = ap2.shape
            return ap2.unsqueeze(1).broadcast_to([p, h, n])

        def rms_heads(st, src, bsrc, r, H, gain_tile, bgain, scale, out_bf, bout_bf, out_f=None, bout_f=None, tag=""):
            key = "rmsh_tmp"
            if key not in st.__dict__:
                st.__dict__[key] = True
                C.rh_sq = sb(st, "rh_sq", (128, 8, 64))
                C.rh_ss = sb(st, "rh_ss", (128, 8))
                C.rh_sd = sb(st, "rh_sd", (128, 8))
                C.rh_rs = sb(st, "rh_rs", (128, 8))
                C.rh_t = sb(st, "rh_t", (128, 8, 64))
                C.b_rh = Buf()
                C.b_rh2 = Buf()
            sq, ss, sd, rs, t = C.rh_sq, C.rh_ss, C.rh_sd, C.rh_rs, C.rh_t
            fw.op("act", lambda e: e.activation(out=sq[0:r, 0:H, :], in_=src, func=AF.Square), reads=[bsrc], writes=[C.b_rh])
            fw.op("dve", lambda e: e.tensor_reduce(out=ss[0:r, 0:H], in_=sq[0:r, 0:H, :], axis=AX.X, op=ALU.add),
                  reads=[C.b_rh], writes=[C.b_rh2])
            fw.op("act", lambda e: e.activation(out=sd[0:r, 0:H], in_=ss[0:r, 0:H], func=AF.Sqrt, scale=1.0 / 64, bias=EPS),
                  reads=[C.b_rh2], writes=[C.b_rh2])
            fw.op("dve", lambda e: e.reciprocal(out=rs[0:r, 0:H], in_=sd[0:r, 0:H]), reads=[C.b_rh2], writes=[C.b_rh2])
            fw.op("dve", lambda e: e.tensor_tensor(out=t[0:r, 0:H, :], in0=src, in1=bc3(rs[0:r, 0:H], 64), op=ALU.mult),
                  reads=[bsrc, C.b_rh2], writes=[C.b_rh])
            if out_f is not None:
                fw.op("dve", lambda e: e.tensor_tensor(out=out_f, in0=t[0:r, 0:H, :], in1=bcm(gain_tile[0:r, :], H), op=ALU.mult),
                      reads=[C.b_rh, bgain], writes=[bout_f])
                fw.op("act", lambda e: e.activation(out=out_bf, in_=out_f, func=AF.Copy, scale=float(scale)),
                      reads=[bout_f], writes=[bout_bf])
            else:
                fw.op("dve", lambda e: e.scalar_tensor_tensor(out=out_bf, in0=t[0:r, 0:H, :], scalar=float(scale),
                                                              in1=bcm(gain_tile[0:r, :], H), op0=ALU.mult, op1=ALU.mult),
                      reads=[C.b_rh, bgain], writes=[bout_bf])

        def load_w_bf(st, src_ap, shape, name, stage_cols=2048):
            a, b_ = shape
            wt = sb(st, name, (128, a, b_), BF16)
            bw = Buf(name)
            if "wstage" not in C.__dict__ or C.wstage_owner is not st:
                C.wstage = [sb(st, "wstage%d" % i, (128, stage_cols)) for i in range(2)]
                C.bwstage = [Buf(), Buf()]
                C.wstage_owner = st
                C.wsc = 0
            per = max(1, stage_cols // b_)
            npart = src_ap.shape[0]
            a0 = 0
            while a0 < a:
                na = min(per, a - a0)
                sl = C.wsc % 2
                C.wsc += 1
                sv = C.wstage[sl][0:npart, 0:na * b_].rearrange("p (a b) -> p a b", a=na)
                fw.dma("sp", lambda e, sv=sv, a0=a0, na=na: e.dma_start(out=sv, in_=src_ap[:, a0:a0 + na, :]),
                       writes=[C.bwstage[sl]])
                fw.op("pool", lambda e, sv=sv, a0=a0, na=na: e.tensor_copy(out=wt[0:npart, a0:a0 + na, :], in_=sv),
                      reads=[C.bwstage[sl]], writes=[bw])
                a0 += na
            return wt, bw

        def bcast_row(st, row_ap, n, name):
            t = sb(st, name, (128, n))
            b = Buf(name)
            fw.dma("sp", lambda e: e.dma_start(out=t[:], in_=row_ap.broadcast_to([128, n])), writes=[b])
            return t, b

        def attn_core(st, q_ap, bq, N, kblocks, out_ap, bout, extra_den=None, nbuf=2):
            if "ac_owner" not in C.__dict__ or C.ac_owner is not st:
                C.ac_owner = st
                C.ac_S = [ps(st, "acS%d" % i, (128, 512)) for i in range(2)]
                C.ac_bS = [Buf(), Buf()]
                C.ac_num = [ps(st, "acN%d" % i, (64, 512)) for i in range(nbuf)]
                C.ac_den = [ps(st, "acD%d" % i, (64, 512)) for i in range(nbuf)]
                C.ac_bnd = [Buf() for _ in range(nbuf)]
                C.ac_nbuf = nbuf
                C.ac_P = [sb(st, "acP%d" % i, (128, 512), BF16) for i in range(3)]
                C.ac_bP = [Buf(), Buf(), Buf()]
                C.ac_rd = [sb(st, "acR%d" % i, (64, 512)) for i in range(2)]
                C.ac_brd = [Buf(), Buf()]
                C.ac_c = [0, 0, 0]
            o = C.ac_c[1] % C.ac_nbuf
            o2 = C.ac_c[1] % 2
            C.ac_c[1] += 1
            num, den, bnd = C.ac_num[o], C.ac_den[o], C.ac_bnd[o]
            nb = len(kblocks)
            for bi, kb in enumerate(kblocks):
                si = C.ac_c[0] % 2
                C.ac_c[0] += 1
                pi = C.ac_c[2] % 3
                C.ac_c[2] += 1
                S, bS, P, bP = C.ac_S[si], C.ac_bS[si], C.ac_P[pi], C.ac_bP[pi]
                nk, c0 = kb["nk"], kb["col0"]
                fw.op("pe", lambda e, S=S, kb=kb, nk=nk, c0=c0: e.matmul(S[0:nk, c0:N], lhsT=kb["kT"], rhs=q_ap[:, c0:N],
                                                                        start=True, stop=True),
                      reads=[bq] + kb["reads"], writes=[bS])
                fw.op("act", lambda e, S=S, P=P, nk=nk, c0=c0: e.activation(out=P[0:nk, c0:N], in_=S[0:nk, c0:N], func=AF.Exp),
                      reads=[bS], writes=[bP])
                if kb.get("mask") is not None:
                    mw = kb.get("mask_w", 128)
                    fw.op("pool", lambda e, P=P, kb=kb, nk=nk, c0=c0, mw=mw: e.tensor_tensor(
                        out=P[0:nk, c0:c0 + mw], in0=P[0:nk, c0:c0 + mw], in1=kb["mask"], op=ALU.mult),
                        reads=[bP] + kb.get("mask_reads", []), writes=[bP])
                fw.op("pe", lambda e, P=P, kb=kb, nk=nk, c0=c0, bi=bi: e.matmul(num[:, c0:N], lhsT=kb["v"], rhs=P[0:nk, c0:N],
                                                                               start=(bi == 0), stop=(bi == nb - 1)),
                      reads=[bP] + kb["reads"], writes=[bnd])
                fw.op("pe", lambda e, P=P, nk=nk, c0=c0, bi=bi: e.matmul(den[:, c0:N], lhsT=ones_bf[0:nk, :], rhs=P[0:nk, c0:N],
                                                                        start=(bi == 0), stop=(bi == nb - 1)),
                      reads=[bP, b_ones], writes=[bnd])
            rd, brd = C.ac_rd[o2], C.ac_brd[o2]
            if extra_den is not None:
                ed_ap, ed_reads = extra_den
                fw.op("dve", lambda e: e.tensor_scalar(out=rd[:, 0:N], in0=den[:, 0:N], scalar1=ed_ap, scalar2=None, op0=ALU.add),
                      reads=[bnd] + ed_reads, writes=[brd])
                fw.op("dve", lambda e: e.reciprocal(out=rd[:, 0:N], in_=rd[:, 0:N]), reads=[brd], writes=[brd])
            else:
                fw.op("dve", lambda e: e.reciprocal(out=rd[:, 0:N], in_=den[:, 0:N]), reads=[bnd], writes=[brd])
            fw.op("dve", lambda e: e.tensor_tensor(out=out_ap, in0=num[:, 0:N], in1=rd[:, 0:N], op=ALU.mult),
                  reads=[bnd, brd], writes=[bout])

        def xattn(l):
            with ExitStack() as st:
                xT = sb(st, "xT", (128, 8, NTOK), BF16)
                b_xT = [Buf() for _ in range(5)]
                with ExitStack() as st2:
                    norm_T(st2, dr["xattn_norm"][l:l + 1, :], xT, b_xT, "n_")
                    fw.flush()
                Wq, bWq = load_w_bf(st, dr["xattn_wq"][l].rearrange("(k p) n -> p k n", p=128), (8, 256), "Wq")
                Wkv, bWkv = load_w_bf(st, dr["xattn_wkv"][l].rearrange("(k p) n -> p k n", p=128), (8, 512), "Wkv")
                Wo, bWo = load_w_bf(st, dr["xattn_wo"][l].rearrange("(h d) n -> d h n", d=64)[:, :, :], (4, D), "Wo")
                Gq, bGq = bcast_row(st, dr["xattn_q_norm"][l:l + 1, :], 64, "Gq")
                Gk, bGk = bcast_row(st, dr["xattn_k_norm"][l:l + 1, :], 64, "Gk")
                Gm, bGm = bcast_row(st, dr["mem_norm"][l:l + 1, :], D, "Gm")
                KT = sb(st, "KT", (64, 3, 4, 256), BF16)
                bKT = [Buf(), Buf(), Buf()]
                VV = sb(st, "VV", (128, 3, 2, 256), BF16)
                bVV = [Buf(), Buf(), Buf()]
                ptk = ps(st, "ptk", (64, 8, 128), BF16)
                bptk = Buf()
                with ExitStack() as st3:
                    pkv = ps(st3, "pkv", (128, 512))
                    bpkv = Buf()
                    mem = sb(st3, "mem", (128, 2, D))
                    bmem = Buf()
                    fw.dma("sp", lambda e: e.dma_start(out=mem[:], in_=dr["memp"].rearrange("(t p) d -> p t d", p=128)), writes=[bmem])
                    mss = sb(st3, "mss", (128, 4))
                    bmss = Buf()
                    mjunk = sb(st3, "mjunk", (128, D))
                    bmj = Buf()
                    mn = sb(st3, "mn", (128, D), BF16)
                    bmn = Buf()
                    memT = sb(st3, "memT", (128, 8, 256), BF16)
                    bmemT = Buf()
                    pmt = ps(st3, "pmt", (128, 8, 128), BF16)
                    bpmt = Buf()
                    knf = sb(st3, "knf", (128, 4, 64))
                    bknf = Buf()
                    knb = sb(st3, "knb", (128, 4, 64), BF16)
                    bknb = Buf()
                    vf = sb(st3, "vf", (128, 256))
                    bvf = Buf()
                    for t in range(2):
                        fw.op("act", lambda e, t=t: e.activation(out=mjunk[:], in_=mem[:, t, :], func=AF.Square, accum_out=mss[:, t:t + 1]),
                              reads=[bmem], writes=[bmj, bmss])
                    fw.op("act", lambda e: e.activation(out=mss[:, 2:4], in_=mss[:, 0:2], func=AF.Sqrt, scale=1.0 / D, bias=EPS),
                          reads=[bmss], writes=[bmss])
                    fw.op("dve", lambda e: e.reciprocal(out=mss[:, 0:2], in_=mss[:, 2:4]), reads=[bmss], writes=[bmss])
                    for t in range(2):
                        fw.op("dve", lambda e, t=t: e.scalar_tensor_tensor(out=mn[:], in0=mem[:, t, :], scalar=mss[:, t:t + 1], in1=Gm[:],
                                                                            op0=ALU.mult, op1=ALU.mult), reads=[bmem, bmss, bGm], writes=[bmn])
                        for k in range(8):
                            fw.op("pe", lambda e, k=k: e.transpose(out=pmt[:, k, :], in_=mn[:, k * 128:(k + 1) * 128], identity=ident[:]),
                                  reads=[bmn, b_ident], writes=[bpmt])
                        fw.op("act", lambda e, t=t: e.copy(out=memT[:, :, t * 128:(t + 1) * 128], in_=pmt[:]), reads=[bpmt], writes=[bmemT])
                    for t in range(2):
                        for k in range(8):
                            fw.op("pe", lambda e, t=t, k=k: e.matmul(pkv[:], lhsT=memT[:, k, t * 128:(t + 1) * 128], rhs=Wkv[:, k, :],
                                                                      start=(k == 0), stop=(k == 7)), reads=[bmemT, bWkv], writes=[bpkv])
                        rms_heads(st3, pkv[:, 0:256].rearrange("p (h d) -> p h d", h=4), bpkv, 128, 4, Gk, bGk, 1.0,
                                  knb[:], bknb, out_f=knf[:], bout_f=bknf)
                        fw.op("act", lambda e: e.copy(out=vf[:], in_=pkv[:, 256:512]), reads=[bpkv], writes=[bvf])
                        fw.op("dve", lambda e, t=t: e.tensor_copy(out=VV[:, 0, t, :], in_=vf[:]), reads=[bvf], writes=[bVV[0]])
                        fw.dma("sp", lambda e, t=t: e.dma_start(out=dr["p_mem_k"][l, t * 128:(t + 1) * 128, :],
                                                                in_=knf[:].rearrange("p h d -> p (h d)")), reads=[bknf])
                        fw.dma("sp", lambda e, t=t: e.dma_start(out=dr["p_mem_v"][l, t * 128:(t + 1) * 128, :], in_=vf[:]), reads=[bvf])
                        for h in range(4):
                            fw.op("pe", lambda e, h=h: e.transpose(out=ptk[:, h, :], in_=knb[:, h, :], identity=ident[:]),
                                  reads=[bknb, b_ident], writes=[bptk])
                        fw.op("act", lambda e, t=t: e.copy(out=KT[:, 0, :, t * 128:(t + 1) * 128], in_=ptk[:, 0:4, :]), reads=[bptk], writes=[bKT[0]])
                    ck = sb(st3, "ck", (128, 2, 256))
                    bck = Buf()
                    ckb = sb(st3, "ckb", (128, 2, 4, 64), BF16)
                    bckb = Buf()
                    cv = sb(st3, "cv", (128, 2, 256))
                    bcv = Buf()
                    for b in range(2):
                        fw.dma("sp", lambda e, b=b: e.dma_start(out=ck[:], in_=dr["cmk"][l, b].rearrange("(t p) d -> p t d", p=128)), writes=[bck])
                        fw.dma("sp", lambda e, b=b: e.dma_start(out=cv[:], in_=dr["cmv"][l, b].rearrange("(t p) d -> p t d", p=128)), writes=[bcv])
                        fw.op("pool", lambda e: e.tensor_copy(out=ckb[:].rearrange("p t h d -> p t (h d)"), in_=ck[:]), reads=[bck], writes=[bckb])
                        fw.op("pool", lambda e, b=b: e.tensor_copy(out=VV[:, 1 + b, :, :], in_=cv[:]), reads=[bcv], writes=[bVV[1 + b]])
                        for t in range(2):
                            for h in range(4):
                                fw.op("pe", lambda e, t=t, h=h: e.transpose(out=ptk[:, h, :], in_=ckb[:, t, h, :], identity=ident[:]),
                                      reads=[bckb, b_ident], writes=[bptk])
                            fw.op("act", lambda e, t=t, b=b: e.copy(out=KT[:, 1 + b, :, t * 128:(t + 1) * 128], in_=ptk[:, 0:4, :]),
                                  reads=[bptk], writes=[bKT[1 + b]])

                    fw.flush()
                QT = sb(st, "QT", (64, 4, NTOK), BF16)
                bQT = [Buf() for _ in range(5)]
                OT = sb(st, "OT", (64, 4, NTOK), BF16)
                bOT = [Buf() for _ in range(5)]
                pq = [ps(st, "pq%d" % i, (128, 512)) for i in range(1)]
                bpq = [Buf()]
                qnb = sb(st, "qnb", (128, 4, 64), BF16)
                bqnb = Buf()
                for i in range(NT):
                    r = tile_rows(i)
                    for k in range(8):
                        fw.op("pe", lambda e, i=i, r=r, k=k: e.matmul(pq[0][0:r, 0:256], lhsT=xT[:, k, i * 128:i * 128 + r], rhs=Wq[:, k, :],
                                                                       start=(k == 0), stop=(k == 7)), reads=[b_xT[i // 4], bWq], writes=[bpq[0]])
                    rms_heads(st, pq[0][0:r, 0:256].rearrange("p (h d) -> p h d", h=4), bpq[0], r, 4, Gq, bGq, 0.125, qnb[0:r], bqnb)
                    for h in range(4):
                        fw.op("pe", lambda e, h=h, r=r: e.transpose(out=ptk[:, h, 0:r], in_=qnb[0:r, h, :], identity=ident[0:r, 0:r]),
                              reads=[bqnb, b_ident], writes=[bptk])
                    fw.op("act", lambda e, i=i, r=r: e.copy(out=QT[:, :, i * 128:i * 128 + r], in_=ptk[:, 0:4, 0:r]),
                          reads=[bptk], writes=[bQT[i // 4]])
                for h in range(4):
                    for tgi, (t0, n, tiles) in enumerate(TOK_GROUPS):
                        if tgi < 4:
                            segs = [(t0, n, 0)]
                        else:
                            segs = [(2048, 32, 1), (2080, 32, 2)]
                        for (q0, nq, sq_) in segs:
                            kbs = [dict(kT=KT[:, sq_, h, kt * 128:(kt + 1) * 128], v=VV[:, sq_, kt, h * 64:(h + 1) * 64], nk=128, col0=0,
                                        reads=[bKT[sq_], bVV[sq_]]) for kt in range(2)]
                            attn_core(st, QT[:, h, q0:q0 + nq], bQT[tgi], nq, kbs, OT[:, h, q0:q0 + nq], bOT[tgi])
                po = C.ac_S
                bpo = C.ac_bS
                c = 0
                for i in range(NT):
                    r = tile_rows(i)
                    for half in range(2):
                        q = c % 2
                        c += 1
                        for h in range(4):
                            fw.op("pe", lambda e, q=q, i=i, r=r, h=h, half=half: e.matmul(
                                po[q][0:r, :], lhsT=OT[:, h, i * 128:i * 128 + r], rhs=Wo[0:64, h, half * 512:(half + 1) * 512],
                                start=(h == 0), stop=(h == 3)), reads=[bOT[i // 4], bWo], writes=[bpo[q]])
                        fw.op("dve", lambda e, q=q, r=r, i=i, half=half: e.tensor_tensor(
                            out=X[0:r, i, half * 512:(half + 1) * 512], in0=po[q][0:r, :], in1=X[0:r, i, half * 512:(half + 1) * 512],
                            op=ALU.add), reads=[bpo[q], bX[i]], writes=[bX[i]])
                fw.flush()


        def fox_part(l, xT, b_xT):
            e_ = l // 2
            wv = dr["ev_w_in"][e_].rearrange("(k p) n -> p k n", p=128)
            with ExitStack() as st:
                Gq, bGq = bcast_row(st, dr["fox_q_norm"][e_:e_ + 1, :], 64, "Gfq")
                Gk, bGk = bcast_row(st, dr["fox_k_norm"][e_:e_ + 1, :], 64, "Gfk")
                Bf, bBf = bcast_row(st, dr["fox_b_f"][e_:e_ + 1, :], 8, "Bf")
                TRI = sb(st, "TRI", (128, 128))
                TRIB = sb(st, "TRIB", (64, 64))
                ONESF = sb(st, "ONESF", (128, 128))
                MASKT = sb(st, "MASKT", (128, 128), BF16)
                bcon = Buf()
                fw.dma("sp", lambda e: e.dma_start(out=TRI[:], in_=dr["c_tri"]), writes=[bcon])
                fw.dma("sp", lambda e: e.dma_start(out=TRIB[:], in_=dr["c_trib"]), writes=[bcon])
                fw.op("pool", lambda e: e.memset(ONESF[:], 1.0), writes=[bcon])
                fw.op("dve", lambda e: e.tensor_copy(out=MASKT[:], in_=TRI[:]), reads=[bcon], writes=[bcon])
                CUM3 = sb(st, "CUM3", (128, NT, 3, 8), BF16)
                CUM3N = sb(st, "CUM3N", (128, NT, 3, 8), BF16)
                bCUM = Buf()
                CC3N = sb(st, "CC3N", (128, 2, 16, 3, 8), BF16)
                bCC = Buf()
                OFF = sb(st, "OFF", (128, 8))
                bOFF = Buf()
                OFFC = sb(st, "OFFC", (128, 2, 8))
                bOFFC = Buf()
                OFF16 = sb(st, "OFF16", (64, 8))
                bOFF16 = Buf()
                LFA = sb(st, "LFA", (128, NT, 8))
                bLFA = Buf()
                CL = sb(st, "CL", (128, 2, 16, 8))
                bCL = Buf()
                pm = ps(st, "pm", (128, 512))
                bpm = Buf()
                ctmp = sb(st, "ctmp", (128, 4, 8))
                bct = Buf()
                CSCR = sb(st, "CSCR", (128, 3, 8), BF16)
                ptk = ps(st, "ptkf", (72, 8, 128), BF16)
                bptk = Buf()
                pp = ps(st, "ppf", (128, 512))
                bpp = Buf()

                def split3(src_ap, r, dst3, dst3n, bdst):
                    t = ctmp
                    fw.op("dve", lambda e: e.tensor_copy(out=dst3[0:r, 0, :], in_=src_ap), reads=[bct], writes=[bdst])
                    fw.op("dve", lambda e: e.tensor_tensor(out=t[0:r, 1, :], in0=src_ap, in1=dst3[0:r, 0, :], op=ALU.subtract),
                          reads=[bct, bdst], writes=[bct])
                    fw.op("dve", lambda e: e.tensor_copy(out=dst3[0:r, 1, :], in_=t[0:r, 1, :]), reads=[bct], writes=[bdst])
                    fw.op("dve", lambda e: e.tensor_tensor(out=t[0:r, 2, :], in0=t[0:r, 1, :], in1=dst3[0:r, 1, :], op=ALU.subtract),
                          reads=[bct, bdst], writes=[bct])
                    fw.op("dve", lambda e: e.tensor_copy(out=dst3[0:r, 2, :], in_=t[0:r, 2, :]), reads=[bct], writes=[bdst])
                    if dst3n is not None:
                        fw.op("dve", lambda e: e.tensor_scalar(out=dst3n[0:r], in0=dst3[0:r], scalar1=-1.0, scalar2=None, op0=ALU.mult),
                              reads=[bdst], writes=[bdst])

                def cum_tile(lf_ap, blf, r, tri_ap, off_ap, boff, dst3, dst3n, bdst, update_off):
                    if not cfg.get("fox_cum", True):
                        return
                    fw.op("pe", lambda e: e.matmul(pm[0:r, 8:16], lhsT=tri_ap, rhs=lf_ap, start=True, stop=True),
                          reads=[blf, bcon], writes=[bpm])
                    if update_off:
                        fw.op("pe", lambda e: e.matmul(pm[0:r, 16:24], lhsT=ONESF[0:r, 0:r], rhs=lf_ap, start=True, stop=True),
                              reads=[blf, bcon], writes=[bpm])
                    fw.op("dve", lambda e: e.tensor_tensor(out=ctmp[0:r, 0, :], in0=pm[0:r, 8:16], in1=off_ap, op=ALU.add),
                          reads=[bpm, boff], writes=[bct])
                    if update_off:
                        fw.op("dve", lambda e: e.tensor_tensor(out=off_ap, in0=pm[0:r, 16:24], in1=off_ap, op=ALU.add),
                              reads=[bpm, boff], writes=[boff])
                    split3(ctmp[0:r, 0, :], r, dst3, dst3n, bdst)

                fw.op("pool", lambda e: e.memset(OFF[:], 0.0), writes=[bOFF])
                fw.op("pool", lambda e: e.memset(OFFC[:], 0.0), writes=[bOFFC])
                for b in range(2):
                    for t in range(16):
                        fw.dma("sp", lambda e, b=b, t=t: e.dma_start(out=CL[:, b, t, :], in_=dr["cfl"][b, t * 128:(t + 1) * 128, :]), writes=[bCL])
                for hg in range(2):
                    with ExitStack() as sh:
                        QT = sb(sh, "QT", (72, 4, NTOK), BF16)
                        bQT = [[Buf() for _ in range(6)] for _ in range(4)]
                        KT = sb(sh, "KT", (72, 4, NTOK), BF16)
                        bKT = [Buf() for _ in range(5)]
                        VA = sb(sh, "VA", (128, NT, 256), BF16)
                        bVA = [Buf() for _ in range(5)]
                        VAn = sb(sh, "VAn", (32, 2, 256), BF16)
                        bVAn = Buf()
                        QA = [sb(sh, "QA%d" % i, (128, 4, 72), BF16) for i in range(2)]
                        KA = [sb(sh, "KA%d" % i, (128, 4, 72), BF16) for i in range(2)]
                        bQA = [Buf(), Buf()]
                        bKA = [Buf(), Buf()]
                        for j in range(2):
                            fw.op("pool", lambda e, j=j: e.memset(QA[j][:], 1.0), writes=[bQA[j]])
                            fw.op("pool", lambda e, j=j: e.memset(KA[j][:], 1.0), writes=[bKA[j]])
                        sw = ExitStack()
                        Wq4, bWq4 = load_w_bf(sw, wv[:, :, hg * 256:hg * 256 + 256], (8, 256), "Wq4", stage_cols=1024)
                        Wk4, bWk4 = load_w_bf(sw, wv[:, :, 512 + hg * 256:512 + hg * 256 + 256], (8, 256), "Wk4", stage_cols=1024)
                        Wv4, bWv4 = load_w_bf(sw, wv[:, :, 1024 + hg * 256:1024 + hg * 256 + 256], (8, 256), "Wv4", stage_cols=1024)
                        if hg == 0:
                            Wff, bWff = load_w_bf(sw, wv[:, :, 1536:1544], (8, 8), "Wff", stage_cols=1024)
                        knf = [sb(sw, "fknf%d" % i, (128, 4, 64)) for i in range(2)]
                        bknf = [Buf(), Buf()]
                        vf = [sb(sw, "fvf%d" % i, (128, 256)) for i in range(2)]
                        bvf = [Buf(), Buf()]
                        for i in range(cfg.get("fox_ntiles", NT)):
                            r = tile_rows(i)
                            j = i % 2
                            rows = slice(i * 128, i * 128 + r)
                            g5 = i // 4
                            if hg == 0:
                                for k in range(8):
                                    fw.op("pe", lambda e, k=k, i=i, r=r: e.matmul(
                                        pm[0:r, 0:8], lhsT=xT[:, k, i * 128:i * 128 + r], rhs=Wff[:, k, :],
                                        start=(k == 0), stop=(k == 7)), reads=[b_xT[g5], bWff], writes=[bpm])
                                lf = LFA[0:r, i, :]
                                fw.op("dve", lambda e, lf=lf, r=r: e.tensor_tensor(out=lf, in0=pm[0:r, 0:8], in1=Bf[0:r, :], op=ALU.add),
                                      reads=[bpm, bBf], writes=[bLFA])
                                fw.op("act", lambda e, lf=lf: e.activation(out=lf, in_=lf, func=AF.Exp, scale=-1.0), reads=[bLFA], writes=[bLFA])
                                fw.op("act", lambda e, lf=lf: e.activation(out=lf, in_=lf, func=AF.Ln, bias=1.0), reads=[bLFA], writes=[bLFA])
                                fw.op("dve", lambda e, lf=lf: e.tensor_scalar(out=lf, in0=lf, scalar1=-1.0, scalar2=None, op0=ALU.mult),
                                      reads=[bLFA], writes=[bLFA])
                                fw.dma("sp", lambda e, lf=lf, rows=rows: e.dma_start(out=dr["fox_logf"][rows, :], in_=lf), reads=[bLFA])
                                if i < 16:
                                    cum_tile(lf, bLFA, 128, TRI[:], OFF[:], bOFF, CUM3[:, i], CUM3N[:, i], bCUM, True)
                                else:
                                    for b in range(2):
                                        for t in range(16):
                                            cum_tile(CL[:, b, t, :], bCL, 128, TRI[:], OFFC[:, b, :], bOFFC, CSCR[:], CC3N[:, b, t], bCC, True)
                                        fw.op("dve", lambda e, b=b: e.tensor_copy(out=OFF16[32 * b:32 * b + 32, :], in_=OFFC[32 * b:32 * b + 32, b, :]),
                                              reads=[bOFFC], writes=[bOFF16])
                                    cum_tile(lf, bLFA, 64, TRIB[:], OFF16[:], bOFF16, CUM3[:, 16], CUM3N[:, 16], bCUM, False)
                            if cfg.get("fox_stage", 9) < 2:
                                continue
                            for k in range(8):
                                fw.op("pe", lambda e, k=k, i=i, r=r: e.matmul(
                                    pp[0:r, 0:256], lhsT=xT[:, k, i * 128:i * 128 + r], rhs=Wq4[:, k, :],
                                    start=(k == 0), stop=(k == 7)), reads=[b_xT[g5], bWq4], writes=[bpp])
                            rms_heads(sw, pp[0:r, 0:256].rearrange("p (h d) -> p h d", h=4), bpp, r, 4, Gq, bGq, 0.125,
                                      QA[j][0:r, :, 0:64], bQA[j])
                            for c3 in range(3):
                                fw.op("dve", lambda e, j=j, r=r, i=i, c3=c3: e.tensor_copy(
                                    out=QA[j][0:r, :, 64 + c3], in_=CUM3[0:r, i, c3, 4 * hg:4 * hg + 4]), reads=[bCUM], writes=[bQA[j]])
                            for h in range(4):
                                fw.op("pe", lambda e, h=h, r=r, j=j: e.transpose(out=ptk[0:72, h, 0:r], in_=QA[j][0:r, h, :], identity=ident[0:r, 0:r]),
                                      reads=[bQA[j], b_ident], writes=[bptk])
                            qbufs = [bQT[h][g5] for h in range(4)] if i < 16 else [bQT[h][4] for h in range(4)] + [bQT[h][5] for h in range(4)]
                            fw.op("act", lambda e, i=i, r=r: e.copy(out=QT[:, :, i * 128:i * 128 + r], in_=ptk[:, 0:4, 0:r]),
                                  reads=[bptk], writes=qbufs)
                            if cfg.get("fox_stage", 9) < 3:
                                continue
                            for k in range(8):
                                fw.op("pe", lambda e, k=k, i=i, r=r: e.matmul(
                                    pp[0:r, 256:512], lhsT=xT[:, k, i * 128:i * 128 + r], rhs=Wk4[:, k, :],
                                    start=(k == 0), stop=(k == 7)), reads=[b_xT[g5], bWk4], writes=[bpp])
                            rms_heads(sw, pp[0:r, 256:512].rearrange("p (h d) -> p h d", h=4), bpp, r, 4, Gk, bGk, 1.0,
                                      KA[j][0:r, :, 0:64], bKA[j], out_f=knf[j][0:r], bout_f=bknf[j])
                            fw.dma("sp", lambda e, j=j, r=r, rows=rows: e.dma_start(
                                out=dr["fox_k"][rows, hg * 256:hg * 256 + 256], in_=knf[j][0:r].rearrange("p h d -> p (h d)")), reads=[bknf[j]])
                            for c3 in range(3):
                                fw.op("dve", lambda e, j=j, r=r, i=i, c3=c3: e.tensor_copy(
                                    out=KA[j][0:r, :, 67 + c3], in_=CUM3N[0:r, i, c3, 4 * hg:4 * hg + 4]), reads=[bCUM], writes=[bKA[j]])
                            for h in range(4):
                                fw.op("pe", lambda e, h=h, r=r, j=j: e.transpose(out=ptk[0:72, h, 0:r], in_=KA[j][0:r, h, :], identity=ident[0:r, 0:r]),
                                      reads=[bKA[j], b_ident], writes=[bptk])
                            fw.op("act", lambda e, i=i, r=r: e.copy(out=KT[:, :, i * 128:i * 128 + r], in_=ptk[:, 0:4, 0:r]),
                                  reads=[bptk], writes=[bKT[g5]])
                            if cfg.get("fox_stage", 9) < 4:
                                continue
                            for k in range(8):
                                fw.op("pe", lambda e, k=k, i=i, r=r: e.matmul(
                                    pp[0:r, 0:256], lhsT=xT[:, k, i * 128:i * 128 + r], rhs=Wv4[:, k, :],
                                    start=(k == 0), stop=(k == 7)), reads=[b_xT[g5], bWv4], writes=[bpp])
                            fw.op("act", lambda e, j=j, r=r: e.copy(out=vf[j][0:r, :], in_=pp[0:r, 0:256]), reads=[bpp], writes=[bvf[j]])
                            fw.op("dve", lambda e, i=i, r=r, j=j: e.tensor_copy(out=VA[0:r, i, :], in_=vf[j][0:r, :]), reads=[bvf[j]], writes=[bVA[g5]])
                            fw.dma("sp", lambda e, j=j, r=r, rows=rows: e.dma_start(out=dr["fox_v"][rows, hg * 256:hg * 256 + 256], in_=vf[j][0:r, :]),
                                   reads=[bvf[j]])
                            if i == 16:
                                for b in range(2):
                                    fw.op("dve", lambda e, b=b: e.tensor_copy(out=VAn[:, b, :], in_=VA[32 * b:32 * b + 32, 16, :]),
                                          reads=[bVA[4]], writes=[bVAn])
                        fw.flush()
                        sw.close()
                        if not (cfg.get("fox_pattn", True) or cfg.get("fox_sattn", True)):
                            continue
                        Wo4, bWo4 = load_w_bf(sh, dr["ev_w_out"][e_][hg * 256:hg * 256 + 256, :].rearrange("(h d) n -> d h n", d=64),
                                              (4, D), "Wo4", stage_cols=1024)
                        OTg = [sb(sh, "OTg%d" % i, (64, 4, 512), BF16) for i in range(2)]
                        bOTg = [Buf(), Buf()]

                        def out_proj(OT_t, bOT_t, tiles):
                            po, bpo = C.ac_S, C.ac_bS
                            for tl_i, i in enumerate(tiles):
                                r = tile_rows(i)
                                for half in range(2):
                                    q = C.opc % 2
                                    C.opc += 1
                                    for h in range(4):
                                        fw.op("pe", lambda e, q=q, tl_i=tl_i, r=r, h=h, half=half: e.matmul(
                                            po[q][0:r, :], lhsT=OT_t[:, h, tl_i * 128:tl_i * 128 + r], rhs=Wo4[0:64, h, half * 512:(half + 1) * 512],
                                            start=(h == 0), stop=(h == 3)), reads=[bOT_t, bWo4], writes=[bpo[q]])
                                    fw.op("dve", lambda e, q=q, r=r, i=i, half=half: e.tensor_tensor(
                                        out=X[0:r, i, half * 512:(half + 1) * 512], in0=po[q][0:r, :], in1=X[0:r, i, half * 512:(half + 1) * 512],
                                        op=ALU.add), reads=[bpo[q], bX[i]], writes=[bX[i]])
                        C.opc = 0
                        for g in (range(4) if cfg.get("fox_pattn", True) else []):
                            s_ = g % 2
                            for h in range(4):
                                kbs = []
                                for kt in range(4 * g + 4):
                                    c0 = max(0, kt * 128 - g * 512)
                                    kbs.append(dict(kT=KT[0:70, h, kt * 128:(kt + 1) * 128], v=VA[:, kt, h * 64:(h + 1) * 64], nk=128, col0=c0,
                                                    reads=[bKT[kt // 4], bVA[kt // 4]],
                                                    mask=(MASKT[:] if kt >= 4 * g else None), mask_reads=[bcon]))
                                attn_core(sh, QT[0:70, h, g * 512:(g + 1) * 512], bQT[h][g], 512, kbs, OTg[s_][:, h, :], bOTg[s_], nbuf=1)
                            out_proj(OTg[s_], bOTg[s_], [4 * g, 4 * g + 1, 4 * g + 2, 4 * g + 3])
                        ckf = sb(sh, "ckf", (128, 4, 256))
                        bckf = Buf()
                        cvf = sb(sh, "cvf", (128, 4, 256))
                        bcvf = Buf()
                        for b in (range(2) if cfg.get("fox_sattn", True) else []):
                            for t4 in range(4):
                                fw.dma("sp", lambda e, b=b, t4=t4: e.dma_start(
                                    out=ckf[:], in_=dr["cfk"][b, t4 * 512:(t4 + 1) * 512, hg * 256:hg * 256 + 256].rearrange("(t p) d -> p t d", p=128)),
                                    writes=[bckf])
                                fw.dma("sp", lambda e, b=b, t4=t4: e.dma_start(
                                    out=cvf[:], in_=dr["cfv"][b, t4 * 512:(t4 + 1) * 512, hg * 256:hg * 256 + 256].rearrange("(t p) d -> p t d", p=128)),
                                    writes=[bcvf])
                                fw.op("pool", lambda e, t4=t4: e.tensor_copy(out=VA[:, 4 * t4:4 * t4 + 4, :], in_=cvf[:]), reads=[bcvf], writes=[bVA[t4]])
                                for tt in range(4):
                                    t = 4 * t4 + tt
                                    j = t % 2
                                    fw.op("pool", lambda e, j=j, tt=tt: e.tensor_copy(
                                        out=KA[j][:, :, 0:64], in_=ckf[:, tt, :].rearrange("p (h d) -> p h d", h=4)), reads=[bckf], writes=[bKA[j]])
                                    for c3 in range(3):
                                        fw.op("dve", lambda e, j=j, b=b, t=t, c3=c3: e.tensor_copy(
                                            out=KA[j][:, :, 67 + c3], in_=CC3N[:, b, t, c3, 4 * hg:4 * hg + 4]), reads=[bCC], writes=[bKA[j]])
                                    for h in range(4):
                                        fw.op("pe", lambda e, h=h, j=j: e.transpose(out=ptk[0:72, h, :], in_=KA[j][:, h, :], identity=ident[:]),
                                              reads=[bKA[j], b_ident], writes=[bptk])
                                    fw.op("act", lambda e, t=t: e.copy(out=KT[:, :, t * 128:(t + 1) * 128], in_=ptk[:, 0:4, :]),
                                          reads=[bptk], writes=[bKT[t4]])
                            for h in range(4):
                                q0 = 2048 + 32 * b
                                kbs = [dict(kT=KT[0:70, h, kt * 128:(kt + 1) * 128], v=VA[:, kt, h * 64:(h + 1) * 64], nk=128, col0=0,
                                            reads=[bKT[kt // 4], bVA[kt // 4]]) for kt in range(16)]
                                kbs.append(dict(kT=KT[0:70, h, q0:q0 + 32], v=VAn[:, b, h * 64:(h + 1) * 64], nk=32, col0=0,
                                                reads=[bKT[4], bVAn], mask=MASKT[0:32, 0:32], mask_reads=[bcon], mask_w=32))
                                attn_core(sh, QT[0:70, h, q0:q0 + 32], bQT[h][4 + b], 32, kbs, OTg[0][:, h, 32 * b:32 * b + 32], bOTg[0], nbuf=1)
                        if cfg.get("fox_sattn", True):
                            out_proj(OTg[0], bOTg[0], [16])
                        fw.flush()


        def rwkv_part(l, xT, b_xT):
            e_ = l // 2
            wv = dr["ev_w_in"][e_].rearrange("(k p) n -> p k n", p=128)
            NG = 128
            with ExitStack() as st:
                pA = ps(st, "rpA", (128, 512))
                pB = ps(st, "rpB", (128, 512))
                pT1 = ps(st, "rpT1", (128, 1024), BF16)
                pT2 = ps(st, "rpT2", (128, 1024), BF16)
                pGK = ps(st, "rpGK", (64, 1024))
                pGB = ps(st, "rpGB", (64, 1024))
                bpA, bpB, bpT1, bpT2, bpGK0, bpGK1, bpGB0, bpGB1 = [Buf() for _ in range(8)]
                PC = sb(st, "PC", (128, 70))
                bPC = Buf()
                OM = sb(st, "OM", (128, 18))
                WAb = sb(st, "WAb", (128, 512), BF16)
                G2b = sb(st, "G2b", (128, 512), BF16)
                BD = sb(st, "BD", (128, 128), BF16)
                MSK = sb(st, "MSK", (64, 3, 64))
                RST = sb(st, "RST", (128, 128))
                bW = Buf()
                with ExitStack() as stt:
                    PR = sb(stt, "PR", (70, 128))
                    bPR = Buf()
                    rows = [("rwkv_mu", 14), ("rwkv_w0", 4), ("rwkv_a0", 4), ("rwkv_k_k", 4), ("rwkv_k_a", 4), ("rwkv_r_k", 4),
                            ("rwkv_ln_g", 4), ("rwkv_ln_b", 4)]
                    r0 = 0
                    for nm, nr in rows:
                        src = dr[nm][e_].rearrange("(c p) -> c p", p=128)
                        fw.dma("sp", lambda e, src=src, r0=r0, nr=nr: e.dma_start(out=PR[r0:r0 + nr, :], in_=src), writes=[bPR])
                        r0 += nr
                    fw.dma("sp", lambda e: e.dma_start(out=PR[42:70, :], in_=dr["srs"].rearrange("b (c p) -> (b c) p", p=128)), writes=[bPR])
                    fw.op("pe", lambda e: e.matmul(pA[:, 0:70], lhsT=PR[0:70, :], rhs=ident_f[0:70, 0:70], start=True, stop=True),
                          reads=[bPR, b_ident], writes=[bpA])
                    fw.op("dve", lambda e: e.tensor_copy(out=PC[:], in_=pA[:, 0:70]), reads=[bpA], writes=[bPC])
                    fw.op("dve", lambda e: e.tensor_scalar(out=OM[:, 0:14], in0=PC[:, 0:14], scalar1=-1.0, scalar2=1.0, op0=ALU.mult, op1=ALU.add),
                          reads=[bPC], writes=[bPC])
                    fw.op("dve", lambda e: e.tensor_scalar(out=OM[:, 14:18], in0=PC[:, 26:30], scalar1=-1.0, scalar2=1.0, op0=ALU.mult, op1=ALU.add),
                          reads=[bPC], writes=[bPC])
                    WAf = sb(stt, "WAf", (128, 512))
                    G2f = sb(stt, "G2f", (128, 512))
                    BDf = sb(stt, "BDf", (128, 128))
                    fw.dma("sp", lambda e: e.dma_start(out=WAf[0:64, :], in_=dr["rwkv_w2"][e_]), writes=[bW])
                    fw.dma("sp", lambda e: e.dma_start(out=WAf[64:128, :], in_=dr["rwkv_a2"][e_]), writes=[bW])
                    fw.dma("sp", lambda e: e.dma_start(out=G2f[:], in_=dr["rwkv_g2"][e_]), writes=[bW])
                    fw.dma("sp", lambda e: e.dma_start(out=BDf[:], in_=dr["c_bd"]), writes=[bW])
                    fw.dma("sp", lambda e: e.dma_start(out=MSK[:], in_=dr["c_msk"]), writes=[bW])
                    fw.dma("sp", lambda e: e.dma_start(out=RST[:], in_=dr["c_rst"][:, 0:128]), writes=[bW])
                    fw.op("pool", lambda e: e.tensor_copy(out=WAb[:], in_=WAf[:]), reads=[bW], writes=[bW])
                    fw.op("pool", lambda e: e.tensor_copy(out=G2b[:], in_=G2f[:]), reads=[bW], writes=[bW])
                    fw.op("pool", lambda e: e.tensor_copy(out=BD[:], in_=BDf[:]), reads=[bW], writes=[bW])
                    fw.flush()
                MU = lambda cc: PC[:, cc:cc + 1]
                OMMU = lambda cc: OM[:, cc:cc + 1]
                W0 = lambda hp: PC[:, 14 + hp:15 + hp]
                A0 = lambda hp: PC[:, 18 + hp:19 + hp]
                KK_ = lambda hp: PC[:, 22 + hp:23 + hp]
                KA_ = lambda hp: PC[:, 26 + hp:27 + hp]
                RK_ = lambda hp: PC[:, 30 + hp:31 + hp]
                LNG = lambda hp: PC[:, 34 + hp:35 + hp]
                LNB = lambda hp: PC[:, 38 + hp:39 + hp]
                OMKA = lambda hp: OM[:, 14 + hp:15 + hp]
                WoR, bWoR = load_w_bf(st, dr["ev_w_out"][e_][512:1024, :].rearrange("(h p) n -> p h n", p=128), (4, D), "WoR", stage_cols=1024)
                Wc = [sb(st, "Wc%d" % i, (128, 8, 128), BF16) for i in range(2)]
                bWc = [Buf(), Buf()]
                wcs = [sb(st, "wcs%d" % i, (128, 8, 128)) for i in range(2)]
                bwcs = [Buf(), Buf()]
                wcc = [0]
                ST = sb(st, "ST", (128, 4, 64))
                STb = sb(st, "STb", (128, 4, 64), BF16)
                bST = Buf()
                CAR = sb(st, "CAR", (128, 14))
                bCAR = Buf()
                HX = sb(st, "HX", (128, 5, NG))
                bHX = [Buf() for _ in range(5)]
                TMPL = sb(st, "TMPL", (128, NG))
                bTMPL = Buf()
                F = {}
                for nm in ("LW", "A", "KKR", "KKN", "KM", "T1", "CUM", "EP", "EN", "EPV", "EH", "BETA"):
                    F[nm] = sb(st, "f_" + nm, (128, NG))
                bF = {nm: Buf() for nm in F}
                SQb = sb(st, "SQb", (128, NG), BF16)
                bSQb = Buf()
                TXW = sb(st, "TXW", (128, NG), BF16)
                SXG = sb(st, "SXG", (128, NG), BF16)
                bTXW = Buf()
                bSXG = Buf()
                CUMC = sb(st, "CUMC", (128, 4))
                SL = []
                for si_ in range(2):
                    d_ = dict(
                        PCS=sb(st, "PCS%d" % si_, (128, 4, 2)), bPCS=Buf(),
                        KR=sb(st, "KR%d" % si_, (128, 4, 2, 2, 64), BF16), KB=sb(st, "KB%d" % si_, (128, 4, 2, 2, 64), BF16),
                        bKR=[Buf() for _ in range(4)], bKB=[Buf() for _ in range(4)],
                        KRo=sb(st, "KRo%d" % si_, (64, 4, 2, 2, 64), BF16), KBo=sb(st, "KBo%d" % si_, (64, 4, 2, 2, 64), BF16),
                        KH=sb(st, "KH%d" % si_, (128, 4, NG), BF16), BH=sb(st, "BH%d" % si_, (128, 4, NG), BF16), VB=sb(st, "VB%d" % si_, (128, 4, NG), BF16),
                        bKH=[Buf() for _ in range(4)],
                        Gt=sb(st, "Gt%d" % si_, (128, 4, NG), BF16), BON=sb(st, "BON%d" % si_, (128, 4, NG), BF16),
                        bGt=[Buf() for _ in range(4)], bBON=[Buf() for _ in range(4)])
                    SL.append(d_)
                STbo = sb(st, "STbo", (64, 4, 64), BF16)
                FT = sb(st, "FT", (128, NG))
                bFT = Buf()
                ONT = sb(st, "ONT", (128, 4, NG), BF16)
                bONT = Buf()
                MO = sb(st, "MO", (128, 4, NG), BF16)
                bMO = Buf()
                TM = sb(st, "TM", (64, 3, 512), BF16)
                bTM = Buf()
                cb = {}
                for nm in ("AKK", "GKR", "NGBR", "X1", "X2", "Y1", "Y2", "T1b", "T2b", "WTs", "UTs", "ONb"):
                    cb[nm] = sb(st, "c_" + nm, (64, 8, 64), BF16)
                bcb = {nm: Buf() for nm in cb}
                OS = sb(st, "OS", (64, 8, 64))
                OSQ = sb(st, "OSQ", (64, 8, 64))
                bOS = Buf()
                bOSQ = Buf()
                STAT = sb(st, "STAT", (64, 6, 8))
                bSTAT = Buf()
                sto = sb(st, "sto", (64, 4, 128))
                bsto = Buf()

                def v3(ap, C_):
                    return ap.rearrange("p (c t) -> p c t", t=C_)

                def project(cc, slot, tok0, n):
                    w = wcc[0] % 2
                    wcc[0] += 1
                    c0 = 1544 + cc * 128
                    fw.dma("sp", lambda e, w=w, c0=c0: e.dma_start(out=wcs[w][:], in_=wv[:, :, c0:c0 + 128]), writes=[bwcs[w]])
                    fw.op("pool", lambda e, w=w: e.tensor_copy(out=Wc[w][:], in_=wcs[w][:]), reads=[bwcs[w]], writes=[bWc[w]])
                    pj, bpj = (pA, bpA) if w == 0 else (pB, bpB)
                    for k in range(8):
                        fw.op("pe", lambda e, k=k, w=w, pj=pj: e.matmul(pj[:, 0:n], lhsT=Wc[w][:, k, :], rhs=xT[:, k, tok0:tok0 + n],
                                                                       start=(k == 0), stop=(k == 7)),
                              reads=[bWc[w], b_xT[min(tok0 // 512, 4)]], writes=[bpj])
                    fw.op("act", lambda e, pj=pj: e.activation(out=HX[:, slot, 0:n], in_=pj[:, 0:n], func=AF.Copy, scale=OMMU(cc)),
                          reads=[bpj, bPC], writes=[bHX[slot]])
                    if n > 1:
                        fw.op("dve", lambda e, pj=pj: e.scalar_tensor_tensor(out=HX[:, slot, 1:n], in0=pj[:, 0:n - 1], scalar=MU(cc), in1=HX[:, slot, 1:n],
                                                                            op0=ALU.mult, op1=ALU.add), reads=[bpj, bPC, bHX[slot]], writes=[bHX[slot]])
                    fw.op("dve", lambda e: e.scalar_tensor_tensor(out=HX[:, slot, 0:1], in0=CAR[:, cc:cc + 1], scalar=MU(cc), in1=HX[:, slot, 0:1],
                                                                  op0=ALU.mult, op1=ALU.add), reads=[bCAR, bPC, bHX[slot]], writes=[bHX[slot]])
                    fw.op("act", lambda e, pj=pj: e.copy(out=CAR[:, cc:cc + 1], in_=pj[:, n - 1:n]), reads=[bpj], writes=[bCAR])

                def prep(slot, tok0, n, C_):
                    nch = n // C_
                    d_ = SL[slot]
                    PCS, bPCS, KR, KB, bKR, bKB, KRo, KBo = d_["PCS"], d_["bPCS"], d_["KR"], d_["KB"], d_["bKR"], d_["bKB"], d_["KRo"], d_["KBo"]
                    KH, BH, VB, bKH, Gt, BON, bGt, bBON = d_["KH"], d_["BH"], d_["VB"], d_["bKH"], d_["Gt"], d_["BON"], d_["bGt"], d_["bBON"]
                    project(12, 0, tok0, n)
                    project(13, 1, tok0, n)
                    fw.op("act", lambda e: e.activation(out=TXW[0:64, 0:n], in_=HX[0:64, 0, 0:n], func=AF.Tanh), reads=[bHX[0]], writes=[bTXW])
                    fw.op("dve", lambda e: e.tensor_copy(out=TXW[64:128, 0:n], in_=HX[64:128, 0, 0:n]), reads=[bHX[0]], writes=[bTXW])
                    fw.op("act", lambda e: e.activation(out=SXG[:, 0:n], in_=HX[:, 1, 0:n], func=AF.Sigmoid), reads=[bHX[1]], writes=[bSXG])
                    def _prep(hp):
                        project(hp, 2, tok0, n)
                        project(4 + hp, 3, tok0, n)
                        project(8 + hp, 4, tok0, n)
                        r_, k_, v_ = HX[:, 2, 0:n], HX[:, 3, 0:n], HX[:, 4, 0:n]
                        rk_reads = [bHX[2], bHX[3], bHX[4]]
                        cs = slice(hp * 128, (hp + 1) * 128)
                        f = lambda nm: F[nm][:, 0:n]
                        fw.op("pe", lambda e: e.matmul(pA[:, 0:n], lhsT=WAb[0:64, cs], rhs=TXW[0:64, 0:n], start=True, stop=True),
                              reads=[bW, bTXW], writes=[bpA])
                        fw.op("act", lambda e: e.activation(out=f("LW"), in_=pA[:, 0:n], func=AF.Sigmoid, bias=W0(hp)), reads=[bpA, bPC], writes=[bF["LW"]])
                        fw.op("dve", lambda e: e.tensor_scalar(out=f("LW"), in0=f("LW"), scalar1=-0.6065306597126334, scalar2=None, op0=ALU.mult),
                              reads=[bF["LW"]], writes=[bF["LW"]])
                        fw.op("pe", lambda e: e.matmul(pB[:, 0:n], lhsT=WAb[64:128, cs], rhs=TXW[64:128, 0:n], start=True, stop=True),
                              reads=[bW, bTXW], writes=[bpB])
                        fw.op("act", lambda e: e.activation(out=f("A"), in_=pB[:, 0:n], func=AF.Sigmoid, bias=A0(hp)), reads=[bpB, bPC], writes=[bF["A"]])
                        fw.op("pe", lambda e: e.matmul(pA[:, 0:n], lhsT=G2b[:, cs], rhs=SXG[:, 0:n], start=True, stop=True),
                              reads=[bW, bSXG], writes=[bpA])
                        fw.op("act", lambda e: e.copy(out=Gt[:, hp, 0:n], in_=pA[:, 0:n]), reads=[bpA], writes=[bGt[hp]])
                        fw.op("dve", lambda e: e.tensor_scalar(out=f("KKR"), in0=k_, scalar1=KK_(hp), scalar2=None, op0=ALU.mult),
                              reads=[bHX[3], bPC], writes=[bF["KKR"]])
                        fw.op("pool", lambda e: e.tensor_tensor(out=SQb[:, 0:n], in0=f("KKR"), in1=f("KKR"), op=ALU.mult), reads=[bF["KKR"]], writes=[bSQb])
                        fw.op("pe", lambda e: e.matmul(pB[:, 0:n], lhsT=BD[:], rhs=SQb[:, 0:n], start=True, stop=True), reads=[bW, bSQb], writes=[bpB])
                        fw.op("act", lambda e: e.activation(out=f("T1"), in_=pB[:, 0:n], func=AF.Sqrt), reads=[bpB], writes=[bF["T1"]])
                        fw.op("dve", lambda e: e.tensor_scalar(out=f("T1"), in0=f("T1"), scalar1=1e-12, scalar2=None, op0=ALU.max), reads=[bF["T1"]], writes=[bF["T1"]])
                        fw.op("dve", lambda e: e.reciprocal(out=f("T1"), in_=f("T1")), reads=[bF["T1"]], writes=[bF["T1"]])
                        fw.op("pool", lambda e: e.tensor_tensor(out=f("KKN"), in0=f("KKR"), in1=f("T1"), op=ALU.mult), reads=[bF["KKR"], bF["T1"]], writes=[bF["KKN"]])
                        fw.op("dve", lambda e: e.tensor_scalar(out=f("T1"), in0=f("A"), scalar1=KA_(hp), scalar2=OMKA(hp), op0=ALU.mult, op1=ALU.add),
                              reads=[bF["A"], bPC], writes=[bF["T1"]])
                        fw.op("pool", lambda e: e.tensor_tensor(out=f("KM"), in0=k_, in1=f("T1"), op=ALU.mult), reads=[bHX[3], bF["T1"]], writes=[bF["KM"]])
                        fw.op("pool", lambda e: e.tensor_tensor(out=f("T1"), in0=r_, in1=f("KM"), op=ALU.mult), reads=[bHX[2], bF["KM"]], writes=[bF["T1"]])
                        fw.op("dve", lambda e: e.tensor_scalar(out=SQb[:, 0:n], in0=f("T1"), scalar1=RK_(hp), scalar2=None, op0=ALU.mult),
                              reads=[bF["T1"], bPC], writes=[bSQb])
                        fw.op("pe", lambda e: e.matmul(pB[:, 0:n], lhsT=BD[:], rhs=SQb[:, 0:n], start=True, stop=True), reads=[bW, bSQb], writes=[bpB])
                        fw.op("dve", lambda e: e.tensor_tensor(out=BON[:, hp, 0:n], in0=pB[:, 0:n], in1=v_, op=ALU.mult), reads=[bpB, bHX[4]], writes=[bBON[hp]])
                        fw.op("dve", lambda e: e.tensor_tensor_scan(out=f("CUM"), data0=RST[:, 0:n], data1=f("LW"), initial=0.0, op0=ALU.mult, op1=ALU.add),
                              reads=[bW, bF["LW"]], writes=[bF["CUM"]])
                        fw.op("act", lambda e: e.activation(out=f("EP"), in_=f("CUM"), func=AF.Exp), reads=[bF["CUM"]], writes=[bF["EP"]])
                        fw.op("act", lambda e: e.activation(out=f("EN"), in_=f("CUM"), func=AF.Exp, scale=-1.0), reads=[bF["CUM"]], writes=[bF["EN"]])
                        fw.op("dve", lambda e: e.tensor_tensor(out=f("EPV"), in0=f("CUM"), in1=f("LW"), op=ALU.subtract), reads=[bF["CUM"], bF["LW"]], writes=[bF["EPV"]])
                        fw.op("act", lambda e: e.activation(out=f("EPV"), in_=f("EPV"), func=AF.Exp), reads=[bF["EPV"]], writes=[bF["EPV"]])
                        fw.op("dve", lambda e: e.tensor_copy(out=CUMC[:, 0:nch], in_=v3(f("CUM"), C_)[:, :, C_ - 1]), reads=[bF["CUM"]], writes=[bPCS])
                        fw.op("act", lambda e: e.activation(out=PCS[:, hp, 0:nch], in_=CUMC[:, 0:nch], func=AF.Exp), reads=[bPCS], writes=[bPCS])
                        fw.op("dve", lambda e: e.tensor_tensor(out=v3(f("EH"), C_), in0=CUMC[:, 0:nch].unsqueeze(2).broadcast_to([128, nch, C_]),
                                                               in1=v3(f("CUM"), C_), op=ALU.subtract), reads=[bPCS, bF["CUM"]], writes=[bF["EH"]])
                        fw.op("act", lambda e: e.activation(out=f("EH"), in_=f("EH"), func=AF.Exp), reads=[bF["EH"]], writes=[bF["EH"]])
                        fw.op("pool", lambda e: e.tensor_tensor(out=KR[:, hp, 0:nch, 0, 0:C_], in0=v3(f("KKN"), C_), in1=v3(f("EPV"), C_), op=ALU.mult),
                              reads=[bF["KKN"], bF["EPV"]], writes=[bKR[hp]])
                        fw.op("dve", lambda e: e.tensor_tensor(out=KR[:, hp, 0:nch, 1, 0:C_], in0=v3(r_, C_), in1=v3(f("EP"), C_), op=ALU.mult),
                              reads=[bHX[2], bF["EP"]], writes=[bKR[hp]])
                        fw.op("pool", lambda e: e.tensor_tensor(out=KB[:, hp, 0:nch, 0, 0:C_], in0=v3(f("KM"), C_), in1=v3(f("EN"), C_), op=ALU.mult),
                              reads=[bF["KM"], bF["EN"]], writes=[bKB[hp]])
                        fw.op("pool", lambda e: e.tensor_tensor(out=f("BETA"), in0=f("KKN"), in1=f("A"), op=ALU.mult), reads=[bF["KKN"], bF["A"]], writes=[bF["BETA"]])
                        fw.op("pool", lambda e: e.tensor_tensor(out=KB[:, hp, 0:nch, 1, 0:C_], in0=v3(f("BETA"), C_), in1=v3(f("EN"), C_), op=ALU.mult),
                              reads=[bF["BETA"], bF["EN"]], writes=[bKB[hp]])
                        fw.op("pool", lambda e: e.tensor_tensor(out=KH[:, hp, 0:n], in0=f("KM"), in1=f("EH"), op=ALU.mult), reads=[bF["KM"], bF["EH"]], writes=[bKH[hp]])
                        fw.op("dve", lambda e: e.scalar_tensor_tensor(out=BH[:, hp, 0:n], in0=f("BETA"), scalar=-1.0, in1=f("EH"), op0=ALU.mult, op1=ALU.mult),
                              reads=[bF["BETA"], bF["EH"]], writes=[bKH[hp]])
                        fw.op("act", lambda e: e.copy(out=VB[:, hp, 0:n], in_=v_), reads=[bHX[4]], writes=[bKH[hp]])
                        fw.op("pool", lambda e: e.tensor_copy(out=KRo[:, hp, 0:nch, :, 0:C_], in_=KR[64:128, hp, 0:nch, :, 0:C_]), reads=[bKR[hp]], writes=[bKR[hp]])
                        fw.op("pool", lambda e: e.tensor_copy(out=KBo[:, hp, 0:nch, :, 0:C_], in_=KB[64:128, hp, 0:nch, :, 0:C_]), reads=[bKB[hp]], writes=[bKB[hp]])
                    for hp_ in range(4):
                        _prep(hp_)

                def post(slot, tok0, n, C_, tiles):
                    nch = n // C_
                    L = {64: 5, 32: 4}[C_]
                    RS = cfg.get("rwkv_stop", 9)
                    d_ = SL[slot]
                    PCS, bPCS, KR, KB, bKR, bKB, KRo, KBo = d_["PCS"], d_["bPCS"], d_["KR"], d_["KB"], d_["bKR"], d_["bKB"], d_["KRo"], d_["KBo"]
                    KH, BH, VB, bKH, Gt, BON, bGt, bBON = d_["KH"], d_["BH"], d_["VB"], d_["bKH"], d_["Gt"], d_["BON"], d_["bGt"], d_["bBON"]
                    MS_ = lambda i_: MSK[0:C_, i_, 0:C_].unsqueeze(1).broadcast_to([C_, 8, C_])
                    IDB = ident[0:C_, 0:C_].unsqueeze(1).broadcast_to([C_, 8, C_])
                    gk = pGK[0:C_, :].rearrange("p (h t) -> p h t", h=8)
                    gb = pGB[0:C_, :].rearrange("p (h t) -> p h t", h=8)
                    hv = lambda ps_, o: ps_[0:C_, o:o + 512].rearrange("p (h t) -> p h t", h=8)
                    cbv = lambda nm: cb[nm][0:C_, :, 0:C_]
                    def _chunk(c):
                        cols = slice(c * C_, (c + 1) * C_)
                        KRh = lambda h, a_: (KR[0:64, h // 2, c, a_, 0:C_] if h % 2 == 0 else KRo[0:64, h // 2, c, a_, 0:C_])
                        KBh = lambda h, a_: (KB[0:64, h // 2, c, a_, 0:C_] if h % 2 == 0 else KBo[0:64, h // 2, c, a_, 0:C_])
                        KR2 = lambda h: ((KR if h % 2 == 0 else KRo)[0:64, h // 2, c, :, :].rearrange("p a t -> p (a t)"))
                        STh = lambda h: (STb[0:64, h // 2, :] if h % 2 == 0 else STbo[0:64, h // 2, :])
                        t1v = pT1[0:C_, :].rearrange("p (q f) -> p q f", q=2)
                        for qi, Q in enumerate((KH, BH)):
                            for hp in range(4):
                                fw.op("pe", lambda e, qi=qi, Q=Q, hp=hp: e.transpose(out=t1v[:, qi, hp * 128:(hp + 1) * 128], in_=Q[:, hp, cols], identity=ident[:]),
                                      reads=[bKH[hp], b_ident], writes=[bpT1])
                        for hp in range(4):
                            fw.op("pe", lambda e, hp=hp: e.transpose(out=pT2[0:C_, hp * 128:(hp + 1) * 128], in_=VB[:, hp, cols], identity=ident[:]),
                                  reads=[bKH[hp], b_ident], writes=[bpT2])
                        fw.op("act", lambda e: e.copy(out=TM[0:C_, 0:2, :], in_=t1v), reads=[bpT1], writes=[bTM])
                        fw.op("act", lambda e: e.copy(out=TM[0:C_, 2, :], in_=pT2[0:C_, 0:512]), reads=[bpT2], writes=[bTM])
                        if RS <= 2.5:
                            return
                        for h in range(8):
                            hp = h // 2
                            bgk = bpGK0 if h < 4 else bpGK1
                            bgb = bpGB0 if h < 4 else bpGB1
                            if C_ == 64:
                                fw.op("pe", lambda e, h=h: e.matmul(gk[:, h, :], lhsT=KBh(h, 0), rhs=KR2(h), start=True, stop=True),
                                      reads=[bKB[hp], bKR[hp]], writes=[bgk])
                                fw.op("pe", lambda e, h=h: e.matmul(gb[:, h, :], lhsT=KBh(h, 1), rhs=KR2(h), start=True, stop=True),
                                      reads=[bKB[hp], bKR[hp]], writes=[bgb])
                            else:
                                for a_ in range(2):
                                    fw.op("pe", lambda e, h=h, a_=a_: e.matmul(gk[:, h, a_ * C_:(a_ + 1) * C_], lhsT=KBh(h, 0), rhs=KRh(h, a_), start=True, stop=True),
                                          reads=[bKB[hp], bKR[hp]], writes=[bgk])
                                    fw.op("pe", lambda e, h=h, a_=a_: e.matmul(gb[:, h, a_ * C_:(a_ + 1) * C_], lhsT=KBh(h, 1), rhs=KRh(h, a_), start=True, stop=True),
                                          reads=[bKB[hp], bKR[hp]], writes=[bgb])
                            fw.op("pe", lambda e, h=h: e.matmul(hv(pA, 0)[:, h, 0:C_], lhsT=KRh(h, 0), rhs=KBh(h, 1), start=True, stop=True),
                                  reads=[bKB[hp], bKR[hp]], writes=[bpA])
                        MS4 = lambda i_: MSK[0:C_, i_, 0:C_].unsqueeze(1).broadcast_to([C_, 4, C_])
                        for hb, (bk, bb) in enumerate(((bpGK0, bpGB0), (bpGK1, bpGB1))):
                            hs_ = slice(4 * hb, 4 * hb + 4)
                            fw.op("dve", lambda e, hs_=hs_: e.tensor_tensor(out=cb["AKK"][0:C_, hs_, 0:C_], in0=gk[:, hs_, 0:C_], in1=MS4(0), op=ALU.mult),
                                  reads=[bk, bW], writes=[bcb["AKK"]])
                            fw.op("dve", lambda e, hs_=hs_: e.tensor_tensor(out=cb["GKR"][0:C_, hs_, 0:C_], in0=gk[:, hs_, C_:2 * C_], in1=MS4(1), op=ALU.mult),
                                  reads=[bk, bW], writes=[bcb["GKR"]])
                            fw.op("dve", lambda e, hs_=hs_: e.scalar_tensor_tensor(out=cb["X1"][0:C_, hs_, 0:C_], in0=gb[:, hs_, 0:C_], scalar=-1.0, in1=MS4(0),
                                                                                 op0=ALU.mult, op1=ALU.mult), reads=[bb, bW], writes=[bcb["X1"]])
                            fw.op("dve", lambda e, hs_=hs_: e.scalar_tensor_tensor(out=cb["NGBR"][0:C_, hs_, 0:C_], in0=gb[:, hs_, C_:2 * C_], scalar=-1.0, in1=MS4(1),
                                                                                 op0=ALU.mult, op1=ALU.mult), reads=[bb, bW], writes=[bcb["NGBR"]])
                        fw.op("dve", lambda e: e.scalar_tensor_tensor(out=cbv("Y1"), in0=hv(pA, 0)[:, :, 0:C_], scalar=-1.0, in1=MS_(2), op0=ALU.mult, op1=ALU.mult),
                              reads=[bpA, bW], writes=[bcb["Y1"]])
                        fw.op("dve", lambda e: e.tensor_tensor(out=cbv("T1b"), in0=cbv("X1"), in1=IDB, op=ALU.add), reads=[bcb["X1"], b_ident], writes=[bcb["T1b"]])
                        if RS <= 3:
                            return
                        xc, yc, tc = "X1", "Y1", "T1b"
                        for lvl in range(L):
                            xn = "X2" if xc == "X1" else "X1"
                            yn = "Y2" if yc == "Y1" else "Y1"
                            tn = "T2b" if tc == "T1b" else "T1b"
                            if lvl < L - 1:
                                for h in range(8):
                                    fw.op("pe", lambda e, h=h, xc=xc, yc=yc: e.matmul(hv(pB, 0)[:, h, 0:C_], lhsT=cb[yc][0:C_, h, 0:C_], rhs=cb[xc][0:C_, h, 0:C_],
                                                                                     start=True, stop=True), reads=[bcb[xc], bcb[yc]], writes=[bpB])
                            for h in range(8):
                                fw.op("pe", lambda e, h=h, xc=xc, yc=yc: e.matmul(hv(pA, 0)[:, h, 0:C_], lhsT=cb[xc][0:C_, h, 0:C_], rhs=cb[yc][0:C_, h, 0:C_],
                                                                                 start=True, stop=True), reads=[bcb[xc], bcb[yc]], writes=[bpA])
                            if lvl < L - 1:
                                fw.op("act", lambda e, xn=xn: e.copy(out=cbv(xn), in_=hv(pB, 0)[:, :, 0:C_]), reads=[bpB], writes=[bcb[xn]])
                            fw.op("dve", lambda e, yn=yn: e.tensor_copy(out=cbv(yn), in_=hv(pA, 0)[:, :, 0:C_]), reads=[bpA], writes=[bcb[yn]])
                            for h in range(8):
                                fw.op("pe", lambda e, h=h, yn=yn, tc=tc: e.matmul(hv(pGK, 0)[:, h, 0:C_], lhsT=cb[yn][0:C_, h, 0:C_], rhs=cb[tc][0:C_, h, 0:C_],
                                                                                 start=True, stop=True), reads=[bcb[yn], bcb[tc]], writes=[bpGK0])
                            fw.op("dve", lambda e, tn=tn, tc=tc: e.tensor_tensor(out=cbv(tn), in0=hv(pGK, 0)[:, :, 0:C_], in1=cbv(tc), op=ALU.add),
                                  reads=[bpGK0, bcb[tc]], writes=[bcb[tn]])
                            xc, yc, tc = xn, yn, tn
                        if RS <= 4:
                            return
                        wt = hv(pGK, 512)
                        ut = hv(pGB, 0)
                        oo = hv(pGB, 512)
                        for h in range(8):
                            hp, base = h // 2, (h % 2) * 64
                            bs = slice(base, base + 64)
                            fw.op("pe", lambda e, h=h: e.matmul(wt[:, h, :], lhsT=KRh(h, 0), rhs=STh(h), start=True, stop=False),
                                  reads=[bKR[hp], bST], writes=[bpGK1])
                            fw.op("pe", lambda e, h=h: e.matmul(wt[:, h, :], lhsT=cb["AKK"][0:C_, h, 0:C_], rhs=TM[0:C_, 2, h * 64:(h + 1) * 64], start=False, stop=True),
                                  reads=[bcb["AKK"], bTM], writes=[bpGK1])
                        fw.op("act", lambda e: e.copy(out=cb["WTs"][0:C_], in_=wt), reads=[bpGK1], writes=[bcb["WTs"]])
                        for h in range(8):
                            fw.op("pe", lambda e, h=h, tc=tc: e.matmul(ut[:, h, :], lhsT=cb[tc][0:C_, h, 0:C_], rhs=cb["WTs"][0:C_, h, :], start=True, stop=True),
                                  reads=[bcb[tc], bcb["WTs"]], writes=[bpGB0])
                        fw.op("dve", lambda e: e.tensor_copy(out=cb["UTs"][0:C_], in_=ut), reads=[bpGB0], writes=[bcb["UTs"]])
                        for h in range(8):
                            hp, base = h // 2, (h % 2) * 64
                            bs = slice(base, base + 64)
                            fw.op("pe", lambda e, h=h: e.matmul(oo[:, h, :], lhsT=KRh(h, 1), rhs=STh(h), start=True, stop=False),
                                  reads=[bKR[hp], bST], writes=[bpGB1])
                            fw.op("pe", lambda e, h=h: e.matmul(oo[:, h, :], lhsT=cb["GKR"][0:C_, h, 0:C_], rhs=TM[0:C_, 2, h * 64:(h + 1) * 64], start=False, stop=False),
                                  reads=[bcb["GKR"], bTM], writes=[bpGB1])
                            fw.op("pe", lambda e, h=h: e.matmul(oo[:, h, :], lhsT=cb["NGBR"][0:C_, h, 0:C_], rhs=cb["UTs"][0:C_, h, :], start=False, stop=True),
                                  reads=[bcb["NGBR"], bcb["UTs"]], writes=[bpGB1])
                        sn = pB[:, :].rearrange("p (a f) -> p a f", a=4)
                        for hp in range(4):
                            fw.op("pe", lambda e, hp=hp: e.matmul(sn[:, hp, :], lhsT=TM[0:C_, 0, hp * 128:(hp + 1) * 128], rhs=TM[0:C_, 2, hp * 128:(hp + 1) * 128],
                                                                  start=True, stop=False), reads=[bTM], writes=[bpB])
                            fw.op("pe", lambda e, hp=hp: e.matmul(sn[:, hp, :], lhsT=TM[0:C_, 1, hp * 128:(hp + 1) * 128],
                                                                  rhs=cb["UTs"][0:C_, 2 * hp:2 * hp + 2, :].rearrange("p a v -> p (a v)"),
                                                                  start=False, stop=True), reads=[bTM, bcb["UTs"]], writes=[bpB])
                        if RS <= 5:
                            return
                        fw.op("act", lambda e: e.copy(out=OS[0:C_], in_=oo), reads=[bpGB1], writes=[bOS])
                        for half in range(2):
                            bs = slice(64 * half, 64 * half + 64)
                            fw.op("dve", lambda e, bs=bs: e.tensor_tensor(out=ST[bs], in0=ST[bs], in1=PCS[bs, :, c].unsqueeze(2).broadcast_to([64, 4, 64]), op=ALU.mult),
                                  reads=[bST, bPCS], writes=[bST])
                            fw.op("dve", lambda e, bs=bs, half=half: e.tensor_tensor(out=ST[bs], in0=sn[bs, :, 64 * half:64 * half + 64], in1=ST[bs], op=ALU.add),
                                  reads=[bST, bpB], writes=[bST])
                        fw.op("act", lambda e: e.copy(out=STb[:], in_=ST[:]), reads=[bST], writes=[bST])
                        fw.op("act", lambda e: e.copy(out=STbo[:], in_=ST[64:128]), reads=[bST], writes=[bST])
                        S_ = lambda i_: STAT[0:C_, i_, :]
                        fw.op("dve", lambda e: e.tensor_reduce(out=S_(0), in_=OS[0:C_], axis=AX.X, op=ALU.add), reads=[bOS], writes=[bSTAT])
                        fw.op("pool", lambda e: e.tensor_tensor(out=OSQ[0:C_], in0=OS[0:C_], in1=OS[0:C_], op=ALU.mult), reads=[bOS], writes=[bOSQ])
                        fw.op("dve", lambda e: e.tensor_reduce(out=S_(1), in_=OSQ[0:C_], axis=AX.X, op=ALU.add), reads=[bOSQ], writes=[bSTAT])
                        fw.op("dve", lambda e: e.tensor_scalar(out=S_(2), in0=S_(0), scalar1=1.0 / 64, scalar2=None, op0=ALU.mult), reads=[bSTAT], writes=[bSTAT])
                        fw.op("dve", lambda e: e.tensor_tensor(out=S_(3), in0=S_(2), in1=S_(2), op=ALU.mult), reads=[bSTAT], writes=[bSTAT])
                        fw.op("dve", lambda e: e.scalar_tensor_tensor(out=S_(4), in0=S_(1), scalar=1.0 / 64, in1=S_(3), op0=ALU.mult, op1=ALU.subtract),
                              reads=[bSTAT], writes=[bSTAT])
                        fw.op("act", lambda e: e.activation(out=S_(5), in_=S_(4), func=AF.Sqrt, bias=64e-5), reads=[bSTAT], writes=[bSTAT])
                        fw.op("dve", lambda e: e.reciprocal(out=S_(5), in_=S_(5)), reads=[bSTAT], writes=[bSTAT])
                        fw.op("dve", lambda e: e.tensor_tensor(out=OSQ[0:C_], in0=OS[0:C_], in1=bc3(S_(2), 64), op=ALU.subtract), reads=[bOS, bSTAT], writes=[bOSQ])
                        fw.op("dve", lambda e: e.tensor_tensor(out=cb["ONb"][0:C_], in0=OSQ[0:C_], in1=bc3(S_(5), 64), op=ALU.mult), reads=[bOSQ, bSTAT], writes=[bcb["ONb"]])
                        t2v = pT2[:, 512:512 + 4 * C_].rearrange("p (a t) -> p a t", a=4)
                        for hp in range(4):
                            fw.op("pe", lambda e, hp=hp: e.transpose(out=t2v[:, hp, :], in_=cb["ONb"][0:C_, 2 * hp:2 * hp + 2, :].rearrange("p a v -> p (a v)"),
                                                                     identity=ident[0:C_, 0:C_]), reads=[bcb["ONb"], b_ident], writes=[bpT2])
                        fw.op("act", lambda e: e.copy(out=ONT[:, :, cols], in_=t2v), reads=[bpT2], writes=[bONT])
                    for c_ in range(nch):
                        _chunk(c_)
                    if RS <= 6:
                        return
                    for hp in range(4):
                        fw.op("dve", lambda e, hp=hp: e.tensor_scalar(out=FT[:, 0:n], in0=ONT[:, hp, 0:n], scalar1=LNG(hp), scalar2=LNB(hp), op0=ALU.mult, op1=ALU.add),
                              reads=[bONT, bPC], writes=[bFT])
                        fw.op("dve", lambda e, hp=hp: e.tensor_tensor(out=FT[:, 0:n], in0=FT[:, 0:n], in1=BON[:, hp, 0:n], op=ALU.add),
                              reads=[bFT, bBON[hp]], writes=[bFT])
                        fw.op("dve", lambda e, hp=hp: e.tensor_tensor(out=MO[:, hp, 0:n], in0=FT[:, 0:n], in1=Gt[:, hp, 0:n], op=ALU.mult),
                              reads=[bFT, bGt[hp]], writes=[bMO])
                    for tl_i, (ti, r0_, r) in enumerate(tiles):
                        for half in range(2):
                            pj, bpj = (pA, bpA) if half == 0 else (pB, bpB)
                            for hp in range(4):
                                fw.op("pe", lambda e, pj=pj, hp=hp, tl_i=tl_i, r=r, half=half: e.matmul(
                                    pj[0:r, :], lhsT=MO[:, hp, tl_i * 128:tl_i * 128 + r], rhs=WoR[:, hp, half * 512:(half + 1) * 512],
                                    start=(hp == 0), stop=(hp == 3)), reads=[bMO, bWoR], writes=[bpj])
                            fw.op("dve", lambda e, pj=pj, ti=ti, r0_=r0_, r=r, half=half: e.tensor_tensor(
                                out=X[r0_:r0_ + r, ti, half * 512:(half + 1) * 512], in0=pj[0:r, :], in1=X[r0_:r0_ + r, ti, half * 512:(half + 1) * 512],
                                op=ALU.add), reads=[bpj, bX[ti]], writes=[bX[ti]])

                def store_state(idx):
                    so = pGK[0:64, 0:512].rearrange("p (a f) -> p a f", a=4)
                    for hp in range(4):
                        fw.op("pe", lambda e, hp=hp: e.matmul(so[:, hp, :], lhsT=ST[:, hp, :], rhs=ident_f[:], start=True, stop=True), reads=[bST, b_ident], writes=[bpGK0])
                    fw.op("dve", lambda e: e.tensor_copy(out=sto[:], in_=so), reads=[bpGK0], writes=[bsto])
                    fw.dma("sp", lambda e: e.dma_start(out=dr["rwkv_state"][idx].rearrange("h v k -> v h k"),
                                                       in_=sto[:].rearrange("p a (b k) -> p (a b) k", b=2)), reads=[bsto])

                fw.op("pool", lambda e: e.memset(ST[:], 0.0), writes=[bST])
                fw.op("pool", lambda e: e.memset(STb[:], 0.0), writes=[bST])
                fw.op("pool", lambda e: e.memset(STbo[:], 0.0), writes=[bST])
                fw.op("pool", lambda e: e.memset(CAR[:], 0.0), writes=[bCAR])
                sti = OS
                bsti = bOS
                seq = [("p", g) for g in range(cfg.get("rwkv_ngroups", 16))] + ([("s", 0), ("s", 1)] if cfg.get("rwkv_sample", True) else [])

                def do_prep(idx):
                    kind, a_ = seq[idx]
                    if kind == "p":
                        prep(idx % 2, a_ * NG, NG, 64)
                    else:
                        fw.op("dve", lambda e, a_=a_: e.tensor_copy(out=CAR[:], in_=PC[:, 42 + 14 * a_:56 + 14 * a_]), reads=[bPC], writes=[bCAR])
                        prep(idx % 2, 2048 + 32 * a_, 32, 32)

                def do_post(idx):
                    kind, a_ = seq[idx]
                    if kind == "p":
                        post(idx % 2, a_ * NG, NG, 64, [(a_, 0, 128)])
                        if a_ == 15:
                            store_state(0)
                    else:
                        b = a_
                        fw.dma("sp", lambda e: e.dma_start(out=sti[:], in_=dr["srw"][b].rearrange("h v k -> v h k")), writes=[bsti])
                        sv_ = pB[:, 0:256].rearrange("p (a v) -> p a v", a=4)
                        for hp in range(4):
                            fw.op("pe", lambda e, hp=hp: e.matmul(sv_[:, hp, :], lhsT=sti[:, 2 * hp:2 * hp + 2, :].rearrange("p a k -> p (a k)"),
                                                                  rhs=ident_f[0:64, 0:64], start=True, stop=True), reads=[bsti, b_ident], writes=[bpB])
                        fw.op("dve", lambda e: e.tensor_copy(out=ST[:], in_=sv_), reads=[bpB], writes=[bST])
                        fw.op("act", lambda e: e.copy(out=STb[:], in_=ST[:]), reads=[bST], writes=[bST])
                        fw.op("act", lambda e: e.copy(out=STbo[:], in_=ST[64:128]), reads=[bST], writes=[bST])
                        post(idx % 2, 2048 + 32 * b, 32, 32, [(16, 32 * b, 32)])
                        store_state(1 + b)

                do_prep(0)
                for idx in range(len(seq)):
                    if idx + 1 < len(seq):
                        do_prep(idx + 1)
                    do_post(idx)
                fw.flush()

        def mix_even(l):
            e_ = l // 2
            with ExitStack() as st:
                xT = sb(st, "xT", (128, 8, NTOK), BF16)
                b_xT = [Buf() for _ in range(5)]
                with ExitStack() as st2:
                    norm_T(st2, dr["mix_norm"][l:l + 1, :], xT, b_xT, "n_")
                    fw.flush()
                wv = dr["ev_w_in"][e_].rearrange("(k p) n -> p k n", p=128)
                if cfg.get("shiftout", True):
                    with ExitStack() as s2:
                        Wr, bWr = load_w_bf(s2, wv[:, :, 1544:3336], (8, 1792), "Wr")
                        pp = [ps(s2, "pps%d" % i, (128, 512)) for i in range(2)]
                        bpp = [Buf() for _ in range(2)]
                        sh_ = sb(s2, "sh", (1, 3, 1792))
                        bsh = Buf()
                        for si, tok in enumerate((2047, 2079, 2111)):
                            for cb, (c0, cw) in enumerate(((0, 512), (512, 512), (1024, 512), (1536, 256))):
                                q = cb % 2
                                for k in range(8):
                                    fw.op("pe", lambda e, q=q, k=k, tok=tok, c0=c0, cw=cw: e.matmul(
                                        pp[q][0:1, 0:cw], lhsT=xT[:, k, tok:tok + 1], rhs=Wr[:, k, c0:c0 + cw],
                                        start=(k == 0), stop=(k == 7)), reads=[b_xT[tok // 512], bWr], writes=[bpp[q]])
                                fw.op("act", lambda e, q=q, si=si, c0=c0, cw=cw: e.copy(out=sh_[0:1, si, c0:c0 + cw], in_=pp[q][0:1, 0:cw]),
                                      reads=[bpp[q]], writes=[bsh])
                        fw.dma("sp", lambda e: e.dma_start(out=dr["rwkv_shift"].rearrange("(o s) n -> o s n", o=1), in_=sh_[:]), reads=[bsh])
                        fw.flush()
                if cfg.get("fox", True):
                    fox_part(l, xT, b_xT)
                if cfg.get("rwkv", True):
                    rwkv_part(l, xT, b_xT)


        def mix_odd(l):
            j_ = l // 2
            wv = dr["od_w_in"][j_].rearrange("(k p) n -> p k n", p=128)
            with ExitStack() as st:
                with ExitStack() as sx:
                    xT = sb(sx, "xT", (128, 8, NTOK), BF16)
                    b_xT = [Buf() for _ in range(5)]
                    with ExitStack() as st2:
                        norm_T(st2, dr["mix_norm"][l:l + 1, :], xT, b_xT, "n_")
                        fw.flush()
                    with ExitStack() as s1:
                        Wu, bWu = load_w_bf(s1, wv[:, :, 768:1280], (8, 512), "Wu", stage_cols=1024)
                        Wv, bWv = load_w_bf(s1, wv[:, :, 1280:1792], (8, 512), "Wv", stage_cols=1024)
                        WoS, bWoS = load_w_bf(s1, dr["od_w_out"][j_][512:1024, :].rearrange("(h p) n -> p h n", p=128), (4, D), "WoS", stage_cols=1024)
                        Gv, bGv = bcast_row(s1, dr["sgu_v_norm"][j_:j_ + 1, :], 512, "Gv")
                        TRI = sb(s1, "gTRI", (128, 128))
                        WS = sb(s1, "WS", (128, 8, 128))
                        WSb = sb(s1, "WSb", (128, 8, 128), BF16)
                        WST = sb(s1, "WST", (128, 8, 128), BF16)
                        SBr = sb(s1, "SBr", (8, 128))
                        BT = sb(s1, "BT", (128, 8))
                        bC = Buf()
                        fw.dma("sp", lambda e: e.dma_start(out=TRI[:], in_=dr["c_tri"]), writes=[bC])
                        fw.dma("sp", lambda e: e.dma_start(out=WS[:], in_=dr["sgu_w_s"][j_].rearrange("g t s -> t g s")), writes=[bC])
                        fw.dma("sp", lambda e: e.dma_start(out=SBr[:], in_=dr["sgu_b"][j_]), writes=[bC])
                        fw.op("pool", lambda e: e.tensor_copy(out=WSb[:], in_=WS[:]), reads=[bC], writes=[bC])
                        pu = ps(s1, "gpu", (128, 512))
                        pv = ps(s1, "gpv", (128, 512))
                        pm = ps(s1, "gpm", (128, 512))
                        po = [ps(s1, "gpo%d" % i, (128, 512)) for i in range(2)]
                        pt = ps(s1, "gpt", (128, 8, 128), BF16)
                        bpu, bpv, bpm, bpt = Buf(), Buf(), Buf(), Buf()
                        bpo = [Buf(), Buf()]
                        for g in range(8):
                            fw.op("pe", lambda e, g=g: e.transpose(out=pt[:, g, :], in_=WSb[:, g, :], identity=ident[:]), reads=[bC, b_ident], writes=[bpt])
                        fw.op("dve", lambda e: e.tensor_tensor(out=WST[:], in0=pt[:], in1=bcm(TRI[:], 8), op=ALU.mult), reads=[bpt, bC], writes=[bC])
                        fw.op("pe", lambda e: e.matmul(pm[:, 0:8], lhsT=SBr[0:8, :], rhs=ident_f[0:8, 0:8], start=True, stop=True), reads=[bC, b_ident], writes=[bpm])
                        fw.op("dve", lambda e: e.tensor_copy(out=BT[:], in_=pm[:, 0:8]), reads=[bpm], writes=[bC])
                        U = sb(s1, "gU", (128, 512))
                        GV = sb(s1, "gGV", (128, 512))
                        VN = sb(s1, "gVN", (128, 512))
                        VNb = sb(s1, "gVNb", (128, 512), BF16)
                        U1 = sb(s1, "gU1", (32, 512))
                        VNb1 = sb(s1, "gVNb1", (32, 512), BF16)
                        Dm = sb(s1, "gDm", (128, 512))
                        Db = sb(s1, "gDb", (128, 512), BF16)
                        DT = sb(s1, "gDT", (128, 4, 128), BF16)
                        gss = sb(s1, "gss", (128, 4))
                        bU, bGV, bVN, bVNb, bU1, bVNb1, bDm, bDb, bDT, bgss, bgj = [Buf() for _ in range(11)]

                        def sgu_tile(i):
                            r = tile_rows(i)
                            tk = slice(i * 128, i * 128 + r)
                            g5 = i // 4
                            for k in range(8):
                                fw.op("pe", lambda e, k=k: e.matmul(pu[0:r, :], lhsT=xT[:, k, tk], rhs=Wu[:, k, :], start=(k == 0), stop=(k == 7)),
                                      reads=[b_xT[g5], bWu], writes=[bpu])
                            for k in range(8):
                                fw.op("pe", lambda e, k=k: e.matmul(pv[0:r, :], lhsT=xT[:, k, tk], rhs=Wv[:, k, :], start=(k == 0), stop=(k == 7)),
                                      reads=[b_xT[g5], bWv], writes=[bpv])
                            fw.op("act", lambda e: e.activation(out=U[0:r, :], in_=pu[0:r, :], func=AF.Gelu_apprx_tanh), reads=[bpu], writes=[bU])
                            fw.op("act", lambda e: e.activation(out=GV[0:r, :], in_=pv[0:r, :], func=AF.Gelu_apprx_tanh), reads=[bpv], writes=[bGV])
                            fw.op("act", lambda e: e.activation(out=Dm[0:r, :], in_=GV[0:r, :], func=AF.Square, accum_out=gss[0:r, 0:1]), reads=[bGV], writes=[bDm, bgss])
                            fw.op("act", lambda e: e.activation(out=gss[0:r, 1:2], in_=gss[0:r, 0:1], func=AF.Sqrt, scale=1.0 / 512, bias=EPS), reads=[bgss], writes=[bgss])
                            fw.op("dve", lambda e: e.reciprocal(out=gss[0:r, 2:3], in_=gss[0:r, 1:2]), reads=[bgss], writes=[bgss])
                            fw.op("dve", lambda e: e.scalar_tensor_tensor(out=VN[0:r, :], in0=GV[0:r, :], scalar=gss[0:r, 2:3], in1=Gv[0:r, :], op0=ALU.mult, op1=ALU.mult),
                                  reads=[bGV, bgss, bGv], writes=[bVN])
                            fw.op("act", lambda e: e.copy(out=VNb[0:r, :], in_=VN[0:r, :]), reads=[bVN], writes=[bVNb])
                            mm = pm[:, :].rearrange("p (g d) -> p g d", g=8)
                            if i < 16:
                                for g in range(8):
                                    fw.op("pe", lambda e, g=g: e.matmul(mm[:, g, :], lhsT=WST[:, g, :], rhs=VNb[:, g * 64:(g + 1) * 64], start=True, stop=True),
                                          reads=[bC, bVNb], writes=[bpm])
                                fw.op("dve", lambda e: e.tensor_tensor(out=Dm[:].rearrange("p (g d) -> p g d", g=8), in0=mm, in1=bc3(BT[:, :], 64), op=ALU.add),
                                      reads=[bpm, bC], writes=[bDm])
                                fw.op("dve", lambda e: e.tensor_tensor(out=Db[:], in0=Dm[:], in1=U[:], op=ALU.mult), reads=[bDm, bU], writes=[bDb])
                                for c4 in range(4):
                                    fw.op("pe", lambda e, c4=c4: e.transpose(out=pt[:, c4, :], in_=Db[:, c4 * 128:(c4 + 1) * 128], identity=ident[:]),
                                          reads=[bDb, b_ident], writes=[bpt])
                                fw.op("act", lambda e: e.copy(out=DT[:], in_=pt[:, 0:4, :]), reads=[bpt], writes=[bDT])
                            else:
                                fw.dma("sp", lambda e: e.dma_start(out=dr["sgu_vo"], in_=VN[0:64, :]), reads=[bVN])
                                fw.op("act", lambda e: e.copy(out=VNb1[:], in_=VN[32:64, :]), reads=[bVN], writes=[bVNb1])
                                fw.op("act", lambda e: e.copy(out=U1[:], in_=U[32:64, :]), reads=[bU], writes=[bU1])
                                for b in range(2):
                                    vsrc = VNb if b == 0 else VNb1
                                    usrc = U if b == 0 else U1
                                    for g in range(8):
                                        fw.op("pe", lambda e, g=g, vsrc=vsrc: e.matmul(mm[0:32, g, :], lhsT=WST[0:32, g, 0:32], rhs=vsrc[0:32, g * 64:(g + 1) * 64],
                                                                                       start=True, stop=True), reads=[bC, bVNb, bVNb1], writes=[bpm])
                                    fw.op("dve", lambda e: e.tensor_tensor(out=Dm[0:32, :].rearrange("p (g d) -> p g d", g=8), in0=mm[0:32], in1=bc3(BT[0:32, :], 64), op=ALU.add),
                                          reads=[bpm, bC], writes=[bDm])
                                    fw.op("dve", lambda e, usrc=usrc: e.tensor_tensor(out=Db[0:32, :], in0=Dm[0:32, :], in1=usrc[0:32, :], op=ALU.mult),
                                          reads=[bDm, bU, bU1], writes=[bDb])
                                    for c4 in range(4):
                                        fw.op("pe", lambda e, c4=c4: e.transpose(out=pt[:, c4, 0:32], in_=Db[0:32, c4 * 128:(c4 + 1) * 128], identity=ident[0:32, 0:32]),
                                              reads=[bDb, b_ident], writes=[bpt])
                                    fw.op("act", lambda e, b=b: e.copy(out=DT[:, :, 32 * b:32 * b + 32], in_=pt[:, 0:4, 0:32]), reads=[bpt], writes=[bDT])
                            for half in range(2):
                                for c4 in range(4):
                                    fw.op("pe", lambda e, c4=c4, half=half: e.matmul(po[half][0:r, :], lhsT=DT[:, c4, 0:r], rhs=WoS[:, c4, half * 512:(half + 1) * 512],
                                                                                    start=(c4 == 0), stop=(c4 == 3)), reads=[bDT, bWoS], writes=[bpo[half]])
                                fw.op("dve", lambda e, half=half: e.tensor_tensor(out=X[0:r, i, half * 512:(half + 1) * 512], in0=po[half][0:r, :],
                                                                                 in1=X[0:r, i, half * 512:(half + 1) * 512], op=ALU.add),
                                      reads=[bpo[half], bX[i]], writes=[bX[i]])
                        if cfg.get("sgu", True):
                            for i in range(NT):
                                sgu_tile(i)
                        fw.flush()
                    QT = sb(sx, "sQT", (64, 8, NTOK), BF16)
                    bQT = [Buf() for _ in range(NT)]
                    KT = sb(sx, "sKT", (64, 2, NTOK), BF16)
                    bKT = [Buf() for _ in range(NT)]
                    VA = sb(sx, "sVA", (128, NT, 128), BF16)
                    bVA = [Buf() for _ in range(NT)]
                    VAn = sb(sx, "sVAn", (32, 128), BF16)
                    bVAn = Buf()
                    with ExitStack() as s2:
                        Wq, bWq = load_w_bf(s2, wv[:, :, 0:512], (8, 512), "sWq", stage_cols=1024)
                        Wkv, bWkv = load_w_bf(s2, wv[:, :, 512:768], (8, 256), "sWkv", stage_cols=1024)
                        Gq, bGq = bcast_row(s2, dr["swa_q_norm"][j_:j_ + 1, :], 64, "sGq")
                        Gk, bGk = bcast_row(s2, dr["swa_k_norm"][j_:j_ + 1, :], 64, "sGk")
                        CS = sb(s2, "CS", (128, 2, NT, 8))
                        bCS = Buf()
                        fw.dma("sp", lambda e: e.dma_start(out=CS[:], in_=dr["c_rope"]), writes=[bCS])
                        pq = ps(s2, "spq", (128, 512))
                        pk = ps(s2, "spk", (128, 512))
                        ptk = ps(s2, "sptk", (64, 8, 128), BF16)
                        ptk2 = ps(s2, "sptk2", (64, 8, 128), BF16)
                        bpq, bpk, bptk, bptk2 = Buf(), Buf(), Buf(), Buf()
                        qf = sb(s2, "sqf", (128, 8, 64))
                        kf = [sb(s2, "skf%d" % i, (128, 2, 64)) for i in range(2)]
                        vf = [sb(s2, "svf%d" % i, (128, 128)) for i in range(2)]
                        qb = sb(s2, "sqb", (128, 8, 64), BF16)
                        kb_ = sb(s2, "skb", (128, 2, 64), BF16)
                        rt = sb(s2, "srt", (128, 4, 8, 8))
                        bqf, bqb, bkb, brt = Buf(), Buf(), Buf(), Buf()
                        bkf = [Buf(), Buf()]
                        bvf = [Buf(), Buf()]

                        def rope(t, bt, r, H, i):
                            cosb = CS[0:r, 0, i, :].unsqueeze(1).broadcast_to([r, H, 8])
                            sinb = CS[0:r, 1, i, :].unsqueeze(1).broadcast_to([r, H, 8])
                            x1, x2 = t[0:r, 0:H, 0:8], t[0:r, 0:H, 8:16]
                            fw.op("dve", lambda e: e.tensor_tensor(out=rt[0:r, 0, 0:H, :], in0=x1, in1=cosb, op=ALU.mult), reads=[bt, bCS], writes=[brt])
                            fw.op("dve", lambda e: e.tensor_tensor(out=rt[0:r, 1, 0:H, :], in0=x2, in1=sinb, op=ALU.mult), reads=[bt, bCS], writes=[brt])
                            fw.op("dve", lambda e: e.tensor_tensor(out=rt[0:r, 2, 0:H, :], in0=x2, in1=cosb, op=ALU.mult), reads=[bt, bCS], writes=[brt])
                            fw.op("dve", lambda e: e.tensor_tensor(out=rt[0:r, 3, 0:H, :], in0=x1, in1=sinb, op=ALU.mult), reads=[bt, bCS], writes=[brt])
                            fw.op("dve", lambda e: e.tensor_tensor(out=x1, in0=rt[0:r, 0, 0:H, :], in1=rt[0:r, 1, 0:H, :], op=ALU.subtract), reads=[brt], writes=[bt])
                            fw.op("dve", lambda e: e.tensor_tensor(out=x2, in0=rt[0:r, 2, 0:H, :], in1=rt[0:r, 3, 0:H, :], op=ALU.add), reads=[brt], writes=[bt])

                        def swa_tile(i):
                            r = tile_rows(i)
                            tk = slice(i * 128, i * 128 + r)
                            g5 = i // 4
                            jj = i % 2
                            for k in range(8):
                                fw.op("pe", lambda e, k=k: e.matmul(pq[0:r, :], lhsT=xT[:, k, tk], rhs=Wq[:, k, :], start=(k == 0), stop=(k == 7)),
                                      reads=[b_xT[g5], bWq], writes=[bpq])
                            for k in range(8):
                                fw.op("pe", lambda e, k=k: e.matmul(pk[0:r, 0:256], lhsT=xT[:, k, tk], rhs=Wkv[:, k, :], start=(k == 0), stop=(k == 7)),
                                      reads=[b_xT[g5], bWkv], writes=[bpk])
                            fw.op("act", lambda e: e.copy(out=vf[jj][0:r, :], in_=pk[0:r, 128:256]), reads=[bpk], writes=[bvf[jj]])
                            fw.op("dve", lambda e: e.tensor_copy(out=VA[0:r, i, :], in_=vf[jj][0:r, :]), reads=[bvf[jj]], writes=[bVA[i]])
                            rms_heads(s2, pq[0:r, :].rearrange("p (h d) -> p h d", h=8), bpq, r, 8, Gq, bGq, 1.0, qb[0:r], bqb, out_f=qf[0:r], bout_f=bqf)
                            rope(qf, bqf, r, 8, i)
                            fw.op("act", lambda e: e.activation(out=qb[0:r], in_=qf[0:r], func=AF.Copy, scale=0.125), reads=[bqf], writes=[bqb])
                            rms_heads(s2, pk[0:r, 0:128].rearrange("p (h d) -> p h d", h=2), bpk, r, 2, Gk, bGk, 1.0, kb_[0:r], bkb, out_f=kf[jj][0:r], bout_f=bkf[jj])
                            rope(kf[jj], bkf[jj], r, 2, i)
                            fw.op("act", lambda e: e.copy(out=kb_[0:r], in_=kf[jj][0:r]), reads=[bkf[jj]], writes=[bkb])
                            for h in range(8):
                                fw.op("pe", lambda e, h=h: e.transpose(out=ptk[:, h, 0:r], in_=qb[0:r, h, :], identity=ident[0:r, 0:r]), reads=[bqb, b_ident], writes=[bptk])
                            fw.op("act", lambda e: e.copy(out=QT[:, :, tk], in_=ptk[:, :, 0:r]), reads=[bptk], writes=[bQT[i]])
                            for h in range(2):
                                fw.op("pe", lambda e, h=h: e.transpose(out=ptk2[:, h, 0:r], in_=kb_[0:r, h, :], identity=ident[0:r, 0:r]), reads=[bkb, b_ident], writes=[bptk2])
                            fw.op("act", lambda e: e.copy(out=KT[:, :, tk], in_=ptk2[:, 0:2, 0:r]), reads=[bptk2], writes=[bKT[i]])
                            if i == 15:
                                fw.dma("sp", lambda e: e.dma_start(out=dr["swa_ko"][0], in_=kf[jj][:].rearrange("p h d -> p (h d)")), reads=[bkf[jj]])
                                fw.dma("sp", lambda e: e.dma_start(out=dr["swa_vo"][0], in_=vf[jj][:]), reads=[bvf[jj]])
                            if i == 16:
                                for b in range(2):
                                    fw.dma("sp", lambda e, b=b: e.dma_start(out=dr["swa_ko"][1 + b, 96:128, :],
                                                                            in_=kf[jj][32 * b:32 * b + 32].rearrange("p h d -> p (h d)")), reads=[bkf[jj]])
                                    fw.dma("sp", lambda e, b=b: e.dma_start(out=dr["swa_vo"][1 + b, 96:128, :], in_=vf[jj][32 * b:32 * b + 32, :]), reads=[bvf[jj]])
                                fw.op("dve", lambda e: e.tensor_copy(out=VAn[:], in_=VA[32:64, 16, :]), reads=[bVA[16]], writes=[bVAn])
                        for i in range(NT):
                            swa_tile(i)
                        fw.flush()
                    with ExitStack() as s3:
                        Wo, bWo = load_w_bf(s3, dr["od_w_out"][j_][0:512, :].rearrange("(h d) n -> d h n", d=64), (8, D), "sWo", stage_cols=1024)
                        SK, bSK = bcast_row(s3, dr["swa_sinks"][j_:j_ + 1, :], 8, "sSK")
                        fw.op("act", lambda e: e.activation(out=SK[:], in_=SK[:], func=AF.Exp), reads=[bSK], writes=[bSK])
                        HM = sb(s3, "HM", (128, 64), BF16)
                        bHM = Buf()
                        fw.op("pool", lambda e: e.memset(HM[0:64, :], 0.0), writes=[bHM])
                        fw.op("pool", lambda e: e.memset(HM[64:128, :], 1.0), writes=[bHM])
                        OTt = [sb(s3, "sOT%d" % i, (64, 8, 128), BF16) for i in range(2)]
                        bOTt = [Buf(), Buf()]
                        ck = sb(s3, "sck", (128, 2, 128))
                        cv = sb(s3, "scv", (128, 2, 128))
                        ckb = sb(s3, "sckb", (128, 2, 2, 64), BF16)
                        cvb = sb(s3, "scvb", (128, 2, 128), BF16)
                        KTc = sb(s3, "sKTc", (64, 2, 2, 128), BF16)
                        bck, bcv, bckb, bcvb, bKTc = Buf(), Buf(), Buf(), Buf(), Buf()
                        ptc = ps(s3, "sptc", (64, 8, 128), BF16)
                        bptc = Buf()
                        for b in range(2):
                            fw.dma("sp", lambda e, b=b: e.dma_start(out=ck[:, b, :], in_=dr["csk"][b]), writes=[bck])
                            fw.dma("sp", lambda e, b=b: e.dma_start(out=cv[:, b, :], in_=dr["csv"][b]), writes=[bcv])
                            fw.dma("sp", lambda e, b=b: e.dma_start(out=dr["swa_ko"][1 + b, 0:96, :], in_=ck[32:128, b, :]), reads=[bck])
                            fw.dma("sp", lambda e, b=b: e.dma_start(out=dr["swa_vo"][1 + b, 0:96, :], in_=cv[32:128, b, :]), reads=[bcv])
                        fw.op("pool", lambda e: e.tensor_copy(out=ckb[:].rearrange("p b n d -> p b (n d)"), in_=ck[:]), reads=[bck], writes=[bckb])
                        fw.op("pool", lambda e: e.tensor_copy(out=cvb[:], in_=cv[:]), reads=[bcv], writes=[bcvb])
                        for b in range(2):
                            for n_ in range(2):
                                fw.op("pe", lambda e, b=b, n_=n_: e.transpose(out=ptc[:, 2 * b + n_, :], in_=ckb[:, b, n_, :], identity=ident[:]),
                                      reads=[bckb, b_ident], writes=[bptc])
                        fw.op("act", lambda e: e.copy(out=KTc[:].rearrange("p b n k -> p (b n) k"), in_=ptc[:, 0:4, :]), reads=[bptc], writes=[bKTc])

                        def out_proj_tile(i, OT_t, bOT_t):
                            r = tile_rows(i)
                            po, bpo = C.ac_S, C.ac_bS
                            for half in range(2):
                                for h in range(8):
                                    fw.op("pe", lambda e, h=h, half=half: e.matmul(po[half][0:r, :], lhsT=OT_t[:, h, 0:r], rhs=Wo[0:64, h, half * 512:(half + 1) * 512],
                                                                                  start=(h == 0), stop=(h == 7)), reads=[bOT_t, bWo], writes=[bpo[half]])
                                fw.op("dve", lambda e, half=half: e.tensor_tensor(out=X[0:r, i, half * 512:(half + 1) * 512], in0=po[half][0:r, :],
                                                                                 in1=X[0:r, i, half * 512:(half + 1) * 512], op=ALU.add),
                                      reads=[bpo[half], bX[i]], writes=[bX[i]])

                        def attn_tile(m):
                            s_ = m % 2
                            for cc in range(2):
                                c = 2 * m + cc
                                for h in range(8):
                                    n_ = h // 4
                                    kbs = []
                                    if cc == 0:
                                        if m >= 1:
                                            kbs.append(dict(kT=KT[:, n_, (m - 1) * 128:m * 128], v=VA[:, m - 1, n_ * 64:(n_ + 1) * 64], nk=128, col0=0,
                                                            reads=[bKT[m - 1], bVA[m - 1]]))
                                        kbs.append(dict(kT=KT[:, n_, m * 128:m * 128 + 64], v=VA[0:64, m, n_ * 64:(n_ + 1) * 64], nk=64, col0=0,
                                                        reads=[bKT[m], bVA[m]]))
                                    else:
                                        if m >= 1:
                                            kbs.append(dict(kT=KT[:, n_, (m - 1) * 128:m * 128], v=VA[:, m - 1, n_ * 64:(n_ + 1) * 64], nk=128, col0=0,
                                                            reads=[bKT[m - 1], bVA[m - 1]], mask=HM[:], mask_reads=[bHM], mask_w=64))
                                        kbs.append(dict(kT=KT[:, n_, m * 128:(m + 1) * 128], v=VA[:, m, n_ * 64:(n_ + 1) * 64], nk=128, col0=0,
                                                        reads=[bKT[m], bVA[m]]))
                                    attn_core(s3, QT[:, h, c * 64:(c + 1) * 64], bQT[m], 64, kbs, OTt[s_][:, h, cc * 64:(cc + 1) * 64], bOTt[s_],
                                              extra_den=(SK[0:64, h:h + 1], [bSK]), nbuf=1)
                            out_proj_tile(m, OTt[s_], bOTt[s_])

                        for m in range(16):
                            attn_tile(m)
                        for b in range(2):
                            for h in range(8):
                                n_ = h // 4
                                q0 = 2048 + 32 * b
                                vnew = VA[0:32, 16, n_ * 64:(n_ + 1) * 64] if b == 0 else VAn[0:32, n_ * 64:(n_ + 1) * 64]
                                kbs = [dict(kT=KTc[:, b, n_, :], v=cvb[:, b, n_ * 64:(n_ + 1) * 64], nk=128, col0=0, reads=[bKTc, bcvb]),
                                       dict(kT=KT[:, n_, q0:q0 + 32], v=vnew, nk=32, col0=0, reads=[bKT[16], bVA[16], bVAn])]
                                attn_core(s3, QT[:, h, q0:q0 + 32], bQT[16], 32, kbs, OTt[0][:, h, 32 * b:32 * b + 32], bOTt[0],
                                          extra_den=(SK[0:64, h:h + 1], [bSK]), nbuf=1)
                        out_proj_tile(16, OTt[0], bOTt[0])
                        fw.flush()

        for l in range(cfg.get("depth", 2)):
            if cfg.get("ffn1", True):
                ffn(l, "ffn1")
            if cfg.get("mix", True) and l % 2 == 0 and not cfg.get("only_odd", False):
                mix_even(l)
            if cfg.get("mix", True) and l % 2 == 1:
                mix_odd(l)
            if cfg.get("xattn", True):
                xattn(l)
            if cfg.get("ffn2", True):
                ffn(l, "ffn2")

        yp_v = dr["y_prompt"].rearrange("(t p) d -> p t d", p=128)
        for g in range(4):
            fw.dma("sp", lambda e, g=g: e.dma_start(out=yp_v[:, 4 * g:4 * g + 4, :], in_=X[:, 4 * g:4 * g + 4, :]),
                   reads=bX[4 * g:4 * g + 4])
        fw.dma("sp", lambda e: e.dma_start(out=dr["y_sample"], in_=X[0:64, 16, :]), reads=[bX[16]])
        fw.flush()
    C.ninstr = fw.ninstr
    return nc, C


def make_consts():
    tri = np.triu(np.ones((128, 128), np.float32))
    trib = np.zeros((64, 64), np.float32)
    trib[0:32, 0:32] = tri[0:32, 0:32]
    trib[32:64, 32:64] = tri[0:32, 0:32]
    bd = np.zeros((128, 128), np.float32)
    bd[0:64, 0:64] = 1.0
    bd[64:128, 64:128] = 1.0
    t64 = np.triu(np.ones((64, 64), np.float32))
    msk = np.stack([np.triu(np.ones((64, 64), np.float32), 1), t64, np.tril(np.ones((64, 64), np.float32), -1)], 1)
    rst = np.ones((128, 256), np.float32)
    rst[:, ::64] = 0.0
    pos = np.zeros((128, NT), np.float32)
    for i in range(16):
        pos[:, i] = i * 128 + np.arange(128)
    pos[:, 16] = 2048 + (np.arange(128) % 32)
    inv_freq = np.power(np.float32(500000.0), -np.arange(8, dtype=np.float32) / np.float32(8)).astype(np.float32)
    ang = (pos[:, :, None] * inv_freq[None, None, :]).astype(np.float32)
    rope_t = np.ascontiguousarray(np.stack([np.cos(ang), np.sin(ang)], 1).astype(np.float32))
    return {"c_ident": np.eye(128, dtype=np.float32), "c_tri": tri, "c_trib": trib, "c_bd": bd, "c_rope": rope_t,
            "c_msk": np.ascontiguousarray(msk), "c_rst": rst}


def kernel(**inputs):
    cfg = inputs.pop("_cfg", {})
    inp = {k: np.asarray(v) for k, v in inputs.items()}
    nc, C = build_program(cfg)
    consts = make_consts()
    in_maps = []
    for c in range(8):
        m = dict(consts)
        m["xp"] = np.ascontiguousarray(inp["x_prompt"][c])
        m["xs"] = np.ascontiguousarray(inp["x_sample"][2 * c:2 * c + 2].reshape(64, D))
        for nm in ("ffn1", "ffn2"):
            for s in ("_norm", "_w_gate", "_w_up", "_w_down"):
                m[nm + s] = inp[nm + s]
        m["memp"] = np.ascontiguousarray(inp["mem_prompt"][c])
        m["cmk"] = np.ascontiguousarray(inp["cache_mem_k"][:, 2 * c:2 * c + 2].reshape(2, 2, 256, 256))
        m["cmv"] = np.ascontiguousarray(inp["cache_mem_v"][:, 2 * c:2 * c + 2].reshape(2, 2, 256, 256))
        for nm in ("mix_norm", "ev_w_in", "fox_b_f", "fox_k_norm", "fox_q_norm", "ev_w_out"):
            m[nm] = inp[nm]
        m["cfk"] = np.ascontiguousarray(inp["cache_fox_k"][0, 2 * c:2 * c + 2].reshape(2, 2048, 512))
        m["cfv"] = np.ascontiguousarray(inp["cache_fox_v"][0, 2 * c:2 * c + 2].reshape(2, 2048, 512))
        m["cfl"] = np.ascontiguousarray(inp["cache_fox_logf"][0, 2 * c:2 * c + 2])
        for nm in ("rwkv_mu", "rwkv_w0", "rwkv_a0", "rwkv_k_k", "rwkv_k_a", "rwkv_ln_g", "rwkv_ln_b", "rwkv_w2", "rwkv_a2", "rwkv_g2"):
            m[nm] = inp[nm]
        m["rwkv_r_k"] = np.ascontiguousarray(inp["rwkv_r_k"].reshape(1, 512))
        for nm in ("od_w_in", "od_w_out", "swa_q_norm", "swa_k_norm", "swa_sinks", "sgu_v_norm", "sgu_w_s", "sgu_b"):
            m[nm] = inp[nm]
        m["csk"] = np.ascontiguousarray(inp["cache_swa_k"][0, 2 * c:2 * c + 2].reshape(2, 128, 128))
        m["csv"] = np.ascontiguousarray(inp["cache_swa_v"][0, 2 * c:2 * c + 2].reshape(2, 128, 128))
        m["srw"] = np.ascontiguousarray(inp["state_rwkv"][0, 2 * c:2 * c + 2])
        m["srs"] = np.ascontiguousarray(inp["state_rwkv_shift"][0, 2 * c:2 * c + 2].reshape(2, 1792))
        for nm in ("xattn_norm", "mem_norm", "xattn_wq", "xattn_wkv", "xattn_q_norm", "xattn_k_norm", "xattn_wo"):
            m[nm] = inp[nm]
        in_maps.append(m)
    if cfg.get("_sim"):
        R = cfg["_sim"](nc, in_maps)
    else:
        res = run_bass_kernel_spmd(nc, in_maps, core_ids=list(range(8)))
        R = res.results
    y_prompt = np.stack([R[c]["y_prompt"] for c in range(8)], 0)
    y_sample = np.concatenate([R[c]["y_sample"].reshape(2, 32, D) for c in range(8)], 0)
    p_mem_k = np.stack([R[c]["p_mem_k"].reshape(2, 256, 4, 64) for c in range(8)], 1)
    p_mem_v = np.stack([R[c]["p_mem_v"].reshape(2, 256, 4, 64) for c in range(8)], 1)
    fk = np.stack([R[c]["fox_k"] for c in range(8)], 0)
    fv = np.stack([R[c]["fox_v"] for c in range(8)], 0)
    fl = np.stack([R[c]["fox_logf"] for c in range(8)], 0)
    rsft = np.stack([R[c]["rwkv_shift"] for c in range(8)], 0)
    p_fox_k = fk[:, :2048].reshape(1, 8, 2048, 8, 64)
    p_fox_v = fv[:, :2048].reshape(1, 8, 2048, 8, 64)
    p_fox_logf = fl[:, :2048].reshape(1, 8, 2048, 8)
    s_fox_k = fk[:, 2048:].reshape(1, 16, 32, 8, 64)
    s_fox_v = fv[:, 2048:].reshape(1, 16, 32, 8, 64)
    s_fox_logf = fl[:, 2048:].reshape(1, 16, 32, 8)
    p_rwkv_shift = rsft[:, 0].reshape(1, 8, 1, 1792)
    s_rwkv_shift = rsft[:, 1:3].reshape(1, 16, 1, 1792)
    rst_ = np.stack([R[c]["rwkv_state"] for c in range(8)], 0)
    p_rwkv_state = rst_[:, 0][None]
    s_rwkv_state = rst_[:, 1:3].reshape(1, 16, 8, 64, 64)
    sk = np.stack([R[c]["swa_ko"] for c in range(8)], 0)
    sv = np.stack([R[c]["swa_vo"] for c in range(8)], 0)
    p_swa_k = sk[:, 0].reshape(1, 8, 128, 2, 64)
    p_swa_v = sv[:, 0].reshape(1, 8, 128, 2, 64)
    s_swa_k = sk[:, 1:3].reshape(1, 16, 128, 2, 64)
    s_swa_v = sv[:, 1:3].reshape(1, 16, 128, 2, 64)
    s_sgu_v = np.stack([R[c]["sgu_vo"].reshape(2, 32, 512) for c in range(8)], 0).reshape(1, 16, 32, 512)
    f32 = np.float32
    z = lambda *sh: np.zeros(sh, f32)
    out = (y_prompt, y_sample, p_fox_k, p_fox_v, p_fox_logf, p_rwkv_state, p_rwkv_shift,
           p_swa_k, p_swa_v, p_mem_k, p_mem_v,
           s_fox_k, s_fox_v, s_fox_logf, s_rwkv_state, s_rwkv_shift,
           s_swa_k, s_swa_v, s_sgu_v)
    return tuple(np.ascontiguousarray(o, dtype=f32) for o in out)
```

```python
import numpy as np
from contextlib import ExitStack
import concourse.bass as bass
import concourse.mybir as mybir
from concourse.bass_utils import run_bass_kernel_spmd

F32 = mybir.dt.float32
BF16 = mybir.dt.bfloat16
AF = mybir.ActivationFunctionType
ALU = mybir.AluOpType
AX = mybir.AxisListType

COMPUTE = ("pe", "act", "dve", "pool")
NSLOT = 12
NT = 17
NTOK = 2112
D = 1024
DFF = 2816
EPS = 1e-6


class Buf:
    __slots__ = ("name", "w", "r")

    def __init__(self, name=""):
        self.name = name
        self.w = {}
        self.r = {}


class _Op:
    __slots__ = ("fn", "waits", "signal", "dma", "val")

    def __init__(self, fn, waits, dma):
        self.fn = fn
        self.waits = waits
        self.signal = False
        self.dma = dma
        self.val = 0


class FW:
    def __init__(self, nc, stack):
        self.nc = nc
        self.streams = {k: [] for k in ("pe", "act", "dve", "pool", "sp")}
        self.known = {k: {} for k in self.streams}
        self.snapc = {k: None for k in self.streams}
        self.tl = {}
        self.slot_rr = {"sp": 0, "pool": 0}
        self.sems = {}
        for k in COMPUTE:
            self.sems[k] = stack.enter_context(nc.semaphore("s_" + k))
        for q in ("sp", "pool"):
            for s in range(NSLOT):
                sk = "d_%s_%d" % (q, s)
                self.sems[sk] = stack.enter_context(nc.semaphore(sk))
        self.sigcount = {k: 0 for k in COMPUTE}
        self.emitted = {k: 0 for k in self.streams}
        self.ninstr = 0
        self.strict = False

    def _snap(self, s):
        if self.snapc[s] is None:
            self.snapc[s] = dict(self.known[s])
        return self.snapc[s]

    def _wait(self, s, tlk, idx, waits):
        kn = self.known[s]
        if kn.get(tlk, -1) >= idx:
            return
        st, oi, snap = self.tl[tlk][idx]
        assert oi >= self.emitted[st], "dependency on already-emitted op"
        self.streams[st][oi].signal = True
        waits.append((tlk, idx))
        kn[tlk] = idx
        for k, v in snap.items():
            if kn.get(k, -1) < v:
                kn[k] = v
        self.snapc[s] = None

    def op(self, eng, fn, reads=(), writes=()):
        deps = {}
        for b in reads:
            for k, i in b.w.items():
                if deps.get(k, -1) < i:
                    deps[k] = i
        strict = self.strict and eng != "pe"
        for b in writes:
            for k, i in b.w.items():
                if (k != eng or strict) and deps.get(k, -1) < i:
                    deps[k] = i
            for k, i in b.r.items():
                if (k != eng or strict) and deps.get(k, -1) < i:
                    deps[k] = i
        waits = []
        for k, i in deps.items():
            self._wait(eng, k, i, waits)
        o = _Op(fn, waits, None)
        st = self.streams[eng]
        st.append(o)
        tl = self.tl.setdefault(eng, [])
        idx = len(tl)
        tl.append((eng, len(st) - 1, self._snap(eng)))
        for b in reads:
            b.r[eng] = idx
        for b in writes:
            b.w = {eng: idx}
            b.r = {}
        return idx

    def dma(self, q, fn, reads=(), writes=()):
        deps = {}
        for b in reads:
            for k, i in b.w.items():
                if deps.get(k, -1) < i:
                    deps[k] = i
        for b in writes:
            for k, i in b.w.items():
                if deps.get(k, -1) < i:
                    deps[k] = i
            for k, i in b.r.items():
                if deps.get(k, -1) < i:
                    deps[k] = i
        slot = self.slot_rr[q]
        self.slot_rr[q] = (slot + 1) % NSLOT
        sk = "d_%s_%d" % (q, slot)
        tl = self.tl.setdefault(sk, [])
        idx = len(tl)
        if idx > 0 and deps.get(sk, -1) < idx - 1:
            deps[sk] = idx - 1
        waits = []
        for k, i in deps.items():
            self._wait(q, k, i, waits)
        o = _Op(fn, waits, (sk, idx))
        st = self.streams[q]
        st.append(o)
        tl.append((q, len(st) - 1, self._snap(q)))
        for b in reads:
            b.r[sk] = idx
        for b in writes:
            b.w = {sk: idx}
            b.r = {}

    def _val_of(self, tlk, idx):
        if tlk.startswith("d_"):
            return 16 * (idx + 1)
        st, oi, _ = self.tl[tlk][idx]
        o = self.streams[st][oi]
        assert o.signal and o.val > 0
        return o.val

    def flush(self):
        for s in self.streams:
            waits = []
            for tlk, lst in self.tl.items():
                if lst:
                    self._wait(s, tlk, len(lst) - 1, waits)
            if waits:
                self.streams[s].append(_Op(None, waits, None))
        for k in COMPUTE:
            c = self.sigcount[k]
            for o in self.streams[k][self.emitted[k]:]:
                if o.signal:
                    c += 1
                o.val = c
            self.sigcount[k] = c
        sems = self.sems

        def run(key, e):
            ops = self.streams[key]
            for o in ops[self.emitted[key]:]:
                for (tlk, idx) in o.waits:
                    e.wait_ge(sems[tlk], self._val_of(tlk, idx))
                    self.ninstr += 1
                if o.fn is None:
                    continue
                ins = o.fn(e)
                self.ninstr += 1
                if o.dma is not None:
                    ins.then_inc(sems[o.dma[0]], 16)
                elif o.signal:
                    ins.then_inc(sems[key], 1)
            self.emitted[key] = len(ops)

        with self.nc.Block() as block:
            @block.tensor
            def _(e):
                run("pe", e)

            @block.scalar
            def _(e):
                run("act", e)

            @block.vector
            def _(e):
                run("dve", e)

            @block.gpsimd
            def _(e):
                run("pool", e)

            @block.sync
            def _(e):
                run("sp", e)


def tile_rows(i):
    return 128 if i < 16 else 64


TOK_GROUPS = [(0, 512, [0, 1, 2, 3]), (512, 512, [4, 5, 6, 7]), (1024, 512, [8, 9, 10, 11]),
              (1536, 512, [12, 13, 14, 15]), (2048, 64, [16])]
FF_GROUPS = [(0, 3), (3, 6), (6, 9), (9, 12), (12, 15), (15, 18), (18, 20), (20, 22)]


class Ctx:
    pass


def build_program(cfg):
    nc = bass.Bass("TRN2", target_bir_lowering=False)
    C = Ctx()
    C.nc = nc
    dr = {}

    def din(name, shape):
        dr[name] = nc.dram_tensor(name, list(shape), F32, kind="ExternalInput").ap()
        return dr[name]

    def dout(name, shape):
        dr[name] = nc.dram_tensor(name, list(shape), F32, kind="ExternalOutput").ap()
        return dr[name]

    din("xp", (2048, D))
    din("xs", (64, D))
    din("c_ident", (128, 128))
    for nm in ("ffn1", "ffn2"):
        din(nm + "_norm", (2, D))
        din(nm + "_w_gate", (2, D, DFF))
        din(nm + "_w_up", (2, D, DFF))
        din(nm + "_w_down", (2, DFF, D))
    din("memp", (256, D))
    din("cmk", (2, 2, 256, 256))
    din("cmv", (2, 2, 256, 256))
    din("xattn_norm", (2, D))
    din("mem_norm", (2, D))
    din("xattn_wq", (2, D, 256))
    din("xattn_wkv", (2, D, 512))
    din("xattn_q_norm", (2, 64))
    din("xattn_k_norm", (2, 64))
    din("xattn_wo", (2, 256, D))
    din("mix_norm", (2, D))
    din("ev_w_in", (1, D, 3336))
    din("fox_b_f", (1, 8))
    din("fox_k_norm", (1, 64))
    din("fox_q_norm", (1, 64))
    din("ev_w_out", (1, D, D))
    din("cfk", (2, 2048, 512))
    din("cfv", (2, 2048, 512))
    din("cfl", (2, 2048, 8))
    for nm, n_ in (("rwkv_mu", 1792), ("rwkv_w0", 512), ("rwkv_a0", 512), ("rwkv_k_k", 512), ("rwkv_k_a", 512), ("rwkv_r_k", 512),
                   ("rwkv_ln_g", 512), ("rwkv_ln_b", 512)):
        din(nm, (1, n_))
    din("rwkv_w2", (1, 64, 512))
    din("rwkv_a2", (1, 64, 512))
    din("rwkv_g2", (1, 128, 512))
    din("srw", (2, 8, 64, 64))
    din("srs", (2, 1792))
    din("c_bd", (128, 128))
    din("c_msk", (64, 3, 64))
    din("c_rst", (128, 256))
    dout("rwkv_state", (3, 8, 64, 64))
    din("c_tri", (128, 128))
    din("c_trib", (64, 64))
    din("od_w_in", (1, D, 1792))
    din("od_w_out", (1, D, D))
    din("swa_q_norm", (1, 64))
    din("swa_k_norm", (1, 64))
    din("swa_sinks", (1, 8))
    din("sgu_v_norm", (1, 512))
    din("sgu_w_s", (1, 8, 128, 128))
    din("sgu_b", (1, 8, 128))
    din("csk", (2, 128, 128))
    din("csv", (2, 128, 128))
    din("c_rope", (128, 2, NT, 8))
    dout("swa_ko", (3, 128, 128))
    dout("swa_vo", (3, 128, 128))
    dout("sgu_vo", (64, 512))
    dout("y_prompt", (2048, D))
    dout("y_sample", (64, D))
    dout("fox_k", (NTOK, 512))
    dout("fox_v", (NTOK, 512))
    dout("fox_logf", (NTOK, 8))
    dout("rwkv_shift", (3, 1792))
    dout("p_mem_k", (2, 256, 256))
    dout("p_mem_v", (2, 256, 256))

    with ExitStack() as top:
        fw = FW(nc, top)
        fw.strict = bool(cfg.get("strict", False))
        C.fw = fw

        uid = [0]

        C.sb_cur = 0
        C.sb_max = 0

        def sb(st, name, shape, dt=F32):
            uid[0] += 1
            nb = int(np.prod(shape[1:])) * (4 if dt == F32 else 2)
            nb = (nb + 31) // 32 * 32

            def _rel(nb=nb):
                C.sb_cur -= nb
            st.callback(_rel)
            C.sb_cur += nb
            if C.sb_cur > C.sb_max:
                C.sb_max = C.sb_cur
                C.sb_max_at = name
            return st.enter_context(nc.sbuf_tensor("%s_%d" % (name, uid[0]), list(shape), dt))

        def ps(st, name, shape, dt=F32):
            nbytes = int(np.prod(shape[1:])) * (4 if dt == F32 else 2)
            assert nbytes % 2048 == 0, ("psum tile must be whole banks", name, shape)
            uid[0] += 1
            return st.enter_context(nc.psum_tensor("%s_%d" % (name, uid[0]), list(shape), dt))

        X = sb(top, "X", (128, NT, D))
        bX = [Buf("X%d" % i) for i in range(NT)]
        ident_f = sb(top, "ident_f", (128, 128))
        ident = sb(top, "ident", (128, 128), BF16)
        b_ident = Buf("ident")
        fw.dma("sp", lambda e: e.dma_start(out=ident_f[:], in_=dr["c_ident"]), writes=[b_ident])
        fw.op("dve", lambda e: e.tensor_copy(out=ident[:], in_=ident_f[:]), reads=[b_ident], writes=[b_ident])
        xp_v = dr["xp"].rearrange("(t p) d -> p t d", p=128)
        for g in range(4):
            fw.dma("sp", lambda e, g=g: e.dma_start(out=X[:, 4 * g:4 * g + 4, :], in_=xp_v[:, 4 * g:4 * g + 4, :]),
                   writes=bX[4 * g:4 * g + 4])
        fw.dma("sp", lambda e: e.dma_start(out=X[0:64, 16, :], in_=dr["xs"]), writes=[bX[16]])

        def norm_T(st, gain_row_ap, xT, b_xT, pfx):
            G = sb(st, pfx + "G", (128, D))
            bG = Buf()
            fw.dma("sp", lambda e: e.dma_start(out=G[:], in_=gain_row_ap.broadcast_to([128, D])), writes=[bG])
            ss = sb(st, pfx + "ss", (128, NT))
            sd = sb(st, pfx + "sd", (128, NT))
            rs = sb(st, pfx + "rs", (128, NT))
            bss = Buf()
            junk = sb(st, pfx + "junk", (128, D))
            bj = Buf()
            fw.op("pool", lambda e: e.memset(ss[:], 1.0), writes=[bss])
            for i in range(NT):
                r = tile_rows(i)
                fw.op("act", lambda e, i=i, r=r: e.activation(out=junk[0:r, :], in_=X[0:r, i, :], func=AF.Square,
                                                              accum_out=ss[0:r, i:i + 1]),
                      reads=[bX[i]], writes=[bj, bss])
            fw.op("act", lambda e: e.activation(out=sd[:], in_=ss[:], func=AF.Sqrt, scale=1.0 / D, bias=EPS),
                  reads=[bss], writes=[bss])
            fw.op("dve", lambda e: e.reciprocal(out=rs[:], in_=sd[:]), reads=[bss], writes=[bss])
            xn = [sb(st, pfx + "xn%d" % j, (128, D), BF16) for j in range(2)]
            bxn = [Buf(), Buf()]
            ptr = [ps(st, pfx + "ptr%d" % j, (128, 8, 128), BF16) for j in range(2)]
            bptr = [Buf(), Buf()]
            for i in range(NT):
                r = tile_rows(i)
                j = i % 2
                fw.op("dve", lambda e, i=i, r=r, j=j: e.scalar_tensor_tensor(
                    out=xn[j][0:r, :], in0=X[0:r, i, :], scalar=rs[0:r, i:i + 1], in1=G[0:r, :],
                    op0=ALU.mult, op1=ALU.mult), reads=[bX[i], bss, bG], writes=[bxn[j]])
                for k in range(8):
                    fw.op("pe", lambda e, r=r, j=j, k=k: e.transpose(
                        out=ptr[j][:, k, 0:r], in_=xn[j][0:r, k * 128:(k + 1) * 128], identity=ident[0:r, 0:r]),
                        reads=[bxn[j], b_ident], writes=[bptr[j]])
                eng = "act" if i % 2 == 0 else "dve"
                if eng == "act":
                    fw.op("act", lambda e, i=i, r=r, j=j: e.copy(out=xT[:, :, i * 128:i * 128 + r], in_=ptr[j][:, :, 0:r]),
                          reads=[bptr[j]], writes=[b_xT[i // 4]])
                else:
                    fw.op("dve", lambda e, i=i, r=r, j=j: e.tensor_copy(out=xT[:, :, i * 128:i * 128 + r], in_=ptr[j][:, :, 0:r]),
                          reads=[bptr[j]], writes=[b_xT[i // 4]])

        def ffn(l, nm):
            with ExitStack() as st:
                xT = sb(st, "xT", (128, 8, NTOK), BF16)
                b_xT = [Buf() for _ in range(5)]
                with ExitStack() as st2:
                    norm_T(st2, dr[nm + "_norm"][l:l + 1, :], xT, b_xT, "n_")
                    fw.flush()
                wg_d = dr[nm + "_w_gate"][l].rearrange("(k p) n -> p k n", p=128)
                wu_d = dr[nm + "_w_up"][l].rearrange("(k p) n -> p k n", p=128)
                wd_d = dr[nm + "_w_down"][l].rearrange("(j p) n -> p j n", p=128)
                NS = 4
                stg = [sb(st, "stg%d" % i, (128, 1536)) for i in range(NS)]
                bstg = [Buf() for _ in range(NS)]
                WG = [sb(st, "WG%d" % i, (128, 8, 384), BF16) for i in range(2)]
                WU = [sb(st, "WU%d" % i, (128, 8, 384), BF16) for i in range(2)]
                WD = [sb(st, "WD%d" % i, (128, 3, D), BF16) for i in range(2)]
                bWG = [Buf(), Buf()]
                bWU = [Buf(), Buf()]
                bWD = [Buf(), Buf()]
                SG = [sb(st, "SG%d" % i, (128, 512)) for i in range(2)]
                bSG = [Buf(), Buf()]
                AT = [sb(st, "AT%d" % i, (128, 3, 512), BF16) for i in range(2)]
                bAT = [Buf(), Buf()]
                pg = [ps(st, "pg%d" % i, (128, 512)) for i in range(2)]
                pu = [ps(st, "pu%d" % i, (128, 512)) for i in range(2)]
                pd = [ps(st, "pd%d" % i, (128, 512)) for i in range(2)]
                bpg = [Buf(), Buf()]
                bpu = [Buf(), Buf()]
                bpd = [Buf(), Buf()]
                sc = [0]
                cnt = {"g": 0, "d": 0, "a": 0}

                def load_cast(src_ap, dst_ap, bdst, shape3):
                    s = sc[0] % NS
                    sc[0] += 1
                    a, b_ = shape3
                    sv = stg[s][:, 0:a * b_].rearrange("p (a b) -> p a b", a=a)
                    fw.dma("sp", lambda e: e.dma_start(out=sv, in_=src_ap), writes=[bstg[s]])
                    fw.op("pool", lambda e: e.tensor_copy(out=dst_ap, in_=sv), reads=[bstg[s]], writes=[bdst])

                def load_group(gi):
                    c0, c1 = FF_GROUPS[gi]
                    nch = c1 - c0
                    ncol = nch * 128
                    s = gi % 2
                    for h in range(2):
                        load_cast(wg_d[:, 4 * h:4 * h + 4, c0 * 128:c0 * 128 + ncol], WG[s][:, 4 * h:4 * h + 4, 0:ncol],
                                  bWG[s], (4, ncol))
                        load_cast(wu_d[:, 4 * h:4 * h + 4, c0 * 128:c0 * 128 + ncol], WU[s][:, 4 * h:4 * h + 4, 0:ncol],
                                  bWU[s], (4, ncol))
                    for j in range(nch):
                        load_cast(wd_d[:, c0 + j:c0 + j + 1, :], WD[s][:, j:j + 1, :], bWD[s], (1, D))

                load_group(0)
                for gi in range(len(FF_GROUPS)):
                    if gi + 1 < len(FF_GROUPS):
                        load_group(gi + 1)
                    c0, c1 = FF_GROUPS[gi]
                    nch = c1 - c0
                    s = gi % 2
                    for tgi, (t0, n, tiles) in enumerate(TOK_GROUPS):
                        a = cnt["a"] % 2
                        cnt["a"] += 1
                        for j in range(nch):
                            q = cnt["g"] % 2
                            cnt["g"] += 1
                            for k in range(8):
                                fw.op("pe", lambda e, q=q, s=s, k=k, j=j, t0=t0, n=n: e.matmul(
                                    pg[q][:, 0:n], lhsT=WG[s][:, k, j * 128:(j + 1) * 128], rhs=xT[:, k, t0:t0 + n],
                                    start=(k == 0), stop=(k == 7)), reads=[bWG[s], b_xT[tgi]], writes=[bpg[q]])
                            for k in range(8):
                                fw.op("pe", lambda e, q=q, s=s, k=k, j=j, t0=t0, n=n: e.matmul(
                                    pu[q][:, 0:n], lhsT=WU[s][:, k, j * 128:(j + 1) * 128], rhs=xT[:, k, t0:t0 + n],
                                    start=(k == 0), stop=(k == 7)), reads=[bWU[s], b_xT[tgi]], writes=[bpu[q]])
                            fw.op("act", lambda e, q=q, n=n: e.activation(out=SG[q][:, 0:n], in_=pg[q][:, 0:n], func=AF.Silu),
                                  reads=[bpg[q]], writes=[bSG[q]])
                            fw.op("dve", lambda e, q=q, n=n, a=a, j=j: e.tensor_tensor(
                                out=AT[a][:, j, 0:n], in0=SG[q][:, 0:n], in1=pu[q][:, 0:n], op=ALU.mult),
                                reads=[bSG[q], bpu[q]], writes=[bAT[a]])
                        for tl_i, ti in enumerate(tiles):
                            r = tile_rows(ti)
                            for half in range(2):
                                q = cnt["d"] % 2
                                cnt["d"] += 1
                                for j in range(nch):
                                    fw.op("pe", lambda e, q=q, a=a, j=j, tl_i=tl_i, r=r, half=half, s=s: e.matmul(
                                        pd[q][0:r, :], lhsT=AT[a][:, j, tl_i * 128:tl_i * 128 + r],
                                        rhs=WD[s][:, j, half * 512:(half + 1) * 512],
                                        start=(j == 0), stop=(j == nch - 1)), reads=[bAT[a], bWD[s]], writes=[bpd[q]])
                                fw.op("dve", lambda e, q=q, r=r, ti=ti, half=half: e.scalar_tensor_tensor(
                                    out=X[0:r, ti, half * 512:(half + 1) * 512], in0=pd[q][0:r, :], scalar=0.5,
                                    in1=X[0:r, ti, half * 512:(half + 1) * 512], op0=ALU.mult, op1=ALU.add),
                                    reads=[bpd[q], bX[ti]], writes=[bX[ti]])
                fw.flush()


        ones_bf = sb(top, "ones_bf", (128, 64), BF16)
        b_ones = Buf("ones")
        fw.op("pool", lambda e: e.memset(ones_bf[:], 1.0), writes=[b_ones])

        def bc3(ap2, n):
            p, h = ap2.shape
            return ap2.unsqueeze(2).broadcast_to([p, h, n])

        def bcm(ap2, h):
            p, n = ap2.shape
            return ap2.unsqueeze(1).broadcast_to([p, h, n])

        def rms_heads(st, src, bsrc, r, H, gain_tile, bgain, scale, out_bf, bout_bf, out_f=None, bout_f=None, tag=""):
            key = "rmsh_tmp"
            if key not in st.__dict__:
                st.__dict__[key] = True
                C.rh_sq = sb(st, "rh_sq", (128, 8, 64))
                C.rh_ss = sb(st, "rh_ss", (128, 8))
                C.rh_sd = sb(st, "rh_sd", (128, 8))
                C.rh_rs = sb(st, "rh_rs", (128, 8))
                C.rh_t = sb(st, "rh_t", (128, 8, 64))
                C.b_rh = Buf()
                C.b_rh2 = Buf()
            sq, ss, sd, rs, t = C.rh_sq, C.rh_ss, C.rh_sd, C.rh_rs, C.rh_t
            fw.op("act", lambda e: e.activation(out=sq[0:r, 0:H, :], in_=src, func=AF.Square), reads=[bsrc], writes=[C.b_rh])
            fw.op("dve", lambda e: e.tensor_reduce(out=ss[0:r, 0:H], in_=sq[0:r, 0:H, :], axis=AX.X, op=ALU.add),
                  reads=[C.b_rh], writes=[C.b_rh2])
            fw.op("act", lambda e: e.activation(out=sd[0:r, 0:H], in_=ss[0:r, 0:H], func=AF.Sqrt, scale=1.0 / 64, bias=EPS),
                  reads=[C.b_rh2], writes=[C.b_rh2])
            fw.op("dve", lambda e: e.reciprocal(out=rs[0:r, 0:H], in_=sd[0:r, 0:H]), reads=[C.b_rh2], writes=[C.b_rh2])
            fw.op("dve", lambda e: e.tensor_tensor(out=t[0:r, 0:H, :], in0=src, in1=bc3(rs[0:r, 0:H], 64), op=ALU.mult),
                  reads=[bsrc, C.b_rh2], writes=[C.b_rh])
            if out_f is not None:
                fw.op("dve", lambda e: e.tensor_tensor(out=out_f, in0=t[0:r, 0:H, :], in1=bcm(gain_tile[0:r, :], H), op=ALU.mult),
                      reads=[C.b_rh, bgain], writes=[bout_f])
                fw.op("act", lambda e: e.activation(out=out_bf, in_=out_f, func=AF.Copy, scale=float(scale)),
                      reads=[bout_f], writes=[bout_bf])
            else:
                fw.op("dve", lambda e: e.scalar_tensor_tensor(out=out_bf, in0=t[0:r, 0:H, :], scalar=float(scale),
                                                              in1=bcm(gain_tile[0:r, :], H), op0=ALU.mult, op1=ALU.mult),
                      reads=[C.b_rh, bgain], writes=[bout_bf])


        def rms_heads_staged(st, src, bsrc, r, H, gain_tile, bgain, scale, out_bf, bout_bf, out_f, bout_f, si):
            key = "rmsh_tmp_s%d" % si
            if key not in st.__dict__:
                st.__dict__[key] = (sb(st, "rhs_sq%d" % si, (128, 8, 64)), sb(st, "rhs_ss%d" % si, (128, 8)), sb(st, "rhs_sd%d" % si, (128, 8)),
                                    sb(st, "rhs_rs%d" % si, (128, 8)), sb(st, "rhs_t%d" % si, (128, 8, 64)), Buf(), Buf())
            sq, ss, sd, rs, t, b1, b2 = st.__dict__[key]
            th = []
            th.append(lambda: fw.op("act", lambda e: e.activation(out=sq[0:r, 0:H, :], in_=src, func=AF.Square), reads=[bsrc], writes=[b1]))
            th.append(lambda: fw.op("dve", lambda e: e.tensor_reduce(out=ss[0:r, 0:H], in_=sq[0:r, 0:H, :], axis=AX.X, op=ALU.add), reads=[b1], writes=[b2]))
            th.append(lambda: fw.op("act", lambda e: e.activation(out=sd[0:r, 0:H], in_=ss[0:r, 0:H], func=AF.Sqrt, scale=1.0 / 64, bias=EPS), reads=[b2], writes=[b2]))
            th.append(lambda: fw.op("dve", lambda e: e.reciprocal(out=rs[0:r, 0:H], in_=sd[0:r, 0:H]), reads=[b2], writes=[b2]))
            th.append(lambda: fw.op("dve", lambda e: e.tensor_tensor(out=t[0:r, 0:H, :], in0=src, in1=bc3(rs[0:r, 0:H], 64), op=ALU.mult), reads=[bsrc, b2], writes=[b1]))
            if out_f is not None:
                th.append(lambda: fw.op("dve", lambda e: e.tensor_tensor(out=out_f, in0=t[0:r, 0:H, :], in1=bcm(gain_tile[0:r, :], H), op=ALU.mult),
                                        reads=[b1, bgain], writes=[bout_f]))
                th.append(lambda: fw.op("act", lambda e: e.activation(out=out_bf, in_=out_f, func=AF.Copy, scale=float(scale)), reads=[bout_f], writes=[bout_bf]))
            else:
                th.append(lambda: fw.op("dve", lambda e: e.scalar_tensor_tensor(out=out_bf, in0=t[0:r, 0:H, :], scalar=float(scale), in1=bcm(gain_tile[0:r, :], H),
                                                                                op0=ALU.mult, op1=ALU.mult), reads=[b1, bgain], writes=[bout_bf]))
            return th

        def load_w_bf(st, src_ap, shape, name, stage_cols=2048):
            a, b_ = shape
            wt = sb(st, name, (128, a, b_), BF16)
            bw = Buf(name)
            if "wstage" not in C.__dict__ or C.wstage_owner is not st:
                C.wstage = [sb(st, "wstage%d" % i, (128, stage_cols)) for i in range(2)]
                C.bwstage = [Buf(), Buf()]
                C.wstage_owner = st
                C.wsc = 0
            per = max(1, stage_cols // b_)
            npart = src_ap.shape[0]
            a0 = 0
            while a0 < a:
                na = min(per, a - a0)
                sl = C.wsc % 2
                C.wsc += 1
                sv = C.wstage[sl][0:npart, 0:na * b_].rearrange("p (a b) -> p a b", a=na)
                fw.dma("sp", lambda e, sv=sv, a0=a0, na=na: e.dma_start(out=sv, in_=src_ap[:, a0:a0 + na, :]),
                       writes=[C.bwstage[sl]])
                fw.op("pool", lambda e, sv=sv, a0=a0, na=na: e.tensor_copy(out=wt[0:npart, a0:a0 + na, :], in_=sv),
                      reads=[C.bwstage[sl]], writes=[bw])
                a0 += na
            return wt, bw

        def bcast_row(st, row_ap, n, name):
            t = sb(st, name, (128, n))
            b = Buf(name)
            fw.dma("sp", lambda e: e.dma_start(out=t[:], in_=row_ap.broadcast_to([128, n])), writes=[b])
            return t, b

        def attn_core(st, q_ap, bq, N, kblocks, out_ap, bout, extra_den=None, nbuf=2):
            if "ac_owner" not in C.__dict__ or C.ac_owner is not st:
                C.ac_owner = st
                C.ac_S = [ps(st, "acS%d" % i, (128, 512)) for i in range(2)]
                C.ac_bS = [Buf(), Buf()]
                C.ac_num = [ps(st, "acN%d" % i, (64, 512)) for i in range(nbuf)]
                C.ac_den = [ps(st, "acD%d" % i, (64, 512)) for i in range(nbuf)]
                C.ac_bnd = [Buf() for _ in range(nbuf)]
                C.ac_nbuf = nbuf
                C.ac_P = [sb(st, "acP%d" % i, (128, 512), BF16) for i in range(3)]
                C.ac_bP = [Buf(), Buf(), Buf()]
                C.ac_rd = [sb(st, "acR%d" % i, (64, 512)) for i in range(2)]
                C.ac_brd = [Buf(), Buf()]
                C.ac_c = [0, 0, 0]
            o = C.ac_c[1] % C.ac_nbuf
            o2 = C.ac_c[1] % 2
            C.ac_c[1] += 1
            num, den, bnd = C.ac_num[o], C.ac_den[o], C.ac_bnd[o]
            nb = len(kblocks)
            for bi, kb in enumerate(kblocks):
                si = C.ac_c[0] % 2
                C.ac_c[0] += 1
                pi = C.ac_c[2] % 3
                C.ac_c[2] += 1
                S, bS, P, bP = C.ac_S[si], C.ac_bS[si], C.ac_P[pi], C.ac_bP[pi]
                nk, c0 = kb["nk"], kb["col0"]
                fw.op("pe", lambda e, S=S, kb=kb, nk=nk, c0=c0: e.matmul(S[0:nk, c0:N], lhsT=kb["kT"], rhs=q_ap[:, c0:N],
                                                                        start=True, stop=True),
                      reads=[bq] + kb["reads"], writes=[bS])
                fw.op("act", lambda e, S=S, P=P, nk=nk, c0=c0: e.activation(out=P[0:nk, c0:N], in_=S[0:nk, c0:N], func=AF.Exp),
                      reads=[bS], writes=[bP])
                if kb.get("mask") is not None:
                    mw = kb.get("mask_w", 128)
                    fw.op("pool", lambda e, P=P, kb=kb, nk=nk, c0=c0, mw=mw: e.tensor_tensor(
                        out=P[0:nk, c0:c0 + mw], in0=P[0:nk, c0:c0 + mw], in1=kb["mask"], op=ALU.mult),
                        reads=[bP] + kb.get("mask_reads", []), writes=[bP])
                fw.op("pe", lambda e, P=P, kb=kb, nk=nk, c0=c0, bi=bi: e.matmul(num[:, c0:N], lhsT=kb["v"], rhs=P[0:nk, c0:N],
                                                                               start=(bi == 0), stop=(bi == nb - 1)),
                      reads=[bP] + kb["reads"], writes=[bnd])
                fw.op("pe", lambda e, P=P, nk=nk, c0=c0, bi=bi: e.matmul(den[:, c0:N], lhsT=ones_bf[0:nk, :], rhs=P[0:nk, c0:N],
                                                                        start=(bi == 0), stop=(bi == nb - 1)),
                      reads=[bP, b_ones], writes=[bnd])
            rd, brd = C.ac_rd[o2], C.ac_brd[o2]
            if extra_den is not None:
                ed_ap, ed_reads = extra_den
                fw.op("dve", lambda e: e.tensor_scalar(out=rd[:, 0:N], in0=den[:, 0:N], scalar1=ed_ap, scalar2=None, op0=ALU.add),
                      reads=[bnd] + ed_reads, writes=[brd])
                fw.op("dve", lambda e: e.reciprocal(out=rd[:, 0:N], in_=rd[:, 0:N]), reads=[brd], writes=[brd])
            else:
                fw.op("dve", lambda e: e.reciprocal(out=rd[:, 0:N], in_=den[:, 0:N]), reads=[bnd], writes=[brd])
            fw.op("dve", lambda e: e.tensor_tensor(out=out_ap, in0=num[:, 0:N], in1=rd[:, 0:N], op=ALU.mult),
                  reads=[bnd, brd], writes=[bout])

        def xattn(l):
            with ExitStack() as st:
                xT = sb(st, "xT", (128, 8, NTOK), BF16)
                b_xT = [Buf() for _ in range(5)]
                with ExitStack() as st2:
                    norm_T(st2, dr["xattn_norm"][l:l + 1, :], xT, b_xT, "n_")
                    fw.flush()
                Wq, bWq = load_w_bf(st, dr["xattn_wq"][l].rearrange("(k p) n -> p k n", p=128), (8, 256), "Wq")
                Wkv, bWkv = load_w_bf(st, dr["xattn_wkv"][l].rearrange("(k p) n -> p k n", p=128), (8, 512), "Wkv")
                Wo, bWo = load_w_bf(st, dr["xattn_wo"][l].rearrange("(h d) n -> d h n", d=64)[:, :, :], (4, D), "Wo")
                Gq, bGq = bcast_row(st, dr["xattn_q_norm"][l:l + 1, :], 64, "Gq")
                Gk, bGk = bcast_row(st, dr["xattn_k_norm"][l:l + 1, :], 64, "Gk")
                Gm, bGm = bcast_row(st, dr["mem_norm"][l:l + 1, :], D, "Gm")
                KT = sb(st, "KT", (64, 3, 4, 256), BF16)
                bKT = [Buf(), Buf(), Buf()]
                VV = sb(st, "VV", (128, 3, 2, 256), BF16)
                bVV = [Buf(), Buf(), Buf()]
                ptk = ps(st, "ptk", (64, 8, 128), BF16)
                bptk = Buf()
                with ExitStack() as st3:
                    pkv = ps(st3, "pkv", (128, 512))
                    bpkv = Buf()
                    mem = sb(st3, "mem", (128, 2, D))
                    bmem = Buf()
                    fw.dma("sp", lambda e: e.dma_start(out=mem[:], in_=dr["memp"].rearrange("(t p) d -> p t d", p=128)), writes=[bmem])
                    mss = sb(st3, "mss", (128, 4))
                    bmss = Buf()
                    mjunk = sb(st3, "mjunk", (128, D))
                    bmj = Buf()
                    mn = sb(st3, "mn", (128, D), BF16)
                    bmn = Buf()
                    memT = sb(st3, "memT", (128, 8, 256), BF16)
                    bmemT = Buf()
                    pmt = ps(st3, "pmt", (128, 8, 128), BF16)
                    bpmt = Buf()
                    knf = sb(st3, "knf", (128, 4, 64))
                    bknf = Buf()
                    knb = sb(st3, "knb", (128, 4, 64), BF16)
                    bknb = Buf()
                    vf = sb(st3, "vf", (128, 256))
                    bvf = Buf()
                    for t in range(2):
                        fw.op("act", lambda e, t=t: e.activation(out=mjunk[:], in_=mem[:, t, :], func=AF.Square, accum_out=mss[:, t:t + 1]),
                              reads=[bmem], writes=[bmj, bmss])
                    fw.op("act", lambda e: e.activation(out=mss[:, 2:4], in_=mss[:, 0:2], func=AF.Sqrt, scale=1.0 / D, bias=EPS),
                          reads=[bmss], writes=[bmss])
                    fw.op("dve", lambda e: e.reciprocal(out=mss[:, 0:2], in_=mss[:, 2:4]), reads=[bmss], writes=[bmss])
                    for t in range(2):
                        fw.op("dve", lambda e, t=t: e.scalar_tensor_tensor(out=mn[:], in0=mem[:, t, :], scalar=mss[:, t:t + 1], in1=Gm[:],
                                                                            op0=ALU.mult, op1=ALU.mult), reads=[bmem, bmss, bGm], writes=[bmn])
                        for k in range(8):
                            fw.op("pe", lambda e, k=k: e.transpose(out=pmt[:, k, :], in_=mn[:, k * 128:(k + 1) * 128], identity=ident[:]),
                                  reads=[bmn, b_ident], writes=[bpmt])
                        fw.op("act", lambda e, t=t: e.copy(out=memT[:, :, t * 128:(t + 1) * 128], in_=pmt[:]), reads=[bpmt], writes=[bmemT])
                    for t in range(2):
                        for k in range(8):
                            fw.op("pe", lambda e, t=t, k=k: e.matmul(pkv[:], lhsT=memT[:, k, t * 128:(t + 1) * 128], rhs=Wkv[:, k, :],
                                                                      start=(k == 0), stop=(k == 7)), reads=[bmemT, bWkv], writes=[bpkv])
                        rms_heads(st3, pkv[:, 0:256].rearrange("p (h d) -> p h d", h=4), bpkv, 128, 4, Gk, bGk, 1.0,
                                  knb[:], bknb, out_f=knf[:], bout_f=bknf)
                        fw.op("act", lambda e: e.copy(out=vf[:], in_=pkv[:, 256:512]), reads=[bpkv], writes=[bvf])
                        fw.op("dve", lambda e, t=t: e.tensor_copy(out=VV[:, 0, t, :], in_=vf[:]), reads=[bvf], writes=[bVV[0]])
                        fw.dma("sp", lambda e, t=t: e.dma_start(out=dr["p_mem_k"][l, t * 128:(t + 1) * 128, :],
                                                                in_=knf[:].rearrange("p h d -> p (h d)")), reads=[bknf])
                        fw.dma("sp", lambda e, t=t: e.dma_start(out=dr["p_mem_v"][l, t * 128:(t + 1) * 128, :], in_=vf[:]), reads=[bvf])
                        for h in range(4):
                            fw.op("pe", lambda e, h=h: e.transpose(out=ptk[:, h, :], in_=knb[:, h, :], identity=ident[:]),
                                  reads=[bknb, b_ident], writes=[bptk])
                        fw.op("act", lambda e, t=t: e.copy(out=KT[:, 0, :, t * 128:(t + 1) * 128], in_=ptk[:, 0:4, :]), reads=[bptk], writes=[bKT[0]])
                    ck = sb(st3, "ck", (128, 2, 256))
                    bck = Buf()
                    ckb = sb(st3, "ckb", (128, 2, 4, 64), BF16)
                    bckb = Buf()
                    cv = sb(st3, "cv", (128, 2, 256))
                    bcv = Buf()
                    for b in range(2):
                        fw.dma("sp", lambda e, b=b: e.dma_start(out=ck[:], in_=dr["cmk"][l, b].rearrange("(t p) d -> p t d", p=128)), writes=[bck])
                        fw.dma("sp", lambda e, b=b: e.dma_start(out=cv[:], in_=dr["cmv"][l, b].rearrange("(t p) d -> p t d", p=128)), writes=[bcv])
                        fw.op("pool", lambda e: e.tensor_copy(out=ckb[:].rearrange("p t h d -> p t (h d)"), in_=ck[:]), reads=[bck], writes=[bckb])
                        fw.op("pool", lambda e, b=b: e.tensor_copy(out=VV[:, 1 + b, :, :], in_=cv[:]), reads=[bcv], writes=[bVV[1 + b]])
                        for t in range(2):
                            for h in range(4):
                                fw.op("pe", lambda e, t=t, h=h: e.transpose(out=ptk[:, h, :], in_=ckb[:, t, h, :], identity=ident[:]),
                                      reads=[bckb, b_ident], writes=[bptk])
                            fw.op("act", lambda e, t=t, b=b: e.copy(out=KT[:, 1 + b, :, t * 128:(t + 1) * 128], in_=ptk[:, 0:4, :]),
                                  reads=[bptk], writes=[bKT[1 + b]])

                    fw.flush()
                QT = sb(st, "QT", (64, 4, NTOK), BF16)
                bQT = [Buf() for _ in range(5)]
                OT = sb(st, "OT", (64, 4, NTOK), BF16)
                bOT = [Buf() for _ in range(5)]
                pq = [ps(st, "pq%d" % i, (128, 512)) for i in range(1)]
                bpq = [Buf()]
                qnb = sb(st, "qnb", (128, 4, 64), BF16)
                bqnb = Buf()
                for i in range(NT):
                    r = tile_rows(i)
                    for k in range(8):
                        fw.op("pe", lambda e, i=i, r=r, k=k: e.matmul(pq[0][0:r, 0:256], lhsT=xT[:, k, i * 128:i * 128 + r], rhs=Wq[:, k, :],
                                                                       start=(k == 0), stop=(k == 7)), reads=[b_xT[i // 4], bWq], writes=[bpq[0]])
                    rms_heads(st, pq[0][0:r, 0:256].rearrange("p (h d) -> p h d", h=4), bpq[0], r, 4, Gq, bGq, 0.125, qnb[0:r], bqnb)
                    for h in range(4):
                        fw.op("pe", lambda e, h=h, r=r: e.transpose(out=ptk[:, h, 0:r], in_=qnb[0:r, h, :], identity=ident[0:r, 0:r]),
                              reads=[bqnb, b_ident], writes=[bptk])
                    fw.op("act", lambda e, i=i, r=r: e.copy(out=QT[:, :, i * 128:i * 128 + r], in_=ptk[:, 0:4, 0:r]),
                          reads=[bptk], writes=[bQT[i // 4]])
                for h in range(4):
                    for tgi, (t0, n, tiles) in enumerate(TOK_GROUPS):
                        if tgi < 4:
                            segs = [(t0, n, 0)]
                        else:
                            segs = [(2048, 32, 1), (2080, 32, 2)]
                        for (q0, nq, sq_) in segs:
                            kbs = [dict(kT=KT[:, sq_, h, kt * 128:(kt + 1) * 128], v=VV[:, sq_, kt, h * 64:(h + 1) * 64], nk=128, col0=0,
                                        reads=[bKT[sq_], bVV[sq_]]) for kt in range(2)]
                            attn_core(st, QT[:, h, q0:q0 + nq], bQT[tgi], nq, kbs, OT[:, h, q0:q0 + nq], bOT[tgi])
                po = C.ac_S
                bpo = C.ac_bS
                c = 0
                for i in range(NT):
                    r = tile_rows(i)
                    for half in range(2):
                        q = c % 2
                        c += 1
                        for h in range(4):
                            fw.op("pe", lambda e, q=q, i=i, r=r, h=h, half=half: e.matmul(
                                po[q][0:r, :], lhsT=OT[:, h, i * 128:i * 128 + r], rhs=Wo[0:64, h, half * 512:(half + 1) * 512],
                                start=(h == 0), stop=(h == 3)), reads=[bOT[i // 4], bWo], writes=[bpo[q]])
                        fw.op("dve", lambda e, q=q, r=r, i=i, half=half: e.tensor_tensor(
                            out=X[0:r, i, half * 512:(half + 1) * 512], in0=po[q][0:r, :], in1=X[0:r, i, half * 512:(half + 1) * 512],
                            op=ALU.add), reads=[bpo[q], bX[i]], writes=[bX[i]])
                fw.flush()


        def fox_part(l, xT, b_xT):
            e_ = l // 2
            wv = dr["ev_w_in"][e_].rearrange("(k p) n -> p k n", p=128)
            with ExitStack() as st:
                Gq, bGq = bcast_row(st, dr["fox_q_norm"][e_:e_ + 1, :], 64, "Gfq")
                Gk, bGk = bcast_row(st, dr["fox_k_norm"][e_:e_ + 1, :], 64, "Gfk")
                Bf, bBf = bcast_row(st, dr["fox_b_f"][e_:e_ + 1, :], 8, "Bf")
                TRI = sb(st, "TRI", (128, 128))
                TRIB = sb(st, "TRIB", (64, 64))
                ONESF = sb(st, "ONESF", (128, 128))
                MASKT = sb(st, "MASKT", (128, 128), BF16)
                bcon = Buf()
                fw.dma("sp", lambda e: e.dma_start(out=TRI[:], in_=dr["c_tri"]), writes=[bcon])
                fw.dma("sp", lambda e: e.dma_start(out=TRIB[:], in_=dr["c_trib"]), writes=[bcon])
                fw.op("pool", lambda e: e.memset(ONESF[:], 1.0), writes=[bcon])
                fw.op("dve", lambda e: e.tensor_copy(out=MASKT[:], in_=TRI[:]), reads=[bcon], writes=[bcon])
                CUM3 = sb(st, "CUM3", (128, NT, 3, 8), BF16)
                CUM3N = sb(st, "CUM3N", (128, NT, 3, 8), BF16)
                bCUM = Buf()
                CC3N = sb(st, "CC3N", (128, 2, 16, 3, 8), BF16)
                bCC = Buf()
                OFF = sb(st, "OFF", (128, 8))
                bOFF = Buf()
                OFFC = sb(st, "OFFC", (128, 2, 8))
                bOFFC = Buf()
                OFF16 = sb(st, "OFF16", (64, 8))
                bOFF16 = Buf()
                LFA = sb(st, "LFA", (128, NT, 8))
                bLFA = Buf()
                CL = sb(st, "CL", (128, 2, 16, 8))
                bCL = Buf()
                pm = ps(st, "pm", (128, 512))
                bpm = Buf()
                ctmp = sb(st, "ctmp", (128, 4, 8))
                bct = Buf()
                CSCR = sb(st, "CSCR", (128, 3, 8), BF16)
                ptk = ps(st, "ptkf", (72, 8, 128), BF16)
                bptk = Buf()
                pp = ps(st, "ppf", (128, 512))
                bpp = Buf()
                pp2 = ps(st, "ppf2", (128, 512))
                bpp2 = Buf()

                def split3(src_ap, r, dst3, dst3n, bdst):
                    t = ctmp
                    fw.op("dve", lambda e: e.tensor_copy(out=dst3[0:r, 0, :], in_=src_ap), reads=[bct], writes=[bdst])
                    fw.op("dve", lambda e: e.tensor_tensor(out=t[0:r, 1, :], in0=src_ap, in1=dst3[0:r, 0, :], op=ALU.subtract),
                          reads=[bct, bdst], writes=[bct])
                    fw.op("dve", lambda e: e.tensor_copy(out=dst3[0:r, 1, :], in_=t[0:r, 1, :]), reads=[bct], writes=[bdst])
                    fw.op("dve", lambda e: e.tensor_tensor(out=t[0:r, 2, :], in0=t[0:r, 1, :], in1=dst3[0:r, 1, :], op=ALU.subtract),
                          reads=[bct, bdst], writes=[bct])
                    fw.op("dve", lambda e: e.tensor_copy(out=dst3[0:r, 2, :], in_=t[0:r, 2, :]), reads=[bct], writes=[bdst])
                    if dst3n is not None:
                        fw.op("dve", lambda e: e.tensor_scalar(out=dst3n[0:r], in0=dst3[0:r], scalar1=-1.0, scalar2=None, op0=ALU.mult),
                              reads=[bdst], writes=[bdst])

                def cum_tile(lf_ap, blf, r, tri_ap, off_ap, boff, dst3, dst3n, bdst, update_off):
                    if not cfg.get("fox_cum", True):
                        return
                    fw.op("pe", lambda e: e.matmul(pm[0:r, 8:16], lhsT=tri_ap, rhs=lf_ap, start=True, stop=True),
                          reads=[blf, bcon], writes=[bpm])
                    if update_off:
                        fw.op("pe", lambda e: e.matmul(pm[0:r, 16:24], lhsT=ONESF[0:r, 0:r], rhs=lf_ap, start=True, stop=True),
                              reads=[blf, bcon], writes=[bpm])
                    fw.op("dve", lambda e: e.tensor_tensor(out=ctmp[0:r, 0, :], in0=pm[0:r, 8:16], in1=off_ap, op=ALU.add),
                          reads=[bpm, boff], writes=[bct])
                    if update_off:
                        fw.op("dve", lambda e: e.tensor_tensor(out=off_ap, in0=pm[0:r, 16:24], in1=off_ap, op=ALU.add),
                              reads=[bpm, boff], writes=[boff])
                    split3(ctmp[0:r, 0, :], r, dst3, dst3n, bdst)

                fw.op("pool", lambda e: e.memset(OFF[:], 0.0), writes=[bOFF])
                fw.op("pool", lambda e: e.memset(OFFC[:], 0.0), writes=[bOFFC])
                for b in range(2):
                    for t in range(16):
                        fw.dma("sp", lambda e, b=b, t=t: e.dma_start(out=CL[:, b, t, :], in_=dr["cfl"][b, t * 128:(t + 1) * 128, :]), writes=[bCL])
                for hg in range(2):
                    with ExitStack() as sh:
                        QT = sb(sh, "QT", (72, 4, NTOK), BF16)
                        bQT = [[Buf() for _ in range(6)] for _ in range(4)]
                        KT = sb(sh, "KT", (72, 4, NTOK), BF16)
                        bKT = [Buf() for _ in range(5)]
                        VA = sb(sh, "VA", (128, NT, 256), BF16)
                        bVA = [Buf() for _ in range(5)]
                        VAn = sb(sh, "VAn", (32, 2, 256), BF16)
                        bVAn = Buf()
                        QA = [sb(sh, "QA%d" % i, (128, 4, 72), BF16) for i in range(2)]
                        KA = [sb(sh, "KA%d" % i, (128, 4, 72), BF16) for i in range(2)]
                        bQA = [Buf(), Buf()]
                        bKA = [Buf(), Buf()]
                        for j in range(2):
                            fw.op("pool", lambda e, j=j: e.memset(QA[j][:], 1.0), writes=[bQA[j]])
                            fw.op("pool", lambda e, j=j: e.memset(KA[j][:], 1.0), writes=[bKA[j]])
                        sw = ExitStack()
                        Wq4, bWq4 = load_w_bf(sw, wv[:, :, hg * 256:hg * 256 + 256], (8, 256), "Wq4", stage_cols=1024)
                        Wk4, bWk4 = load_w_bf(sw, wv[:, :, 512 + hg * 256:512 + hg * 256 + 256], (8, 256), "Wk4", stage_cols=1024)
                        Wv4, bWv4 = load_w_bf(sw, wv[:, :, 1024 + hg * 256:1024 + hg * 256 + 256], (8, 256), "Wv4", stage_cols=1024)
                        if hg == 0:
                            Wff, bWff = load_w_bf(sw, wv[:, :, 1536:1544], (8, 8), "Wff", stage_cols=1024)
                        knf = [sb(sw, "fknf%d" % i, (128, 4, 64)) for i in range(2)]
                        bknf = [Buf(), Buf()]
                        vf = [sb(sw, "fvf%d" % i, (128, 256)) for i in range(2)]
                        bvf = [Buf(), Buf()]
                        for i in range(cfg.get("fox_ntiles", NT)):
                            r = tile_rows(i)
                            j = i % 2
                            rows = slice(i * 128, i * 128 + r)
                            g5 = i // 4
                            if hg == 0:
                                for k in range(8):
                                    fw.op("pe", lambda e, k=k, i=i, r=r: e.matmul(
                                        pm[0:r, 0:8], lhsT=xT[:, k, i * 128:i * 128 + r], rhs=Wff[:, k, :],
                                        start=(k == 0), stop=(k == 7)), reads=[b_xT[g5], bWff], writes=[bpm])
                                lf = LFA[0:r, i, :]
                                fw.op("dve", lambda e, lf=lf, r=r: e.tensor_tensor(out=lf, in0=pm[0:r, 0:8], in1=Bf[0:r, :], op=ALU.add),
                                      reads=[bpm, bBf], writes=[bLFA])
                                fw.op("act", lambda e, lf=lf: e.activation(out=lf, in_=lf, func=AF.Exp, scale=-1.0), reads=[bLFA], writes=[bLFA])
                                fw.op("act", lambda e, lf=lf: e.activation(out=lf, in_=lf, func=AF.Ln, bias=1.0), reads=[bLFA], writes=[bLFA])
                                fw.op("dve", lambda e, lf=lf: e.tensor_scalar(out=lf, in0=lf, scalar1=-1.0, scalar2=None, op0=ALU.mult),
                                      reads=[bLFA], writes=[bLFA])
                                fw.dma("sp", lambda e, lf=lf, rows=rows: e.dma_start(out=dr["fox_logf"][rows, :], in_=lf), reads=[bLFA])
                                if i < 16:
                                    cum_tile(lf, bLFA, 128, TRI[:], OFF[:], bOFF, CUM3[:, i], CUM3N[:, i], bCUM, True)
                                else:
                                    for b in range(2):
                                        for t in range(16):
                                            cum_tile(CL[:, b, t, :], bCL, 128, TRI[:], OFFC[:, b, :], bOFFC, CSCR[:], CC3N[:, b, t], bCC, True)
                                        fw.op("dve", lambda e, b=b: e.tensor_copy(out=OFF16[32 * b:32 * b + 32, :], in_=OFFC[32 * b:32 * b + 32, b, :]),
                                              reads=[bOFFC], writes=[bOFF16])
                                    cum_tile(lf, bLFA, 64, TRIB[:], OFF16[:], bOFF16, CUM3[:, 16], CUM3N[:, 16], bCUM, False)
                            if cfg.get("fox_stage", 9) < 2:
                                continue
                            for k in range(8):
                                fw.op("pe", lambda e, k=k, i=i, r=r: e.matmul(
                                    pp[0:r, 0:256], lhsT=xT[:, k, i * 128:i * 128 + r], rhs=Wq4[:, k, :],
                                    start=(k == 0), stop=(k == 7)), reads=[b_xT[g5], bWq4], writes=[bpp])
                            for k in range(8):
                                fw.op("pe", lambda e, k=k, i=i, r=r: e.matmul(
                                    pp2[0:r, 0:256], lhsT=xT[:, k, i * 128:i * 128 + r], rhs=Wk4[:, k, :],
                                    start=(k == 0), stop=(k == 7)), reads=[b_xT[g5], bWk4], writes=[bpp2])
                            thq = rms_heads_staged(sw, pp[0:r, 0:256].rearrange("p (h d) -> p h d", h=4), bpp, r, 4, Gq, bGq, 0.125,
                                                   QA[j][0:r, :, 0:64], bQA[j], None, None, 0)
                            thk = rms_heads_staged(sw, pp2[0:r, 0:256].rearrange("p (h d) -> p h d", h=4), bpp2, r, 4, Gk, bGk, 1.0,
                                                   KA[j][0:r, :, 0:64], bKA[j], knf[j][0:r], bknf[j], 1)
                            for ti_ in range(max(len(thq), len(thk))):
                                if ti_ < len(thq):
                                    thq[ti_]()
                                if ti_ < len(thk):
                                    thk[ti_]()
                            for c3 in range(3):
                                fw.op("dve", lambda e, j=j, r=r, i=i, c3=c3: e.tensor_copy(
                                    out=QA[j][0:r, :, 64 + c3], in_=CUM3[0:r, i, c3, 4 * hg:4 * hg + 4]), reads=[bCUM], writes=[bQA[j]])
                            for c3 in range(3):
                                fw.op("dve", lambda e, j=j, r=r, i=i, c3=c3: e.tensor_copy(
                                    out=KA[j][0:r, :, 67 + c3], in_=CUM3N[0:r, i, c3, 4 * hg:4 * hg + 4]), reads=[bCUM], writes=[bKA[j]])
                            for h in range(4):
                                fw.op("pe", lambda e, h=h, r=r, j=j: e.transpose(out=ptk[0:72, h, 0:r], in_=QA[j][0:r, h, :], identity=ident[0:r, 0:r]),
                                      reads=[bQA[j], b_ident], writes=[bptk])
                            qbufs = [bQT[h][g5] for h in range(4)] if i < 16 else [bQT[h][4] for h in range(4)] + [bQT[h][5] for h in range(4)]
                            fw.op("act", lambda e, i=i, r=r: e.copy(out=QT[:, :, i * 128:i * 128 + r], in_=ptk[:, 0:4, 0:r]),
                                  reads=[bptk], writes=qbufs)
                            fw.dma("sp", lambda e, j=j, r=r, rows=rows: e.dma_start(
                                out=dr["fox_k"][rows, hg * 256:hg * 256 + 256], in_=knf[j][0:r].rearrange("p h d -> p (h d)")), reads=[bknf[j]])
                            for h in range(4):
                                fw.op("pe", lambda e, h=h, r=r, j=j: e.transpose(out=ptk[0:72, h + 4, 0:r], in_=KA[j][0:r, h, :], identity=ident[0:r, 0:r]),
                                      reads=[bKA[j], b_ident], writes=[bptk])
                            fw.op("act", lambda e, i=i, r=r: e.copy(out=KT[:, :, i * 128:i * 128 + r], in_=ptk[:, 4:8, 0:r]),
                                  reads=[bptk], writes=[bKT[g5]])
                            if cfg.get("fox_stage", 9) < 4:
                                continue
                            for k in range(8):
                                fw.op("pe", lambda e, k=k, i=i, r=r: e.matmul(
                                    pp[0:r, 0:256], lhsT=xT[:, k, i * 128:i * 128 + r], rhs=Wv4[:, k, :],
                                    start=(k == 0), stop=(k == 7)), reads=[b_xT[g5], bWv4], writes=[bpp])
                            fw.op("act", lambda e, j=j, r=r: e.copy(out=vf[j][0:r, :], in_=pp[0:r, 0:256]), reads=[bpp], writes=[bvf[j]])
                            fw.op("dve", lambda e, i=i, r=r, j=j: e.tensor_copy(out=VA[0:r, i, :], in_=vf[j][0:r, :]), reads=[bvf[j]], writes=[bVA[g5]])
                            fw.dma("sp", lambda e, j=j, r=r, rows=rows: e.dma_start(out=dr["fox_v"][rows, hg * 256:hg * 256 + 256], in_=vf[j][0:r, :]),
                                   reads=[bvf[j]])
                            if i == 16:
                                for b in range(2):
                                    fw.op("dve", lambda e, b=b: e.tensor_copy(out=VAn[:, b, :], in_=VA[32 * b:32 * b + 32, 16, :]),
                                          reads=[bVA[4]], writes=[bVAn])
                        fw.flush()
                        sw.close()
                        if not (cfg.get("fox_pattn", True) or cfg.get("fox_sattn", True)):
                            continue
                        Wo4, bWo4 = load_w_bf(sh, dr["ev_w_out"][e_][hg * 256:hg * 256 + 256, :].rearrange("(h d) n -> d h n", d=64),
                                              (4, D), "Wo4", stage_cols=1024)
                        OTg = [sb(sh, "OTg%d" % i, (64, 4, 512), BF16) for i in range(2)]
                        bOTg = [Buf(), Buf()]

                        def out_proj(OT_t, bOT_t, tiles):
                            po, bpo = C.ac_S, C.ac_bS
                            for tl_i, i in enumerate(tiles):
                                r = tile_rows(i)
                                for half in range(2):
                                    q = C.opc % 2
                                    C.opc += 1
                                    for h in range(4):
                                        fw.op("pe", lambda e, q=q, tl_i=tl_i, r=r, h=h, half=half: e.matmul(
                                            po[q][0:r, :], lhsT=OT_t[:, h, tl_i * 128:tl_i * 128 + r], rhs=Wo4[0:64, h, half * 512:(half + 1) * 512],
                                            start=(h == 0), stop=(h == 3)), reads=[bOT_t, bWo4], writes=[bpo[q]])
                                    fw.op("dve", lambda e, q=q, r=r, i=i, half=half: e.tensor_tensor(
                                        out=X[0:r, i, half * 512:(half + 1) * 512], in0=po[q][0:r, :], in1=X[0:r, i, half * 512:(half + 1) * 512],
                                        op=ALU.add), reads=[bpo[q], bX[i]], writes=[bX[i]])
                        C.opc = 0
                        for g in (range(4) if cfg.get("fox_pattn", True) else []):
                            s_ = g % 2
                            for h in range(4):
                                kbs = []
                                for kt in range(4 * g + 4):
                                    c0 = max(0, kt * 128 - g * 512)
                                    kbs.append(dict(kT=KT[0:70, h, kt * 128:(kt + 1) * 128], v=VA[:, kt, h * 64:(h + 1) * 64], nk=128, col0=c0,
                                                    reads=[bKT[kt // 4], bVA[kt // 4]],
                                                    mask=(MASKT[:] if kt >= 4 * g else None), mask_reads=[bcon]))
                                attn_core(sh, QT[0:70, h, g * 512:(g + 1) * 512], bQT[h][g], 512, kbs, OTg[s_][:, h, :], bOTg[s_], nbuf=1)
                            out_proj(OTg[s_], bOTg[s_], [4 * g, 4 * g + 1, 4 * g + 2, 4 * g + 3])
                        ckf = sb(sh, "ckf", (128, 4, 256))
                        bckf = Buf()
                        cvf = sb(sh, "cvf", (128, 4, 256))
                        bcvf = Buf()
                        for b in (range(2) if cfg.get("fox_sattn", True) else []):
                            for t4 in range(4):
                                fw.dma("sp", lambda e, b=b, t4=t4: e.dma_start(
                                    out=ckf[:], in_=dr["cfk"][b, t4 * 512:(t4 + 1) * 512, hg * 256:hg * 256 + 256].rearrange("(t p) d -> p t d", p=128)),
                                    writes=[bckf])
                                fw.dma("sp", lambda e, b=b, t4=t4: e.dma_start(
                                    out=cvf[:], in_=dr["cfv"][b, t4 * 512:(t4 + 1) * 512, hg * 256:hg * 256 + 256].rearrange("(t p) d -> p t d", p=128)),
                                    writes=[bcvf])
                                fw.op("pool", lambda e, t4=t4: e.tensor_copy(out=VA[:, 4 * t4:4 * t4 + 4, :], in_=cvf[:]), reads=[bcvf], writes=[bVA[t4]])
                                for tt in range(4):
                                    t = 4 * t4 + tt
                                    j = t % 2
                                    fw.op("pool", lambda e, j=j, tt=tt: e.tensor_copy(
                                        out=KA[j][:, :, 0:64], in_=ckf[:, tt, :].rearrange("p (h d) -> p h d", h=4)), reads=[bckf], writes=[bKA[j]])
                                    for c3 in range(3):
                                        fw.op("dve", lambda e, j=j, b=b, t=t, c3=c3: e.tensor_copy(
                                            out=KA[j][:, :, 67 + c3], in_=CC3N[:, b, t, c3, 4 * hg:4 * hg + 4]), reads=[bCC], writes=[bKA[j]])
                                    for h in range(4):
                                        fw.op("pe", lambda e, h=h, j=j: e.transpose(out=ptk[0:72, h, :], in_=KA[j][:, h, :], identity=ident[:]),
                                              reads=[bKA[j], b_ident], writes=[bptk])
                                    fw.op("act", lambda e, t=t: e.copy(out=KT[:, :, t * 128:(t + 1) * 128], in_=ptk[:, 0:4, :]),
                                          reads=[bptk], writes=[bKT[t4]])
                            for h in range(4):
                                q0 = 2048 + 32 * b
                                kbs = [dict(kT=KT[0:70, h, kt * 128:(kt + 1) * 128], v=VA[:, kt, h * 64:(h + 1) * 64], nk=128, col0=0,
                                            reads=[bKT[kt // 4], bVA[kt // 4]]) for kt in range(16)]
                                kbs.append(dict(kT=KT[0:70, h, q0:q0 + 32], v=VAn[:, b, h * 64:(h + 1) * 64], nk=32, col0=0,
                                                reads=[bKT[4], bVAn], mask=MASKT[0:32, 0:32], mask_reads=[bcon], mask_w=32))
                                attn_core(sh, QT[0:70, h, q0:q0 + 32], bQT[h][4 + b], 32, kbs, OTg[0][:, h, 32 * b:32 * b + 32], bOTg[0], nbuf=1)
                        if cfg.get("fox_sattn", True):
                            out_proj(OTg[0], bOTg[0], [16])
                        fw.flush()


        def rwkv_part(l, xT, b_xT):
            e_ = l // 2
            wv = dr["ev_w_in"][e_].rearrange("(k p) n -> p k n", p=128)
            NG = 128
            with ExitStack() as st:
                pA = ps(st, "rpA", (128, 512))
                pB = ps(st, "rpB", (128, 512))
                pT1 = ps(st, "rpT1", (128, 1024), BF16)
                pT2 = ps(st, "rpT2", (128, 1024), BF16)
                pGK = ps(st, "rpGK", (64, 1024))
                pGB = ps(st, "rpGB", (64, 1024))
                bpA, bpB, bpT1, bpT2, bpGK0, bpGK1, bpGB0, bpGB1 = [Buf() for _ in range(8)]
                PC = sb(st, "PC", (128, 70))
                bPC = Buf()
                OM = sb(st, "OM", (128, 18))
                WAb = sb(st, "WAb", (128, 512), BF16)
                G2b = sb(st, "G2b", (128, 512), BF16)
                BD = sb(st, "BD", (128, 128), BF16)
                MSK = sb(st, "MSK", (64, 3, 64))
                RST = sb(st, "RST", (128, 128))
                bW = Buf()
                with ExitStack() as stt:
                    PR = sb(stt, "PR", (70, 128))
                    bPR = Buf()
                    rows = [("rwkv_mu", 14), ("rwkv_w0", 4), ("rwkv_a0", 4), ("rwkv_k_k", 4), ("rwkv_k_a", 4), ("rwkv_r_k", 4),
                            ("rwkv_ln_g", 4), ("rwkv_ln_b", 4)]
                    r0 = 0
                    for nm, nr in rows:
                        src = dr[nm][e_].rearrange("(c p) -> c p", p=128)
                        fw.dma("sp", lambda e, src=src, r0=r0, nr=nr: e.dma_start(out=PR[r0:r0 + nr, :], in_=src), writes=[bPR])
                        r0 += nr
                    fw.dma("sp", lambda e: e.dma_start(out=PR[42:70, :], in_=dr["srs"].rearrange("b (c p) -> (b c) p", p=128)), writes=[bPR])
                    fw.op("pe", lambda e: e.matmul(pA[:, 0:70], lhsT=PR[0:70, :], rhs=ident_f[0:70, 0:70], start=True, stop=True),
                          reads=[bPR, b_ident], writes=[bpA])
                    fw.op("dve", lambda e: e.tensor_copy(out=PC[:], in_=pA[:, 0:70]), reads=[bpA], writes=[bPC])
                    fw.op("dve", lambda e: e.tensor_scalar(out=OM[:, 0:14], in0=PC[:, 0:14], scalar1=-1.0, scalar2=1.0, op0=ALU.mult, op1=ALU.add),
                          reads=[bPC], writes=[bPC])
                    fw.op("dve", lambda e: e.tensor_scalar(out=OM[:, 14:18], in0=PC[:, 26:30], scalar1=-1.0, scalar2=1.0, op0=ALU.mult, op1=ALU.add),
                          reads=[bPC], writes=[bPC])
                    WAf = sb(stt, "WAf", (128, 512))
                    G2f = sb(stt, "G2f", (128, 512))
                    BDf = sb(stt, "BDf", (128, 128))
                    fw.dma("sp", lambda e: e.dma_start(out=WAf[0:64, :], in_=dr["rwkv_w2"][e_]), writes=[bW])
                    fw.dma("sp", lambda e: e.dma_start(out=WAf[64:128, :], in_=dr["rwkv_a2"][e_]), writes=[bW])
                    fw.dma("sp", lambda e: e.dma_start(out=G2f[:], in_=dr["rwkv_g2"][e_]), writes=[bW])
                    fw.dma("sp", lambda e: e.dma_start(out=BDf[:], in_=dr["c_bd"]), writes=[bW])
                    fw.dma("sp", lambda e: e.dma_start(out=MSK[:], in_=dr["c_msk"]), writes=[bW])
                    fw.dma("sp", lambda e: e.dma_start(out=RST[:], in_=dr["c_rst"][:, 0:128]), writes=[bW])
                    fw.op("pool", lambda e: e.tensor_copy(out=WAb[:], in_=WAf[:]), reads=[bW], writes=[bW])
                    fw.op("pool", lambda e: e.tensor_copy(out=G2b[:], in_=G2f[:]), reads=[bW], writes=[bW])
                    fw.op("pool", lambda e: e.tensor_copy(out=BD[:], in_=BDf[:]), reads=[bW], writes=[bW])
                    fw.flush()
                MU = lambda cc: PC[:, cc:cc + 1]
                OMMU = lambda cc: OM[:, cc:cc + 1]
                W0 = lambda hp: PC[:, 14 + hp:15 + hp]
                A0 = lambda hp: PC[:, 18 + hp:19 + hp]
                KK_ = lambda hp: PC[:, 22 + hp:23 + hp]
                KA_ = lambda hp: PC[:, 26 + hp:27 + hp]
                RK_ = lambda hp: PC[:, 30 + hp:31 + hp]
                LNG = lambda hp: PC[:, 34 + hp:35 + hp]
                LNB = lambda hp: PC[:, 38 + hp:39 + hp]
                OMKA = lambda hp: OM[:, 14 + hp:15 + hp]
                WoR, bWoR = load_w_bf(st, dr["ev_w_out"][e_][512:1024, :].rearrange("(h p) n -> p h n", p=128), (4, D), "WoR", stage_cols=1024)
                Wc = [sb(st, "Wc%d" % i, (128, 8, 128), BF16) for i in range(2)]
                bWc = [Buf(), Buf()]
                wcs = [sb(st, "wcs%d" % i, (128, 8, 128)) for i in range(2)]
                bwcs = [Buf(), Buf()]
                wcc = [0]
                ST = sb(st, "ST", (128, 4, 64))
                STb = sb(st, "STb", (128, 4, 64), BF16)
                bST = Buf()
                CAR = sb(st, "CAR", (128, 14))
                bCAR = Buf()
                HX = sb(st, "HX", (128, 5, NG))
                bHX = [Buf() for _ in range(5)]
                TMPL = sb(st, "TMPL", (128, NG))
                bTMPL = Buf()
                F = {}
                for nm in ("LW", "A", "KKR", "KKN", "KM", "T1", "CUM", "EP", "EN", "EPV", "EH", "BETA"):
                    F[nm] = sb(st, "f_" + nm, (128, NG))
                bF = {nm: Buf() for nm in F}
                SQb = sb(st, "SQb", (128, NG), BF16)
                bSQb = Buf()
                TXW = sb(st, "TXW", (128, NG), BF16)
                SXG = sb(st, "SXG", (128, NG), BF16)
                bTXW = Buf()
                bSXG = Buf()
                CUMC = sb(st, "CUMC", (128, 4))
                SL = []
                for si_ in range(2):
                    d_ = dict(
                        PCS=sb(st, "PCS%d" % si_, (128, 4, 2)), bPCS=Buf(),
                        KR=sb(st, "KR%d" % si_, (128, 4, 2, 2, 64), BF16), KB=sb(st, "KB%d" % si_, (128, 4, 2, 2, 64), BF16),
                        bKR=[Buf() for _ in range(4)], bKB=[Buf() for _ in range(4)],
                        KRo=sb(st, "KRo%d" % si_, (64, 4, 2, 2, 64), BF16), KBo=sb(st, "KBo%d" % si_, (64, 4, 2, 2, 64), BF16),
                        KH=sb(st, "KH%d" % si_, (128, 4, NG), BF16), BH=sb(st, "BH%d" % si_, (128, 4, NG), BF16), VB=sb(st, "VB%d" % si_, (128, 4, NG), BF16),
                        bKH=[Buf() for _ in range(4)],
                        Gt=sb(st, "Gt%d" % si_, (128, 4, NG), BF16), BON=sb(st, "BON%d" % si_, (128, 4, NG), BF16),
                        bGt=[Buf() for _ in range(4)], bBON=[Buf() for _ in range(4)])
                    SL.append(d_)
                STbo = sb(st, "STbo", (64, 4, 64), BF16)
                FT = sb(st, "FT", (128, NG))
                bFT = Buf()
                ONT = sb(st, "ONT", (128, 4, NG), BF16)
                bONT = Buf()
                MO = sb(st, "MO", (128, 4, NG), BF16)
                bMO = Buf()
                TM = sb(st, "TM", (64, 3, 512), BF16)
                bTM = Buf()
                cb = {}
                for nm in ("AKK", "GKR", "NGBR", "X1", "X2", "Y1", "Y2", "T1b", "T2b", "WTs", "UTs", "ONb"):
                    cb[nm] = sb(st, "c_" + nm, (64, 8, 64), BF16)
                bcb = {nm: Buf() for nm in cb}
                OS = sb(st, "OS", (64, 8, 64))
                OSQ = sb(st, "OSQ", (64, 8, 64))
                bOS = Buf()
                bOSQ = Buf()
                STAT = sb(st, "STAT", (64, 6, 8))
                bSTAT = Buf()
                sto = sb(st, "sto", (64, 4, 128))
                bsto = Buf()

                def v3(ap, C_):
                    return ap.rearrange("p (c t) -> p c t", t=C_)

                def project(cc, slot, tok0, n):
                    w = wcc[0] % 2
                    wcc[0] += 1
                    c0 = 1544 + cc * 128
                    fw.dma("sp", lambda e, w=w, c0=c0: e.dma_start(out=wcs[w][:], in_=wv[:, :, c0:c0 + 128]), writes=[bwcs[w]])
                    fw.op("pool", lambda e, w=w: e.tensor_copy(out=Wc[w][:], in_=wcs[w][:]), reads=[bwcs[w]], writes=[bWc[w]])
                    pj, bpj = (pA, bpA) if w == 0 else (pB, bpB)
                    for k in range(8):
                        fw.op("pe", lambda e, k=k, w=w, pj=pj: e.matmul(pj[:, 0:n], lhsT=Wc[w][:, k, :], rhs=xT[:, k, tok0:tok0 + n],
                                                                       start=(k == 0), stop=(k == 7)),
                              reads=[bWc[w], b_xT[min(tok0 // 512, 4)]], writes=[bpj])
                    fw.op("dve", lambda e: e.tensor_scalar(out=TMPL[:, 0:1], in0=CAR[:, cc:cc + 1], scalar1=MU(cc), scalar2=None, op0=ALU.mult),
                          reads=[bCAR, bPC], writes=[bTMPL])
                    if n > 1:
                        fw.op("dve", lambda e, pj=pj: e.tensor_scalar(out=TMPL[:, 1:n], in0=pj[:, 0:n - 1], scalar1=MU(cc), scalar2=None, op0=ALU.mult),
                              reads=[bpj, bPC], writes=[bTMPL])
                    fw.op("dve", lambda e, pj=pj: e.scalar_tensor_tensor(out=HX[:, slot, 0:n], in0=pj[:, 0:n], scalar=OMMU(cc), in1=TMPL[:, 0:n],
                                                                        op0=ALU.mult, op1=ALU.add), reads=[bpj, bPC, bTMPL], writes=[bHX[slot]])
                    fw.op("dve", lambda e, pj=pj: e.tensor_copy(out=CAR[:, cc:cc + 1], in_=pj[:, n - 1:n]), reads=[bpj], writes=[bCAR])

                def prep(slot, tok0, n, C_):
                    nch = n // C_
                    d_ = SL[slot]
                    PCS, bPCS, KR, KB, bKR, bKB, KRo, KBo = d_["PCS"], d_["bPCS"], d_["KR"], d_["KB"], d_["bKR"], d_["bKB"], d_["KRo"], d_["KBo"]
                    KH, BH, VB, bKH, Gt, BON, bGt, bBON = d_["KH"], d_["BH"], d_["VB"], d_["bKH"], d_["Gt"], d_["BON"], d_["bGt"], d_["bBON"]
                    project(12, 0, tok0, n)
                    project(13, 1, tok0, n)
                    fw.op("act", lambda e: e.activation(out=TXW[0:64, 0:n], in_=HX[0:64, 0, 0:n], func=AF.Tanh), reads=[bHX[0]], writes=[bTXW])
                    fw.op("dve", lambda e: e.tensor_copy(out=TXW[64:128, 0:n], in_=HX[64:128, 0, 0:n]), reads=[bHX[0]], writes=[bTXW])
                    fw.op("act", lambda e: e.activation(out=SXG[:, 0:n], in_=HX[:, 1, 0:n], func=AF.Sigmoid), reads=[bHX[1]], writes=[bSXG])
                    def _prep(hp):
                        project(hp, 2, tok0, n)
                        project(4 + hp, 3, tok0, n)
                        project(8 + hp, 4, tok0, n)
                        r_, k_, v_ = HX[:, 2, 0:n], HX[:, 3, 0:n], HX[:, 4, 0:n]
                        rk_reads = [bHX[2], bHX[3], bHX[4]]
                        cs = slice(hp * 128, (hp + 1) * 128)
                        f = lambda nm: F[nm][:, 0:n]
                        fw.op("pe", lambda e: e.matmul(pA[:, 0:n], lhsT=WAb[0:64, cs], rhs=TXW[0:64, 0:n], start=True, stop=True),
                              reads=[bW, bTXW], writes=[bpA])
                        fw.op("act", lambda e: e.activation(out=f("LW"), in_=pA[:, 0:n], func=AF.Sigmoid, bias=W0(hp)), reads=[bpA, bPC], writes=[bF["LW"]])
                        fw.op("dve", lambda e: e.tensor_scalar(out=f("LW"), in0=f("LW"), scalar1=-0.6065306597126334, scalar2=None, op0=ALU.mult),
                              reads=[bF["LW"]], writes=[bF["LW"]])
                        fw.op("pe", lambda e: e.matmul(pB[:, 0:n], lhsT=WAb[64:128, cs], rhs=TXW[64:128, 0:n], start=True, stop=True),
                              reads=[bW, bTXW], writes=[bpB])
                        fw.op("act", lambda e: e.activation(out=f("A"), in_=pB[:, 0:n], func=AF.Sigmoid, bias=A0(hp)), reads=[bpB, bPC], writes=[bF["A"]])
                        fw.op("pe", lambda e: e.matmul(pA[:, 0:n], lhsT=G2b[:, cs], rhs=SXG[:, 0:n], start=True, stop=True),
                              reads=[bW, bSXG], writes=[bpA])
                        fw.op("act", lambda e: e.copy(out=Gt[:, hp, 0:n], in_=pA[:, 0:n]), reads=[bpA], writes=[bGt[hp]])
                        fw.op("dve", lambda e: e.tensor_scalar(out=f("KKR"), in0=k_, scalar1=KK_(hp), scalar2=None, op0=ALU.mult),
                              reads=[bHX[3], bPC], writes=[bF["KKR"]])
                        fw.op("dve", lambda e: e.tensor_tensor(out=SQb[:, 0:n], in0=f("KKR"), in1=f("KKR"), op=ALU.mult), reads=[bF["KKR"]], writes=[bSQb])
                        fw.op("pe", lambda e: e.matmul(pB[:, 0:n], lhsT=BD[:], rhs=SQb[:, 0:n], start=True, stop=True), reads=[bW, bSQb], writes=[bpB])
                        fw.op("act", lambda e: e.activation(out=f("T1"), in_=pB[:, 0:n], func=AF.Sqrt), reads=[bpB], writes=[bF["T1"]])
                        fw.op("dve", lambda e: e.tensor_scalar(out=f("T1"), in0=f("T1"), scalar1=1e-12, scalar2=None, op0=ALU.max), reads=[bF["T1"]], writes=[bF["T1"]])
                        fw.op("dve", lambda e: e.reciprocal(out=f("T1"), in_=f("T1")), reads=[bF["T1"]], writes=[bF["T1"]])
                        fw.op("dve", lambda e: e.tensor_tensor(out=f("KKN"), in0=f("KKR"), in1=f("T1"), op=ALU.mult), reads=[bF["KKR"], bF["T1"]], writes=[bF["KKN"]])
                        fw.op("dve", lambda e: e.tensor_scalar(out=f("T1"), in0=f("A"), scalar1=KA_(hp), scalar2=OMKA(hp), op0=ALU.mult, op1=ALU.add),
                              reads=[bF["A"], bPC], writes=[bF["T1"]])
                        fw.op("dve", lambda e: e.tensor_tensor(out=f("KM"), in0=k_, in1=f("T1"), op=ALU.mult), reads=[bHX[3], bF["T1"]], writes=[bF["KM"]])
                        fw.op("dve", lambda e: e.tensor_tensor(out=f("T1"), in0=r_, in1=f("KM"), op=ALU.mult), reads=[bHX[2], bF["KM"]], writes=[bF["T1"]])
                        fw.op("dve", lambda e: e.tensor_scalar(out=SQb[:, 0:n], in0=f("T1"), scalar1=RK_(hp), scalar2=None, op0=ALU.mult),
                              reads=[bF["T1"], bPC], writes=[bSQb])
                        fw.op("pe", lambda e: e.matmul(pB[:, 0:n], lhsT=BD[:], rhs=SQb[:, 0:n], start=True, stop=True), reads=[bW, bSQb], writes=[bpB])
                        fw.op("dve", lambda e: e.tensor_tensor(out=BON[:, hp, 0:n], in0=pB[:, 0:n], in1=v_, op=ALU.mult), reads=[bpB, bHX[4]], writes=[bBON[hp]])
                        fw.op("dve", lambda e: e.tensor_tensor_scan(out=f("CUM"), data0=RST[:, 0:n], data1=f("LW"), initial=0.0, op0=ALU.mult, op1=ALU.add),
                              reads=[bW, bF["LW"]], writes=[bF["CUM"]])
                        fw.op("act", lambda e: e.activation(out=f("EP"), in_=f("CUM"), func=AF.Exp), reads=[bF["CUM"]], writes=[bF["EP"]])
                        fw.op("act", lambda e: e.activation(out=f("EN"), in_=f("CUM"), func=AF.Exp, scale=-1.0), reads=[bF["CUM"]], writes=[bF["EN"]])
                        fw.op("dve", lambda e: e.tensor_tensor(out=f("EPV"), in0=f("CUM"), in1=f("LW"), op=ALU.subtract), reads=[bF["CUM"], bF["LW"]], writes=[bF["EPV"]])
                        fw.op("act", lambda e: e.activation(out=f("EPV"), in_=f("EPV"), func=AF.Exp), reads=[bF["EPV"]], writes=[bF["EPV"]])
                        fw.op("dve", lambda e: e.tensor_copy(out=CUMC[:, 0:nch], in_=v3(f("CUM"), C_)[:, :, C_ - 1]), reads=[bF["CUM"]], writes=[bPCS])
                        fw.op("act", lambda e: e.activation(out=PCS[:, hp, 0:nch], in_=CUMC[:, 0:nch], func=AF.Exp), reads=[bPCS], writes=[bPCS])
                        fw.op("dve", lambda e: e.tensor_tensor(out=v3(f("EH"), C_), in0=CUMC[:, 0:nch].unsqueeze(2).broadcast_to([128, nch, C_]),
                                                               in1=v3(f("CUM"), C_), op=ALU.subtract), reads=[bPCS, bF["CUM"]], writes=[bF["EH"]])
                        fw.op("act", lambda e: e.activation(out=f("EH"), in_=f("EH"), func=AF.Exp), reads=[bF["EH"]], writes=[bF["EH"]])
                        fw.op("dve", lambda e: e.tensor_tensor(out=KR[:, hp, 0:nch, 0, 0:C_], in0=v3(f("KKN"), C_), in1=v3(f("EPV"), C_), op=ALU.mult),
                              reads=[bF["KKN"], bF["EPV"]], writes=[bKR[hp]])
                        fw.op("dve", lambda e: e.tensor_tensor(out=KR[:, hp, 0:nch, 1, 0:C_], in0=v3(r_, C_), in1=v3(f("EP"), C_), op=ALU.mult),
                              reads=[bHX[2], bF["EP"]], writes=[bKR[hp]])
                        fw.op("dve", lambda e: e.tensor_tensor(out=KB[:, hp, 0:nch, 0, 0:C_], in0=v3(f("KM"), C_), in1=v3(f("EN"), C_), op=ALU.mult),
                              reads=[bF["KM"], bF["EN"]], writes=[bKB[hp]])
                        fw.op("dve", lambda e: e.tensor_tensor(out=f("BETA"), in0=f("KKN"), in1=f("A"), op=ALU.mult), reads=[bF["KKN"], bF["A"]], writes=[bF["BETA"]])
                        fw.op("dve", lambda e: e.tensor_tensor(out=KB[:, hp, 0:nch, 1, 0:C_], in0=v3(f("BETA"), C_), in1=v3(f("EN"), C_), op=ALU.mult),
                              reads=[bF["BETA"], bF["EN"]], writes=[bKB[hp]])
                        fw.op("dve", lambda e: e.tensor_tensor(out=KH[:, hp, 0:n], in0=f("KM"), in1=f("EH"), op=ALU.mult), reads=[bF["KM"], bF["EH"]], writes=[bKH[hp]])
                        fw.op("dve", lambda e: e.scalar_tensor_tensor(out=BH[:, hp, 0:n], in0=f("BETA"), scalar=-1.0, in1=f("EH"), op0=ALU.mult, op1=ALU.mult),
                              reads=[bF["BETA"], bF["EH"]], writes=[bKH[hp]])
                        fw.op("act", lambda e: e.copy(out=VB[:, hp, 0:n], in_=v_), reads=[bHX[4]], writes=[bKH[hp]])
                        fw.op("pool", lambda e: e.tensor_copy(out=KRo[:, hp, 0:nch, :, 0:C_], in_=KR[64:128, hp, 0:nch, :, 0:C_]), reads=[bKR[hp]], writes=[bKR[hp]])
                        fw.op("pool", lambda e: e.tensor_copy(out=KBo[:, hp, 0:nch, :, 0:C_], in_=KB[64:128, hp, 0:nch, :, 0:C_]), reads=[bKB[hp]], writes=[bKB[hp]])
                    for hp_ in range(4):
                        _prep(hp_)

                def post(slot, tok0, n, C_, tiles):
                    nch = n // C_
                    L = {64: 5, 32: 4}[C_]
                    RS = cfg.get("rwkv_stop", 9)
                    d_ = SL[slot]
                    PCS, bPCS, KR, KB, bKR, bKB, KRo, KBo = d_["PCS"], d_["bPCS"], d_["KR"], d_["KB"], d_["bKR"], d_["bKB"], d_["KRo"], d_["KBo"]
                    KH, BH, VB, bKH, Gt, BON, bGt, bBON = d_["KH"], d_["BH"], d_["VB"], d_["bKH"], d_["Gt"], d_["BON"], d_["bGt"], d_["bBON"]
                    MS_ = lambda i_: MSK[0:C_, i_, 0:C_].unsqueeze(1).broadcast_to([C_, 8, C_])
                    IDB = ident[0:C_, 0:C_].unsqueeze(1).broadcast_to([C_, 8, C_])
                    gk = pGK[0:C_, :].rearrange("p (h t) -> p h t", h=8)
                    gb = pGB[0:C_, :].rearrange("p (h t) -> p h t", h=8)
                    hv = lambda ps_, o: ps_[0:C_, o:o + 512].rearrange("p (h t) -> p h t", h=8)
                    cbv = lambda nm: cb[nm][0:C_, :, 0:C_]
                    def _chunk(c):
                        cols = slice(c * C_, (c + 1) * C_)
                        KRh = lambda h, a_: (KR[0:64, h // 2, c, a_, 0:C_] if h % 2 == 0 else KRo[0:64, h // 2, c, a_, 0:C_])
                        KBh = lambda h, a_: (KB[0:64, h // 2, c, a_, 0:C_] if h % 2 == 0 else KBo[0:64, h // 2, c, a_, 0:C_])
                        KR2 = lambda h: ((KR if h % 2 == 0 else KRo)[0:64, h // 2, c, :, :].rearrange("p a t -> p (a t)"))
                        STh = lambda h: (STb[0:64, h // 2, :] if h % 2 == 0 else STbo[0:64, h // 2, :])
                        t1v = pT1[0:C_, :].rearrange("p (q f) -> p q f", q=2)
                        for qi, Q in enumerate((KH, BH)):
                            for hp in range(4):
                                fw.op("pe", lambda e, qi=qi, Q=Q, hp=hp: e.transpose(out=t1v[:, qi, hp * 128:(hp + 1) * 128], in_=Q[:, hp, cols], identity=ident[:]),
                                      reads=[bKH[hp], b_ident], writes=[bpT1])
                        for hp in range(4):
                            fw.op("pe", lambda e, hp=hp: e.transpose(out=pT2[0:C_, hp * 128:(hp + 1) * 128], in_=VB[:, hp, cols], identity=ident[:]),
                                  reads=[bKH[hp], b_ident], writes=[bpT2])
                        fw.op("act", lambda e: e.copy(out=TM[0:C_, 0:2, :], in_=t1v), reads=[bpT1], writes=[bTM])
                        fw.op("act", lambda e: e.copy(out=TM[0:C_, 2, :], in_=pT2[0:C_, 0:512]), reads=[bpT2], writes=[bTM])
                        if RS <= 2.5:
                            return
                        for h in range(8):
                            hp = h // 2
                            bgk = bpGK0 if h < 4 else bpGK1
                            bgb = bpGB0 if h < 4 else bpGB1
                            if C_ == 64:
                                fw.op("pe", lambda e, h=h: e.matmul(gk[:, h, :], lhsT=KBh(h, 0), rhs=KR2(h), start=True, stop=True),
                                      reads=[bKB[hp], bKR[hp]], writes=[bgk])
                                fw.op("pe", lambda e, h=h: e.matmul(gb[:, h, :], lhsT=KBh(h, 1), rhs=KR2(h), start=True, stop=True),
                                      reads=[bKB[hp], bKR[hp]], writes=[bgb])
                            else:
                                for a_ in range(2):
                                    fw.op("pe", lambda e, h=h, a_=a_: e.matmul(gk[:, h, a_ * C_:(a_ + 1) * C_], lhsT=KBh(h, 0), rhs=KRh(h, a_), start=True, stop=True),
                                          reads=[bKB[hp], bKR[hp]], writes=[bgk])
                                    fw.op("pe", lambda e, h=h, a_=a_: e.matmul(gb[:, h, a_ * C_:(a_ + 1) * C_], lhsT=KBh(h, 1), rhs=KRh(h, a_), start=True, stop=True),
                                          reads=[bKB[hp], bKR[hp]], writes=[bgb])
                            fw.op("pe", lambda e, h=h: e.matmul(hv(pA, 0)[:, h, 0:C_], lhsT=KRh(h, 0), rhs=KBh(h, 1), start=True, stop=True),
                                  reads=[bKB[hp], bKR[hp]], writes=[bpA])
                        MS4 = lambda i_: MSK[0:C_, i_, 0:C_].unsqueeze(1).broadcast_to([C_, 4, C_])
                        for hb, (bk, bb) in enumerate(((bpGK0, bpGB0), (bpGK1, bpGB1))):
                            hs_ = slice(4 * hb, 4 * hb + 4)
                            fw.op("dve", lambda e, hs_=hs_: e.tensor_tensor(out=cb["AKK"][0:C_, hs_, 0:C_], in0=gk[:, hs_, 0:C_], in1=MS4(0), op=ALU.mult),
                                  reads=[bk, bW], writes=[bcb["AKK"]])
                            fw.op("dve", lambda e, hs_=hs_: e.tensor_tensor(out=cb["GKR"][0:C_, hs_, 0:C_], in0=gk[:, hs_, C_:2 * C_], in1=MS4(1), op=ALU.mult),
                                  reads=[bk, bW], writes=[bcb["GKR"]])
                            fw.op("dve", lambda e, hs_=hs_: e.scalar_tensor_tensor(out=cb["X1"][0:C_, hs_, 0:C_], in0=gb[:, hs_, 0:C_], scalar=-1.0, in1=MS4(0),
                                                                                 op0=ALU.mult, op1=ALU.mult), reads=[bb, bW], writes=[bcb["X1"]])
                            fw.op("dve", lambda e, hs_=hs_: e.scalar_tensor_tensor(out=cb["NGBR"][0:C_, hs_, 0:C_], in0=gb[:, hs_, C_:2 * C_], scalar=-1.0, in1=MS4(1),
                                                                                 op0=ALU.mult, op1=ALU.mult), reads=[bb, bW], writes=[bcb["NGBR"]])
                        fw.op("dve", lambda e: e.scalar_tensor_tensor(out=cbv("Y1"), in0=hv(pA, 0)[:, :, 0:C_], scalar=-1.0, in1=MS_(2), op0=ALU.mult, op1=ALU.mult),
                              reads=[bpA, bW], writes=[bcb["Y1"]])
                        fw.op("dve", lambda e: e.tensor_tensor(out=cbv("T1b"), in0=cbv("X1"), in1=IDB, op=ALU.add), reads=[bcb["X1"], b_ident], writes=[bcb["T1b"]])
                        if RS <= 3:
                            return
                        xc, yc, tc = "X1", "Y1", "T1b"
                        for lvl in range(L):
                            xn = "X2" if xc == "X1" else "X1"
                            yn = "Y2" if yc == "Y1" else "Y1"
                            tn = "T2b" if tc == "T1b" else "T1b"
                            if lvl < L - 1:
                                for h in range(8):
                                    fw.op("pe", lambda e, h=h, xc=xc, yc=yc: e.matmul(hv(pB, 0)[:, h, 0:C_], lhsT=cb[yc][0:C_, h, 0:C_], rhs=cb[xc][0:C_, h, 0:C_],
                                                                                     start=True, stop=True), reads=[bcb[xc], bcb[yc]], writes=[bpB])
                            for h in range(8):
                                fw.op("pe", lambda e, h=h, xc=xc, yc=yc: e.matmul(hv(pA, 0)[:, h, 0:C_], lhsT=cb[xc][0:C_, h, 0:C_], rhs=cb[yc][0:C_, h, 0:C_],
                                                                                 start=True, stop=True), reads=[bcb[xc], bcb[yc]], writes=[bpA])
                            if lvl < L - 1:
                                fw.op("act", lambda e, xn=xn: e.copy(out=cbv(xn), in_=hv(pB, 0)[:, :, 0:C_]), reads=[bpB], writes=[bcb[xn]])
                            fw.op("dve", lambda e, yn=yn: e.tensor_copy(out=cbv(yn), in_=hv(pA, 0)[:, :, 0:C_]), reads=[bpA], writes=[bcb[yn]])
                            for h in range(8):
                                fw.op("pe", lambda e, h=h, yn=yn, tc=tc: e.matmul(hv(pGK, 0)[:, h, 0:C_], lhsT=cb[yn][0:C_, h, 0:C_], rhs=cb[tc][0:C_, h, 0:C_],
                                                                                 start=True, stop=True), reads=[bcb[yn], bcb[tc]], writes=[bpGK0])
                            fw.op("dve", lambda e, tn=tn, tc=tc: e.tensor_tensor(out=cbv(tn), in0=hv(pGK, 0)[:, :, 0:C_], in1=cbv(tc), op=ALU.add),
                                  reads=[bpGK0, bcb[tc]], writes=[bcb[tn]])
                            xc, yc, tc = xn, yn, tn
                        if RS <= 4:
                            return
                        wt = hv(pGK, 512)
                        ut = hv(pGB, 0)
                        oo = hv(pGB, 512)
                        for h in range(8):
                            hp, base = h // 2, (h % 2) * 64
                            bs = slice(base, base + 64)
                            fw.op("pe", lambda e, h=h: e.matmul(wt[:, h, :], lhsT=KRh(h, 0), rhs=STh(h), start=True, stop=False),
                                  reads=[bKR[hp], bST], writes=[bpGK1])
                            fw.op("pe", lambda e, h=h: e.matmul(wt[:, h, :], lhsT=cb["AKK"][0:C_, h, 0:C_], rhs=TM[0:C_, 2, h * 64:(h + 1) * 64], start=False, stop=True),
                                  reads=[bcb["AKK"], bTM], writes=[bpGK1])
                        fw.op("act", lambda e: e.copy(out=cb["WTs"][0:C_], in_=wt), reads=[bpGK1], writes=[bcb["WTs"]])
                        for h in range(8):
                            fw.op("pe", lambda e, h=h, tc=tc: e.matmul(ut[:, h, :], lhsT=cb[tc][0:C_, h, 0:C_], rhs=cb["WTs"][0:C_, h, :], start=True, stop=True),
                                  reads=[bcb[tc], bcb["WTs"]], writes=[bpGB0])
                        fw.op("dve", lambda e: e.tensor_copy(out=cb["UTs"][0:C_], in_=ut), reads=[bpGB0], writes=[bcb["UTs"]])
                        for h in range(8):
                            hp, base = h // 2, (h % 2) * 64
                            bs = slice(base, base + 64)
                            fw.op("pe", lambda e, h=h: e.matmul(oo[:, h, :], lhsT=KRh(h, 1), rhs=STh(h), start=True, stop=False),
                                  reads=[bKR[hp], bST], writes=[bpGB1])
                            fw.op("pe", lambda e, h=h: e.matmul(oo[:, h, :], lhsT=cb["GKR"][0:C_, h, 0:C_], rhs=TM[0:C_, 2, h * 64:(h + 1) * 64], start=False, stop=False),
                                  reads=[bcb["GKR"], bTM], writes=[bpGB1])
                            fw.op("pe", lambda e, h=h: e.matmul(oo[:, h, :], lhsT=cb["NGBR"][0:C_, h, 0:C_], rhs=cb["UTs"][0:C_, h, :], start=False, stop=True),
                                  reads=[bcb["NGBR"], bcb["UTs"]], writes=[bpGB1])
                        sn = pB[:, :].rearrange("p (a f) -> p a f", a=4)
                        for hp in range(4):
                            fw.op("pe", lambda e, hp=hp: e.matmul(sn[:, hp, :], lhsT=TM[0:C_, 0, hp * 128:(hp + 1) * 128], rhs=TM[0:C_, 2, hp * 128:(hp + 1) * 128],
                                                                  start=True, stop=False), reads=[bTM], writes=[bpB])
                            fw.op("pe", lambda e, hp=hp: e.matmul(sn[:, hp, :], lhsT=TM[0:C_, 1, hp * 128:(hp + 1) * 128],
                                                                  rhs=cb["UTs"][0:C_, 2 * hp:2 * hp + 2, :].rearrange("p a v -> p (a v)"),
                                                                  start=False, stop=True), reads=[bTM, bcb["UTs"]], writes=[bpB])
                        if RS <= 5:
                            return
                        fw.op("act", lambda e: e.copy(out=OS[0:C_], in_=oo), reads=[bpGB1], writes=[bOS])
                        for half in range(2):
                            bs = slice(64 * half, 64 * half + 64)
                            fw.op("dve", lambda e, bs=bs: e.tensor_tensor(out=ST[bs], in0=ST[bs], in1=PCS[bs, :, c].unsqueeze(2).broadcast_to([64, 4, 64]), op=ALU.mult),
                                  reads=[bST, bPCS], writes=[bST])
                            fw.op("dve", lambda e, bs=bs, half=half: e.tensor_tensor(out=ST[bs], in0=sn[bs, :, 64 * half:64 * half + 64], in1=ST[bs], op=ALU.add),
                                  reads=[bST, bpB], writes=[bST])
                        fw.op("act", lambda e: e.copy(out=STb[:], in_=ST[:]), reads=[bST], writes=[bST])
                        fw.op("act", lambda e: e.copy(out=STbo[:], in_=ST[64:128]), reads=[bST], writes=[bST])
                        S_ = lambda i_: STAT[0:C_, i_, :]
                        fw.op("dve", lambda e: e.tensor_reduce(out=S_(0), in_=OS[0:C_], axis=AX.X, op=ALU.add), reads=[bOS], writes=[bSTAT])
                        fw.op("pool", lambda e: e.tensor_tensor(out=OSQ[0:C_], in0=OS[0:C_], in1=OS[0:C_], op=ALU.mult), reads=[bOS], writes=[bOSQ])
                        fw.op("dve", lambda e: e.tensor_reduce(out=S_(1), in_=OSQ[0:C_], axis=AX.X, op=ALU.add), reads=[bOSQ], writes=[bSTAT])
                        fw.op("dve", lambda e: e.tensor_scalar(out=S_(2), in0=S_(0), scalar1=1.0 / 64, scalar2=None, op0=ALU.mult), reads=[bSTAT], writes=[bSTAT])
                        fw.op("dve", lambda e: e.tensor_tensor(out=S_(3), in0=S_(2), in1=S_(2), op=ALU.mult), reads=[bSTAT], writes=[bSTAT])
                        fw.op("dve", lambda e: e.scalar_tensor_tensor(out=S_(4), in0=S_(1), scalar=1.0 / 64, in1=S_(3), op0=ALU.mult, op1=ALU.subtract),
                              reads=[bSTAT], writes=[bSTAT])
                        fw.op("act", lambda e: e.activation(out=S_(5), in_=S_(4), func=AF.Sqrt, bias=64e-5), reads=[bSTAT], writes=[bSTAT])
                        fw.op("dve", lambda e: e.reciprocal(out=S_(5), in_=S_(5)), reads=[bSTAT], writes=[bSTAT])
                        fw.op("dve", lambda e: e.tensor_tensor(out=OSQ[0:C_], in0=OS[0:C_], in1=bc3(S_(2), 64), op=ALU.subtract), reads=[bOS, bSTAT], writes=[bOSQ])
                        fw.op("dve", lambda e: e.tensor_tensor(out=cb["ONb"][0:C_], in0=OSQ[0:C_], in1=bc3(S_(5), 64), op=ALU.mult), reads=[bOSQ, bSTAT], writes=[bcb["ONb"]])
                        t2v = pT2[:, 512:512 + 4 * C_].rearrange("p (a t) -> p a t", a=4)
                        for hp in range(4):
                            fw.op("pe", lambda e, hp=hp: e.transpose(out=t2v[:, hp, :], in_=cb["ONb"][0:C_, 2 * hp:2 * hp + 2, :].rearrange("p a v -> p (a v)"),
                                                                     identity=ident[0:C_, 0:C_]), reads=[bcb["ONb"], b_ident], writes=[bpT2])
                        fw.op("act", lambda e: e.copy(out=ONT[:, :, cols], in_=t2v), reads=[bpT2], writes=[bONT])
                    for c_ in range(nch):
                        _chunk(c_)
                    if RS <= 6:
                        return
                    for hp in range(4):
                        fw.op("dve", lambda e, hp=hp: e.tensor_scalar(out=FT[:, 0:n], in0=ONT[:, hp, 0:n], scalar1=LNG(hp), scalar2=LNB(hp), op0=ALU.mult, op1=ALU.add),
                              reads=[bONT, bPC], writes=[bFT])
                        fw.op("dve", lambda e, hp=hp: e.tensor_tensor(out=FT[:, 0:n], in0=FT[:, 0:n], in1=BON[:, hp, 0:n], op=ALU.add),
                              reads=[bFT, bBON[hp]], writes=[bFT])
                        fw.op("dve", lambda e, hp=hp: e.tensor_tensor(out=MO[:, hp, 0:n], in0=FT[:, 0:n], in1=Gt[:, hp, 0:n], op=ALU.mult),
                              reads=[bFT, bGt[hp]], writes=[bMO])
                    for tl_i, (ti, r0_, r) in enumerate(tiles):
                        for half in range(2):
                            pj, bpj = (pA, bpA) if half == 0 else (pB, bpB)
                            for hp in range(4):
                                fw.op("pe", lambda e, pj=pj, hp=hp, tl_i=tl_i, r=r, half=half: e.matmul(
                                    pj[0:r, :], lhsT=MO[:, hp, tl_i * 128:tl_i * 128 + r], rhs=WoR[:, hp, half * 512:(half + 1) * 512],
                                    start=(hp == 0), stop=(hp == 3)), reads=[bMO, bWoR], writes=[bpj])
                            fw.op("dve", lambda e, pj=pj, ti=ti, r0_=r0_, r=r, half=half: e.tensor_tensor(
                                out=X[r0_:r0_ + r, ti, half * 512:(half + 1) * 512], in0=pj[0:r, :], in1=X[r0_:r0_ + r, ti, half * 512:(half + 1) * 512],
                                op=ALU.add), reads=[bpj, bX[ti]], writes=[bX[ti]])

                def store_state(idx):
                    so = pGK[0:64, 0:512].rearrange("p (a f) -> p a f", a=4)
                    for hp in range(4):
                        fw.op("pe", lambda e, hp=hp: e.matmul(so[:, hp, :], lhsT=ST[:, hp, :], rhs=ident_f[:], start=True, stop=True), reads=[bST, b_ident], writes=[bpGK0])
                    fw.op("dve", lambda e: e.tensor_copy(out=sto[:], in_=so), reads=[bpGK0], writes=[bsto])
                    fw.dma("sp", lambda e: e.dma_start(out=dr["rwkv_state"][idx].rearrange("h v k -> v h k"),
                                                       in_=sto[:].rearrange("p a (b k) -> p (a b) k", b=2)), reads=[bsto])

                fw.op("pool", lambda e: e.memset(ST[:], 0.0), writes=[bST])
                fw.op("pool", lambda e: e.memset(STb[:], 0.0), writes=[bST])
                fw.op("pool", lambda e: e.memset(STbo[:], 0.0), writes=[bST])
                fw.op("pool", lambda e: e.memset(CAR[:], 0.0), writes=[bCAR])
                sti = OS
                bsti = bOS
                seq = [("p", g) for g in range(cfg.get("rwkv_ngroups", 16))] + ([("s", 0), ("s", 1)] if cfg.get("rwkv_sample", True) else [])

                def do_prep(idx):
                    kind, a_ = seq[idx]
                    if kind == "p":
                        prep(idx % 2, a_ * NG, NG, 64)
                    else:
                        fw.op("dve", lambda e, a_=a_: e.tensor_copy(out=CAR[:], in_=PC[:, 42 + 14 * a_:56 + 14 * a_]), reads=[bPC], writes=[bCAR])
                        prep(idx % 2, 2048 + 32 * a_, 32, 32)

                def do_post(idx):
                    kind, a_ = seq[idx]
                    if kind == "p":
                        post(idx % 2, a_ * NG, NG, 64, [(a_, 0, 128)])
                        if a_ == 15:
                            store_state(0)
                    else:
                        b = a_
                        fw.dma("sp", lambda e: e.dma_start(out=sti[:], in_=dr["srw"][b].rearrange("h v k -> v h k")), writes=[bsti])
                        sv_ = pB[:, 0:256].rearrange("p (a v) -> p a v", a=4)
                        for hp in range(4):
                            fw.op("pe", lambda e, hp=hp: e.matmul(sv_[:, hp, :], lhsT=sti[:, 2 * hp:2 * hp + 2, :].rearrange("p a k -> p (a k)"),
                                                                  rhs=ident_f[0:64, 0:64], start=True, stop=True), reads=[bsti, b_ident], writes=[bpB])
                        fw.op("dve", lambda e: e.tensor_copy(out=ST[:], in_=sv_), reads=[bpB], writes=[bST])
                        fw.op("act", lambda e: e.copy(out=STb[:], in_=ST[:]), reads=[bST], writes=[bST])
                        fw.op("act", lambda e: e.copy(out=STbo[:], in_=ST[64:128]), reads=[bST], writes=[bST])
                        post(idx % 2, 2048 + 32 * b, 32, 32, [(16, 32 * b, 32)])
                        store_state(1 + b)

                do_prep(0)
                for idx in range(len(seq)):
                    if idx + 1 < len(seq):
                        do_prep(idx + 1)
                    do_post(idx)
                fw.flush()

        def mix_even(l):
            e_ = l // 2
            with ExitStack() as st:
                xT = sb(st, "xT", (128, 8, NTOK), BF16)
                b_xT = [Buf() for _ in range(5)]
                with ExitStack() as st2:
                    norm_T(st2, dr["mix_norm"][l:l + 1, :], xT, b_xT, "n_")
                    fw.flush()
                wv = dr["ev_w_in"][e_].rearrange("(k p) n -> p k n", p=128)
                if cfg.get("shiftout", True):
                    with ExitStack() as s2:
                        Wr, bWr = load_w_bf(s2, wv[:, :, 1544:3336], (8, 1792), "Wr")
                        pp = [ps(s2, "pps%d" % i, (128, 512)) for i in range(2)]
                        bpp = [Buf() for _ in range(2)]
                        sh_ = sb(s2, "sh", (1, 3, 1792))
                        bsh = Buf()
                        for si, tok in enumerate((2047, 2079, 2111)):
                            for cb, (c0, cw) in enumerate(((0, 512), (512, 512), (1024, 512), (1536, 256))):
                                q = cb % 2
                                for k in range(8):
                                    fw.op("pe", lambda e, q=q, k=k, tok=tok, c0=c0, cw=cw: e.matmul(
                                        pp[q][0:1, 0:cw], lhsT=xT[:, k, tok:tok + 1], rhs=Wr[:, k, c0:c0 + cw],
                                        start=(k == 0), stop=(k == 7)), reads=[b_xT[tok // 512], bWr], writes=[bpp[q]])
                                fw.op("act", lambda e, q=q, si=si, c0=c0, cw=cw: e.copy(out=sh_[0:1, si, c0:c0 + cw], in_=pp[q][0:1, 0:cw]),
                                      reads=[bpp[q]], writes=[bsh])
                        fw.dma("sp", lambda e: e.dma_start(out=dr["rwkv_shift"].rearrange("(o s) n -> o s n", o=1), in_=sh_[:]), reads=[bsh])
                        fw.flush()
                if cfg.get("fox", True):
                    fox_part(l, xT, b_xT)
                if cfg.get("rwkv", True):
                    rwkv_part(l, xT, b_xT)


        def mix_odd(l):
            j_ = l // 2
            wv = dr["od_w_in"][j_].rearrange("(k p) n -> p k n", p=128)
            with ExitStack() as st:
                with ExitStack() as sx:
                    xT = sb(sx, "xT", (128, 8, NTOK), BF16)
                    b_xT = [Buf() for _ in range(5)]
                    with ExitStack() as st2:
                        norm_T(st2, dr["mix_norm"][l:l + 1, :], xT, b_xT, "n_")
                        fw.flush()
                    with ExitStack() as s1:
                        Wu, bWu = load_w_bf(s1, wv[:, :, 768:1280], (8, 512), "Wu", stage_cols=1024)
                        Wv, bWv = load_w_bf(s1, wv[:, :, 1280:1792], (8, 512), "Wv", stage_cols=1024)
                        WoS, bWoS = load_w_bf(s1, dr["od_w_out"][j_][512:1024, :].rearrange("(h p) n -> p h n", p=128), (4, D), "WoS", stage_cols=1024)
                        Gv, bGv = bcast_row(s1, dr["sgu_v_norm"][j_:j_ + 1, :], 512, "Gv")
                        TRI = sb(s1, "gTRI", (128, 128))
                        WS = sb(s1, "WS", (128, 8, 128))
                        WSb = sb(s1, "WSb", (128, 8, 128), BF16)
                        WST = sb(s1, "WST", (128, 8, 128), BF16)
                        SBr = sb(s1, "SBr", (8, 128))
                        BT = sb(s1, "BT", (128, 8))
                        bC = Buf()
                        fw.dma("sp", lambda e: e.dma_start(out=TRI[:], in_=dr["c_tri"]), writes=[bC])
                        fw.dma("sp", lambda e: e.dma_start(out=WS[:], in_=dr["sgu_w_s"][j_].rearrange("g t s -> t g s")), writes=[bC])
                        fw.dma("sp", lambda e: e.dma_start(out=SBr[:], in_=dr["sgu_b"][j_]), writes=[bC])
                        fw.op("pool", lambda e: e.tensor_copy(out=WSb[:], in_=WS[:]), reads=[bC], writes=[bC])
                        pu = ps(s1, "gpu", (128, 512))
                        pv = ps(s1, "gpv", (128, 512))
                        pm = ps(s1, "gpm", (128, 512))
                        po = [ps(s1, "gpo%d" % i, (128, 512)) for i in range(2)]
                        pt = ps(s1, "gpt", (128, 8, 128), BF16)
                        bpu, bpv, bpm, bpt = Buf(), Buf(), Buf(), Buf()
                        bpo = [Buf(), Buf()]
                        for g in range(8):
                            fw.op("pe", lambda e, g=g: e.transpose(out=pt[:, g, :], in_=WSb[:, g, :], identity=ident[:]), reads=[bC, b_ident], writes=[bpt])
                        fw.op("dve", lambda e: e.tensor_tensor(out=WST[:], in0=pt[:], in1=bcm(TRI[:], 8), op=ALU.mult), reads=[bpt, bC], writes=[bC])
                        fw.op("pe", lambda e: e.matmul(pm[:, 0:8], lhsT=SBr[0:8, :], rhs=ident_f[0:8, 0:8], start=True, stop=True), reads=[bC, b_ident], writes=[bpm])
                        fw.op("dve", lambda e: e.tensor_copy(out=BT[:], in_=pm[:, 0:8]), reads=[bpm], writes=[bC])
                        U = sb(s1, "gU", (128, 512))
                        GV = sb(s1, "gGV", (128, 512))
                        VN = sb(s1, "gVN", (128, 512))
                        VNb = sb(s1, "gVNb", (128, 512), BF16)
                        U1 = sb(s1, "gU1", (32, 512))
                        VNb1 = sb(s1, "gVNb1", (32, 512), BF16)
                        Dm = sb(s1, "gDm", (128, 512))
                        Db = sb(s1, "gDb", (128, 512), BF16)
                        DT = sb(s1, "gDT", (128, 4, 128), BF16)
                        gss = sb(s1, "gss", (128, 4))
                        bU, bGV, bVN, bVNb, bU1, bVNb1, bDm, bDb, bDT, bgss, bgj = [Buf() for _ in range(11)]

                        def sgu_tile(i):
                            r = tile_rows(i)
                            tk = slice(i * 128, i * 128 + r)
                            g5 = i // 4
                            for k in range(8):
                                fw.op("pe", lambda e, k=k: e.matmul(pu[0:r, :], lhsT=xT[:, k, tk], rhs=Wu[:, k, :], start=(k == 0), stop=(k == 7)),
                                      reads=[b_xT[g5], bWu], writes=[bpu])
                            for k in range(8):
                                fw.op("pe", lambda e, k=k: e.matmul(pv[0:r, :], lhsT=xT[:, k, tk], rhs=Wv[:, k, :], start=(k == 0), stop=(k == 7)),
                                      reads=[b_xT[g5], bWv], writes=[bpv])
                            fw.op("act", lambda e: e.activation(out=U[0:r, :], in_=pu[0:r, :], func=AF.Gelu_apprx_tanh), reads=[bpu], writes=[bU])
                            fw.op("act", lambda e: e.activation(out=GV[0:r, :], in_=pv[0:r, :], func=AF.Gelu_apprx_tanh), reads=[bpv], writes=[bGV])
                            fw.op("act", lambda e: e.activation(out=Dm[0:r, :], in_=GV[0:r, :], func=AF.Square, accum_out=gss[0:r, 0:1]), reads=[bGV], writes=[bDm, bgss])
                            fw.op("act", lambda e: e.activation(out=gss[0:r, 1:2], in_=gss[0:r, 0:1], func=AF.Sqrt, scale=1.0 / 512, bias=EPS), reads=[bgss], writes=[bgss])
                            fw.op("dve", lambda e: e.reciprocal(out=gss[0:r, 2:3], in_=gss[0:r, 1:2]), reads=[bgss], writes=[bgss])
                            fw.op("dve", lambda e: e.scalar_tensor_tensor(out=VN[0:r, :], in0=GV[0:r, :], scalar=gss[0:r, 2:3], in1=Gv[0:r, :], op0=ALU.mult, op1=ALU.mult),
                                  reads=[bGV, bgss, bGv], writes=[bVN])
                            fw.op("act", lambda e: e.copy(out=VNb[0:r, :], in_=VN[0:r, :]), reads=[bVN], writes=[bVNb])
                            mm = pm[:, :].rearrange("p (g d) -> p g d", g=8)
                            if i < 16:
                                for g in range(8):
                                    fw.op("pe", lambda e, g=g: e.matmul(mm[:, g, :], lhsT=WST[:, g, :], rhs=VNb[:, g * 64:(g + 1) * 64], start=True, stop=True),
                                          reads=[bC, bVNb], writes=[bpm])
                                fw.op("dve", lambda e: e.tensor_tensor(out=Dm[:].rearrange("p (g d) -> p g d", g=8), in0=mm, in1=bc3(BT[:, :], 64), op=ALU.add),
                                      reads=[bpm, bC], writes=[bDm])
                                fw.op("dve", lambda e: e.tensor_tensor(out=Db[:], in0=Dm[:], in1=U[:], op=ALU.mult), reads=[bDm, bU], writes=[bDb])
                                for c4 in range(4):
                                    fw.op("pe", lambda e, c4=c4: e.transpose(out=pt[:, c4, :], in_=Db[:, c4 * 128:(c4 + 1) * 128], identity=ident[:]),
                                          reads=[bDb, b_ident], writes=[bpt])
                                fw.op("act", lambda e: e.copy(out=DT[:], in_=pt[:, 0:4, :]), reads=[bpt], writes=[bDT])
                            else:
                                fw.dma("sp", lambda e: e.dma_start(out=dr["sgu_vo"], in_=VN[0:64, :]), reads=[bVN])
                                fw.op("act", lambda e: e.copy(out=VNb1[:], in_=VN[32:64, :]), reads=[bVN], writes=[bVNb1])
                                fw.op("act", lambda e: e.copy(out=U1[:], in_=U[32:64, :]), reads=[bU], writes=[bU1])
                                for b in range(2):
                                    vsrc = VNb if b == 0 else VNb1
                                    usrc = U if b == 0 else U1
                                    for g in range(8):
                                        fw.op("pe", lambda e, g=g, vsrc=vsrc: e.matmul(mm[0:32, g, :], lhsT=WST[0:32, g, 0:32], rhs=vsrc[0:32, g * 64:(g + 1) * 64],
                                                                                       start=True, stop=True), reads=[bC, bVNb, bVNb1], writes=[bpm])
                                    fw.op("dve", lambda e: e.tensor_tensor(out=Dm[0:32, :].rearrange("p (g d) -> p g d", g=8), in0=mm[0:32], in1=bc3(BT[0:32, :], 64), op=ALU.add),
                                          reads=[bpm, bC], writes=[bDm])
                                    fw.op("dve", lambda e, usrc=usrc: e.tensor_tensor(out=Db[0:32, :], in0=Dm[0:32, :], in1=usrc[0:32, :], op=ALU.mult),
                                          reads=[bDm, bU, bU1], writes=[bDb])
                                    for c4 in range(4):
                                        fw.op("pe", lambda e, c4=c4: e.transpose(out=pt[:, c4, 0:32], in_=Db[0:32, c4 * 128:(c4 + 1) * 128], identity=ident[0:32, 0:32]),
                                              reads=[bDb, b_ident], writes=[bpt])
                                    fw.op("act", lambda e, b=b: e.copy(out=DT[:, :, 32 * b:32 * b + 32], in_=pt[:, 0:4, 0:32]), reads=[bpt], writes=[bDT])
                            for half in range(2):
                                for c4 in range(4):
                                    fw.op("pe", lambda e, c4=c4, half=half: e.matmul(po[half][0:r, :], lhsT=DT[:, c4, 0:r], rhs=WoS[:, c4, half * 512:(half + 1) * 512],
                                                                                    start=(c4 == 0), stop=(c4 == 3)), reads=[bDT, bWoS], writes=[bpo[half]])
                                fw.op("dve", lambda e, half=half: e.tensor_tensor(out=X[0:r, i, half * 512:(half + 1) * 512], in0=po[half][0:r, :],
                                                                                 in1=X[0:r, i, half * 512:(half + 1) * 512], op=ALU.add),
                                      reads=[bpo[half], bX[i]], writes=[bX[i]])
                        if cfg.get("sgu", True):
                            for i in range(NT):
                                sgu_tile(i)
                        fw.flush()
                    QT = sb(sx, "sQT", (64, 8, NTOK), BF16)
                    bQT = [Buf() for _ in range(NT)]
                    KT = sb(sx, "sKT", (64, 2, NTOK), BF16)
                    bKT = [Buf() for _ in range(NT)]
                    VA = sb(sx, "sVA", (128, NT, 128), BF16)
                    bVA = [Buf() for _ in range(NT)]
                    VAn = sb(sx, "sVAn", (32, 128), BF16)
                    bVAn = Buf()
                    with ExitStack() as s2:
                        Wq, bWq = load_w_bf(s2, wv[:, :, 0:512], (8, 512), "sWq", stage_cols=1024)
                        Wkv, bWkv = load_w_bf(s2, wv[:, :, 512:768], (8, 256), "sWkv", stage_cols=1024)
                        Gq, bGq = bcast_row(s2, dr["swa_q_norm"][j_:j_ + 1, :], 64, "sGq")
                        Gk, bGk = bcast_row(s2, dr["swa_k_norm"][j_:j_ + 1, :], 64, "sGk")
                        CS = sb(s2, "CS", (128, 2, NT, 8))
                        bCS = Buf()
                        fw.dma("sp", lambda e: e.dma_start(out=CS[:], in_=dr["c_rope"]), writes=[bCS])
                        pq = ps(s2, "spq", (128, 512))
                        pk = ps(s2, "spk", (128, 512))
                        ptk = ps(s2, "sptk", (64, 8, 128), BF16)
                        ptk2 = ps(s2, "sptk2", (64, 8, 128), BF16)
                        bpq, bpk, bptk, bptk2 = Buf(), Buf(), Buf(), Buf()
                        qf = sb(s2, "sqf", (128, 8, 64))
                        kf = [sb(s2, "skf%d" % i, (128, 2, 64)) for i in range(2)]
                        vf = [sb(s2, "svf%d" % i, (128, 128)) for i in range(2)]
                        qb = sb(s2, "sqb", (128, 8, 64), BF16)
                        kb_ = sb(s2, "skb", (128, 2, 64), BF16)
                        rt = sb(s2, "srt", (128, 4, 8, 8))
                        bqf, bqb, bkb, brt = Buf(), Buf(), Buf(), Buf()
                        bkf = [Buf(), Buf()]
                        bvf = [Buf(), Buf()]

                        def rope(t, bt, r, H, i):
                            cosb = CS[0:r, 0, i, :].unsqueeze(1).broadcast_to([r, H, 8])
                            sinb = CS[0:r, 1, i, :].unsqueeze(1).broadcast_to([r, H, 8])
                            x1, x2 = t[0:r, 0:H, 0:8], t[0:r, 0:H, 8:16]
                            fw.op("dve", lambda e: e.tensor_tensor(out=rt[0:r, 0, 0:H, :], in0=x1, in1=cosb, op=ALU.mult), reads=[bt, bCS], writes=[brt])
                            fw.op("dve", lambda e: e.tensor_tensor(out=rt[0:r, 1, 0:H, :], in0=x2, in1=sinb, op=ALU.mult), reads=[bt, bCS], writes=[brt])
                            fw.op("dve", lambda e: e.tensor_tensor(out=rt[0:r, 2, 0:H, :], in0=x2, in1=cosb, op=ALU.mult), reads=[bt, bCS], writes=[brt])
                            fw.op("dve", lambda e: e.tensor_tensor(out=rt[0:r, 3, 0:H, :], in0=x1, in1=sinb, op=ALU.mult), reads=[bt, bCS], writes=[brt])
                            fw.op("dve", lambda e: e.tensor_tensor(out=x1, in0=rt[0:r, 0, 0:H, :], in1=rt[0:r, 1, 0:H, :], op=ALU.subtract), reads=[brt], writes=[bt])
                            fw.op("dve", lambda e: e.tensor_tensor(out=x2, in0=rt[0:r, 2, 0:H, :], in1=rt[0:r, 3, 0:H, :], op=ALU.add), reads=[brt], writes=[bt])

                        def swa_tile(i):
                            r = tile_rows(i)
                            tk = slice(i * 128, i * 128 + r)
                            g5 = i // 4
                            jj = i % 2
                            for k in range(8):
                                fw.op("pe", lambda e, k=k: e.matmul(pq[0:r, :], lhsT=xT[:, k, tk], rhs=Wq[:, k, :], start=(k == 0), stop=(k == 7)),
                                      reads=[b_xT[g5], bWq], writes=[bpq])
                            for k in range(8):
                                fw.op("pe", lambda e, k=k: e.matmul(pk[0:r, 0:256], lhsT=xT[:, k, tk], rhs=Wkv[:, k, :], start=(k == 0), stop=(k == 7)),
                                      reads=[b_xT[g5], bWkv], writes=[bpk])
                            fw.op("act", lambda e: e.copy(out=vf[jj][0:r, :], in_=pk[0:r, 128:256]), reads=[bpk], writes=[bvf[jj]])
                            fw.op("dve", lambda e: e.tensor_copy(out=VA[0:r, i, :], in_=vf[jj][0:r, :]), reads=[bvf[jj]], writes=[bVA[i]])
                            rms_heads(s2, pq[0:r, :].rearrange("p (h d) -> p h d", h=8), bpq, r, 8, Gq, bGq, 1.0, qb[0:r], bqb, out_f=qf[0:r], bout_f=bqf)
                            rope(qf, bqf, r, 8, i)
                            fw.op("act", lambda e: e.activation(out=qb[0:r], in_=qf[0:r], func=AF.Copy, scale=0.125), reads=[bqf], writes=[bqb])
                            rms_heads(s2, pk[0:r, 0:128].rearrange("p (h d) -> p h d", h=2), bpk, r, 2, Gk, bGk, 1.0, kb_[0:r], bkb, out_f=kf[jj][0:r], bout_f=bkf[jj])
                            rope(kf[jj], bkf[jj], r, 2, i)
                            fw.op("act", lambda e: e.copy(out=kb_[0:r], in_=kf[jj][0:r]), reads=[bkf[jj]], writes=[bkb])
                            for h in range(8):
                                fw.op("pe", lambda e, h=h: e.transpose(out=ptk[:, h, 0:r], in_=qb[0:r, h, :], identity=ident[0:r, 0:r]), reads=[bqb, b_ident], writes=[bptk])
                            fw.op("act", lambda e: e.copy(out=QT[:, :, tk], in_=ptk[:, :, 0:r]), reads=[bptk], writes=[bQT[i]])
                            for h in range(2):
                                fw.op("pe", lambda e, h=h: e.transpose(out=ptk2[:, h, 0:r], in_=kb_[0:r, h, :], identity=ident[0:r, 0:r]), reads=[bkb, b_ident], writes=[bptk2])
                            fw.op("act", lambda e: e.copy(out=KT[:, :, tk], in_=ptk2[:, 0:2, 0:r]), reads=[bptk2], writes=[bKT[i]])
                            if i == 15:
                                fw.dma("sp", lambda e: e.dma_start(out=dr["swa_ko"][0], in_=kf[jj][:].rearrange("p h d -> p (h d)")), reads=[bkf[jj]])
                                fw.dma("sp", lambda e: e.dma_start(out=dr["swa_vo"][0], in_=vf[jj][:]), reads=[bvf[jj]])
                            if i == 16:
                                for b in range(2):
                                    fw.dma("sp", lambda e, b=b: e.dma_start(out=dr["swa_ko"][1 + b, 96:128, :],
                                                                            in_=kf[jj][32 * b:32 * b + 32].rearrange("p h d -> p (h d)")), reads=[bkf[jj]])
                                    fw.dma("sp", lambda e, b=b: e.dma_start(out=dr["swa_vo"][1 + b, 96:128, :], in_=vf[jj][32 * b:32 * b + 32, :]), reads=[bvf[jj]])
                                fw.op("dve", lambda e: e.tensor_copy(out=VAn[:], in_=VA[32:64, 16, :]), reads=[bVA[16]], writes=[bVAn])
                        for i in range(NT):
                            swa_tile(i)
                        fw.flush()
                    with ExitStack() as s3:
                        Wo, bWo = load_w_bf(s3, dr["od_w_out"][j_][0:512, :].rearrange("(h d) n -> d h n", d=64), (8, D), "sWo", stage_cols=1024)
                        SK, bSK = bcast_row(s3, dr["swa_sinks"][j_:j_ + 1, :], 8, "sSK")
                        fw.op("act", lambda e: e.activation(out=SK[:], in_=SK[:], func=AF.Exp), reads=[bSK], writes=[bSK])
                        HM = sb(s3, "HM", (128, 64), BF16)
                        bHM = Buf()
                        fw.op("pool", lambda e: e.memset(HM[0:64, :], 0.0), writes=[bHM])
                        fw.op("pool", lambda e: e.memset(HM[64:128, :], 1.0), writes=[bHM])
                        OTt = [sb(s3, "sOT%d" % i, (64, 8, 128), BF16) for i in range(2)]
                        bOTt = [Buf(), Buf()]
                        ck = sb(s3, "sck", (128, 2, 128))
                        cv = sb(s3, "scv", (128, 2, 128))
                        ckb = sb(s3, "sckb", (128, 2, 2, 64), BF16)
                        cvb = sb(s3, "scvb", (128, 2, 128), BF16)
                        KTc = sb(s3, "sKTc", (64, 2, 2, 128), BF16)
                        bck, bcv, bckb, bcvb, bKTc = Buf(), Buf(), Buf(), Buf(), Buf()
                        ptc = ps(s3, "sptc", (64, 8, 128), BF16)
                        bptc = Buf()
                        for b in range(2):
                            fw.dma("sp", lambda e, b=b: e.dma_start(out=ck[:, b, :], in_=dr["csk"][b]), writes=[bck])
                            fw.dma("sp", lambda e, b=b: e.dma_start(out=cv[:, b, :], in_=dr["csv"][b]), writes=[bcv])
                            fw.dma("sp", lambda e, b=b: e.dma_start(out=dr["swa_ko"][1 + b, 0:96, :], in_=ck[32:128, b, :]), reads=[bck])
                            fw.dma("sp", lambda e, b=b: e.dma_start(out=dr["swa_vo"][1 + b, 0:96, :], in_=cv[32:128, b, :]), reads=[bcv])
                        fw.op("pool", lambda e: e.tensor_copy(out=ckb[:].rearrange("p b n d -> p b (n d)"), in_=ck[:]), reads=[bck], writes=[bckb])
                        fw.op("pool", lambda e: e.tensor_copy(out=cvb[:], in_=cv[:]), reads=[bcv], writes=[bcvb])
                        for b in range(2):
                            for n_ in range(2):
                                fw.op("pe", lambda e, b=b, n_=n_: e.transpose(out=ptc[:, 2 * b + n_, :], in_=ckb[:, b, n_, :], identity=ident[:]),
                                      reads=[bckb, b_ident], writes=[bptc])
                        fw.op("act", lambda e: e.copy(out=KTc[:].rearrange("p b n k -> p (b n) k"), in_=ptc[:, 0:4, :]), reads=[bptc], writes=[bKTc])

                        def out_proj_tile(i, OT_t, bOT_t):
                            r = tile_rows(i)
                            po, bpo = C.ac_S, C.ac_bS
                            for half in range(2):
                                for h in range(8):
                                    fw.op("pe", lambda e, h=h, half=half: e.matmul(po[half][0:r, :], lhsT=OT_t[:, h, 0:r], rhs=Wo[0:64, h, half * 512:(half + 1) * 512],
                                                                                  start=(h == 0), stop=(h == 7)), reads=[bOT_t, bWo], writes=[bpo[half]])
                                fw.op("dve", lambda e, half=half: e.tensor_tensor(out=X[0:r, i, half * 512:(half + 1) * 512], in0=po[half][0:r, :],
                                                                                 in1=X[0:r, i, half * 512:(half + 1) * 512], op=ALU.add),
                                      reads=[bpo[half], bX[i]], writes=[bX[i]])

                        def attn_tile(m):
                            s_ = m % 2
                            for cc in range(2):
                                c = 2 * m + cc
                                for h in range(8):
                                    n_ = h // 4
                                    kbs = []
                                    if cc == 0:
                                        if m >= 1:
                                            kbs.append(dict(kT=KT[:, n_, (m - 1) * 128:m * 128], v=VA[:, m - 1, n_ * 64:(n_ + 1) * 64], nk=128, col0=0,
                                                            reads=[bKT[m - 1], bVA[m - 1]]))
                                        kbs.append(dict(kT=KT[:, n_, m * 128:m * 128 + 64], v=VA[0:64, m, n_ * 64:(n_ + 1) * 64], nk=64, col0=0,
                                                        reads=[bKT[m], bVA[m]]))
                                    else:
                                        if m >= 1:
                                            kbs.append(dict(kT=KT[:, n_, (m - 1) * 128:m * 128], v=VA[:, m - 1, n_ * 64:(n_ + 1) * 64], nk=128, col0=0,
                                                            reads=[bKT[m - 1], bVA[m - 1]], mask=HM[:], mask_reads=[bHM], mask_w=64))
                                        kbs.append(dict(kT=KT[:, n_, m * 128:(m + 1) * 128], v=VA[:, m, n_ * 64:(n_ + 1) * 64], nk=128, col0=0,
                                                        reads=[bKT[m], bVA[m]]))
                                    attn_core(s3, QT[:, h, c * 64:(c + 1) * 64], bQT[m], 64, kbs, OTt[s_][:, h, cc * 64:(cc + 1) * 64], bOTt[s_],
                                              extra_den=(SK[0:64, h:h + 1], [bSK]), nbuf=1)
                            out_proj_tile(m, OTt[s_], bOTt[s_])

                        for m in range(16):
                            attn_tile(m)
                        for b in range(2):
                            for h in range(8):
                                n_ = h // 4
                                q0 = 2048 + 32 * b
                                vnew = VA[0:32, 16, n_ * 64:(n_ + 1) * 64] if b == 0 else VAn[0:32, n_ * 64:(n_ + 1) * 64]
                                kbs = [dict(kT=KTc[:, b, n_, :], v=cvb[:, b, n_ * 64:(n_ + 1) * 64], nk=128, col0=0, reads=[bKTc, bcvb]),
                                       dict(kT=KT[:, n_, q0:q0 + 32], v=vnew, nk=32, col0=0, reads=[bKT[16], bVA[16], bVAn])]
                                attn_core(s3, QT[:, h, q0:q0 + 32], bQT[16], 32, kbs, OTt[0][:, h, 32 * b:32 * b + 32], bOTt[0],
                                          extra_den=(SK[0:64, h:h + 1], [bSK]), nbuf=1)
                        out_proj_tile(16, OTt[0], bOTt[0])
                        fw.flush()

        for l in range(cfg.get("depth", 2)):
            if cfg.get("ffn1", True):
                ffn(l, "ffn1")
            if cfg.get("mix", True) and l % 2 == 0 and not cfg.get("only_odd", False):
                mix_even(l)
            if cfg.get("mix", True) and l % 2 == 1:
                mix_odd(l)
            if cfg.get("xattn", True):
                xattn(l)
            if cfg.get("ffn2", True):
                ffn(l, "ffn2")

        yp_v = dr["y_prompt"].rearrange("(t p) d -> p t d", p=128)
        for g in range(4):
            fw.dma("sp", lambda e, g=g: e.dma_start(out=yp_v[:, 4 * g:4 * g + 4, :], in_=X[:, 4 * g:4 * g + 4, :]),
                   reads=bX[4 * g:4 * g + 4])
        fw.dma("sp", lambda e: e.dma_start(out=dr["y_sample"], in_=X[0:64, 16, :]), reads=[bX[16]])
        fw.flush()
    C.ninstr = fw.ninstr
    return nc, C


def make_consts():
    tri = np.triu(np.ones((128, 128), np.float32))
    trib = np.zeros((64, 64), np.float32)
    trib[0:32, 0:32] = tri[0:32, 0:32]
    trib[32:64, 32:64] = tri[0:32, 0:32]
    bd = np.zeros((128, 128), np.float32)
    bd[0:64, 0:64] = 1.0
    bd[64:128, 64:128] = 1.0
    t64 = np.triu(np.ones((64, 64), np.float32))
    msk = np.stack([np.triu(np.ones((64, 64), np.float32), 1), t64, np.tril(np.ones((64, 64), np.float32), -1)], 1)
    rst = np.ones((128, 256), np.float32)
    rst[:, ::64] = 0.0
    pos = np.zeros((128, NT), np.float32)
    for i in range(16):
        pos[:, i] = i * 128 + np.arange(128)
    pos[:, 16] = 2048 + (np.arange(128) % 32)
    inv_freq = np.power(np.float32(500000.0), -np.arange(8, dtype=np.float32) / np.float32(8)).astype(np.float32)
    ang = (pos[:, :, None] * inv_freq[None, None, :]).astype(np.float32)
    rope_t = np.ascontiguousarray(np.stack([np.cos(ang), np.sin(ang)], 1).astype(np.float32))
    return {"c_ident": np.eye(128, dtype=np.float32), "c_tri": tri, "c_trib": trib, "c_bd": bd, "c_rope": rope_t,
            "c_msk": np.ascontiguousarray(msk), "c_rst": rst}


def kernel(**inputs):
    cfg = inputs.pop("_cfg", {})
    inp = {k: np.asarray(v) for k, v in inputs.items()}
    nc, C = build_program(cfg)
    consts = make_consts()
    in_maps = []
    for c in range(8):
        m = dict(consts)
        m["xp"] = np.ascontiguousarray(inp["x_prompt"][c])
        m["xs"] = np.ascontiguousarray(inp["x_sample"][2 * c:2 * c + 2].reshape(64, D))
        for nm in ("ffn1", "ffn2"):
            for s in ("_norm", "_w_gate", "_w_up", "_w_down"):
                m[nm + s] = inp[nm + s]
        m["memp"] = np.ascontiguousarray(inp["mem_prompt"][c])
        m["cmk"] = np.ascontiguousarray(inp["cache_mem_k"][:, 2 * c:2 * c + 2].reshape(2, 2, 256, 256))
        m["cmv"] = np.ascontiguousarray(inp["cache_mem_v"][:, 2 * c:2 * c + 2].reshape(2, 2, 256, 256))
        for nm in ("mix_norm", "ev_w_in", "fox_b_f", "fox_k_norm", "fox_q_norm", "ev_w_out"):
            m[nm] = inp[nm]
        m["cfk"] = np.ascontiguousarray(inp["cache_fox_k"][0, 2 * c:2 * c + 2].reshape(2, 2048, 512))
        m["cfv"] = np.ascontiguousarray(inp["cache_fox_v"][0, 2 * c:2 * c + 2].reshape(2, 2048, 512))
        m["cfl"] = np.ascontiguousarray(inp["cache_fox_logf"][0, 2 * c:2 * c + 2])
        for nm in ("rwkv_mu", "rwkv_w0", "rwkv_a0", "rwkv_k_k", "rwkv_k_a", "rwkv_ln_g", "rwkv_ln_b", "rwkv_w2", "rwkv_a2", "rwkv_g2"):
            m[nm] = inp[nm]
        m["rwkv_r_k"] = np.ascontiguousarray(inp["rwkv_r_k"].reshape(1, 512))
        for nm in ("od_w_in", "od_w_out", "swa_q_norm", "swa_k_norm", "swa_sinks", "sgu_v_norm", "sgu_w_s", "sgu_b"):
            m[nm] = inp[nm]
        m["csk"] = np.ascontiguousarray(inp["cache_swa_k"][0, 2 * c:2 * c + 2].reshape(2, 128, 128))
        m["csv"] = np.ascontiguousarray(inp["cache_swa_v"][0, 2 * c:2 * c + 2].reshape(2, 128, 128))
        m["srw"] = np.ascontiguousarray(inp["state_rwkv"][0, 2 * c:2 * c + 2])
        m["srs"] = np.ascontiguousarray(inp["state_rwkv_shift"][0, 2 * c:2 * c + 2].reshape(2, 1792))
        for nm in ("xattn_norm", "mem_norm", "xattn_wq", "xattn_wkv", "xattn_q_norm", "xattn_k_norm", "xattn_wo"):
            m[nm] = inp[nm]
        in_maps.append(m)
    if cfg.get("_sim"):
        R = cfg["_sim"](nc, in_maps)
    else:
        res = run_bass_kernel_spmd(nc, in_maps, core_ids=list(range(8)))
        R = res.results
    y_prompt = np.stack([R[c]["y_prompt"] for c in range(8)], 0)
    y_sample = np.concatenate([R[c]["y_sample"].reshape(2, 32, D) for c in range(8)], 0)
    p_mem_k = np.stack([R[c]["p_mem_k"].reshape(2, 256, 4, 64) for c in range(8)], 1)
    p_mem_v = np.stack([R[c]["p_mem_v"].reshape(2, 256, 4, 64) for c in range(8)], 1)
    fk = np.stack([R[c]["fox_k"] for c in range(8)], 0)
    fv = np.stack([R[c]["fox_v"] for c in range(8)], 0)
    fl = np.stack([R[c]["fox_logf"] for c in range(8)], 0)
    rsft = np.stack([R[c]["rwkv_shift"] for c in range(8)], 0)
    p_fox_k = fk[:, :2048].reshape(1, 8, 2048, 8, 64)
    p_fox_v = fv[:, :2048].reshape(1, 8, 2048, 8, 64)
    p_fox_logf = fl[:, :2048].reshape(1, 8, 2048, 8)
    s_fox_k = fk[:, 2048:].reshape(1, 16, 32, 8, 64)
    s_fox_v = fv[:, 2048:].reshape(1, 16, 32, 8, 64)
    s_fox_logf = fl[:, 2048:].reshape(1, 16, 32, 8)
    p_rwkv_shift = rsft[:, 0].reshape(1, 8, 1, 1792)
    s_rwkv_shift = rsft[:, 1:3].reshape(1, 16, 1, 1792)
    rst_ = np.stack([R[c]["rwkv_state"] for c in range(8)], 0)
    p_rwkv_state = rst_[:, 0][None]
    s_rwkv_state = rst_[:, 1:3].reshape(1, 16, 8, 64, 64)
    sk = np.stack([R[c]["swa_ko"] for c in range(8)], 0)
    sv = np.stack([R[c]["swa_vo"] for c in range(8)], 0)
    p_swa_k = sk[:, 0].reshape(1, 8, 128, 2, 64)
    p_swa_v = sv[:, 0].reshape(1, 8, 128, 2, 64)
    s_swa_k = sk[:, 1:3].reshape(1, 16, 128, 2, 64)
    s_swa_v = sv[:, 1:3].reshape(1, 16, 128, 2, 64)
    s_sgu_v = np.stack([R[c]["sgu_vo"].reshape(2, 32, 512) for c in range(8)], 0).reshape(1, 16, 32, 512)
    f32 = np.float32
    z = lambda *sh: np.zeros(sh, f32)
    out = (y_prompt, y_sample, p_fox_k, p_fox_v, p_fox_logf, p_rwkv_state, p_rwkv_shift,
           p_swa_k, p_swa_v, p_mem_k, p_mem_v,
           s_fox_k, s_fox_v, s_fox_logf, s_rwkv_state, s_rwkv_shift,
           s_swa_k, s_swa_v, s_sgu_v)
    return tuple(np.ascontiguousarray(o, dtype=f32) for o in out)
```

```python
import numpy as np
from contextlib import ExitStack
import concourse.bass as bass
import concourse.mybir as mybir
from concourse.bass_utils import run_bass_kernel_spmd

F32 = mybir.dt.float32
BF16 = mybir.dt.bfloat16
AF = mybir.ActivationFunctionType
ALU = mybir.AluOpType
AX = mybir.AxisListType

COMPUTE = ("pe", "act", "dve", "pool")
NSLOT = 12
NT = 17
NTOK = 2112
D = 1024
DFF = 2816
EPS = 1e-6


class Buf:
    __slots__ = ("name", "w", "r")

    def __init__(self, name=""):
        self.name = name
        self.w = {}
        self.r = {}


class _Op:
    __slots__ = ("fn", "waits", "signal", "dma", "val")

    def __init__(self, fn, waits, dma):
        self.fn = fn
        self.waits = waits
        self.signal = False
        self.dma = dma
        self.val = 0


class FW:
    def __init__(self, nc, stack):
        self.nc = nc
        self.streams = {k: [] for k in ("pe", "act", "dve", "pool", "sp")}
        self.known = {k: {} for k in self.streams}
        self.snapc = {k: None for k in self.streams}
        self.tl = {}
        self.slot_rr = {"sp": 0, "pool": 0}
        self.sems = {}
        for k in COMPUTE:
            self.sems[k] = stack.enter_context(nc.semaphore("s_" + k))
        for q in ("sp", "pool"):
            for s in range(NSLOT):
                sk = "d_%s_%d" % (q, s)
                self.sems[sk] = stack.enter_context(nc.semaphore(sk))
        self.sigcount = {k: 0 for k in COMPUTE}
        self.emitted = {k: 0 for k in self.streams}
        self.ninstr = 0
        self.strict = False

    def _snap(self, s):
        if self.snapc[s] is None:
            self.snapc[s] = dict(self.known[s])
        return self.snapc[s]

    def _wait(self, s, tlk, idx, waits):
        kn = self.known[s]
        if kn.get(tlk, -1) >= idx:
            return
        st, oi, snap = self.tl[tlk][idx]
        assert oi >= self.emitted[st], "dependency on already-emitted op"
        self.streams[st][oi].signal = True
        waits.append((tlk, idx))
        kn[tlk] = idx
        for k, v in snap.items():
            if kn.get(k, -1) < v:
                kn[k] = v
        self.snapc[s] = None

    def op(self, eng, fn, reads=(), writes=()):
        deps = {}
        for b in reads:
            for k, i in b.w.items():
                if deps.get(k, -1) < i:
                    deps[k] = i
        strict = self.strict and eng != "pe"
        for b in writes:
            for k, i in b.w.items():
                if (k != eng or strict) and deps.get(k, -1) < i:
                    deps[k] = i
            for k, i in b.r.items():
                if (k != eng or strict) and deps.get(k, -1) < i:
                    deps[k] = i
        waits = []
        for k, i in deps.items():
            self._wait(eng, k, i, waits)
        o = _Op(fn, waits, None)
        st = self.streams[eng]
        st.append(o)
        tl = self.tl.setdefault(eng, [])
        idx = len(tl)
        tl.append((eng, len(st) - 1, self._snap(eng)))
        for b in reads:
            b.r[eng] = idx
        for b in writes:
            b.w = {eng: idx}
            b.r = {}
        return idx

    def dma(self, q, fn, reads=(), writes=()):
        deps = {}
        for b in reads:
            for k, i in b.w.items():
                if deps.get(k, -1) < i:
                    deps[k] = i
        for b in writes:
            for k, i in b.w.items():
                if deps.get(k, -1) < i:
                    deps[k] = i
            for k, i in b.r.items():
                if deps.get(k, -1) < i:
                    deps[k] = i
        slot = self.slot_rr[q]
        self.slot_rr[q] = (slot + 1) % NSLOT
        sk = "d_%s_%d" % (q, slot)
        tl = self.tl.setdefault(sk, [])
        idx = len(tl)
        if idx > 0 and deps.get(sk, -1) < idx - 1:
            deps[sk] = idx - 1
        waits = []
        for k, i in deps.items():
            self._wait(q, k, i, waits)
        o = _Op(fn, waits, (sk, idx))
        st = self.streams[q]
        st.append(o)
        tl.append((q, len(st) - 1, self._snap(q)))
        for b in reads:
            b.r[sk] = idx
        for b in writes:
            b.w = {sk: idx}
            b.r = {}

    def _val_of(self, tlk, idx):
        if tlk.startswith("d_"):
            return 16 * (idx + 1)
        st, oi, _ = self.tl[tlk][idx]
        o = self.streams[st][oi]
        assert o.signal and o.val > 0
        return o.val

    def flush(self):
        for s in self.streams:
            waits = []
            for tlk, lst in self.tl.items():
                if lst:
                    self._wait(s, tlk, len(lst) - 1, waits)
            if waits:
                self.streams[s].append(_Op(None, waits, None))
        for k in COMPUTE:
            c = self.sigcount[k]
            for o in self.streams[k][self.emitted[k]:]:
                if o.signal:
                    c += 1
                o.val = c
            self.sigcount[k] = c
        sems = self.sems

        def run(key, e):
            ops = self.streams[key]
            for o in ops[self.emitted[key]:]:
                for (tlk, idx) in o.waits:
                    e.wait_ge(sems[tlk], self._val_of(tlk, idx))
                    self.ninstr += 1
                if o.fn is None:
                    continue
                ins = o.fn(e)
                self.ninstr += 1
                if o.dma is not None:
                    ins.then_inc(sems[o.dma[0]], 16)
                elif o.signal:
                    ins.then_inc(sems[key], 1)
            self.emitted[key] = len(ops)

        with self.nc.Block() as block:
            @block.tensor
            def _(e):
                run("pe", e)

            @block.scalar
            def _(e):
                run("act", e)

            @block.vector
            def _(e):
                run("dve", e)

            @block.gpsimd
            def _(e):
                run("pool", e)

            @block.sync
            def _(e):
                run("sp", e)


def tile_rows(i):
    return 128 if i < 16 else 64


TOK_GROUPS = [(0, 512, [0, 1, 2, 3]), (512, 512, [4, 5, 6, 7]), (1024, 512, [8, 9, 10, 11]),
              (1536, 512, [12, 13, 14, 15]), (2048, 64, [16])]
FF_GROUPS = [(0, 3), (3, 6), (6, 9), (9, 12), (12, 15), (15, 18), (18, 20), (20, 22)]


class Ctx:
    pass


def build_program(cfg):
    nc = bass.Bass("TRN2", target_bir_lowering=False)
    C = Ctx()
    C.nc = nc
    dr = {}

    def din(name, shape):
        dr[name] = nc.dram_tensor(name, list(shape), F32, kind="ExternalInput").ap()
        return dr[name]

    def dout(name, shape):
        dr[name] = nc.dram_tensor(name, list(shape), F32, kind="ExternalOutput").ap()
        return dr[name]

    din("xp", (2048, D))
    din("xs", (64, D))
    din("c_ident", (128, 128))
    for nm in ("ffn1", "ffn2"):
        din(nm + "_norm", (2, D))
        din(nm + "_w_gate", (2, D, DFF))
        din(nm + "_w_up", (2, D, DFF))
        din(nm + "_w_down", (2, DFF, D))
    din("memp", (256, D))
    din("cmk", (2, 2, 256, 256))
    din("cmv", (2, 2, 256, 256))
    din("xattn_norm", (2, D))
    din("mem_norm", (2, D))
    din("xattn_wq", (2, D, 256))
    din("xattn_wkv", (2, D, 512))
    din("xattn_q_norm", (2, 64))
    din("xattn_k_norm", (2, 64))
    din("xattn_wo", (2, 256, D))
    din("mix_norm", (2, D))
    din("ev_w_in", (1, D, 3336))
    din("fox_b_f", (1, 8))
    din("fox_k_norm", (1, 64))
    din("fox_q_norm", (1, 64))
    din("ev_w_out", (1, D, D))
    din("cfk", (2, 2048, 512))
    din("cfv", (2, 2048, 512))
    din("cfl", (2, 2048, 8))
    for nm, n_ in (("rwkv_mu", 1792), ("rwkv_w0", 512), ("rwkv_a0", 512), ("rwkv_k_k", 512), ("rwkv_k_a", 512), ("rwkv_r_k", 512),
                   ("rwkv_ln_g", 512), ("rwkv_ln_b", 512)):
        din(nm, (1, n_))
    din("rwkv_w2", (1, 64, 512))
    din("rwkv_a2", (1, 64, 512))
    din("rwkv_g2", (1, 128, 512))
    din("srw", (2, 8, 64, 64))
    din("srs", (2, 1792))
    din("c_bd", (128, 128))
    din("c_msk", (64, 3, 64))
    din("c_rst", (128, 256))
    dout("rwkv_state", (3, 8, 64, 64))
    din("c_tri", (128, 128))
    din("c_trib", (64, 64))
    din("od_w_in", (1, D, 1792))
    din("od_w_out", (1, D, D))
    din("swa_q_norm", (1, 64))
    din("swa_k_norm", (1, 64))
    din("swa_sinks", (1, 8))
    din("sgu_v_norm", (1, 512))
    din("sgu_w_s", (1, 8, 128, 128))
    din("sgu_b", (1, 8, 128))
    din("csk", (2, 128, 128))
    din("csv", (2, 128, 128))
    din("c_rope", (128, 2, NT, 8))
    dout("swa_ko", (3, 128, 128))
    dout("swa_vo", (3, 128, 128))
    dout("sgu_vo", (64, 512))
    dout("y_prompt", (2048, D))
    dout("y_sample", (64, D))
    dout("fox_k", (NTOK, 512))
    dout("fox_v", (NTOK, 512))
    dout("fox_logf", (NTOK, 8))
    dout("rwkv_shift", (3, 1792))
    dout("p_mem_k", (2, 256, 256))
    dout("p_mem_v", (2, 256, 256))

    with ExitStack() as top:
        fw = FW(nc, top)
        fw.strict = bool(cfg.get("strict", False))
        C.fw = fw

        uid = [0]

        C.sb_cur = 0
        C.sb_max = 0

        def sb(st, name, shape, dt=F32):
            uid[0] += 1
            nb = int(np.prod(shape[1:])) * (4 if dt == F32 else 2)
            nb = (nb + 31) // 32 * 32

            def _rel(nb=nb):
                C.sb_cur -= nb
            st.callback(_rel)
            C.sb_cur += nb
            if C.sb_cur > C.sb_max:
                C.sb_max = C.sb_cur
                C.sb_max_at = name
            return st.enter_context(nc.sbuf_tensor("%s_%d" % (name, uid[0]), list(shape), dt))

        def ps(st, name, shape, dt=F32):
            nbytes = int(np.prod(shape[1:])) * (4 if dt == F32 else 2)
            assert nbytes % 2048 == 0, ("psum tile must be whole banks", name, shape)
            uid[0] += 1
            return st.enter_context(nc.psum_tensor("%s_%d" % (name, uid[0]), list(shape), dt))

        X = sb(top, "X", (128, NT, D))
        bX = [Buf("X%d" % i) for i in range(NT)]
        ident_f = sb(top, "ident_f", (128, 128))
        ident = sb(top, "ident", (128, 128), BF16)
        b_ident = Buf("ident")
        fw.dma("sp", lambda e: e.dma_start(out=ident_f[:], in_=dr["c_ident"]), writes=[b_ident])
        fw.op("dve", lambda e: e.tensor_copy(out=ident[:], in_=ident_f[:]), reads=[b_ident], writes=[b_ident])
        xp_v = dr["xp"].rearrange("(t p) d -> p t d", p=128)
        for g in range(4):
            fw.dma("sp", lambda e, g=g: e.dma_start(out=X[:, 4 * g:4 * g + 4, :], in_=xp_v[:, 4 * g:4 * g + 4, :]),
                   writes=bX[4 * g:4 * g + 4])
        fw.dma("sp", lambda e: e.dma_start(out=X[0:64, 16, :], in_=dr["xs"]), writes=[bX[16]])

        def norm_T(st, gain_row_ap, xT, b_xT, pfx):
            G = sb(st, pfx + "G", (128, D))
            bG = Buf()
            fw.dma("sp", lambda e: e.dma_start(out=G[:], in_=gain_row_ap.broadcast_to([128, D])), writes=[bG])
            ss = sb(st, pfx + "ss", (128, NT))
            sd = sb(st, pfx + "sd", (128, NT))
            rs = sb(st, pfx + "rs", (128, NT))
            bss = Buf()
            junk = sb(st, pfx + "junk", (128, D))
            bj = Buf()
            fw.op("pool", lambda e: e.memset(ss[:], 1.0), writes=[bss])
            for i in range(NT):
                r = tile_rows(i)
                fw.op("act", lambda e, i=i, r=r: e.activation(out=junk[0:r, :], in_=X[0:r, i, :], func=AF.Square,
                                                              accum_out=ss[0:r, i:i + 1]),
                      reads=[bX[i]], writes=[bj, bss])
            fw.op("act", lambda e: e.activation(out=sd[:], in_=ss[:], func=AF.Sqrt, scale=1.0 / D, bias=EPS),
                  reads=[bss], writes=[bss])
            fw.op("dve", lambda e: e.reciprocal(out=rs[:], in_=sd[:]), reads=[bss], writes=[bss])
            xn = [sb(st, pfx + "xn%d" % j, (128, D), BF16) for j in range(2)]
            bxn = [Buf(), Buf()]
            ptr = [ps(st, pfx + "ptr%d" % j, (128, 8, 128), BF16) for j in range(2)]
            bptr = [Buf(), Buf()]
            for i in range(NT):
                r = tile_rows(i)
                j = i % 2
                fw.op("dve", lambda e, i=i, r=r, j=j: e.scalar_tensor_tensor(
                    out=xn[j][0:r, :], in0=X[0:r, i, :], scalar=rs[0:r, i:i + 1], in1=G[0:r, :],
                    op0=ALU.mult, op1=ALU.mult), reads=[bX[i], bss, bG], writes=[bxn[j]])
                for k in range(8):
                    fw.op("pe", lambda e, r=r, j=j, k=k: e.transpose(
                        out=ptr[j][:, k, 0:r], in_=xn[j][0:r, k * 128:(k + 1) * 128], identity=ident[0:r, 0:r]),
                        reads=[bxn[j], b_ident], writes=[bptr[j]])
                eng = "act" if i % 2 == 0 else "dve"
                if eng == "act":
                    fw.op("act", lambda e, i=i, r=r, j=j: e.copy(out=xT[:, :, i * 128:i * 128 + r], in_=ptr[j][:, :, 0:r]),
                          reads=[bptr[j]], writes=[b_xT[i // 4]])
                else:
                    fw.op("dve", lambda e, i=i, r=r, j=j: e.tensor_copy(out=xT[:, :, i * 128:i * 128 + r], in_=ptr[j][:, :, 0:r]),
                          reads=[bptr[j]], writes=[b_xT[i // 4]])

        def ffn(l, nm):
            with ExitStack() as st:
                xT = sb(st, "xT", (128, 8, NTOK), BF16)
                b_xT = [Buf() for _ in range(5)]
                with ExitStack() as st2:
                    norm_T(st2, dr[nm + "_norm"][l:l + 1, :], xT, b_xT, "n_")
                    fw.flush()
                wg_d = dr[nm + "_w_gate"][l].rearrange("(k p) n -> p k n", p=128)
                wu_d = dr[nm + "_w_up"][l].rearrange("(k p) n -> p k n", p=128)
                wd_d = dr[nm + "_w_down"][l].rearrange("(j p) n -> p j n", p=128)
                NS = 4
                stg = [sb(st, "stg%d" % i, (128, 1536)) for i in range(NS)]
                bstg = [Buf() for _ in range(NS)]
                WG = [sb(st, "WG%d" % i, (128, 8, 384), BF16) for i in range(2)]
                WU = [sb(st, "WU%d" % i, (128, 8, 384), BF16) for i in range(2)]
                WD = [sb(st, "WD%d" % i, (128, 3, D), BF16) for i in range(2)]
                bWG = [Buf(), Buf()]
                bWU = [Buf(), Buf()]
                bWD = [Buf(), Buf()]
                SG = [sb(st, "SG%d" % i, (128, 512)) for i in range(2)]
                bSG = [Buf(), Buf()]
                AT = [sb(st, "AT%d" % i, (128, 3, 512), BF16) for i in range(2)]
                bAT = [Buf(), Buf()]
                pg = [ps(st, "pg%d" % i, (128, 512)) for i in range(2)]
                pu = [ps(st, "pu%d" % i, (128, 512)) for i in range(2)]
                pd = [ps(st, "pd%d" % i, (128, 512)) for i in range(2)]
                bpg = [Buf(), Buf()]
                bpu = [Buf(), Buf()]
                bpd = [Buf(), Buf()]
                sc = [0]
                cnt = {"g": 0, "d": 0, "a": 0}

                def load_cast(src_ap, dst_ap, bdst, shape3):
                    s = sc[0] % NS
                    sc[0] += 1
                    a, b_ = shape3
                    sv = stg[s][:, 0:a * b_].rearrange("p (a b) -> p a b", a=a)
                    fw.dma("sp", lambda e: e.dma_start(out=sv, in_=src_ap), writes=[bstg[s]])
                    fw.op("pool", lambda e: e.tensor_copy(out=dst_ap, in_=sv), reads=[bstg[s]], writes=[bdst])

                def load_group(gi):
                    c0, c1 = FF_GROUPS[gi]
                    nch = c1 - c0
                    ncol = nch * 128
                    s = gi % 2
                    for h in range(2):
                        load_cast(wg_d[:, 4 * h:4 * h + 4, c0 * 128:c0 * 128 + ncol], WG[s][:, 4 * h:4 * h + 4, 0:ncol],
                                  bWG[s], (4, ncol))
                        load_cast(wu_d[:, 4 * h:4 * h + 4, c0 * 128:c0 * 128 + ncol], WU[s][:, 4 * h:4 * h + 4, 0:ncol],
                                  bWU[s], (4, ncol))
                    for j in range(nch):
                        load_cast(wd_d[:, c0 + j:c0 + j + 1, :], WD[s][:, j:j + 1, :], bWD[s], (1, D))

                load_group(0)
                for gi in range(len(FF_GROUPS)):
                    if gi + 1 < len(FF_GROUPS):
                        load_group(gi + 1)
                    c0, c1 = FF_GROUPS[gi]
                    nch = c1 - c0
                    s = gi % 2
                    for tgi, (t0, n, tiles) in enumerate(TOK_GROUPS):
                        a = cnt["a"] % 2
                        cnt["a"] += 1
                        for j in range(nch):
                            q = cnt["g"] % 2
                            cnt["g"] += 1
                            for k in range(8):
                                fw.op("pe", lambda e, q=q, s=s, k=k, j=j, t0=t0, n=n: e.matmul(
                                    pg[q][:, 0:n], lhsT=WG[s][:, k, j * 128:(j + 1) * 128], rhs=xT[:, k, t0:t0 + n],
                                    start=(k == 0), stop=(k == 7)), reads=[bWG[s], b_xT[tgi]], writes=[bpg[q]])
                            for k in range(8):
                                fw.op("pe", lambda e, q=q, s=s, k=k, j=j, t0=t0, n=n: e.matmul(
                                    pu[q][:, 0:n], lhsT=WU[s][:, k, j * 128:(j + 1) * 128], rhs=xT[:, k, t0:t0 + n],
                                    start=(k == 0), stop=(k == 7)), reads=[bWU[s], b_xT[tgi]], writes=[bpu[q]])
                            fw.op("act", lambda e, q=q, n=n: e.activation(out=SG[q][:, 0:n], in_=pg[q][:, 0:n], func=AF.Silu),
                                  reads=[bpg[q]], writes=[bSG[q]])
                            fw.op("dve", lambda e, q=q, n=n, a=a, j=j: e.tensor_tensor(
                                out=AT[a][:, j, 0:n], in0=SG[q][:, 0:n], in1=pu[q][:, 0:n], op=ALU.mult),
                                reads=[bSG[q], bpu[q]], writes=[bAT[a]])
                        for tl_i, ti in enumerate(tiles):
                            r = tile_rows(ti)
                            for half in range(2):
                                q = cnt["d"] % 2
                                cnt["d"] += 1
                                for j in range(nch):
                                    fw.op("pe", lambda e, q=q, a=a, j=j, tl_i=tl_i, r=r, half=half, s=s: e.matmul(
                                        pd[q][0:r, :], lhsT=AT[a][:, j, tl_i * 128:tl_i * 128 + r],
                                        rhs=WD[s][:, j, half * 512:(half + 1) * 512],
                                        start=(j == 0), stop=(j == nch - 1)), reads=[bAT[a], bWD[s]], writes=[bpd[q]])
                                fw.op("dve", lambda e, q=q, r=r, ti=ti, half=half: e.scalar_tensor_tensor(
                                    out=X[0:r, ti, half * 512:(half + 1) * 512], in0=pd[q][0:r, :], scalar=0.5,
                                    in1=X[0:r, ti, half * 512:(half + 1) * 512], op0=ALU.mult, op1=ALU.add),
                                    reads=[bpd[q], bX[ti]], writes=[bX[ti]])
                fw.flush()


        ones_bf = sb(top, "ones_bf", (128, 64), BF16)
        b_ones = Buf("ones")
        fw.op("pool", lambda e: e.memset(ones_bf[:], 1.0), writes=[b_ones])

        def bc3(ap2, n):
            p, h = ap2.shape
            return ap2.unsqueeze(2).broadcast_to([p, h, n])

        def bcm(ap2, h):
            p, n = ap2.shape
            return ap2.unsqueeze(1).broadcast_to([p, h, n])

        def rms_heads(st, src, bsrc, r, H, gain_tile, bgain, scale, out_bf, bout_bf, out_f=None, bout_f=None, tag=""):
            key = "rmsh_tmp"
            if key not in st.__dict__:
                st.__dict__[key] = True
                C.rh_sq = sb(st, "rh_sq", (128, 8, 64))
                C.rh_ss = sb(st, "rh_ss", (128, 8))
                C.rh_sd = sb(st, "rh_sd", (128, 8))
                C.rh_rs = sb(st, "rh_rs", (128, 8))
                C.rh_t = sb(st, "rh_t", (128, 8, 64))
                C.b_rh = Buf()
                C.b_rh2 = Buf()
            sq, ss, sd, rs, t = C.rh_sq, C.rh_ss, C.rh_sd, C.rh_rs, C.rh_t
            fw.op("act", lambda e: e.activation(out=sq[0:r, 0:H, :], in_=src, func=AF.Square), reads=[bsrc], writes=[C.b_rh])
            fw.op("dve", lambda e: e.tensor_reduce(out=ss[0:r, 0:H], in_=sq[0:r, 0:H, :], axis=AX.X, op=ALU.add),
                  reads=[C.b_rh], writes=[C.b_rh2])
            fw.op("act", lambda e: e.activation(out=sd[0:r, 0:H], in_=ss[0:r, 0:H], func=AF.Sqrt, scale=1.0 / 64, bias=EPS),
                  reads=[C.b_rh2], writes=[C.b_rh2])
            fw.op("dve", lambda e: e.reciprocal(out=rs[0:r, 0:H], in_=sd[0:r, 0:H]), reads=[C.b_rh2], writes=[C.b_rh2])
            fw.op("dve", lambda e: e.tensor_tensor(out=t[0:r, 0:H, :], in0=src, in1=bc3(rs[0:r, 0:H], 64), op=ALU.mult),
                  reads=[bsrc, C.b_rh2], writes=[C.b_rh])
            if out_f is not None:
                fw.op("dve", lambda e: e.tensor_tensor(out=out_f, in0=t[0:r, 0:H, :], in1=bcm(gain_tile[0:r, :], H), op=ALU.mult),
                      reads=[C.b_rh, bgain], writes=[bout_f])
                fw.op("act", lambda e: e.activation(out=out_bf, in_=out_f, func=AF.Copy, scale=float(scale)),
                      reads=[bout_f], writes=[bout_bf])
            else:
                fw.op("dve", lambda e: e.scalar_tensor_tensor(out=out_bf, in0=t[0:r, 0:H, :], scalar=float(scale),
                                                              in1=bcm(gain_tile[0:r, :], H), op0=ALU.mult, op1=ALU.mult),
                      reads=[C.b_rh, bgain], writes=[bout_bf])


        def rms_heads_staged(st, src, bsrc, r, H, gain_tile, bgain, scale, out_bf, bout_bf, out_f, bout_f, si):
            key = "rmsh_tmp_s%d" % si
            if key not in st.__dict__:
                st.__dict__[key] = (sb(st, "rhs_sq%d" % si, (128, 8, 64)), sb(st, "rhs_ss%d" % si, (128, 8)), sb(st, "rhs_sd%d" % si, (128, 8)),
                                    sb(st, "rhs_rs%d" % si, (128, 8)), sb(st, "rhs_t%d" % si, (128, 8, 64)), Buf(), Buf())
            sq, ss, sd, rs, t, b1, b2 = st.__dict__[key]
            th = []
            th.append(lambda: fw.op("act", lambda e: e.activation(out=sq[0:r, 0:H, :], in_=src, func=AF.Square), reads=[bsrc], writes=[b1]))
            th.append(lambda: fw.op("dve", lambda e: e.tensor_reduce(out=ss[0:r, 0:H], in_=sq[0:r, 0:H, :], axis=AX.X, op=ALU.add), reads=[b1], writes=[b2]))
            th.append(lambda: fw.op("act", lambda e: e.activation(out=sd[0:r, 0:H], in_=ss[0:r, 0:H], func=AF.Sqrt, scale=1.0 / 64, bias=EPS), reads=[b2], writes=[b2]))
            th.append(lambda: fw.op("dve", lambda e: e.reciprocal(out=rs[0:r, 0:H], in_=sd[0:r, 0:H]), reads=[b2], writes=[b2]))
            th.append(lambda: fw.op("dve", lambda e: e.tensor_tensor(out=t[0:r, 0:H, :], in0=src, in1=bc3(rs[0:r, 0:H], 64), op=ALU.mult), reads=[bsrc, b2], writes=[b1]))
            if out_f is not None:
                th.append(lambda: fw.op("dve", lambda e: e.tensor_tensor(out=out_f, in0=t[0:r, 0:H, :], in1=bcm(gain_tile[0:r, :], H), op=ALU.mult),
                                        reads=[b1, bgain], writes=[bout_f]))
                th.append(lambda: fw.op("act", lambda e: e.activation(out=out_bf, in_=out_f, func=AF.Copy, scale=float(scale)), reads=[bout_f], writes=[bout_bf]))
            else:
                th.append(lambda: fw.op("dve", lambda e: e.scalar_tensor_tensor(out=out_bf, in0=t[0:r, 0:H, :], scalar=float(scale), in1=bcm(gain_tile[0:r, :], H),
                                                                                op0=ALU.mult, op1=ALU.mult), reads=[b1, bgain], writes=[bout_bf]))
            return th

        def load_w_bf(st, src_ap, shape, name, stage_cols=2048):
            a, b_ = shape
            wt = sb(st, name, (128, a, b_), BF16)
            bw = Buf(name)
            if "wstage" not in C.__dict__ or C.wstage_owner is not st:
                C.wstage = [sb(st, "wstage%d" % i, (128, stage_cols)) for i in range(2)]
                C.bwstage = [Buf(), Buf()]
                C.wstage_owner = st
                C.wsc = 0
            per = max(1, stage_cols // b_)
            npart = src_ap.shape[0]
            a0 = 0
            while a0 < a:
                na = min(per, a - a0)
                sl = C.wsc % 2
                C.wsc += 1
                sv = C.wstage[sl][0:npart, 0:na * b_].rearrange("p (a b) -> p a b", a=na)
                fw.dma("sp", lambda e, sv=sv, a0=a0, na=na: e.dma_start(out=sv, in_=src_ap[:, a0:a0 + na, :]),
                       writes=[C.bwstage[sl]])
                fw.op("pool", lambda e, sv=sv, a0=a0, na=na: e.tensor_copy(out=wt[0:npart, a0:a0 + na, :], in_=sv),
                      reads=[C.bwstage[sl]], writes=[bw])
                a0 += na
            return wt, bw

        def bcast_row(st, row_ap, n, name):
            t = sb(st, name, (128, n))
            b = Buf(name)
            fw.dma("sp", lambda e: e.dma_start(out=t[:], in_=row_ap.broadcast_to([128, n])), writes=[b])
            return t, b

        def attn_core(st, q_ap, bq, N, kblocks, out_ap, bout, extra_den=None, nbuf=2):
            if "ac_owner" not in C.__dict__ or C.ac_owner is not st:
                C.ac_owner = st
                C.ac_S = [ps(st, "acS%d" % i, (128, 512)) for i in range(2)]
                C.ac_bS = [Buf(), Buf()]
                C.ac_num = [ps(st, "acN%d" % i, (64, 512)) for i in range(nbuf)]
                C.ac_den = [ps(st, "acD%d" % i, (64, 512)) for i in range(nbuf)]
                C.ac_bnd = [Buf() for _ in range(nbuf)]
                C.ac_nbuf = nbuf
                C.ac_P = [sb(st, "acP%d" % i, (128, 512), BF16) for i in range(3)]
                C.ac_bP = [Buf(), Buf(), Buf()]
                C.ac_rd = [sb(st, "acR%d" % i, (64, 512)) for i in range(2)]
                C.ac_brd = [Buf(), Buf()]
                C.ac_c = [0, 0, 0]
            o = C.ac_c[1] % C.ac_nbuf
            o2 = C.ac_c[1] % 2
            C.ac_c[1] += 1
            num, den, bnd = C.ac_num[o], C.ac_den[o], C.ac_bnd[o]
            nb = len(kblocks)
            for bi, kb in enumerate(kblocks):
                si = C.ac_c[0] % 2
                C.ac_c[0] += 1
                pi = C.ac_c[2] % 3
                C.ac_c[2] += 1
                S, bS, P, bP = C.ac_S[si], C.ac_bS[si], C.ac_P[pi], C.ac_bP[pi]
                nk, c0 = kb["nk"], kb["col0"]
                fw.op("pe", lambda e, S=S, kb=kb, nk=nk, c0=c0: e.matmul(S[0:nk, c0:N], lhsT=kb["kT"], rhs=q_ap[:, c0:N],
                                                                        start=True, stop=True),
                      reads=[bq] + kb["reads"], writes=[bS])
                fw.op("act", lambda e, S=S, P=P, nk=nk, c0=c0: e.activation(out=P[0:nk, c0:N], in_=S[0:nk, c0:N], func=AF.Exp),
                      reads=[bS], writes=[bP])
                if kb.get("mask") is not None:
                    mw = kb.get("mask_w", 128)
                    fw.op("pool", lambda e, P=P, kb=kb, nk=nk, c0=c0, mw=mw: e.tensor_tensor(
                        out=P[0:nk, c0:c0 + mw], in0=P[0:nk, c0:c0 + mw], in1=kb["mask"], op=ALU.mult),
                        reads=[bP] + kb.get("mask_reads", []), writes=[bP])
                fw.op("pe", lambda e, P=P, kb=kb, nk=nk, c0=c0, bi=bi: e.matmul(num[:, c0:N], lhsT=kb["v"], rhs=P[0:nk, c0:N],
                                                                               start=(bi == 0), stop=(bi == nb - 1)),
                      reads=[bP] + kb["reads"], writes=[bnd])
                fw.op("pe", lambda e, P=P, nk=nk, c0=c0, bi=bi: e.matmul(den[:, c0:N], lhsT=ones_bf[0:nk, :], rhs=P[0:nk, c0:N],
                                                                        start=(bi == 0), stop=(bi == nb - 1)),
                      reads=[bP, b_ones], writes=[bnd])
            rd, brd = C.ac_rd[o2], C.ac_brd[o2]
            if extra_den is not None:
                ed_ap, ed_reads = extra_den
                fw.op("dve", lambda e: e.tensor_scalar(out=rd[:, 0:N], in0=den[:, 0:N], scalar1=ed_ap, scalar2=None, op0=ALU.add),
                      reads=[bnd] + ed_reads, writes=[brd])
                fw.op("dve", lambda e: e.reciprocal(out=rd[:, 0:N], in_=rd[:, 0:N]), reads=[brd], writes=[brd])
            else:
                fw.op("dve", lambda e: e.reciprocal(out=rd[:, 0:N], in_=den[:, 0:N]), reads=[bnd], writes=[brd])
            fw.op("dve", lambda e: e.tensor_tensor(out=out_ap, in0=num[:, 0:N], in1=rd[:, 0:N], op=ALU.mult),
                  reads=[bnd, brd], writes=[bout])

        def xattn(l):
            with ExitStack() as st:
                xT = sb(st, "xT", (128, 8, NTOK), BF16)
                b_xT = [Buf() for _ in range(5)]
                with ExitStack() as st2:
                    norm_T(st2, dr["xattn_norm"][l:l + 1, :], xT, b_xT, "n_")
                    fw.flush()
                Wq, bWq = load_w_bf(st, dr["xattn_wq"][l].rearrange("(k p) n -> p k n", p=128), (8, 256), "Wq")
                Wkv, bWkv = load_w_bf(st, dr["xattn_wkv"][l].rearrange("(k p) n -> p k n", p=128), (8, 512), "Wkv")
                Wo, bWo = load_w_bf(st, dr["xattn_wo"][l].rearrange("(h d) n -> d h n", d=64)[:, :, :], (4, D), "Wo")
                Gq, bGq = bcast_row(st, dr["xattn_q_norm"][l:l + 1, :], 64, "Gq")
                Gk, bGk = bcast_row(st, dr["xattn_k_norm"][l:l + 1, :], 64, "Gk")
                Gm, bGm = bcast_row(st, dr["mem_norm"][l:l + 1, :], D, "Gm")
                KT = sb(st, "KT", (64, 3, 4, 256), BF16)
                bKT = [Buf(), Buf(), Buf()]
                VV = sb(st, "VV", (128, 3, 2, 256), BF16)
                bVV = [Buf(), Buf(), Buf()]
                ptk = ps(st, "ptk", (64, 8, 128), BF16)
                bptk = Buf()
                with ExitStack() as st3:
                    pkv = ps(st3, "pkv", (128, 512))
                    bpkv = Buf()
                    mem = sb(st3, "mem", (128, 2, D))
                    bmem = Buf()
                    fw.dma("sp", lambda e: e.dma_start(out=mem[:], in_=dr["memp"].rearrange("(t p) d -> p t d", p=128)), writes=[bmem])
                    mss = sb(st3, "mss", (128, 4))
                    bmss = Buf()
                    mjunk = sb(st3, "mjunk", (128, D))
                    bmj = Buf()
                    mn = sb(st3, "mn", (128, D), BF16)
                    bmn = Buf()
                    memT = sb(st3, "memT", (128, 8, 256), BF16)
                    bmemT = Buf()
                    pmt = ps(st3, "pmt", (128, 8, 128), BF16)
                    bpmt = Buf()
                    knf = sb(st3, "knf", (128, 4, 64))
                    bknf = Buf()
                    knb = sb(st3, "knb", (128, 4, 64), BF16)
                    bknb = Buf()
                    vf = sb(st3, "vf", (128, 256))
                    bvf = Buf()
                    for t in range(2):
                        fw.op("act", lambda e, t=t: e.activation(out=mjunk[:], in_=mem[:, t, :], func=AF.Square, accum_out=mss[:, t:t + 1]),
                              reads=[bmem], writes=[bmj, bmss])
                    fw.op("act", lambda e: e.activation(out=mss[:, 2:4], in_=mss[:, 0:2], func=AF.Sqrt, scale=1.0 / D, bias=EPS),
                          reads=[bmss], writes=[bmss])
                    fw.op("dve", lambda e: e.reciprocal(out=mss[:, 0:2], in_=mss[:, 2:4]), reads=[bmss], writes=[bmss])
                    for t in range(2):
                        fw.op("dve", lambda e, t=t: e.scalar_tensor_tensor(out=mn[:], in0=mem[:, t, :], scalar=mss[:, t:t + 1], in1=Gm[:],
                                                                            op0=ALU.mult, op1=ALU.mult), reads=[bmem, bmss, bGm], writes=[bmn])
                        for k in range(8):
                            fw.op("pe", lambda e, k=k: e.transpose(out=pmt[:, k, :], in_=mn[:, k * 128:(k + 1) * 128], identity=ident[:]),
                                  reads=[bmn, b_ident], writes=[bpmt])
                        fw.op("act", lambda e, t=t: e.copy(out=memT[:, :, t * 128:(t + 1) * 128], in_=pmt[:]), reads=[bpmt], writes=[bmemT])
                    for t in range(2):
                        for k in range(8):
                            fw.op("pe", lambda e, t=t, k=k: e.matmul(pkv[:], lhsT=memT[:, k, t * 128:(t + 1) * 128], rhs=Wkv[:, k, :],
                                                                      start=(k == 0), stop=(k == 7)), reads=[bmemT, bWkv], writes=[bpkv])
                        rms_heads(st3, pkv[:, 0:256].rearrange("p (h d) -> p h d", h=4), bpkv, 128, 4, Gk, bGk, 1.0,
                                  knb[:], bknb, out_f=knf[:], bout_f=bknf)
                        fw.op("act", lambda e: e.copy(out=vf[:], in_=pkv[:, 256:512]), reads=[bpkv], writes=[bvf])
                        fw.op("dve", lambda e, t=t: e.tensor_copy(out=VV[:, 0, t, :], in_=vf[:]), reads=[bvf], writes=[bVV[0]])
                        fw.dma("sp", lambda e, t=t: e.dma_start(out=dr["p_mem_k"][l, t * 128:(t + 1) * 128, :],
                                                                in_=knf[:].rearrange("p h d -> p (h d)")), reads=[bknf])
                        fw.dma("sp", lambda e, t=t: e.dma_start(out=dr["p_mem_v"][l, t * 128:(t + 1) * 128, :], in_=vf[:]), reads=[bvf])
                        for h in range(4):
                            fw.op("pe", lambda e, h=h: e.transpose(out=ptk[:, h, :], in_=knb[:, h, :], identity=ident[:]),
                                  reads=[bknb, b_ident], writes=[bptk])
                        fw.op("act", lambda e, t=t: e.copy(out=KT[:, 0, :, t * 128:(t + 1) * 128], in_=ptk[:, 0:4, :]), reads=[bptk], writes=[bKT[0]])
                    ck = sb(st3, "ck", (128, 2, 256))
                    bck = Buf()
                    ckb = sb(st3, "ckb", (128, 2, 4, 64), BF16)
                    bckb = Buf()
                    cv = sb(st3, "cv", (128, 2, 256))
                    bcv = Buf()
                    for b in range(2):
                        fw.dma("sp", lambda e, b=b: e.dma_start(out=ck[:], in_=dr["cmk"][l, b].rearrange("(t p) d -> p t d", p=128)), writes=[bck])
                        fw.dma("sp", lambda e, b=b: e.dma_start(out=cv[:], in_=dr["cmv"][l, b].rearrange("(t p) d -> p t d", p=128)), writes=[bcv])
                        fw.op("pool", lambda e: e.tensor_copy(out=ckb[:].rearrange("p t h d -> p t (h d)"), in_=ck[:]), reads=[bck], writes=[bckb])
                        fw.op("pool", lambda e, b=b: e.tensor_copy(out=VV[:, 1 + b, :, :], in_=cv[:]), reads=[bcv], writes=[bVV[1 + b]])
                        for t in range(2):
                            for h in range(4):
                                fw.op("pe", lambda e, t=t, h=h: e.transpose(out=ptk[:, h, :], in_=ckb[:, t, h, :], identity=ident[:]),
                                      reads=[bckb, b_ident], writes=[bptk])
                            fw.op("act", lambda e, t=t, b=b: e.copy(out=KT[:, 1 + b, :, t * 128:(t + 1) * 128], in_=ptk[:, 0:4, :]),
                                  reads=[bptk], writes=[bKT[1 + b]])

                    fw.flush()
                QT = sb(st, "QT", (64, 4, NTOK), BF16)
                bQT = [Buf() for _ in range(5)]
                OT = sb(st, "OT", (64, 4, NTOK), BF16)
                bOT = [Buf() for _ in range(5)]
                pq = [ps(st, "pq%d" % i, (128, 512)) for i in range(1)]
                bpq = [Buf()]
                qnb = sb(st, "qnb", (128, 4, 64), BF16)
                bqnb = Buf()
                for i in range(NT):
                    r = tile_rows(i)
                    for k in range(8):
                        fw.op("pe", lambda e, i=i, r=r, k=k: e.matmul(pq[0][0:r, 0:256], lhsT=xT[:, k, i * 128:i * 128 + r], rhs=Wq[:, k, :],
                                                                       start=(k == 0), stop=(k == 7)), reads=[b_xT[i // 4], bWq], writes=[bpq[0]])
                    rms_heads(st, pq[0][0:r, 0:256].rearrange("p (h d) -> p h d", h=4), bpq[0], r, 4, Gq, bGq, 0.125, qnb[0:r], bqnb)
                    for h in range(4):
                        fw.op("pe", lambda e, h=h, r=r: e.transpose(out=ptk[:, h, 0:r], in_=qnb[0:r, h, :], identity=ident[0:r, 0:r]),
                              reads=[bqnb, b_ident], writes=[bptk])
                    fw.op("act", lambda e, i=i, r=r: e.copy(out=QT[:, :, i * 128:i * 128 + r], in_=ptk[:, 0:4, 0:r]),
                          reads=[bptk], writes=[bQT[i // 4]])
                for h in range(4):
                    for tgi, (t0, n, tiles) in enumerate(TOK_GROUPS):
                        if tgi < 4:
                            segs = [(t0, n, 0)]
                        else:
                            segs = [(2048, 32, 1), (2080, 32, 2)]
                        for (q0, nq, sq_) in segs:
                            kbs = [dict(kT=KT[:, sq_, h, kt * 128:(kt + 1) * 128], v=VV[:, sq_, kt, h * 64:(h + 1) * 64], nk=128, col0=0,
                                        reads=[bKT[sq_], bVV[sq_]]) for kt in range(2)]
                            attn_core(st, QT[:, h, q0:q0 + nq], bQT[tgi], nq, kbs, OT[:, h, q0:q0 + nq], bOT[tgi])
                po = C.ac_S
                bpo = C.ac_bS
                c = 0
                for i in range(NT):
                    r = tile_rows(i)
                    for half in range(2):
                        q = c % 2
                        c += 1
                        for h in range(4):
                            fw.op("pe", lambda e, q=q, i=i, r=r, h=h, half=half: e.matmul(
                                po[q][0:r, :], lhsT=OT[:, h, i * 128:i * 128 + r], rhs=Wo[0:64, h, half * 512:(half + 1) * 512],
                                start=(h == 0), stop=(h == 3)), reads=[bOT[i // 4], bWo], writes=[bpo[q]])
                        fw.op("dve", lambda e, q=q, r=r, i=i, half=half: e.tensor_tensor(
                            out=X[0:r, i, half * 512:(half + 1) * 512], in0=po[q][0:r, :], in1=X[0:r, i, half * 512:(half + 1) * 512],
                            op=ALU.add), reads=[bpo[q], bX[i]], writes=[bX[i]])
                fw.flush()


        def fox_part(l, xT, b_xT):
            e_ = l // 2
            wv = dr["ev_w_in"][e_].rearrange("(k p) n -> p k n", p=128)
            with ExitStack() as st:
                Gq, bGq = bcast_row(st, dr["fox_q_norm"][e_:e_ + 1, :], 64, "Gfq")
                Gk, bGk = bcast_row(st, dr["fox_k_norm"][e_:e_ + 1, :], 64, "Gfk")
                Bf, bBf = bcast_row(st, dr["fox_b_f"][e_:e_ + 1, :], 8, "Bf")
                TRI = sb(st, "TRI", (128, 128))
                TRIB = sb(st, "TRIB", (64, 64))
                ONESF = sb(st, "ONESF", (128, 128))
                MASKT = sb(st, "MASKT", (128, 128), BF16)
                bcon = Buf()
                fw.dma("sp", lambda e: e.dma_start(out=TRI[:], in_=dr["c_tri"]), writes=[bcon])
                fw.dma("sp", lambda e: e.dma_start(out=TRIB[:], in_=dr["c_trib"]), writes=[bcon])
                fw.op("pool", lambda e: e.memset(ONESF[:], 1.0), writes=[bcon])
                fw.op("dve", lambda e: e.tensor_copy(out=MASKT[:], in_=TRI[:]), reads=[bcon], writes=[bcon])
                CUM3 = sb(st, "CUM3", (128, NT, 3, 8), BF16)
                CUM3N = sb(st, "CUM3N", (128, NT, 3, 8), BF16)
                bCUM = Buf()
                CC3N = sb(st, "CC3N", (128, 2, 16, 3, 8), BF16)
                bCC = Buf()
                OFF = sb(st, "OFF", (128, 8))
                bOFF = Buf()
                OFFC = sb(st, "OFFC", (128, 2, 8))
                bOFFC = Buf()
                OFF16 = sb(st, "OFF16", (64, 8))
                bOFF16 = Buf()
                LFA = sb(st, "LFA", (128, NT, 8))
                bLFA = Buf()
                CL = sb(st, "CL", (128, 2, 16, 8))
                bCL = Buf()
                pm = ps(st, "pm", (128, 512))
                bpm = Buf()
                ctmp = sb(st, "ctmp", (128, 4, 8))
                bct = Buf()
                CSCR = sb(st, "CSCR", (128, 3, 8), BF16)
                ptk = ps(st, "ptkf", (72, 8, 128), BF16)
                bptk = Buf()
                pp = ps(st, "ppf", (128, 512))
                bpp = Buf()
                pp2 = ps(st, "ppf2", (128, 512))
                bpp2 = Buf()

                def split3(src_ap, r, dst3, dst3n, bdst):
                    t = ctmp
                    fw.op("dve", lambda e: e.tensor_copy(out=dst3[0:r, 0, :], in_=src_ap), reads=[bct], writes=[bdst])
                    fw.op("dve", lambda e: e.tensor_tensor(out=t[0:r, 1, :], in0=src_ap, in1=dst3[0:r, 0, :], op=ALU.subtract),
                          reads=[bct, bdst], writes=[bct])
                    fw.op("dve", lambda e: e.tensor_copy(out=dst3[0:r, 1, :], in_=t[0:r, 1, :]), reads=[bct], writes=[bdst])
                    fw.op("dve", lambda e: e.tensor_tensor(out=t[0:r, 2, :], in0=t[0:r, 1, :], in1=dst3[0:r, 1, :], op=ALU.subtract),
                          reads=[bct, bdst], writes=[bct])
                    fw.op("dve", lambda e: e.tensor_copy(out=dst3[0:r, 2, :], in_=t[0:r, 2, :]), reads=[bct], writes=[bdst])
                    if dst3n is not None:
                        fw.op("dve", lambda e: e.tensor_scalar(out=dst3n[0:r], in0=dst3[0:r], scalar1=-1.0, scalar2=None, op0=ALU.mult),
                              reads=[bdst], writes=[bdst])

                def cum_tile(lf_ap, blf, r, tri_ap, off_ap, boff, dst3, dst3n, bdst, update_off):
                    if not cfg.get("fox_cum", True):
                        return
                    fw.op("pe", lambda e: e.matmul(pm[0:r, 8:16], lhsT=tri_ap, rhs=lf_ap, start=True, stop=True),
                          reads=[blf, bcon], writes=[bpm])
                    if update_off:
                        fw.op("pe", lambda e: e.matmul(pm[0:r, 16:24], lhsT=ONESF[0:r, 0:r], rhs=lf_ap, start=True, stop=True),
                              reads=[blf, bcon], writes=[bpm])
                    fw.op("dve", lambda e: e.tensor_tensor(out=ctmp[0:r, 0, :], in0=pm[0:r, 8:16], in1=off_ap, op=ALU.add),
                          reads=[bpm, boff], writes=[bct])
                    if update_off:
                        fw.op("dve", lambda e: e.tensor_tensor(out=off_ap, in0=pm[0:r, 16:24], in1=off_ap, op=ALU.add),
                              reads=[bpm, boff], writes=[boff])
                    split3(ctmp[0:r, 0, :], r, dst3, dst3n, bdst)

                fw.op("pool", lambda e: e.memset(OFF[:], 0.0), writes=[bOFF])
                fw.op("pool", lambda e: e.memset(OFFC[:], 0.0), writes=[bOFFC])
                for b in range(2):
                    for t in range(16):
                        fw.dma("sp", lambda e, b=b, t=t: e.dma_start(out=CL[:, b, t, :], in_=dr["cfl"][b, t * 128:(t + 1) * 128, :]), writes=[bCL])
                for hg in range(2):
                    with ExitStack() as sh:
                        QT = sb(sh, "QT", (72, 4, NTOK), BF16)
                        bQT = [[Buf() for _ in range(6)] for _ in range(4)]
                        KT = sb(sh, "KT", (72, 4, NTOK), BF16)
                        bKT = [Buf() for _ in range(5)]
                        VA = sb(sh, "VA", (128, NT, 256), BF16)
                        bVA = [Buf() for _ in range(5)]
                        VAn = sb(sh, "VAn", (32, 2, 256), BF16)
                        bVAn = Buf()
                        QA = [sb(sh, "QA%d" % i, (128, 4, 72), BF16) for i in range(2)]
                        KA = [sb(sh, "KA%d" % i, (128, 4, 72), BF16) for i in range(2)]
                        bQA = [Buf(), Buf()]
                        bKA = [Buf(), Buf()]
                        for j in range(2):
                            fw.op("pool", lambda e, j=j: e.memset(QA[j][:], 1.0), writes=[bQA[j]])
                            fw.op("pool", lambda e, j=j: e.memset(KA[j][:], 1.0), writes=[bKA[j]])
                        sw = ExitStack()
                        Wq4, bWq4 = load_w_bf(sw, wv[:, :, hg * 256:hg * 256 + 256], (8, 256), "Wq4", stage_cols=1024)
                        Wk4, bWk4 = load_w_bf(sw, wv[:, :, 512 + hg * 256:512 + hg * 256 + 256], (8, 256), "Wk4", stage_cols=1024)
                        Wv4, bWv4 = load_w_bf(sw, wv[:, :, 1024 + hg * 256:1024 + hg * 256 + 256], (8, 256), "Wv4", stage_cols=1024)
                        if hg == 0:
                            Wff, bWff = load_w_bf(sw, wv[:, :, 1536:1544], (8, 8), "Wff", stage_cols=1024)
                        knf = [sb(sw, "fknf%d" % i, (128, 4, 64)) for i in range(2)]
                        bknf = [Buf(), Buf()]
                        vf = [sb(sw, "fvf%d" % i, (128, 256)) for i in range(2)]
                        bvf = [Buf(), Buf()]
                        for i in range(cfg.get("fox_ntiles", NT)):
                            r = tile_rows(i)
                            j = i % 2
                            rows = slice(i * 128, i * 128 + r)
                            g5 = i // 4
                            if hg == 0:
                                for k in range(8):
                                    fw.op("pe", lambda e, k=k, i=i, r=r: e.matmul(
                                        pm[0:r, 0:8], lhsT=xT[:, k, i * 128:i * 128 + r], rhs=Wff[:, k, :],
                                        start=(k == 0), stop=(k == 7)), reads=[b_xT[g5], bWff], writes=[bpm])
                                lf = LFA[0:r, i, :]
                                fw.op("dve", lambda e, lf=lf, r=r: e.tensor_tensor(out=lf, in0=pm[0:r, 0:8], in1=Bf[0:r, :], op=ALU.add),
                                      reads=[bpm, bBf], writes=[bLFA])
                                fw.op("act", lambda e, lf=lf: e.activation(out=lf, in_=lf, func=AF.Exp, scale=-1.0), reads=[bLFA], writes=[bLFA])
                                fw.op("act", lambda e, lf=lf: e.activation(out=lf, in_=lf, func=AF.Ln, bias=1.0), reads=[bLFA], writes=[bLFA])
                                fw.op("dve", lambda e, lf=lf: e.tensor_scalar(out=lf, in0=lf, scalar1=-1.0, scalar2=None, op0=ALU.mult),
                                      reads=[bLFA], writes=[bLFA])
                                fw.dma("sp", lambda e, lf=lf, rows=rows: e.dma_start(out=dr["fox_logf"][rows, :], in_=lf), reads=[bLFA])
                                if i < 16:
                                    cum_tile(lf, bLFA, 128, TRI[:], OFF[:], bOFF, CUM3[:, i], CUM3N[:, i], bCUM, True)
                                else:
                                    for b in range(2):
                                        for t in range(16):
                                            cum_tile(CL[:, b, t, :], bCL, 128, TRI[:], OFFC[:, b, :], bOFFC, CSCR[:], CC3N[:, b, t], bCC, True)
                                        fw.op("dve", lambda e, b=b: e.tensor_copy(out=OFF16[32 * b:32 * b + 32, :], in_=OFFC[32 * b:32 * b + 32, b, :]),
                                              reads=[bOFFC], writes=[bOFF16])
                                    cum_tile(lf, bLFA, 64, TRIB[:], OFF16[:], bOFF16, CUM3[:, 16], CUM3N[:, 16], bCUM, False)
                            if cfg.get("fox_stage", 9) < 2:
                                continue
                            for k in range(8):
                                fw.op("pe", lambda e, k=k, i=i, r=r: e.matmul(
                                    pp[0:r, 0:256], lhsT=xT[:, k, i * 128:i * 128 + r], rhs=Wq4[:, k, :],
                                    start=(k == 0), stop=(k == 7)), reads=[b_xT[g5], bWq4], writes=[bpp])
                            for k in range(8):
                                fw.op("pe", lambda e, k=k, i=i, r=r: e.matmul(
                                    pp2[0:r, 0:256], lhsT=xT[:, k, i * 128:i * 128 + r], rhs=Wk4[:, k, :],
                                    start=(k == 0), stop=(k == 7)), reads=[b_xT[g5], bWk4], writes=[bpp2])
                            thq = rms_heads_staged(sw, pp[0:r, 0:256].rearrange("p (h d) -> p h d", h=4), bpp, r, 4, Gq, bGq, 0.125,
                                                   QA[j][0:r, :, 0:64], bQA[j], None, None, 0)
                            thk = rms_heads_staged(sw, pp2[0:r, 0:256].rearrange("p (h d) -> p h d", h=4), bpp2, r, 4, Gk, bGk, 1.0,
                                                   KA[j][0:r, :, 0:64], bKA[j], knf[j][0:r], bknf[j], 1)
                            for ti_ in range(max(len(thq), len(thk))):
                                if ti_ < len(thq):
                                    thq[ti_]()
                                if ti_ < len(thk):
                                    thk[ti_]()
                            for c3 in range(3):
                                fw.op("dve", lambda e, j=j, r=r, i=i, c3=c3: e.tensor_copy(
                                    out=QA[j][0:r, :, 64 + c3], in_=CUM3[0:r, i, c3, 4 * hg:4 * hg + 4]), reads=[bCUM], writes=[bQA[j]])
                            for c3 in range(3):
                                fw.op("dve", lambda e, j=j, r=r, i=i, c3=c3: e.tensor_copy(
                                    out=KA[j][0:r, :, 67 + c3], in_=CUM3N[0:r, i, c3, 4 * hg:4 * hg + 4]), reads=[bCUM], writes=[bKA[j]])
                            for h in range(4):
                                fw.op("pe", lambda e, h=h, r=r, j=j: e.transpose(out=ptk[0:72, h, 0:r], in_=QA[j][0:r, h, :], identity=ident[0:r, 0:r]),
                                      reads=[bQA[j], b_ident], writes=[bptk])
                            qbufs = [bQT[h][g5] for h in range(4)] if i < 16 else [bQT[h][4] for h in range(4)] + [bQT[h][5] for h in range(4)]
                            fw.op("act", lambda e, i=i, r=r: e.copy(out=QT[:, :, i * 128:i * 128 + r], in_=ptk[:, 0:4, 0:r]),
                                  reads=[bptk], writes=qbufs)
                            fw.dma("sp", lambda e, j=j, r=r, rows=rows: e.dma_start(
                                out=dr["fox_k"][rows, hg * 256:hg * 256 + 256], in_=knf[j][0:r].rearrange("p h d -> p (h d)")), reads=[bknf[j]])
                            for h in range(4):
                                fw.op("pe", lambda e, h=h, r=r, j=j: e.transpose(out=ptk[0:72, h + 4, 0:r], in_=KA[j][0:r, h, :], identity=ident[0:r, 0:r]),
                                      reads=[bKA[j], b_ident], writes=[bptk])
                            fw.op("act", lambda e, i=i, r=r: e.copy(out=KT[:, :, i * 128:i * 128 + r], in_=ptk[:, 4:8, 0:r]),
                                  reads=[bptk], writes=[bKT[g5]])
                            if cfg.get("fox_stage", 9) < 4:
                                continue
                            for k in range(8):
                                fw.op("pe", lambda e, k=k, i=i, r=r: e.matmul(
                                    pp[0:r, 0:256], lhsT=xT[:, k, i * 128:i * 128 + r], rhs=Wv4[:, k, :],
                                    start=(k == 0), stop=(k == 7)), reads=[b_xT[g5], bWv4], writes=[bpp])
                            fw.op("act", lambda e, j=j, r=r: e.copy(out=vf[j][0:r, :], in_=pp[0:r, 0:256]), reads=[bpp], writes=[bvf[j]])
                            fw.op("dve", lambda e, i=i, r=r, j=j: e.tensor_copy(out=VA[0:r, i, :], in_=vf[j][0:r, :]), reads=[bvf[j]], writes=[bVA[g5]])
                            fw.dma("sp", lambda e, j=j, r=r, rows=rows: e.dma_start(out=dr["fox_v"][rows, hg * 256:hg * 256 + 256], in_=vf[j][0:r, :]),
                                   reads=[bvf[j]])
                            if i == 16:
                                for b in range(2):
                                    fw.op("dve", lambda e, b=b: e.tensor_copy(out=VAn[:, b, :], in_=VA[32 * b:32 * b + 32, 16, :]),
                                          reads=[bVA[4]], writes=[bVAn])
                        fw.flush()
                        sw.close()
                        if not (cfg.get("fox_pattn", True) or cfg.get("fox_sattn", True)):
                            continue
                        Wo4, bWo4 = load_w_bf(sh, dr["ev_w_out"][e_][hg * 256:hg * 256 + 256, :].rearrange("(h d) n -> d h n", d=64),
                                              (4, D), "Wo4", stage_cols=1024)
                        OTg = [sb(sh, "OTg%d" % i, (64, 4, 512), BF16) for i in range(2)]
                        bOTg = [Buf(), Buf()]

                        def out_proj(OT_t, bOT_t, tiles):
                            po, bpo = C.ac_S, C.ac_bS
                            for tl_i, i in enumerate(tiles):
                                r = tile_rows(i)
                                for half in range(2):
                                    q = C.opc % 2
                                    C.opc += 1
                                    for h in range(4):
                                        fw.op("pe", lambda e, q=q, tl_i=tl_i, r=r, h=h, half=half: e.matmul(
                                            po[q][0:r, :], lhsT=OT_t[:, h, tl_i * 128:tl_i * 128 + r], rhs=Wo4[0:64, h, half * 512:(half + 1) * 512],
                                            start=(h == 0), stop=(h == 3)), reads=[bOT_t, bWo4], writes=[bpo[q]])
                                    fw.op("dve", lambda e, q=q, r=r, i=i, half=half: e.tensor_tensor(
                                        out=X[0:r, i, half * 512:(half + 1) * 512], in0=po[q][0:r, :], in1=X[0:r, i, half * 512:(half + 1) * 512],
                                        op=ALU.add), reads=[bpo[q], bX[i]], writes=[bX[i]])
                        C.opc = 0
                        for g in (range(4) if cfg.get("fox_pattn", True) else []):
                            s_ = g % 2
                            for h in range(4):
                                kbs = []
                                for kt in range(4 * g + 4):
                                    c0 = max(0, kt * 128 - g * 512)
                                    kbs.append(dict(kT=KT[0:70, h, kt * 128:(kt + 1) * 128], v=VA[:, kt, h * 64:(h + 1) * 64], nk=128, col0=c0,
                                                    reads=[bKT[kt // 4], bVA[kt // 4]],
                                                    mask=(MASKT[:] if kt >= 4 * g else None), mask_reads=[bcon]))
                                attn_core(sh, QT[0:70, h, g * 512:(g + 1) * 512], bQT[h][g], 512, kbs, OTg[s_][:, h, :], bOTg[s_], nbuf=1)
                            out_proj(OTg[s_], bOTg[s_], [4 * g, 4 * g + 1, 4 * g + 2, 4 * g + 3])
                        ckf = sb(sh, "ckf", (128, 4, 256))
                        bckf = Buf()
                        cvf = sb(sh, "cvf", (128, 4, 256))
                        bcvf = Buf()
                        for b in (range(2) if cfg.get("fox_sattn", True) else []):
                            for t4 in range(4):
                                fw.dma("sp", lambda e, b=b, t4=t4: e.dma_start(
                                    out=ckf[:], in_=dr["cfk"][b, t4 * 512:(t4 + 1) * 512, hg * 256:hg * 256 + 256].rearrange("(t p) d -> p t d", p=128)),
                                    writes=[bckf])
                                fw.dma("sp", lambda e, b=b, t4=t4: e.dma_start(
                                    out=cvf[:], in_=dr["cfv"][b, t4 * 512:(t4 + 1) * 512, hg * 256:hg * 256 + 256].rearrange("(t p) d -> p t d", p=128)),
                                    writes=[bcvf])
                                fw.op("pool", lambda e, t4=t4: e.tensor_copy(out=VA[:, 4 * t4:4 * t4 + 4, :], in_=cvf[:]), reads=[bcvf], writes=[bVA[t4]])
                                for tt in range(4):
                                    t = 4 * t4 + tt
                                    j = t % 2
                                    fw.op("pool", lambda e, j=j, tt=tt: e.tensor_copy(
                                        out=KA[j][:, :, 0:64], in_=ckf[:, tt, :].rearrange("p (h d) -> p h d", h=4)), reads=[bckf], writes=[bKA[j]])
                                    for c3 in range(3):
                                        fw.op("dve", lambda e, j=j, b=b, t=t, c3=c3: e.tensor_copy(
                                            out=KA[j][:, :, 67 + c3], in_=CC3N[:, b, t, c3, 4 * hg:4 * hg + 4]), reads=[bCC], writes=[bKA[j]])
                                    for h in range(4):
                                        fw.op("pe", lambda e, h=h, j=j: e.transpose(out=ptk[0:72, h, :], in_=KA[j][:, h, :], identity=ident[:]),
                                              reads=[bKA[j], b_ident], writes=[bptk])
                                    fw.op("act", lambda e, t=t: e.copy(out=KT[:, :, t * 128:(t + 1) * 128], in_=ptk[:, 0:4, :]),
                                          reads=[bptk], writes=[bKT[t4]])
                            for h in range(4):
                                q0 = 2048 + 32 * b
                                kbs = [dict(kT=KT[0:70, h, kt * 128:(kt + 1) * 128], v=VA[:, kt, h * 64:(h + 1) * 64], nk=128, col0=0,
                                            reads=[bKT[kt // 4], bVA[kt // 4]]) for kt in range(16)]
                                kbs.append(dict(kT=KT[0:70, h, q0:q0 + 32], v=VAn[:, b, h * 64:(h + 1) * 64], nk=32, col0=0,
                                                reads=[bKT[4], bVAn], mask=MASKT[0:32, 0:32], mask_reads=[bcon], mask_w=32))
                                attn_core(sh, QT[0:70, h, q0:q0 + 32], bQT[h][4 + b], 32, kbs, OTg[0][:, h, 32 * b:32 * b + 32], bOTg[0], nbuf=1)
                        if cfg.get("fox_sattn", True):
                            out_proj(OTg[0], bOTg[0], [16])
                        fw.flush()


        def rwkv_part(l, xT, b_xT):
            e_ = l // 2
            wv = dr["ev_w_in"][e_].rearrange("(k p) n -> p k n", p=128)
            NG = 128
            with ExitStack() as st:
                pA = ps(st, "rpA", (128, 512))
                pB = ps(st, "rpB", (128, 512))
                pT1 = ps(st, "rpT1", (128, 1024), BF16)
                pT2 = ps(st, "rpT2", (128, 1024), BF16)
                pGK = ps(st, "rpGK", (64, 1024))
                pGB = ps(st, "rpGB", (64, 1024))
                bpA, bpB, bpT1, bpT2, bpGK0, bpGK1, bpGB0, bpGB1 = [Buf() for _ in range(8)]
                PC = sb(st, "PC", (128, 70))
                bPC = Buf()
                OM = sb(st, "OM", (128, 18))
                WAb = sb(st, "WAb", (128, 512), BF16)
                G2b = sb(st, "G2b", (128, 512), BF16)
                BD = sb(st, "BD", (128, 128), BF16)
                MSK = sb(st, "MSK", (64, 3, 64))
                RST = sb(st, "RST", (128, 128))
                bW = Buf()
                with ExitStack() as stt:
                    PR = sb(stt, "PR", (70, 128))
                    bPR = Buf()
                    rows = [("rwkv_mu", 14), ("rwkv_w0", 4), ("rwkv_a0", 4), ("rwkv_k_k", 4), ("rwkv_k_a", 4), ("rwkv_r_k", 4),
                            ("rwkv_ln_g", 4), ("rwkv_ln_b", 4)]
                    r0 = 0
                    for nm, nr in rows:
                        src = dr[nm][e_].rearrange("(c p) -> c p", p=128)
                        fw.dma("sp", lambda e, src=src, r0=r0, nr=nr: e.dma_start(out=PR[r0:r0 + nr, :], in_=src), writes=[bPR])
                        r0 += nr
                    fw.dma("sp", lambda e: e.dma_start(out=PR[42:70, :], in_=dr["srs"].rearrange("b (c p) -> (b c) p", p=128)), writes=[bPR])
                    fw.op("pe", lambda e: e.matmul(pA[:, 0:70], lhsT=PR[0:70, :], rhs=ident_f[0:70, 0:70], start=True, stop=True),
                          reads=[bPR, b_ident], writes=[bpA])
                    fw.op("dve", lambda e: e.tensor_copy(out=PC[:], in_=pA[:, 0:70]), reads=[bpA], writes=[bPC])
                    fw.op("dve", lambda e: e.tensor_scalar(out=OM[:, 0:14], in0=PC[:, 0:14], scalar1=-1.0, scalar2=1.0, op0=ALU.mult, op1=ALU.add),
                          reads=[bPC], writes=[bPC])
                    fw.op("dve", lambda e: e.tensor_scalar(out=OM[:, 14:18], in0=PC[:, 26:30], scalar1=-1.0, scalar2=1.0, op0=ALU.mult, op1=ALU.add),
                          reads=[bPC], writes=[bPC])
                    WAf = sb(stt, "WAf", (128, 512))
                    G2f = sb(stt, "G2f", (128, 512))
                    BDf = sb(stt, "BDf", (128, 128))
                    fw.dma("sp", lambda e: e.dma_start(out=WAf[0:64, :], in_=dr["rwkv_w2"][e_]), writes=[bW])
                    fw.dma("sp", lambda e: e.dma_start(out=WAf[64:128, :], in_=dr["rwkv_a2"][e_]), writes=[bW])
                    fw.dma("sp", lambda e: e.dma_start(out=G2f[:], in_=dr["rwkv_g2"][e_]), writes=[bW])
                    fw.dma("sp", lambda e: e.dma_start(out=BDf[:], in_=dr["c_bd"]), writes=[bW])
                    fw.dma("sp", lambda e: e.dma_start(out=MSK[:], in_=dr["c_msk"]), writes=[bW])
                    fw.dma("sp", lambda e: e.dma_start(out=RST[:], in_=dr["c_rst"][:, 0:128]), writes=[bW])
                    fw.op("pool", lambda e: e.tensor_copy(out=WAb[:], in_=WAf[:]), reads=[bW], writes=[bW])
                    fw.op("pool", lambda e: e.tensor_copy(out=G2b[:], in_=G2f[:]), reads=[bW], writes=[bW])
                    fw.op("pool", lambda e: e.tensor_copy(out=BD[:], in_=BDf[:]), reads=[bW], writes=[bW])
                    fw.flush()
                MU = lambda cc: PC[:, cc:cc + 1]
                OMMU = lambda cc: OM[:, cc:cc + 1]
                W0 = lambda hp: PC[:, 14 + hp:15 + hp]
                A0 = lambda hp: PC[:, 18 + hp:19 + hp]
                KK_ = lambda hp: PC[:, 22 + hp:23 + hp]
                KA_ = lambda hp: PC[:, 26 + hp:27 + hp]
                RK_ = lambda hp: PC[:, 30 + hp:31 + hp]
                LNG = lambda hp: PC[:, 34 + hp:35 + hp]
                LNB = lambda hp: PC[:, 38 + hp:39 + hp]
                OMKA = lambda hp: OM[:, 14 + hp:15 + hp]
                WoR, bWoR = load_w_bf(st, dr["ev_w_out"][e_][512:1024, :].rearrange("(h p) n -> p h n", p=128), (4, D), "WoR", stage_cols=1024)
                Wc = [sb(st, "Wc%d" % i, (128, 8, 128), BF16) for i in range(2)]
                bWc = [Buf(), Buf()]
                wcs = [sb(st, "wcs%d" % i, (128, 8, 128)) for i in range(2)]
                bwcs = [Buf(), Buf()]
                wcc = [0]
                ST = sb(st, "ST", (128, 4, 64))
                STb = sb(st, "STb", (128, 4, 64), BF16)
                bST = Buf()
                CAR = sb(st, "CAR", (128, 14))
                bCAR = Buf()
                HX = sb(st, "HX", (128, 5, NG))
                bHX = [Buf() for _ in range(5)]
                TMPL = sb(st, "TMPL", (128, NG))
                bTMPL = Buf()
                F = {}
                for nm in ("LW", "A", "KKR", "KKN", "KM", "T1", "CUM", "EP", "EN", "EPV", "EH", "BETA"):
                    F[nm] = sb(st, "f_" + nm, (128, NG))
                bF = {nm: Buf() for nm in F}
                SQb = sb(st, "SQb", (128, NG), BF16)
                bSQb = Buf()
                TXW = sb(st, "TXW", (128, NG), BF16)
                SXG = sb(st, "SXG", (128, NG), BF16)
                bTXW = Buf()
                bSXG = Buf()
                CUMC = sb(st, "CUMC", (128, 4))
                SL = []
                for si_ in range(2):
                    d_ = dict(
                        PCS=sb(st, "PCS%d" % si_, (128, 4, 2)), bPCS=Buf(),
                        KR=sb(st, "KR%d" % si_, (128, 4, 2, 2, 64), BF16), KB=sb(st, "KB%d" % si_, (128, 4, 2, 2, 64), BF16),
                        bKR=[Buf() for _ in range(4)], bKB=[Buf() for _ in range(4)],
                        KRo=sb(st, "KRo%d" % si_, (64, 4, 2, 2, 64), BF16), KBo=sb(st, "KBo%d" % si_, (64, 4, 2, 2, 64), BF16),
                        KH=sb(st, "KH%d" % si_, (128, 4, NG), BF16), BH=sb(st, "BH%d" % si_, (128, 4, NG), BF16), VB=sb(st, "VB%d" % si_, (128, 4, NG), BF16),
                        bKH=[Buf() for _ in range(4)],
                        Gt=sb(st, "Gt%d" % si_, (128, 4, NG), BF16), BON=sb(st, "BON%d" % si_, (128, 4, NG), BF16),
                        bGt=[Buf() for _ in range(4)], bBON=[Buf() for _ in range(4)])
                    SL.append(d_)
                STbo = sb(st, "STbo", (64, 4, 64), BF16)
                FT = sb(st, "FT", (128, NG))
                bFT = Buf()
                ONT = sb(st, "ONT", (128, 4, NG), BF16)
                bONT = Buf()
                MO = sb(st, "MO", (128, 4, NG), BF16)
                bMO = Buf()
                TM = sb(st, "TM", (64, 3, 512), BF16)
                bTM = Buf()
                cb = {}
                for nm in ("AKK", "GKR", "NGBR", "X1", "X2", "Y1", "Y2", "T1b", "T2b", "WTs", "UTs", "ONb"):
                    cb[nm] = sb(st, "c_" + nm, (64, 8, 64), BF16)
                bcb = {nm: Buf() for nm in cb}
                OS = sb(st, "OS", (64, 8, 64))
                OSQ = sb(st, "OSQ", (64, 8, 64))
                bOS = Buf()
                bOSQ = Buf()
                STAT = sb(st, "STAT", (64, 6, 8))
                bSTAT = Buf()
                sto = sb(st, "sto", (64, 4, 128))
                bsto = Buf()

                def v3(ap, C_):
                    return ap.rearrange("p (c t) -> p c t", t=C_)

                def project(cc, slot, tok0, n):
                    w = wcc[0] % 2
                    wcc[0] += 1
                    c0 = 1544 + cc * 128
                    fw.dma("sp", lambda e, w=w, c0=c0: e.dma_start(out=wcs[w][:], in_=wv[:, :, c0:c0 + 128]), writes=[bwcs[w]])
                    fw.op("pool", lambda e, w=w: e.tensor_copy(out=Wc[w][:], in_=wcs[w][:]), reads=[bwcs[w]], writes=[bWc[w]])
                    pj, bpj = (pA, bpA) if w == 0 else (pB, bpB)
                    for k in range(8):
                        fw.op("pe", lambda e, k=k, w=w, pj=pj: e.matmul(pj[:, 0:n], lhsT=Wc[w][:, k, :], rhs=xT[:, k, tok0:tok0 + n],
                                                                       start=(k == 0), stop=(k == 7)),
                              reads=[bWc[w], b_xT[min(tok0 // 512, 4)]], writes=[bpj])
                    fw.op("dve", lambda e: e.tensor_scalar(out=TMPL[:, 0:1], in0=CAR[:, cc:cc + 1], scalar1=MU(cc), scalar2=None, op0=ALU.mult),
                          reads=[bCAR, bPC], writes=[bTMPL])
                    if n > 1:
                        fw.op("dve", lambda e, pj=pj: e.tensor_scalar(out=TMPL[:, 1:n], in0=pj[:, 0:n - 1], scalar1=MU(cc), scalar2=None, op0=ALU.mult),
                              reads=[bpj, bPC], writes=[bTMPL])
                    fw.op("dve", lambda e, pj=pj: e.scalar_tensor_tensor(out=HX[:, slot, 0:n], in0=pj[:, 0:n], scalar=OMMU(cc), in1=TMPL[:, 0:n],
                                                                        op0=ALU.mult, op1=ALU.add), reads=[bpj, bPC, bTMPL], writes=[bHX[slot]])
                    fw.op("dve", lambda e, pj=pj: e.tensor_copy(out=CAR[:, cc:cc + 1], in_=pj[:, n - 1:n]), reads=[bpj], writes=[bCAR])

                def prep(slot, tok0, n, C_):
                    nch = n // C_
                    d_ = SL[slot]
                    PCS, bPCS, KR, KB, bKR, bKB, KRo, KBo = d_["PCS"], d_["bPCS"], d_["KR"], d_["KB"], d_["bKR"], d_["bKB"], d_["KRo"], d_["KBo"]
                    KH, BH, VB, bKH, Gt, BON, bGt, bBON = d_["KH"], d_["BH"], d_["VB"], d_["bKH"], d_["Gt"], d_["BON"], d_["bGt"], d_["bBON"]
                    project(12, 0, tok0, n)
                    project(13, 1, tok0, n)
                    fw.op("act", lambda e: e.activation(out=TXW[0:64, 0:n], in_=HX[0:64, 0, 0:n], func=AF.Tanh), reads=[bHX[0]], writes=[bTXW])
                    fw.op("dve", lambda e: e.tensor_copy(out=TXW[64:128, 0:n], in_=HX[64:128, 0, 0:n]), reads=[bHX[0]], writes=[bTXW])
                    fw.op("act", lambda e: e.activation(out=SXG[:, 0:n], in_=HX[:, 1, 0:n], func=AF.Sigmoid), reads=[bHX[1]], writes=[bSXG])
                    def _prep(hp):
                        project(hp, 2, tok0, n)
                        project(4 + hp, 3, tok0, n)
                        project(8 + hp, 4, tok0, n)
                        r_, k_, v_ = HX[:, 2, 0:n], HX[:, 3, 0:n], HX[:, 4, 0:n]
                        rk_reads = [bHX[2], bHX[3], bHX[4]]
                        cs = slice(hp * 128, (hp + 1) * 128)
                        f = lambda nm: F[nm][:, 0:n]
                        fw.op("pe", lambda e: e.matmul(pA[:, 0:n], lhsT=WAb[0:64, cs], rhs=TXW[0:64, 0:n], start=True, stop=True),
                              reads=[bW, bTXW], writes=[bpA])
                        fw.op("act", lambda e: e.activation(out=f("LW"), in_=pA[:, 0:n], func=AF.Sigmoid, bias=W0(hp)), reads=[bpA, bPC], writes=[bF["LW"]])
                        fw.op("dve", lambda e: e.tensor_scalar(out=f("LW"), in0=f("LW"), scalar1=-0.6065306597126334, scalar2=None, op0=ALU.mult),
                              reads=[bF["LW"]], writes=[bF["LW"]])
                        fw.op("pe", lambda e: e.matmul(pB[:, 0:n], lhsT=WAb[64:128, cs], rhs=TXW[64:128, 0:n], start=True, stop=True),
                              reads=[bW, bTXW], writes=[bpB])
                        fw.op("act", lambda e: e.activation(out=f("A"), in_=pB[:, 0:n], func=AF.Sigmoid, bias=A0(hp)), reads=[bpB, bPC], writes=[bF["A"]])
                        fw.op("pe", lambda e: e.matmul(pA[:, 0:n], lhsT=G2b[:, cs], rhs=SXG[:, 0:n], start=True, stop=True),
                              reads=[bW, bSXG], writes=[bpA])
                        fw.op("act", lambda e: e.copy(out=Gt[:, hp, 0:n], in_=pA[:, 0:n]), reads=[bpA], writes=[bGt[hp]])
                        fw.op("dve", lambda e: e.tensor_scalar(out=f("KKR"), in0=k_, scalar1=KK_(hp), scalar2=None, op0=ALU.mult),
                              reads=[bHX[3], bPC], writes=[bF["KKR"]])
                        fw.op("dve", lambda e: e.tensor_tensor(out=SQb[:, 0:n], in0=f("KKR"), in1=f("KKR"), op=ALU.mult), reads=[bF["KKR"]], writes=[bSQb])
                        fw.op("pe", lambda e: e.matmul(pB[:, 0:n], lhsT=BD[:], rhs=SQb[:, 0:n], start=True, stop=True), reads=[bW, bSQb], writes=[bpB])
                        fw.op("act", lambda e: e.activation(out=f("T1"), in_=pB[:, 0:n], func=AF.Sqrt), reads=[bpB], writes=[bF["T1"]])
                        fw.op("dve", lambda e: e.tensor_scalar(out=f("T1"), in0=f("T1"), scalar1=1e-12, scalar2=None, op0=ALU.max), reads=[bF["T1"]], writes=[bF["T1"]])
                        fw.op("dve", lambda e: e.reciprocal(out=f("T1"), in_=f("T1")), reads=[bF["T1"]], writes=[bF["T1"]])
                        fw.op("dve", lambda e: e.tensor_tensor(out=f("KKN"), in0=f("KKR"), in1=f("T1"), op=ALU.mult), reads=[bF["KKR"], bF["T1"]], writes=[bF["KKN"]])
                        fw.op("dve", lambda e: e.tensor_scalar(out=f("T1"), in0=f("A"), scalar1=KA_(hp), scalar2=OMKA(hp), op0=ALU.mult, op1=ALU.add),
                              reads=[bF["A"], bPC], writes=[bF["T1"]])
                        fw.op("dve", lambda e: e.tensor_tensor(out=f("KM"), in0=k_, in1=f("T1"), op=ALU.mult), reads=[bHX[3], bF["T1"]], writes=[bF["KM"]])
                        fw.op("dve", lambda e: e.tensor_tensor(out=f("T1"), in0=r_, in1=f("KM"), op=ALU.mult), reads=[bHX[2], bF["KM"]], writes=[bF["T1"]])
                        fw.op("dve", lambda e: e.tensor_scalar(out=SQb[:, 0:n], in0=f("T1"), scalar1=RK_(hp), scalar2=None, op0=ALU.mult),
                              reads=[bF["T1"], bPC], writes=[bSQb])
                        fw.op("pe", lambda e: e.matmul(pB[:, 0:n], lhsT=BD[:], rhs=SQb[:, 0:n], start=True, stop=True), reads=[bW, bSQb], writes=[bpB])
                        fw.op("dve", lambda e: e.tensor_tensor(out=BON[:, hp, 0:n], in0=pB[:, 0:n], in1=v_, op=ALU.mult), reads=[bpB, bHX[4]], writes=[bBON[hp]])
                        fw.op("dve", lambda e: e.tensor_tensor_scan(out=f("CUM"), data0=RST[:, 0:n], data1=f("LW"), initial=0.0, op0=ALU.mult, op1=ALU.add),
                              reads=[bW, bF["LW"]], writes=[bF["CUM"]])
                        fw.op("act", lambda e: e.activation(out=f("EP"), in_=f("CUM"), func=AF.Exp), reads=[bF["CUM"]], writes=[bF["EP"]])
                        fw.op("act", lambda e: e.activation(out=f("EN"), in_=f("CUM"), func=AF.Exp, scale=-1.0), reads=[bF["CUM"]], writes=[bF["EN"]])
                        fw.op("dve", lambda e: e.tensor_tensor(out=f("EPV"), in0=f("CUM"), in1=f("LW"), op=ALU.subtract), reads=[bF["CUM"], bF["LW"]], writes=[bF["EPV"]])
                        fw.op("act", lambda e: e.activation(out=f("EPV"), in_=f("EPV"), func=AF.Exp), reads=[bF["EPV"]], writes=[bF["EPV"]])
                        fw.op("dve", lambda e: e.tensor_copy(out=CUMC[:, 0:nch], in_=v3(f("CUM"), C_)[:, :, C_ - 1]), reads=[bF["CUM"]], writes=[bPCS])
                        fw.op("act", lambda e: e.activation(out=PCS[:, hp, 0:nch], in_=CUMC[:, 0:nch], func=AF.Exp), reads=[bPCS], writes=[bPCS])
                        fw.op("dve", lambda e: e.tensor_tensor(out=v3(f("EH"), C_), in0=CUMC[:, 0:nch].unsqueeze(2).broadcast_to([128, nch, C_]),
                                                               in1=v3(f("CUM"), C_), op=ALU.subtract), reads=[bPCS, bF["CUM"]], writes=[bF["EH"]])
                        fw.op("act", lambda e: e.activation(out=f("EH"), in_=f("EH"), func=AF.Exp), reads=[bF["EH"]], writes=[bF["EH"]])
                        fw.op("dve", lambda e: e.tensor_tensor(out=KR[:, hp, 0:nch, 0, 0:C_], in0=v3(f("KKN"), C_), in1=v3(f("EPV"), C_), op=ALU.mult),
                              reads=[bF["KKN"], bF["EPV"]], writes=[bKR[hp]])
                        fw.op("dve", lambda e: e.tensor_tensor(out=KR[:, hp, 0:nch, 1, 0:C_], in0=v3(r_, C_), in1=v3(f("EP"), C_), op=ALU.mult),
                              reads=[bHX[2], bF["EP"]], writes=[bKR[hp]])
                        fw.op("dve", lambda e: e.tensor_tensor(out=KB[:, hp, 0:nch, 0, 0:C_], in0=v3(f("KM"), C_), in1=v3(f("EN"), C_), op=ALU.mult),
                              reads=[bF["KM"], bF["EN"]], writes=[bKB[hp]])
                        fw.op("dve", lambda e: e.tensor_tensor(out=f("BETA"), in0=f("KKN"), in1=f("A"), op=ALU.mult), reads=[bF["KKN"], bF["A"]], writes=[bF["BETA"]])
                        fw.op("dve", lambda e: e.tensor_tensor(out=KB[:, hp, 0:nch, 1, 0:C_], in0=v3(f("BETA"), C_), in1=v3(f("EN"), C_), op=ALU.mult),
                              reads=[bF["BETA"], bF["EN"]], writes=[bKB[hp]])
                        fw.op("dve", lambda e: e.tensor_tensor(out=KH[:, hp, 0:n], in0=f("KM"), in1=f("EH"), op=ALU.mult), reads=[bF["KM"], bF["EH"]], writes=[bKH[hp]])
                        fw.op("dve", lambda e: e.scalar_tensor_tensor(out=BH[:, hp, 0:n], in0=f("BETA"), scalar=-1.0, in1=f("EH"), op0=ALU.mult, op1=ALU.mult),
                              reads=[bF["BETA"], bF["EH"]], writes=[bKH[hp]])
                        fw.op("act", lambda e: e.copy(out=VB[:, hp, 0:n], in_=v_), reads=[bHX[4]], writes=[bKH[hp]])
                        fw.op("pool", lambda e: e.tensor_copy(out=KRo[:, hp, 0:nch, :, 0:C_], in_=KR[64:128, hp, 0:nch, :, 0:C_]), reads=[bKR[hp]], writes=[bKR[hp]])
                        fw.op("pool", lambda e: e.tensor_copy(out=KBo[:, hp, 0:nch, :, 0:C_], in_=KB[64:128, hp, 0:nch, :, 0:C_]), reads=[bKB[hp]], writes=[bKB[hp]])
                    for hp_ in range(4):
                        _prep(hp_)

                def post(slot, tok0, n, C_, tiles):
                    nch = n // C_
                    L = {64: 5, 32: 4}[C_]
                    RS = cfg.get("rwkv_stop", 9)
                    d_ = SL[slot]
                    PCS, bPCS, KR, KB, bKR, bKB, KRo, KBo = d_["PCS"], d_["bPCS"], d_["KR"], d_["KB"], d_["bKR"], d_["bKB"], d_["KRo"], d_["KBo"]
                    KH, BH, VB, bKH, Gt, BON, bGt, bBON = d_["KH"], d_["BH"], d_["VB"], d_["bKH"], d_["Gt"], d_["BON"], d_["bGt"], d_["bBON"]
                    MS_ = lambda i_: MSK[0:C_, i_, 0:C_].unsqueeze(1).broadcast_to([C_, 8, C_])
                    IDB = ident[0:C_, 0:C_].unsqueeze(1).broadcast_to([C_, 8, C_])
                    gk = pGK[0:C_, :].rearrange("p (h t) -> p h t", h=8)
                    gb = pGB[0:C_, :].rearrange("p (h t) -> p h t", h=8)
                    hv = lambda ps_, o: ps_[0:C_, o:o + 512].rearrange("p (h t) -> p h t", h=8)
                    cbv = lambda nm: cb[nm][0:C_, :, 0:C_]
                    def _chunk(c):
                        cols = slice(c * C_, (c + 1) * C_)
                        KRh = lambda h, a_: (KR[0:64, h // 2, c, a_, 0:C_] if h % 2 == 0 else KRo[0:64, h // 2, c, a_, 0:C_])
                        KBh = lambda h, a_: (KB[0:64, h // 2, c, a_, 0:C_] if h % 2 == 0 else KBo[0:64, h // 2, c, a_, 0:C_])
                        KR2 = lambda h: ((KR if h % 2 == 0 else KRo)[0:64, h // 2, c, :, :].rearrange("p a t -> p (a t)"))
                        STh = lambda h: (STb[0:64, h // 2, :] if h % 2 == 0 else STbo[0:64, h // 2, :])
                        t1v = pT1[0:C_, :].rearrange("p (q f) -> p q f", q=2)
                        for qi, Q in enumerate((KH, BH)):
                            for hp in range(4):
                                fw.op("pe", lambda e, qi=qi, Q=Q, hp=hp: e.transpose(out=t1v[:, qi, hp * 128:(hp + 1) * 128], in_=Q[:, hp, cols], identity=ident[:]),
                                      reads=[bKH[hp], b_ident], writes=[bpT1])
                        for hp in range(4):
                            fw.op("pe", lambda e, hp=hp: e.transpose(out=pT2[0:C_, hp * 128:(hp + 1) * 128], in_=VB[:, hp, cols], identity=ident[:]),
                                  reads=[bKH[hp], b_ident], writes=[bpT2])
                        fw.op("act", lambda e: e.copy(out=TM[0:C_, 0:2, :], in_=t1v), reads=[bpT1], writes=[bTM])
                        fw.op("act", lambda e: e.copy(out=TM[0:C_, 2, :], in_=pT2[0:C_, 0:512]), reads=[bpT2], writes=[bTM])
                        if RS <= 2.5:
                            return
                        for h in range(8):
                            hp = h // 2
                            bgk = bpGK0 if h < 4 else bpGK1
                            bgb = bpGB0 if h < 4 else bpGB1
                            if C_ == 64:
                                fw.op("pe", lambda e, h=h: e.matmul(gk[:, h, :], lhsT=KBh(h, 0), rhs=KR2(h), start=True, stop=True),
                                      reads=[bKB[hp], bKR[hp]], writes=[bgk])
                                fw.op("pe", lambda e, h=h: e.matmul(gb[:, h, :], lhsT=KBh(h, 1), rhs=KR2(h), start=True, stop=True),
                                      reads=[bKB[hp], bKR[hp]], writes=[bgb])
                            else:
                                for a_ in range(2):
                                    fw.op("pe", lambda e, h=h, a_=a_: e.matmul(gk[:, h, a_ * C_:(a_ + 1) * C_], lhsT=KBh(h, 0), rhs=KRh(h, a_), start=True, stop=True),
                                          reads=[bKB[hp], bKR[hp]], writes=[bgk])
                                    fw.op("pe", lambda e, h=h, a_=a_: e.matmul(gb[:, h, a_ * C_:(a_ + 1) * C_], lhsT=KBh(h, 1), rhs=KRh(h, a_), start=True, stop=True),
                                          reads=[bKB[hp], bKR[hp]], writes=[bgb])
                            fw.op("pe", lambda e, h=h: e.matmul(hv(pA, 0)[:, h, 0:C_], lhsT=KRh(h, 0), rhs=KBh(h, 1), start=True, stop=True),
                                  reads=[bKB[hp], bKR[hp]], writes=[bpA])
                        MS4 = lambda i_: MSK[0:C_, i_, 0:C_].unsqueeze(1).broadcast_to([C_, 4, C_])
                        for hb, (bk, bb) in enumerate(((bpGK0, bpGB0), (bpGK1, bpGB1))):
                            hs_ = slice(4 * hb, 4 * hb + 4)
                            fw.op("dve", lambda e, hs_=hs_: e.tensor_tensor(out=cb["AKK"][0:C_, hs_, 0:C_], in0=gk[:, hs_, 0:C_], in1=MS4(0), op=ALU.mult),
                                  reads=[bk, bW], writes=[bcb["AKK"]])
                            fw.op("dve", lambda e, hs_=hs_: e.tensor_tensor(out=cb["GKR"][0:C_, hs_, 0:C_], in0=gk[:, hs_, C_:2 * C_], in1=MS4(1), op=ALU.mult),
                                  reads=[bk, bW], writes=[bcb["GKR"]])
                            fw.op("dve", lambda e, hs_=hs_: e.scalar_tensor_tensor(out=cb["X1"][0:C_, hs_, 0:C_], in0=gb[:, hs_, 0:C_], scalar=-1.0, in1=MS4(0),
                                                                                 op0=ALU.mult, op1=ALU.mult), reads=[bb, bW], writes=[bcb["X1"]])
                            fw.op("dve", lambda e, hs_=hs_: e.scalar_tensor_tensor(out=cb["NGBR"][0:C_, hs_, 0:C_], in0=gb[:, hs_, C_:2 * C_], scalar=-1.0, in1=MS4(1),
                                                                                 op0=ALU.mult, op1=ALU.mult), reads=[bb, bW], writes=[bcb["NGBR"]])
                        fw.op("dve", lambda e: e.scalar_tensor_tensor(out=cbv("Y1"), in0=hv(pA, 0)[:, :, 0:C_], scalar=-1.0, in1=MS_(2), op0=ALU.mult, op1=ALU.mult),
                              reads=[bpA, bW], writes=[bcb["Y1"]])
                        fw.op("dve", lambda e: e.tensor_tensor(out=cbv("T1b"), in0=cbv("X1"), in1=IDB, op=ALU.add), reads=[bcb["X1"], b_ident], writes=[bcb["T1b"]])
                        if RS <= 3:
                            return
                        xc, yc, tc = "X1", "Y1", "T1b"
                        for lvl in range(L):
                            xn = "X2" if xc == "X1" else "X1"
                            yn = "Y2" if yc == "Y1" else "Y1"
                            tn = "T2b" if tc == "T1b" else "T1b"
                            if lvl < L - 1:
                                for h in range(8):
                                    fw.op("pe", lambda e, h=h, xc=xc, yc=yc: e.matmul(hv(pB, 0)[:, h, 0:C_], lhsT=cb[yc][0:C_, h, 0:C_], rhs=cb[xc][0:C_, h, 0:C_],
                                                                                     start=True, stop=True), reads=[bcb[xc], bcb[yc]], writes=[bpB])
                            for h in range(8):
                                fw.op("pe", lambda e, h=h, xc=xc, yc=yc: e.matmul(hv(pA, 0)[:, h, 0:C_], lhsT=cb[xc][0:C_, h, 0:C_], rhs=cb[yc][0:C_, h, 0:C_],
                                                                                 start=True, stop=True), reads=[bcb[xc], bcb[yc]], writes=[bpA])
                            if lvl < L - 1:
                                fw.op("act", lambda e, xn=xn: e.copy(out=cbv(xn), in_=hv(pB, 0)[:, :, 0:C_]), reads=[bpB], writes=[bcb[xn]])
                            fw.op("dve", lambda e, yn=yn: e.tensor_copy(out=cbv(yn), in_=hv(pA, 0)[:, :, 0:C_]), reads=[bpA], writes=[bcb[yn]])
                            for h in range(8):
                                fw.op("pe", lambda e, h=h, yn=yn, tc=tc: e.matmul(hv(pGK, 0)[:, h, 0:C_], lhsT=cb[yn][0:C_, h, 0:C_], rhs=cb[tc][0:C_, h, 0:C_],
                                                                                 start=True, stop=True), reads=[bcb[yn], bcb[tc]], writes=[bpGK0])
                            fw.op("dve", lambda e, tn=tn, tc=tc: e.tensor_tensor(out=cbv(tn), in0=hv(pGK, 0)[:, :, 0:C_], in1=cbv(tc), op=ALU.add),
                                  reads=[bpGK0, bcb[tc]], writes=[bcb[tn]])
                            xc, yc, tc = xn, yn, tn
                        if RS <= 4:
                            return
                        wt = hv(pGK, 512)
                        ut = hv(pGB, 0)
                        oo = hv(pGB, 512)
                        for h in range(8):
                            hp, base = h // 2, (h % 2) * 64
                            bs = slice(base, base + 64)
                            fw.op("pe", lambda e, h=h: e.matmul(wt[:, h, :], lhsT=KRh(h, 0), rhs=STh(h), start=True, stop=False),
                                  reads=[bKR[hp], bST], writes=[bpGK1])
                            fw.op("pe", lambda e, h=h: e.matmul(wt[:, h, :], lhsT=cb["AKK"][0:C_, h, 0:C_], rhs=TM[0:C_, 2, h * 64:(h + 1) * 64], start=False, stop=True),
                                  reads=[bcb["AKK"], bTM], writes=[bpGK1])
                        fw.op("act", lambda e: e.copy(out=cb["WTs"][0:C_], in_=wt), reads=[bpGK1], writes=[bcb["WTs"]])
                        for h in range(8):
                            fw.op("pe", lambda e, h=h, tc=tc: e.matmul(ut[:, h, :], lhsT=cb[tc][0:C_, h, 0:C_], rhs=cb["WTs"][0:C_, h, :], start=True, stop=True),
                                  reads=[bcb[tc], bcb["WTs"]], writes=[bpGB0])
                        fw.op("dve", lambda e: e.tensor_copy(out=cb["UTs"][0:C_], in_=ut), reads=[bpGB0], writes=[bcb["UTs"]])
                        for h in range(8):
                            hp, base = h // 2, (h % 2) * 64
                            bs = slice(base, base + 64)
                            fw.op("pe", lambda e, h=h: e.matmul(oo[:, h, :], lhsT=KRh(h, 1), rhs=STh(h), start=True, stop=False),
                                  reads=[bKR[hp], bST], writes=[bpGB1])
                            fw.op("pe", lambda e, h=h: e.matmul(oo[:, h, :], lhsT=cb["GKR"][0:C_, h, 0:C_], rhs=TM[0:C_, 2, h * 64:(h + 1) * 64], start=False, stop=False),
                                  reads=[bcb["GKR"], bTM], writes=[bpGB1])
                            fw.op("pe", lambda e, h=h: e.matmul(oo[:, h, :], lhsT=cb["NGBR"][0:C_, h, 0:C_], rhs=cb["UTs"][0:C_, h, :], start=False, stop=True),
                                  reads=[bcb["NGBR"], bcb["UTs"]], writes=[bpGB1])
                        sn = pB[:, :].rearrange("p (a f) -> p a f", a=4)
                        for hp in range(4):
                            fw.op("pe", lambda e, hp=hp: e.matmul(sn[:, hp, :], lhsT=TM[0:C_, 0, hp * 128:(hp + 1) * 128], rhs=TM[0:C_, 2, hp * 128:(hp + 1) * 128],
                                                                  start=True, stop=False), reads=[bTM], writes=[bpB])
                            fw.op("pe", lambda e, hp=hp: e.matmul(sn[:, hp, :], lhsT=TM[0:C_, 1, hp * 128:(hp + 1) * 128],
                                                                  rhs=cb["UTs"][0:C_, 2 * hp:2 * hp + 2, :].rearrange("p a v -> p (a v)"),
                                                                  start=False, stop=True), reads=[bTM, bcb["UTs"]], writes=[bpB])
                        if RS <= 5:
                            return
                        fw.op("act", lambda e: e.copy(out=OS[0:C_], in_=oo), reads=[bpGB1], writes=[bOS])
                        for half in range(2):
                            bs = slice(64 * half, 64 * half + 64)
                            fw.op("dve", lambda e, bs=bs: e.tensor_tensor(out=ST[bs], in0=ST[bs], in1=PCS[bs, :, c].unsqueeze(2).broadcast_to([64, 4, 64]), op=ALU.mult),
                                  reads=[bST, bPCS], writes=[bST])
                            fw.op("dve", lambda e, bs=bs, half=half: e.tensor_tensor(out=ST[bs], in0=sn[bs, :, 64 * half:64 * half + 64], in1=ST[bs], op=ALU.add),
                                  reads=[bST, bpB], writes=[bST])
                        fw.op("act", lambda e: e.copy(out=STb[:], in_=ST[:]), reads=[bST], writes=[bST])
                        fw.op("act", lambda e: e.copy(out=STbo[:], in_=ST[64:128]), reads=[bST], writes=[bST])
                        S_ = lambda i_: STAT[0:C_, i_, :]
                        fw.op("dve", lambda e: e.tensor_reduce(out=S_(0), in_=OS[0:C_], axis=AX.X, op=ALU.add), reads=[bOS], writes=[bSTAT])
                        fw.op("pool", lambda e: e.tensor_tensor(out=OSQ[0:C_], in0=OS[0:C_], in1=OS[0:C_], op=ALU.mult), reads=[bOS], writes=[bOSQ])
                        fw.op("dve", lambda e: e.tensor_reduce(out=S_(1), in_=OSQ[0:C_], axis=AX.X, op=ALU.add), reads=[bOSQ], writes=[bSTAT])
                        fw.op("dve", lambda e: e.tensor_scalar(out=S_(2), in0=S_(0), scalar1=1.0 / 64, scalar2=None, op0=ALU.mult), reads=[bSTAT], writes=[bSTAT])
                        fw.op("dve", lambda e: e.tensor_tensor(out=S_(3), in0=S_(2), in1=S_(2), op=ALU.mult), reads=[bSTAT], writes=[bSTAT])
                        fw.op("dve", lambda e: e.scalar_tensor_tensor(out=S_(4), in0=S_(1), scalar=1.0 / 64, in1=S_(3), op0=ALU.mult, op1=ALU.subtract),
                              reads=[bSTAT], writes=[bSTAT])
                        fw.op("act", lambda e: e.activation(out=S_(5), in_=S_(4), func=AF.Sqrt, bias=64e-5), reads=[bSTAT], writes=[bSTAT])
                        fw.op("dve", lambda e: e.reciprocal(out=S_(5), in_=S_(5)), reads=[bSTAT], writes=[bSTAT])
                        fw.op("dve", lambda e: e.tensor_tensor(out=OSQ[0:C_], in0=OS[0:C_], in1=bc3(S_(2), 64), op=ALU.subtract), reads=[bOS, bSTAT], writes=[bOSQ])
                        fw.op("dve", lambda e: e.tensor_tensor(out=cb["ONb"][0:C_], in0=OSQ[0:C_], in1=bc3(S_(5), 64), op=ALU.mult), reads=[bOSQ, bSTAT], writes=[bcb["ONb"]])
                        t2v = pT2[:, 512:512 + 4 * C_].rearrange("p (a t) -> p a t", a=4)
                        for hp in range(4):
                            fw.op("pe", lambda e, hp=hp: e.transpose(out=t2v[:, hp, :], in_=cb["ONb"][0:C_, 2 * hp:2 * hp + 2, :].rearrange("p a v -> p (a v)"),
                                                                     identity=ident[0:C_, 0:C_]), reads=[bcb["ONb"], b_ident], writes=[bpT2])
                        fw.op("act", lambda e: e.copy(out=ONT[:, :, cols], in_=t2v), reads=[bpT2], writes=[bONT])
                    for c_ in range(nch):
                        _chunk(c_)
                    if RS <= 6:
                        return
                    for hp in range(4):
                        fw.op("dve", lambda e, hp=hp: e.tensor_scalar(out=FT[:, 0:n], in0=ONT[:, hp, 0:n], scalar1=LNG(hp), scalar2=LNB(hp), op0=ALU.mult, op1=ALU.add),
                              reads=[bONT, bPC], writes=[bFT])
                        fw.op("dve", lambda e, hp=hp: e.tensor_tensor(out=FT[:, 0:n], in0=FT[:, 0:n], in1=BON[:, hp, 0:n], op=ALU.add),
                              reads=[bFT, bBON[hp]], writes=[bFT])
                        fw.op("dve", lambda e, hp=hp: e.tensor_tensor(out=MO[:, hp, 0:n], in0=FT[:, 0:n], in1=Gt[:, hp, 0:n], op=ALU.mult),
                              reads=[bFT, bGt[hp]], writes=[bMO])
                    for tl_i, (ti, r0_, r) in enumerate(tiles):
                        for half in range(2):
                            pj, bpj = (pA, bpA) if half == 0 else (pB, bpB)
                            for hp in range(4):
                                fw.op("pe", lambda e, pj=pj, hp=hp, tl_i=tl_i, r=r, half=half: e.matmul(
                                    pj[0:r, :], lhsT=MO[:, hp, tl_i * 128:tl_i * 128 + r], rhs=WoR[:, hp, half * 512:(half + 1) * 512],
                                    start=(hp == 0), stop=(hp == 3)), reads=[bMO, bWoR], writes=[bpj])
                            fw.op("dve", lambda e, pj=pj, ti=ti, r0_=r0_, r=r, half=half: e.tensor_tensor(
                                out=X[r0_:r0_ + r, ti, half * 512:(half + 1) * 512], in0=pj[0:r, :], in1=X[r0_:r0_ + r, ti, half * 512:(half + 1) * 512],
                                op=ALU.add), reads=[bpj, bX[ti]], writes=[bX[ti]])

                def store_state(idx):
                    so = pGK[0:64, 0:512].rearrange("p (a f) -> p a f", a=4)
                    for hp in range(4):
                        fw.op("pe", lambda e, hp=hp: e.matmul(so[:, hp, :], lhsT=ST[:, hp, :], rhs=ident_f[:], start=True, stop=True), reads=[bST, b_ident], writes=[bpGK0])
                    fw.op("dve", lambda e: e.tensor_copy(out=sto[:], in_=so), reads=[bpGK0], writes=[bsto])
                    fw.dma("sp", lambda e: e.dma_start(out=dr["rwkv_state"][idx].rearrange("h v k -> v h k"),
                                                       in_=sto[:].rearrange("p a (b k) -> p (a b) k", b=2)), reads=[bsto])

                fw.op("pool", lambda e: e.memset(ST[:], 0.0), writes=[bST])
                fw.op("pool", lambda e: e.memset(STb[:], 0.0), writes=[bST])
                fw.op("pool", lambda e: e.memset(STbo[:], 0.0), writes=[bST])
                fw.op("pool", lambda e: e.memset(CAR[:], 0.0), writes=[bCAR])
                sti = OS
                bsti = bOS
                seq = [("p", g) for g in range(cfg.get("rwkv_ngroups", 16))] + ([("s", 0), ("s", 1)] if cfg.get("rwkv_sample", True) else [])

                def do_prep(idx):
                    kind, a_ = seq[idx]
                    if kind == "p":
                        prep(idx % 2, a_ * NG, NG, 64)
                    else:
                        fw.op("dve", lambda e, a_=a_: e.tensor_copy(out=CAR[:], in_=PC[:, 42 + 14 * a_:56 + 14 * a_]), reads=[bPC], writes=[bCAR])
                        prep(idx % 2, 2048 + 32 * a_, 32, 32)

                def do_post(idx):
                    kind, a_ = seq[idx]
                    if kind == "p":
                        post(idx % 2, a_ * NG, NG, 64, [(a_, 0, 128)])
                        if a_ == 15:
                            store_state(0)
                    else:
                        b = a_
                        fw.dma("sp", lambda e: e.dma_start(out=sti[:], in_=dr["srw"][b].rearrange("h v k -> v h k")), writes=[bsti])
                        sv_ = pB[:, 0:256].rearrange("p (a v) -> p a v", a=4)
                        for hp in range(4):
                            fw.op("pe", lambda e, hp=hp: e.matmul(sv_[:, hp, :], lhsT=sti[:, 2 * hp:2 * hp + 2, :].rearrange("p a k -> p (a k)"),
                                                                  rhs=ident_f[0:64, 0:64], start=True, stop=True), reads=[bsti, b_ident], writes=[bpB])
                        fw.op("dve", lambda e: e.tensor_copy(out=ST[:], in_=sv_), reads=[bpB], writes=[bST])
                        fw.op("act", lambda e: e.copy(out=STb[:], in_=ST[:]), reads=[bST], writes=[bST])
                        fw.op("act", lambda e: e.copy(out=STbo[:], in_=ST[64:128]), reads=[bST], writes=[bST])
                        post(idx % 2, 2048 + 32 * b, 32, 32, [(16, 32 * b, 32)])
                        store_state(1 + b)

                do_prep(0)
                for idx in range(len(seq)):
                    if idx + 1 < len(seq):
                        do_prep(idx + 1)
                    do_post(idx)
                fw.flush()

        def mix_even(l):
            e_ = l // 2
            with ExitStack() as st:
                xT = sb(st, "xT", (128, 8, NTOK), BF16)
                b_xT = [Buf() for _ in range(5)]
                with ExitStack() as st2:
                    norm_T(st2, dr["mix_norm"][l:l + 1, :], xT, b_xT, "n_")
                    fw.flush()
                wv = dr["ev_w_in"][e_].rearrange("(k p) n -> p k n", p=128)
                if cfg.get("shiftout", True):
                    with ExitStack() as s2:
                        Wr, bWr = load_w_bf(s2, wv[:, :, 1544:3336], (8, 1792), "Wr")
                        pp = [ps(s2, "pps%d" % i, (128, 512)) for i in range(2)]
                        bpp = [Buf() for _ in range(2)]
                        sh_ = sb(s2, "sh", (1, 3, 1792))
                        bsh = Buf()
                        for si, tok in enumerate((2047, 2079, 2111)):
                            for cb, (c0, cw) in enumerate(((0, 512), (512, 512), (1024, 512), (1536, 256))):
                                q = cb % 2
                                for k in range(8):
                                    fw.op("pe", lambda e, q=q, k=k, tok=tok, c0=c0, cw=cw: e.matmul(
                                        pp[q][0:1, 0:cw], lhsT=xT[:, k, tok:tok + 1], rhs=Wr[:, k, c0:c0 + cw],
                                        start=(k == 0), stop=(k == 7)), reads=[b_xT[tok // 512], bWr], writes=[bpp[q]])
                                fw.op("act", lambda e, q=q, si=si, c0=c0, cw=cw: e.copy(out=sh_[0:1, si, c0:c0 + cw], in_=pp[q][0:1, 0:cw]),
                                      reads=[bpp[q]], writes=[bsh])
                        fw.dma("sp", lambda e: e.dma_start(out=dr["rwkv_shift"].rearrange("(o s) n -> o s n", o=1), in_=sh_[:]), reads=[bsh])
                        fw.flush()
                if cfg.get("fox", True):
                    fox_part(l, xT, b_xT)
                if cfg.get("rwkv", True):
                    rwkv_part(l, xT, b_xT)


        def mix_odd(l):
            j_ = l // 2
            wv = dr["od_w_in"][j_].rearrange("(k p) n -> p k n", p=128)
            with ExitStack() as st:
                with ExitStack() as sx:
                    xT = sb(sx, "xT", (128, 8, NTOK), BF16)
                    b_xT = [Buf() for _ in range(5)]
                    with ExitStack() as st2:
                        norm_T(st2, dr["mix_norm"][l:l + 1, :], xT, b_xT, "n_")
                        fw.flush()
                    with ExitStack() as s1:
                        Wu, bWu = load_w_bf(s1, wv[:, :, 768:1280], (8, 512), "Wu", stage_cols=1024)
                        Wv, bWv = load_w_bf(s1, wv[:, :, 1280:1792], (8, 512), "Wv", stage_cols=1024)
                        WoS, bWoS = load_w_bf(s1, dr["od_w_out"][j_][512:1024, :].rearrange("(h p) n -> p h n", p=128), (4, D), "WoS", stage_cols=1024)
                        Gv, bGv = bcast_row(s1, dr["sgu_v_norm"][j_:j_ + 1, :], 512, "Gv")
                        TRI = sb(s1, "gTRI", (128, 128))
                        WS = sb(s1, "WS", (128, 8, 128))
                        WSb = sb(s1, "WSb", (128, 8, 128), BF16)
                        WST = sb(s1, "WST", (128, 8, 128), BF16)
                        SBr = sb(s1, "SBr", (8, 128))
                        BT = sb(s1, "BT", (128, 8))
                        bC = Buf()
                        fw.dma("sp", lambda e: e.dma_start(out=TRI[:], in_=dr["c_tri"]), writes=[bC])
                        fw.dma("sp", lambda e: e.dma_start(out=WS[:], in_=dr["sgu_w_s"][j_].rearrange("g t s -> t g s")), writes=[bC])
                        fw.dma("sp", lambda e: e.dma_start(out=SBr[:], in_=dr["sgu_b"][j_]), writes=[bC])
                        fw.op("pool", lambda e: e.tensor_copy(out=WSb[:], in_=WS[:]), reads=[bC], writes=[bC])
                        pu = ps(s1, "gpu", (128, 512))
                        pv = ps(s1, "gpv", (128, 512))
                        pm = ps(s1, "gpm", (128, 512))
                        po = [ps(s1, "gpo%d" % i, (128, 512)) for i in range(2)]
                        pt = ps(s1, "gpt", (128, 8, 128), BF16)
                        bpu, bpv, bpm, bpt = Buf(), Buf(), Buf(), Buf()
                        bpo = [Buf(), Buf()]
                        for g in range(8):
                            fw.op("pe", lambda e, g=g: e.transpose(out=pt[:, g, :], in_=WSb[:, g, :], identity=ident[:]), reads=[bC, b_ident], writes=[bpt])
                        fw.op("dve", lambda e: e.tensor_tensor(out=WST[:], in0=pt[:], in1=bcm(TRI[:], 8), op=ALU.mult), reads=[bpt, bC], writes=[bC])
                        fw.op("pe", lambda e: e.matmul(pm[:, 0:8], lhsT=SBr[0:8, :], rhs=ident_f[0:8, 0:8], start=True, stop=True), reads=[bC, b_ident], writes=[bpm])
                        fw.op("dve", lambda e: e.tensor_copy(out=BT[:], in_=pm[:, 0:8]), reads=[bpm], writes=[bC])
                        U = sb(s1, "gU", (128, 512))
                        GV = sb(s1, "gGV", (128, 512))
                        VN = sb(s1, "gVN", (128, 512))
                        VNb = sb(s1, "gVNb", (128, 512), BF16)
                        U1 = sb(s1, "gU1", (32, 512))
                        VNb1 = sb(s1, "gVNb1", (32, 512), BF16)
                        Dm = sb(s1, "gDm", (128, 512))
                        Db = sb(s1, "gDb", (128, 512), BF16)
                        DT = sb(s1, "gDT", (128, 4, 128), BF16)
                        gss = sb(s1, "gss", (128, 4))
                        bU, bGV, bVN, bVNb, bU1, bVNb1, bDm, bDb, bDT, bgss, bgj = [Buf() for _ in range(11)]

                        def sgu_tile(i):
                            r = tile_rows(i)
                            tk = slice(i * 128, i * 128 + r)
                            g5 = i // 4
                            for k in range(8):
                                fw.op("pe", lambda e, k=k: e.matmul(pu[0:r, :], lhsT=xT[:, k, tk], rhs=Wu[:, k, :], start=(k == 0), stop=(k == 7)),
                                      reads=[b_xT[g5], bWu], writes=[bpu])
                            for k in range(8):
                                fw.op("pe", lambda e, k=k: e.matmul(pv[0:r, :], lhsT=xT[:, k, tk], rhs=Wv[:, k, :], start=(k == 0), stop=(k == 7)),
                                      reads=[b_xT[g5], bWv], writes=[bpv])
                            fw.op("act", lambda e: e.activation(out=U[0:r, :], in_=pu[0:r, :], func=AF.Gelu_apprx_tanh), reads=[bpu], writes=[bU])
                            fw.op("act", lambda e: e.activation(out=GV[0:r, :], in_=pv[0:r, :], func=AF.Gelu_apprx_tanh), reads=[bpv], writes=[bGV])
                            fw.op("act", lambda e: e.activation(out=Dm[0:r, :], in_=GV[0:r, :], func=AF.Square, accum_out=gss[0:r, 0:1]), reads=[bGV], writes=[bDm, bgss])
                            fw.op("act", lambda e: e.activation(out=gss[0:r, 1:2], in_=gss[0:r, 0:1], func=AF.Sqrt, scale=1.0 / 512, bias=EPS), reads=[bgss], writes=[bgss])
                            fw.op("dve", lambda e: e.reciprocal(out=gss[0:r, 2:3], in_=gss[0:r, 1:2]), reads=[bgss], writes=[bgss])
                            fw.op("dve", lambda e: e.scalar_tensor_tensor(out=VN[0:r, :], in0=GV[0:r, :], scalar=gss[0:r, 2:3], in1=Gv[0:r, :], op0=ALU.mult, op1=ALU.mult),
                                  reads=[bGV, bgss, bGv], writes=[bVN])
                            fw.op("act", lambda e: e.copy(out=VNb[0:r, :], in_=VN[0:r, :]), reads=[bVN], writes=[bVNb])
                            mm = pm[:, :].rearrange("p (g d) -> p g d", g=8)
                            if i < 16:
                                for g in range(8):
                                    fw.op("pe", lambda e, g=g: e.matmul(mm[:, g, :], lhsT=WST[:, g, :], rhs=VNb[:, g * 64:(g + 1) * 64], start=True, stop=True),
                                          reads=[bC, bVNb], writes=[bpm])
                                fw.op("dve", lambda e: e.tensor_tensor(out=Dm[:].rearrange("p (g d) -> p g d", g=8), in0=mm, in1=bc3(BT[:, :], 64), op=ALU.add),
                                      reads=[bpm, bC], writes=[bDm])
                                fw.op("dve", lambda e: e.tensor_tensor(out=Db[:], in0=Dm[:], in1=U[:], op=ALU.mult), reads=[bDm, bU], writes=[bDb])
                                for c4 in range(4):
                                    fw.op("pe", lambda e, c4=c4: e.transpose(out=pt[:, c4, :], in_=Db[:, c4 * 128:(c4 + 1) * 128], identity=ident[:]),
                                          reads=[bDb, b_ident], writes=[bpt])
                                fw.op("act", lambda e: e.copy(out=DT[:], in_=pt[:, 0:4, :]), reads=[bpt], writes=[bDT])
                            else:
                                fw.dma("sp", lambda e: e.dma_start(out=dr["sgu_vo"], in_=VN[0:64, :]), reads=[bVN])
                                fw.op("act", lambda e: e.copy(out=VNb1[:], in_=VN[32:64, :]), reads=[bVN], writes=[bVNb1])
                                fw.op("act", lambda e: e.copy(out=U1[:], in_=U[32:64, :]), reads=[bU], writes=[bU1])
                                for b in range(2):
                                    vsrc = VNb if b == 0 else VNb1
                                    usrc = U if b == 0 else U1
                                    for g in range(8):
                                        fw.op("pe", lambda e, g=g, vsrc=vsrc: e.matmul(mm[0:32, g, :], lhsT=WST[0:32, g, 0:32], rhs=vsrc[0:32, g * 64:(g + 1) * 64],
                                                                                       start=True, stop=True), reads=[bC, bVNb, bVNb1], writes=[bpm])
                                    fw.op("dve", lambda e: e.tensor_tensor(out=Dm[0:32, :].rearrange("p (g d) -> p g d", g=8), in0=mm[0:32], in1=bc3(BT[0:32, :], 64), op=ALU.add),
                                          reads=[bpm, bC], writes=[bDm])
                                    fw.op("dve", lambda e, usrc=usrc: e.tensor_tensor(out=Db[0:32, :], in0=Dm[0:32, :], in1=usrc[0:32, :], op=ALU.mult),
                                          reads=[bDm, bU, bU1], writes=[bDb])
                                    for c4 in range(4):
                                        fw.op("pe", lambda e, c4=c4: e.transpose(out=pt[:, c4, 0:32], in_=Db[0:32, c4 * 128:(c4 + 1) * 128], identity=ident[0:32, 0:32]),
                                              reads=[bDb, b_ident], writes=[bpt])
                                    fw.op("act", lambda e, b=b: e.copy(out=DT[:, :, 32 * b:32 * b + 32], in_=pt[:, 0:4, 0:32]), reads=[bpt], writes=[bDT])
                            for half in range(2):
                                for c4 in range(4):
                                    fw.op("pe", lambda e, c4=c4, half=half: e.matmul(po[half][0:r, :], lhsT=DT[:, c4, 0:r], rhs=WoS[:, c4, half * 512:(half + 1) * 512],
                                                                                    start=(c4 == 0), stop=(c4 == 3)), reads=[bDT, bWoS], writes=[bpo[half]])
                                fw.op("dve", lambda e, half=half: e.tensor_tensor(out=X[0:r, i, half * 512:(half + 1) * 512], in0=po[half][0:r, :],
                                                                                 in1=X[0:r, i, half * 512:(half + 1) * 512], op=ALU.add),
                                      reads=[bpo[half], bX[i]], writes=[bX[i]])
                        if cfg.get("sgu", True):
                            for i in range(NT):
                                sgu_tile(i)
                        fw.flush()
                    QT = sb(sx, "sQT", (64, 8, NTOK), BF16)
                    bQT = [Buf() for _ in range(NT)]
                    KT = sb(sx, "sKT", (64, 2, NTOK), BF16)
                    bKT = [Buf() for _ in range(NT)]
                    VA = sb(sx, "sVA", (128, NT, 128), BF16)
                    bVA = [Buf() for _ in range(NT)]
                    VAn = sb(sx, "sVAn", (32, 128), BF16)
                    bVAn = Buf()
                    with ExitStack() as s2:
                        Wq, bWq = load_w_bf(s2, wv[:, :, 0:512], (8, 512), "sWq", stage_cols=1024)
                        Wkv, bWkv = load_w_bf(s2, wv[:, :, 512:768], (8, 256), "sWkv", stage_cols=1024)
                        Gq, bGq = bcast_row(s2, dr["swa_q_norm"][j_:j_ + 1, :], 64, "sGq")
                        Gk, bGk = bcast_row(s2, dr["swa_k_norm"][j_:j_ + 1, :], 64, "sGk")
                        CS = sb(s2, "CS", (128, 2, NT, 8))
                        bCS = Buf()
                        fw.dma("sp", lambda e: e.dma_start(out=CS[:], in_=dr["c_rope"]), writes=[bCS])
                        pq = ps(s2, "spq", (128, 512))
                        pk = ps(s2, "spk", (128, 512))
                        ptk = ps(s2, "sptk", (64, 8, 128), BF16)
                        ptk2 = ps(s2, "sptk2", (64, 8, 128), BF16)
                        bpq, bpk, bptk, bptk2 = Buf(), Buf(), Buf(), Buf()
                        qf = sb(s2, "sqf", (128, 8, 64))
                        kf = [sb(s2, "skf%d" % i, (128, 2, 64)) for i in range(2)]
                        vf = [sb(s2, "svf%d" % i, (128, 128)) for i in range(2)]
                        qb = sb(s2, "sqb", (128, 8, 64), BF16)
                        kb_ = sb(s2, "skb", (128, 2, 64), BF16)
                        rt = sb(s2, "srt", (128, 4, 8, 8))
                        bqf, bqb, bkb, brt = Buf(), Buf(), Buf(), Buf()
                        bkf = [Buf(), Buf()]
                        bvf = [Buf(), Buf()]

                        def rope(t, bt, r, H, i):
                            cosb = CS[0:r, 0, i, :].unsqueeze(1).broadcast_to([r, H, 8])
                            sinb = CS[0:r, 1, i, :].unsqueeze(1).broadcast_to([r, H, 8])
                            x1, x2 = t[0:r, 0:H, 0:8], t[0:r, 0:H, 8:16]
                            fw.op("dve", lambda e: e.tensor_tensor(out=rt[0:r, 0, 0:H, :], in0=x1, in1=cosb, op=ALU.mult), reads=[bt, bCS], writes=[brt])
                            fw.op("dve", lambda e: e.tensor_tensor(out=rt[0:r, 1, 0:H, :], in0=x2, in1=sinb, op=ALU.mult), reads=[bt, bCS], writes=[brt])
                            fw.op("dve", lambda e: e.tensor_tensor(out=rt[0:r, 2, 0:H, :], in0=x2, in1=cosb, op=ALU.mult), reads=[bt, bCS], writes=[brt])
                            fw.op("dve", lambda e: e.tensor_tensor(out=rt[0:r, 3, 0:H, :], in0=x1, in1=sinb, op=ALU.mult), reads=[bt, bCS], writes=[brt])
                            fw.op("dve", lambda e: e.tensor_tensor(out=x1, in0=rt[0:r, 0, 0:H, :], in1=rt[0:r, 1, 0:H, :], op=ALU.subtract), reads=[brt], writes=[bt])
                            fw.op("dve", lambda e: e.tensor_tensor(out=x2, in0=rt[0:r, 2, 0:H, :], in1=rt[0:r, 3, 0:H, :], op=ALU.add), reads=[brt], writes=[bt])

                        def swa_tile(i):
                            r = tile_rows(i)
                            tk = slice(i * 128, i * 128 + r)
                            g5 = i // 4
                            jj = i % 2
                            for k in range(8):
                                fw.op("pe", lambda e, k=k: e.matmul(pq[0:r, :], lhsT=xT[:, k, tk], rhs=Wq[:, k, :], start=(k == 0), stop=(k == 7)),
                                      reads=[b_xT[g5], bWq], writes=[bpq])
                            for k in range(8):
                                fw.op("pe", lambda e, k=k: e.matmul(pk[0:r, 0:256], lhsT=xT[:, k, tk], rhs=Wkv[:, k, :], start=(k == 0), stop=(k == 7)),
                                      reads=[b_xT[g5], bWkv], writes=[bpk])
                            fw.op("act", lambda e: e.copy(out=vf[jj][0:r, :], in_=pk[0:r, 128:256]), reads=[bpk], writes=[bvf[jj]])
                            fw.op("dve", lambda e: e.tensor_copy(out=VA[0:r, i, :], in_=vf[jj][0:r, :]), reads=[bvf[jj]], writes=[bVA[i]])
                            rms_heads(s2, pq[0:r, :].rearrange("p (h d) -> p h d", h=8), bpq, r, 8, Gq, bGq, 1.0, qb[0:r], bqb, out_f=qf[0:r], bout_f=bqf)
                            rope(qf, bqf, r, 8, i)
                            fw.op("act", lambda e: e.activation(out=qb[0:r], in_=qf[0:r], func=AF.Copy, scale=0.125), reads=[bqf], writes=[bqb])
                            rms_heads(s2, pk[0:r, 0:128].rearrange("p (h d) -> p h d", h=2), bpk, r, 2, Gk, bGk, 1.0, kb_[0:r], bkb, out_f=kf[jj][0:r], bout_f=bkf[jj])
                            rope(kf[jj], bkf[jj], r, 2, i)
                            fw.op("act", lambda e: e.copy(out=kb_[0:r], in_=kf[jj][0:r]), reads=[bkf[jj]], writes=[bkb])
                            for h in range(8):
                                fw.op("pe", lambda e, h=h: e.transpose(out=ptk[:, h, 0:r], in_=qb[0:r, h, :], identity=ident[0:r, 0:r]), reads=[bqb, b_ident], writes=[bptk])
                            fw.op("act", lambda e: e.copy(out=QT[:, :, tk], in_=ptk[:, :, 0:r]), reads=[bptk], writes=[bQT[i]])
                            for h in range(2):
                                fw.op("pe", lambda e, h=h: e.transpose(out=ptk2[:, h, 0:r], in_=kb_[0:r, h, :], identity=ident[0:r, 0:r]), reads=[bkb, b_ident], writes=[bptk2])
                            fw.op("act", lambda e: e.copy(out=KT[:, :, tk], in_=ptk2[:, 0:2, 0:r]), reads=[bptk2], writes=[bKT[i]])
                            if i == 15:
                                fw.dma("sp", lambda e: e.dma_start(out=dr["swa_ko"][0], in_=kf[jj][:].rearrange("p h d -> p (h d)")), reads=[bkf[jj]])
                                fw.dma("sp", lambda e: e.dma_start(out=dr["swa_vo"][0], in_=vf[jj][:]), reads=[bvf[jj]])
                            if i == 16:
                                for b in range(2):
                                    fw.dma("sp", lambda e, b=b: e.dma_start(out=dr["swa_ko"][1 + b, 96:128, :],
                                                                            in_=kf[jj][32 * b:32 * b + 32].rearrange("p h d -> p (h d)")), reads=[bkf[jj]])
                                    fw.dma("sp", lambda e, b=b: e.dma_start(out=dr["swa_vo"][1 + b, 96:128, :], in_=vf[jj][32 * b:32 * b + 32, :]), reads=[bvf[jj]])
                                fw.op("dve", lambda e: e.tensor_copy(out=VAn[:], in_=VA[32:64, 16, :]), reads=[bVA[16]], writes=[bVAn])
                        for i in range(NT):
                            swa_tile(i)
                        fw.flush()
                    with ExitStack() as s3:
                        Wo, bWo = load_w_bf(s3, dr["od_w_out"][j_][0:512, :].rearrange("(h d) n -> d h n", d=64), (8, D), "sWo", stage_cols=1024)
                        SK, bSK = bcast_row(s3, dr["swa_sinks"][j_:j_ + 1, :], 8, "sSK")
                        fw.op("act", lambda e: e.activation(out=SK[:], in_=SK[:], func=AF.Exp), reads=[bSK], writes=[bSK])
                        HM = sb(s3, "HM", (128, 64), BF16)
                        bHM = Buf()
                        fw.op("pool", lambda e: e.memset(HM[0:64, :], 0.0), writes=[bHM])
                        fw.op("pool", lambda e: e.memset(HM[64:128, :], 1.0), writes=[bHM])
                        OTt = [sb(s3, "sOT%d" % i, (64, 8, 128), BF16) for i in range(2)]
                        bOTt = [Buf(), Buf()]
                        ck = sb(s3, "sck", (128, 2, 128))
                        cv = sb(s3, "scv", (128, 2, 128))
                        ckb = sb(s3, "sckb", (128, 2, 2, 64), BF16)
                        cvb = sb(s3, "scvb", (128, 2, 128), BF16)
                        KTc = sb(s3, "sKTc", (64, 2, 2, 128), BF16)
                        bck, bcv, bckb, bcvb, bKTc = Buf(), Buf(), Buf(), Buf(), Buf()
                        ptc = ps(s3, "sptc", (64, 8, 128), BF16)
                        bptc = Buf()
                        for b in range(2):
                            fw.dma("sp", lambda e, b=b: e.dma_start(out=ck[:, b, :], in_=dr["csk"][b]), writes=[bck])
                            fw.dma("sp", lambda e, b=b: e.dma_start(out=cv[:, b, :], in_=dr["csv"][b]), writes=[bcv])
                            fw.dma("sp", lambda e, b=b: e.dma_start(out=dr["swa_ko"][1 + b, 0:96, :], in_=ck[32:128, b, :]), reads=[bck])
                            fw.dma("sp", lambda e, b=b: e.dma_start(out=dr["swa_vo"][1 + b, 0:96, :], in_=cv[32:128, b, :]), reads=[bcv])
                        fw.op("pool", lambda e: e.tensor_copy(out=ckb[:].rearrange("p b n d -> p b (n d)"), in_=ck[:]), reads=[bck], writes=[bckb])
                        fw.op("pool", lambda e: e.tensor_copy(out=cvb[:], in_=cv[:]), reads=[bcv], writes=[bcvb])
                        for b in range(2):
                            for n_ in range(2):
                                fw.op("pe", lambda e, b=b, n_=n_: e.transpose(out=ptc[:, 2 * b + n_, :], in_=ckb[:, b, n_, :], identity=ident[:]),
                                      reads=[bckb, b_ident], writes=[bptc])
                        fw.op("act", lambda e: e.copy(out=KTc[:].rearrange("p b n k -> p (b n) k"), in_=ptc[:, 0:4, :]), reads=[bptc], writes=[bKTc])

                        def out_proj_tile(i, OT_t, bOT_t):
                            r = tile_rows(i)
                            po, bpo = C.ac_S, C.ac_bS
                            for half in range(2):
                                for h in range(8):
                                    fw.op("pe", lambda e, h=h, half=half: e.matmul(po[half][0:r, :], lhsT=OT_t[:, h, 0:r], rhs=Wo[0:64, h, half * 512:(half + 1) * 512],
                                                                                  start=(h == 0), stop=(h == 7)), reads=[bOT_t, bWo], writes=[bpo[half]])
                                fw.op("dve", lambda e, half=half: e.tensor_tensor(out=X[0:r, i, half * 512:(half + 1) * 512], in0=po[half][0:r, :],
                                                                                 in1=X[0:r, i, half * 512:(half + 1) * 512], op=ALU.add),
                                      reads=[bpo[half], bX[i]], writes=[bX[i]])

                        def attn_tile(m):
                            s_ = m % 2
                            for cc in range(2):
                                c = 2 * m + cc
                                for h in range(8):
                                    n_ = h // 4
                                    kbs = []
                                    if cc == 0:
                                        if m >= 1:
                                            kbs.append(dict(kT=KT[:, n_, (m - 1) * 128:m * 128], v=VA[:, m - 1, n_ * 64:(n_ + 1) * 64], nk=128, col0=0,
                                                            reads=[bKT[m - 1], bVA[m - 1]]))
                                        kbs.append(dict(kT=KT[:, n_, m * 128:m * 128 + 64], v=VA[0:64, m, n_ * 64:(n_ + 1) * 64], nk=64, col0=0,
                                                        reads=[bKT[m], bVA[m]]))
                                    else:
                                        if m >= 1:
                                            kbs.append(dict(kT=KT[:, n_, (m - 1) * 128:m * 128], v=VA[:, m - 1, n_ * 64:(n_ + 1) * 64], nk=128, col0=0,
                                                            reads=[bKT[m - 1], bVA[m - 1]], mask=HM[:], mask_reads=[bHM], mask_w=64))
                                        kbs.append(dict(kT=KT[:, n_, m * 128:(m + 1) * 128], v=VA[:, m, n_ * 64:(n_ + 1) * 64], nk=128, col0=0,
                                                        reads=[bKT[m], bVA[m]]))
                                    attn_core(s3, QT[:, h, c * 64:(c + 1) * 64], bQT[m], 64, kbs, OTt[s_][:, h, cc * 64:(cc + 1) * 64], bOTt[s_],
                                              extra_den=(SK[0:64, h:h + 1], [bSK]), nbuf=2)
                            out_proj_tile(m, OTt[s_], bOTt[s_])

                        for m in range(16):
                            attn_tile(m)
                        for b in range(2):
                            for h in range(8):
                                n_ = h // 4
                                q0 = 2048 + 32 * b
                                vnew = VA[0:32, 16, n_ * 64:(n_ + 1) * 64] if b == 0 else VAn[0:32, n_ * 64:(n_ + 1) * 64]
                                kbs = [dict(kT=KTc[:, b, n_, :], v=cvb[:, b, n_ * 64:(n_ + 1) * 64], nk=128, col0=0, reads=[bKTc, bcvb]),
                                       dict(kT=KT[:, n_, q0:q0 + 32], v=vnew, nk=32, col0=0, reads=[bKT[16], bVA[16], bVAn])]
                                attn_core(s3, QT[:, h, q0:q0 + 32], bQT[16], 32, kbs, OTt[0][:, h, 32 * b:32 * b + 32], bOTt[0],
                                          extra_den=(SK[0:64, h:h + 1], [bSK]), nbuf=2)
                        out_proj_tile(16, OTt[0], bOTt[0])
                        fw.flush()

        for l in range(cfg.get("depth", 2)):
            if cfg.get("ffn1", True):
                ffn(l, "ffn1")
            if cfg.get("mix", True) and l % 2 == 0 and not cfg.get("only_odd", False):
                mix_even(l)
            if cfg.get("mix", True) and l % 2 == 1:
                mix_odd(l)
            if cfg.get("xattn", True):
                xattn(l)
            if cfg.get("ffn2", True):
                ffn(l, "ffn2")

        yp_v = dr["y_prompt"].rearrange("(t p) d -> p t d", p=128)
        for g in range(4):
            fw.dma("sp", lambda e, g=g: e.dma_start(out=yp_v[:, 4 * g:4 * g + 4, :], in_=X[:, 4 * g:4 * g + 4, :]),
                   reads=bX[4 * g:4 * g + 4])
        fw.dma("sp", lambda e: e.dma_start(out=dr["y_sample"], in_=X[0:64, 16, :]), reads=[bX[16]])
        fw.flush()
    C.ninstr = fw.ninstr
    return nc, C


def make_consts():
    tri = np.triu(np.ones((128, 128), np.float32))
    trib = np.zeros((64, 64), np.float32)
    trib[0:32, 0:32] = tri[0:32, 0:32]
    trib[32:64, 32:64] = tri[0:32, 0:32]
    bd = np.zeros((128, 128), np.float32)
    bd[0:64, 0:64] = 1.0
    bd[64:128, 64:128] = 1.0
    t64 = np.triu(np.ones((64, 64), np.float32))
    msk = np.stack([np.triu(np.ones((64, 64), np.float32), 1), t64, np.tril(np.ones((64, 64), np.float32), -1)], 1)
    rst = np.ones((128, 256), np.float32)
    rst[:, ::64] = 0.0
    pos = np.zeros((128, NT), np.float32)
    for i in range(16):
        pos[:, i] = i * 128 + np.arange(128)
    pos[:, 16] = 2048 + (np.arange(128) % 32)
    inv_freq = np.power(np.float32(500000.0), -np.arange(8, dtype=np.float32) / np.float32(8)).astype(np.float32)
    ang = (pos[:, :, None] * inv_freq[None, None, :]).astype(np.float32)
    rope_t = np.ascontiguousarray(np.stack([np.cos(ang), np.sin(ang)], 1).astype(np.float32))
    return {"c_ident": np.eye(128, dtype=np.float32), "c_tri": tri, "c_trib": trib, "c_bd": bd, "c_rope": rope_t,
            "c_msk": np.ascontiguousarray(msk), "c_rst": rst}


def kernel(**inputs):
    cfg = inputs.pop("_cfg", {})
    inp = {k: np.asarray(v) for k, v in inputs.items()}
    nc, C = build_program(cfg)
    consts = make_consts()
    in_maps = []
    for c in range(8):
        m = dict(consts)
        m["xp"] = np.ascontiguousarray(inp["x_prompt"][c])
        m["xs"] = np.ascontiguousarray(inp["x_sample"][2 * c:2 * c + 2].reshape(64, D))
        for nm in ("ffn1", "ffn2"):
            for s in ("_norm", "_w_gate", "_w_up", "_w_down"):
                m[nm + s] = inp[nm + s]
        m["memp"] = np.ascontiguousarray(inp["mem_prompt"][c])
        m["cmk"] = np.ascontiguousarray(inp["cache_mem_k"][:, 2 * c:2 * c + 2].reshape(2, 2, 256, 256))
        m["cmv"] = np.ascontiguousarray(inp["cache_mem_v"][:, 2 * c:2 * c + 2].reshape(2, 2, 256, 256))
        for nm in ("mix_norm", "ev_w_in", "fox_b_f", "fox_k_norm", "fox_q_norm", "ev_w_out"):
            m[nm] = inp[nm]
        m["cfk"] = np.ascontiguousarray(inp["cache_fox_k"][0, 2 * c:2 * c + 2].reshape(2, 2048, 512))
        m["cfv"] = np.ascontiguousarray(inp["cache_fox_v"][0, 2 * c:2 * c + 2].reshape(2, 2048, 512))
        m["cfl"] = np.ascontiguousarray(inp["cache_fox_logf"][0, 2 * c:2 * c + 2])
        for nm in ("rwkv_mu", "rwkv_w0", "rwkv_a0", "rwkv_k_k", "rwkv_k_a", "rwkv_ln_g", "rwkv_ln_b", "rwkv_w2", "rwkv_a2", "rwkv_g2"):
            m[nm] = inp[nm]
        m["rwkv_r_k"] = np.ascontiguousarray(inp["rwkv_r_k"].reshape(1, 512))
        for nm in ("od_w_in", "od_w_out", "swa_q_norm", "swa_k_norm", "swa_sinks", "sgu_v_norm", "sgu_w_s", "sgu_b"):
            m[nm] = inp[nm]
        m["csk"] = np.ascontiguousarray(inp["cache_swa_k"][0, 2 * c:2 * c + 2].reshape(2, 128, 128))
        m["csv"] = np.ascontiguousarray(inp["cache_swa_v"][0, 2 * c:2 * c + 2].reshape(2, 128, 128))
        m["srw"] = np.ascontiguousarray(inp["state_rwkv"][0, 2 * c:2 * c + 2])
        m["srs"] = np.ascontiguousarray(inp["state_rwkv_shift"][0, 2 * c:2 * c + 2].reshape(2, 1792))
        for nm in ("xattn_norm", "mem_norm", "xattn_wq", "xattn_wkv", "xattn_q_norm", "xattn_k_norm", "xattn_wo"):
            m[nm] = inp[nm]
        in_maps.append(m)
    if cfg.get("_sim"):
        R = cfg["_sim"](nc, in_maps)
    else:
        res = run_bass_kernel_spmd(nc, in_maps, core_ids=list(range(8)))
        R = res.results
    y_prompt = np.stack([R[c]["y_prompt"] for c in range(8)], 0)
    y_sample = np.concatenate([R[c]["y_sample"].reshape(2, 32, D) for c in range(8)], 0)
    p_mem_k = np.stack([R[c]["p_mem_k"].reshape(2, 256, 4, 64) for c in range(8)], 1)
    p_mem_v = np.stack([R[c]["p_mem_v"].reshape(2, 256, 4, 64) for c in range(8)], 1)
    fk = np.stack([R[c]["fox_k"] for c in range(8)], 0)
    fv = np.stack([R[c]["fox_v"] for c in range(8)], 0)
    fl = np.stack([R[c]["fox_logf"] for c in range(8)], 0)
    rsft = np.stack([R[c]["rwkv_shift"] for c in range(8)], 0)
    p_fox_k = fk[:, :2048].reshape(1, 8, 2048, 8, 64)
    p_fox_v = fv[:, :2048].reshape(1, 8, 2048, 8, 64)
    p_fox_logf = fl[:, :2048].reshape(1, 8, 2048, 8)
    s_fox_k = fk[:, 2048:].reshape(1, 16, 32, 8, 64)
    s_fox_v = fv[:, 2048:].reshape(1, 16, 32, 8, 64)
    s_fox_logf = fl[:, 2048:].reshape(1, 16, 32, 8)
    p_rwkv_shift = rsft[:, 0].reshape(1, 8, 1, 1792)
    s_rwkv_shift = rsft[:, 1:3].reshape(1, 16, 1, 1792)
    rst_ = np.stack([R[c]["rwkv_state"] for c in range(8)], 0)
    p_rwkv_state = rst_[:, 0][None]
    s_rwkv_state = rst_[:, 1:3].reshape(1, 16, 8, 64, 64)
    sk = np.stack([R[c]["swa_ko"] for c in range(8)], 0)
    sv = np.stack([R[c]["swa_vo"] for c in range(8)], 0)
    p_swa_k = sk[:, 0].reshape(1, 8, 128, 2, 64)
    p_swa_v = sv[:, 0].reshape(1, 8, 128, 2, 64)
    s_swa_k = sk[:, 1:3].reshape(1, 16, 128, 2, 64)
    s_swa_v = sv[:, 1:3].reshape(1, 16, 128, 2, 64)
    s_sgu_v = np.stack([R[c]["sgu_vo"].reshape(2, 32, 512) for c in range(8)], 0).reshape(1, 16, 32, 512)
    f32 = np.float32
    z = lambda *sh: np.zeros(sh, f32)
    out = (y_prompt, y_sample, p_fox_k, p_fox_v, p_fox_logf, p_rwkv_state, p_rwkv_shift,
           p_swa_k, p_swa_v, p_mem_k, p_mem_v,
           s_fox_k, s_fox_v, s_fox_logf, s_rwkv_state, s_rwkv_shift,
           s_swa_k, s_swa_v, s_sgu_v)
    return tuple(np.ascontiguousarray(o, dtype=f32) for o in out)
```
